# Optimizing a Trainium2 kernel written in Bass

```python
import math
import jax, jax.numpy as jnp
from jax import lax
import numpy as np

D_MODEL = 1024
BATCH = 2
SEQ = 8192
DEPTH = 1
DEC_BATCH = 128
DEC_SEQ = 1
PAST_LEN = 2048
PAGE_SIZE = 128

SSM_HEAD_DIM = 64
D_SSM = D_MODEL
SSM_HEADS = D_SSM // SSM_HEAD_DIM
SSM_GROUPS = 4
SSM_STATE = 128
CONV_WIDTH = 4
CONV_DIM = D_SSM + 2 * SSM_GROUPS * SSM_STATE
SSD_CHUNK = 128
HEAD_DIM = 64
ATT_HEADS = D_MODEL // HEAD_DIM
KV_HEADS = ATT_HEADS // 4
Q_PER_KV = ATT_HEADS // KV_HEADS
D_ATT = ATT_HEADS * HEAD_DIM
MOBA_BLOCK = 256
MOBA_TOPK = 3
MOBA_Q_BLOCK = 32
ATT_SCALE = HEAD_DIM ** -0.5
ROPE_THETA = 10000.0
NORM_EPS = 1e-5
DEEPNORM_ALPHA = (2 * DEPTH) ** 0.25
DEEPNORM_BETA = (8 * DEPTH) ** -0.25
IN_SPLITS = (D_SSM, D_SSM, SSM_GROUPS * SSM_STATE, SSM_GROUPS * SSM_STATE, SSM_HEADS,
             D_ATT, KV_HEADS * HEAD_DIM, KV_HEADS * HEAD_DIM, D_ATT, D_MODEL, D_MODEL)
D_IN_PROJ = sum(IN_SPLITS)
SPLIT_POINTS = tuple(int(p) for p in np.cumsum(IN_SPLITS)[:-1])

kernel_name = 'hybrid_ssd_moba_deepnorm_step'


def _layer_norm(h, g, b):
    hf = h.astype(jnp.float32)
    mu = jnp.mean(hf, axis=-1, keepdims=True)
    var = jnp.mean(jnp.square(hf - mu), axis=-1, keepdims=True)
    out = (hf - mu) * lax.rsqrt(var + NORM_EPS) * g.astype(jnp.float32) + b.astype(jnp.float32)
    return out.astype(h.dtype)


def _gated_rmsnorm(y, z, w):
    g = y * jax.nn.silu(z.astype(jnp.float32))
    gg = g.reshape(g.shape[:-1] + (SSM_GROUPS, D_SSM // SSM_GROUPS))
    gg = gg * lax.rsqrt(jnp.mean(jnp.square(gg), axis=-1, keepdims=True) + NORM_EPS)
    return gg.reshape(g.shape) * w.astype(jnp.float32)


def _rope(x, pos):
    half = x.shape[-1] // 2
    inv_freq = jnp.power(ROPE_THETA, -jnp.arange(half, dtype=jnp.float32) * 2.0 / x.shape[-1])
    ang = pos.astype(jnp.float32)[:, None] * inv_freq[None, :]
    cos = jnp.cos(ang)[None, :, None, :]
    sin = jnp.sin(ang)[None, :, None, :]
    xf = x.astype(jnp.float32)
    x1, x2 = xf[..., :half], xf[..., half:]
    return jnp.concatenate([x1 * cos - x2 * sin, x2 * cos + x1 * sin], axis=-1).astype(x.dtype)


def _causal_conv(xbc, prev, w, b):
    full = jnp.concatenate([prev.astype(xbc.dtype), xbc], axis=1)
    y = lax.conv_general_dilated(full, w[:, None, :].astype(xbc.dtype), window_strides=(1,), padding='VALID',
                                 dimension_numbers=('NWC', 'WIO', 'NWC'), feature_group_count=xbc.shape[-1])
    return jax.nn.silu(y + b), full[:, full.shape[1] - (CONV_WIDTH - 1):]


def _ssd_scan(x, dt, a, bm, cm, h0, chunk):
    f32 = jnp.float32
    n, l, nh, hp = x.shape
    g, s = bm.shape[2], bm.shape[3]
    r = nh // g
    nc = -(-l // chunk)
    pad = nc * chunk - l
    pw = ((0, 0), (0, pad))
    xdt = jnp.pad(x.astype(f32) * dt[..., None], pw + ((0, 0), (0, 0))).reshape(n, nc, chunk, g, r, hp)
    adt = jnp.pad(dt * a, pw + ((0, 0),)).reshape(n, nc, chunk, g, r)
    bc = jnp.pad(bm.astype(f32), pw + ((0, 0), (0, 0))).reshape(n, nc, chunk, g, s)
    cc = jnp.pad(cm.astype(f32), pw + ((0, 0), (0, 0))).reshape(n, nc, chunk, g, s)
    acs = jnp.cumsum(adt, axis=2)
    causal = jnp.tril(jnp.ones((chunk, chunk), dtype=bool))[None, None, :, :, None, None]
    seg = acs[:, :, :, None] - acs[:, :, None, :]
    decay = jnp.exp(jnp.where(causal, seg, -jnp.inf))
    cb = jnp.einsum('ncigs,ncjgs->ncijg', cc, bc)
    y_diag = jnp.einsum('ncijg,ncijgr,ncjgrp->ncigrp', cb, decay, xdt)
    to_end = jnp.exp(acs[:, :, -1:] - acs)
    states = jnp.einsum('ncjgs,ncjgr,ncjgrp->ncgrps', bc, to_end, xdt)
    chunk_decay = jnp.exp(acs[:, :, -1])

    def step(h, inp):
        st, cd = inp
        return cd[..., None, None] * h + st, h

    h_last, h_in = lax.scan(step, h0.astype(f32).reshape(n, g, r, hp, s),
                            (jnp.moveaxis(states, 1, 0), jnp.moveaxis(chunk_decay, 1, 0)))
    h_in = jnp.moveaxis(h_in, 0, 1)
    y_off = jnp.einsum('ncigs,ncgrps,ncigr->ncigrp', cc, h_in, jnp.exp(acs))
    y = (y_diag + y_off).reshape(n, nc * chunk, nh, hp)[:, :l]
    return y, h_last.reshape(n, nh, hp, s)


def _moba_chunk(q, qpos, kblk, vblk, kmean):
    n, nq = q.shape[0], q.shape[1]
    nb = kblk.shape[1]
    qblk = qpos[0] // MOBA_BLOCK
    qg = q.reshape(n, nq, KV_HEADS, Q_PER_KV, HEAD_DIM)
    s_blk = jnp.einsum('nqkgd,nbkd->nqkgb', qg.astype(jnp.float32), kmean).reshape(n, nq, ATT_HEADS, nb)
    s_blk = jnp.where(jnp.arange(nb) < qblk, s_blk, -jnp.inf)
    kk = min(MOBA_TOPK, nb)
    _, idx = lax.top_k(s_blk, kk)
    n_i = jnp.arange(n)[:, None, None, None]
    kv_i = (jnp.arange(ATT_HEADS) // Q_PER_KV)[None, None, :, None]
    ksel = kblk[n_i, idx, :, kv_i]
    vsel = vblk[n_i, idx, :, kv_i]
    s_sel = jnp.einsum('nqhd,nqhcjd->nqhcj', q, ksel).reshape(n, nq, ATT_HEADS, kk * MOBA_BLOCK)
    kown = lax.dynamic_index_in_dim(kblk, qblk, axis=1, keepdims=False)
    vown = lax.dynamic_index_in_dim(vblk, qblk, axis=1, keepdims=False)
    s_own = jnp.einsum('nqkgd,njkd->nqkgj', qg, kown).reshape(n, nq, ATT_HEADS, MOBA_BLOCK)
    sel_ok = jnp.broadcast_to(jnp.repeat(jnp.arange(kk) < qblk, MOBA_BLOCK)[None, :], (nq, kk * MOBA_BLOCK))
    own_ok = (qblk * MOBA_BLOCK + jnp.arange(MOBA_BLOCK))[None, :] <= qpos[:, None]
    mask = jnp.concatenate([sel_ok, own_ok], axis=-1)[None, :, None, :]
    s = jnp.concatenate([s_sel, s_own], axis=-1).astype(jnp.float32) * ATT_SCALE
    p = jax.nn.softmax(jnp.where(mask, s, -jnp.inf), axis=-1).astype(vblk.dtype)
    p_sel = p[..., :kk * MOBA_BLOCK].reshape(n, nq, ATT_HEADS, kk, MOBA_BLOCK)
    p_own = p[..., kk * MOBA_BLOCK:].reshape(n, nq, KV_HEADS, Q_PER_KV, MOBA_BLOCK)
    o_sel = jnp.einsum('nqhcj,nqhcjd->nqhd', p_sel, vsel)
    o_own = jnp.einsum('nqkgj,njkd->nqkgd', p_own, vown).reshape(n, nq, ATT_HEADS, HEAD_DIM)
    return o_sel + o_own


def _moba_attention(q, qpos, k, v, q_block):
    n, lk = k.shape[0], k.shape[1]
    nb = -(-lk // MOBA_BLOCK)
    pw = ((0, 0), (0, nb * MOBA_BLOCK - lk), (0, 0), (0, 0))
    kblk = jnp.pad(k, pw).reshape(n, nb, MOBA_BLOCK, KV_HEADS, HEAD_DIM)
    vblk = jnp.pad(v, pw).reshape(n, nb, MOBA_BLOCK, KV_HEADS, HEAD_DIM)
    kmean = jnp.mean(kblk.astype(jnp.float32), axis=2)
    lq = q.shape[1]
    nc = lq // q_block
    qs = jnp.swapaxes(q.reshape(n, nc, q_block, ATT_HEADS, HEAD_DIM), 0, 1)
    ps = qpos.reshape(nc, q_block)
    out = lax.map(lambda a: _moba_chunk(a[0], a[1], kblk, vblk, kmean), (qs, ps))
    return jnp.swapaxes(out, 0, 1).reshape(n, lq, ATT_HEADS, HEAD_DIM)


def _mixer_layer(x, pos, conv_prev, ssm_prev, k_past, v_past, q_block,
                 w_in, conv_w, conv_b, dt_bias, a_log, d_skip, ssm_norm_w, w_a_out, w_b_out, w_out, ln_g, ln_b):
    n, l, _ = x.shape
    u = x @ w_in
    z_a, x_a, b_a, c_a, dt_a, q, k, v, z_b, g_a, g_b = jnp.split(u, SPLIT_POINTS, axis=-1)
    xbc, conv_new = _causal_conv(jnp.concatenate([x_a, b_a, c_a], axis=-1), conv_prev, conv_w, conv_b)
    xs, bm, cm = jnp.split(xbc, (D_SSM, D_SSM + SSM_GROUPS * SSM_STATE), axis=-1)
    xs = xs.reshape(n, l, SSM_HEADS, SSM_HEAD_DIM)
    bm = bm.reshape(n, l, SSM_GROUPS, SSM_STATE)
    cm = cm.reshape(n, l, SSM_GROUPS, SSM_STATE)
    dt = jax.nn.softplus(dt_a.astype(jnp.float32) + dt_bias.astype(jnp.float32))
    a = -jnp.exp(a_log.astype(jnp.float32))
    ys, ssm_new = _ssd_scan(xs, dt, a, bm, cm, ssm_prev, min(SSD_CHUNK, l))
    ys = ys + d_skip.astype(jnp.float32)[:, None] * xs.astype(jnp.float32)
    ya = _gated_rmsnorm(ys.reshape(n, l, D_SSM), z_a, ssm_norm_w).astype(x.dtype)
    br_a = ya @ w_a_out
    q = _rope(q.reshape(n, l, ATT_HEADS, HEAD_DIM), pos)
    k = _rope(k.reshape(n, l, KV_HEADS, HEAD_DIM), pos)
    v = v.reshape(n, l, KV_HEADS, HEAD_DIM)
    if k_past is None:
        k_all, v_all = k, v
    else:
        k_all = jnp.concatenate([k_past.astype(k.dtype), k], axis=1)
        v_all = jnp.concatenate([v_past.astype(v.dtype), v], axis=1)
    o = _moba_attention(q, pos, k_all, v_all, q_block).reshape(n, l, D_ATT)
    br_b = (o * jax.nn.silu(z_b)) @ w_b_out
    mixed = jax.nn.sigmoid(g_a) * br_a + jax.nn.sigmoid(g_b) * br_b
    y = _layer_norm(DEEPNORM_ALPHA * x + mixed @ w_out, ln_g, ln_b)
    return y, k, v, conv_new, ssm_new.astype(x.dtype)


def setup_inputs(seed: int = 0) -> dict:
    key = jax.random.key(seed)
    ks = jax.random.split(key, 20)
    f32 = jnp.float32
    n_pages = PAST_LEN // PAGE_SIZE
    n_used = DEC_BATCH * n_pages
    n_pool = n_used + max(1, n_used // 4)

    def nrm(k, shape, scale):
        return scale * jax.random.normal(k, shape, f32)

    x_prompt = nrm(ks[0], (BATCH, SEQ, D_MODEL), 1.0)
    x_sample = nrm(ks[1], (DEC_BATCH, DEC_SEQ, D_MODEL), 1.0)
    cache_k = nrm(ks[2], (DEPTH, n_pool, PAGE_SIZE, KV_HEADS, HEAD_DIM), 1.0)
    cache_v = nrm(ks[3], (DEPTH, n_pool, PAGE_SIZE, KV_HEADS, HEAD_DIM), 1.0)
    state_conv = nrm(ks[4], (DEPTH, DEC_BATCH, CONV_WIDTH - 1, CONV_DIM), 1.0)
    state_ssm = nrm(ks[5], (DEPTH, DEC_BATCH, SSM_HEADS, SSM_HEAD_DIM, SSM_STATE), 0.5)
    page_table = jax.random.permutation(ks[6], n_pool)[:n_used].reshape(DEC_BATCH, n_pages).astype(jnp.int32)
    w_in = nrm(ks[7], (DEPTH, D_MODEL, D_IN_PROJ), D_MODEL ** -0.5)
    conv_w = nrm(ks[8], (DEPTH, CONV_WIDTH, CONV_DIM), CONV_WIDTH ** -0.5)
    conv_b = nrm(ks[9], (DEPTH, CONV_DIM), 0.02)
    dt0 = jnp.exp(jax.random.uniform(ks[10], (DEPTH, SSM_HEADS), f32, math.log(1e-3), math.log(1e-1)))
    dt_bias = dt0 + jnp.log(-jnp.expm1(-dt0))
    a_log = jnp.log(jax.random.uniform(ks[11], (DEPTH, SSM_HEADS), f32, 1.0, 16.0))
    d_skip = 1.0 + nrm(ks[12], (DEPTH, SSM_HEADS), 0.1)
    ssm_norm_w = 1.0 + nrm(ks[13], (DEPTH, D_SSM), 0.05)
    w_a_out = nrm(ks[14], (DEPTH, D_SSM, D_MODEL), DEEPNORM_BETA * D_SSM ** -0.5)
    w_b_out = nrm(ks[15], (DEPTH, D_ATT, D_MODEL), DEEPNORM_BETA * D_ATT ** -0.5)
    w_out = nrm(ks[16], (DEPTH, D_MODEL, D_MODEL), DEEPNORM_BETA * D_MODEL ** -0.5)
    ln_g = 1.0 + nrm(ks[17], (DEPTH, D_MODEL), 0.05)
    ln_b = nrm(ks[18], (DEPTH, D_MODEL), 0.02)
    return {'x_prompt': x_prompt, 'x_sample': x_sample, 'cache_k': cache_k, 'cache_v': cache_v,
            'state_conv': state_conv, 'state_ssm': state_ssm, 'page_table': page_table,
            'w_in': w_in, 'conv_w': conv_w, 'conv_b': conv_b, 'dt_bias': dt_bias, 'a_log': a_log,
            'd_skip': d_skip, 'ssm_norm_w': ssm_norm_w, 'w_a_out': w_a_out, 'w_b_out': w_b_out,
            'w_out': w_out, 'ln_g': ln_g, 'ln_b': ln_b}


def reference(x_prompt, x_sample, cache_k, cache_v, state_conv, state_ssm, page_table,
              w_in, conv_w, conv_b, dt_bias, a_log, d_skip, ssm_norm_w, w_a_out, w_b_out, w_out, ln_g, ln_b):
    n_p, l_p = x_prompt.shape[0], x_prompt.shape[1]
    n_s, l_s = x_sample.shape[0], x_sample.shape[1]
    past_len = page_table.shape[1] * cache_k.shape[2]
    pos_p = jnp.arange(l_p, dtype=jnp.int32)
    pos_s = past_len + jnp.arange(l_s, dtype=jnp.int32)
    xp, xs = x_prompt, x_sample
    kp, vp, ksm, vsm, cp, cs, sp, ss = [], [], [], [], [], [], [], []
    for i in range(DEPTH):
        params = (w_in[i], conv_w[i], conv_b[i], dt_bias[i], a_log[i], d_skip[i], ssm_norm_w[i],
                  w_a_out[i], w_b_out[i], w_out[i], ln_g[i], ln_b[i])
        conv0 = jnp.zeros((n_p, CONV_WIDTH - 1, CONV_DIM), xp.dtype)
        ssm0 = jnp.zeros((n_p, SSM_HEADS, SSM_HEAD_DIM, SSM_STATE), xp.dtype)
        xp, k_new, v_new, c_new, s_new = _mixer_layer(xp, pos_p, conv0, ssm0, None, None, MOBA_Q_BLOCK, *params)
        kp.append(k_new); vp.append(v_new); cp.append(c_new); sp.append(s_new)
        past_k = cache_k[i][page_table].reshape(n_s, past_len, KV_HEADS, HEAD_DIM)
        past_v = cache_v[i][page_table].reshape(n_s, past_len, KV_HEADS, HEAD_DIM)
        xs, k_new, v_new, c_new, s_new = _mixer_layer(xs, pos_s, state_conv[i], state_ssm[i], past_k, past_v, 1, *params)
        ksm.append(k_new); vsm.append(v_new); cs.append(c_new); ss.append(s_new)
    return (xp, xs, jnp.stack(kp), jnp.stack(vp), jnp.stack(ksm), jnp.stack(vsm),
            jnp.stack(cp), jnp.stack(cs), jnp.stack(sp), jnp.stack(ss))
```

```python
import contextlib
import os

import numpy as np

import concourse.bass as bass
import concourse.mybir as mybir
from concourse.bass_utils import run_bass_kernel_spmd

F32 = mybir.dt.float32
BF16 = mybir.dt.bfloat16
I32 = mybir.dt.int32
AF = mybir.ActivationFunctionType
ALU = mybir.AluOpType
AX = mybir.AxisListType

D = 1024
SEQ = 8192
NCORE = 8
DIN = 7696
C_ZA, C_XA, C_B, C_C, C_DT, C_Q, C_K, C_V, C_ZB, C_GA, C_GB = (
    0, 1024, 2048, 2560, 3072, 3088, 4112, 4368, 4624, 5648, 6672)
EPS = 1e-5
ALPHA = 2.0 ** 0.25
BIG = 30000.0
SCALE = 0.125
NS = 16
PAST = 2048
NPG = 16


class Buf:
    def __init__(self, name, t):
        self.name = name
        self.t = t
        self.w = None
        self.r = []
        self.dsem = None
        self.dcnt = 0

    def __getitem__(self, k):
        return self.t[k]


class View:
    def __init__(self, name, parent, t):
        self.name = name
        self.parent = parent
        self.t = t

    def __getitem__(self, k):
        return self.t[k]

    @property
    def w(self):
        return self.parent.w

    @w.setter
    def w(self, v):
        self.parent.w = v

    @property
    def r(self):
        return self.parent.r

    @r.setter
    def r(self, v):
        self.parent.r = v


class Prog:
    ENG = ("sp", "pe", "act", "dve", "pool")

    def __init__(self, nc, stack):
        self.nc = nc
        self.stack = stack
        self.q = {e: [] for e in self.ENG}
        self.cnt = {e: 0 for e in self.ENG}
        self.seen = {e: {} for e in self.ENG}
        self.esem = {}
        for e in ("pe", "act", "dve", "pool"):
            self.esem[e] = stack.enter_context(nc.semaphore("es_" + e))
        self.nsem = 4
        self.out_tokens = []
        self.dbufs = []
        self.n = 0
        self.stop = int(os.environ.get("MK_STOP", "1000000000"))
        self.verbose = bool(os.environ.get("MK_VERBOSE"))

    def _skip(self, what):
        self.n += 1
        if self.verbose:
            import traceback
            fr = traceback.extract_stack()[-3]
            print("OP %d %s line %d" % (self.n, what, fr.lineno))
        return self.n > self.stop

    def sb(self, name, shape, dt, stack=None):
        t = (stack or self.stack).enter_context(self.nc.sbuf_tensor("s_" + name, list(shape), dt))
        return Buf(name, t)

    def ps(self, name, stack=None):
        t = (stack or self.stack).enter_context(self.nc.psum_tensor("p_" + name, [128, 512], F32))
        return Buf(name, t)

    def view(self, name, bank, ap):
        return View(name, bank, ap)

    def _dsem(self, b):
        if b.dsem is None:
            b.dsem = self.stack.enter_context(self.nc.semaphore("ds%d" % self.nsem))
            self.nsem += 1
            self.dbufs.append(b)
        return b.dsem

    def barrier(self):
        if self.n > self.stop:
            return
        toks = [(self.esem[e], self.cnt[e]) for e in ("pe", "act", "dve", "pool") if self.cnt[e] > 0]
        toks += [(b.dsem, b.dcnt) for b in self.dbufs if b.dcnt > 0]
        for eng in self.ENG:
            seen = self.seen[eng]
            waits = []
            for sem, val in toks:
                if seen.get(sem, 0) < val:
                    seen[sem] = val
                    waits.append((sem, val))
            self._wait(eng, waits)

    def _deps(self, eng, reads, writes):
        toks = []
        for b in reads:
            if b.w is not None:
                toks.append(b.w)
        for b in writes:
            if b.w is not None:
                toks.append(b.w)
            toks.extend(b.r)
        need = {}
        for sem, val in toks:
            if need.get(sem, 0) < val:
                need[sem] = val
        out = []
        seen = self.seen[eng]
        for sem, val in need.items():
            if seen.get(sem, 0) < val:
                seen[sem] = val
                out.append((sem, val))
        return out

    def _commit(self, tok, reads, writes):
        for b in reads:
            b.r.append(tok)
        for b in writes:
            b.w = tok
            b.r = []

    def _eng(self, eng):
        nc = self.nc
        return {"sp": nc.sync, "pe": nc.tensor, "act": nc.scalar, "dve": nc.vector, "pool": nc.gpsimd}[eng]

    def _wait(self, eng, waits):
        e = self._eng(eng)
        for sem, val in waits:
            if eng == "pe" and sem is self.esem["pe"]:
                continue
            e.wait_ge(sem, val)

    def op(self, eng, fn, reads=(), writes=()):
        if self._skip(eng):
            return None
        waits = self._deps(eng, reads, writes)
        self._wait(eng, waits)
        ins = fn(self._eng(eng))
        ins.then_inc(self.esem[eng], 1)
        if self.verbose and self.n in (62, 69):
            print("INS", self.n, str(ins))
        self.cnt[eng] += 1
        tok = (self.esem[eng], self.cnt[eng])
        self._commit(tok, reads, writes)
        return tok

    def dma(self, eng, out, in_, buf, reads=(), writes=(), is_out=False, **kw):
        if self._skip("dma"):
            return None
        waits = self._deps(eng, reads, writes)
        self._wait(eng, waits)
        sem = self._dsem(buf)
        buf.dcnt += 16
        tok = (sem, buf.dcnt)
        self._eng(eng).dma_start(out=out, in_=in_, **kw).then_inc(sem, 16)
        self._commit(tok, reads, writes)
        if is_out:
            self.out_tokens.append(tok)
        return tok

    def idma(self, out, in_, idx_ap, buf, reads=(), writes=(), bound=None):
        waits = self._deps("pool", reads, writes)
        self._wait("pool", waits)
        sem = self._dsem(buf)
        buf.dcnt += 16
        tok = (sem, buf.dcnt)
        self.nc.gpsimd.indirect_dma_start(
            out=out, out_offset=None, in_=in_,
            in_offset=bass.IndirectOffsetOnAxis(ap=idx_ap, axis=0),
            bounds_check=bound, oob_is_err=False).then_inc(sem, 16)
        self._commit(tok, reads, writes)
        return tok

    def emit(self):
        need = {}
        for sem, val in self.out_tokens:
            if need.get(sem, 0) < val:
                need[sem] = val
        for sem, val in need.items():
            self.nc.sync.wait_ge(sem, val)


def tr(e, out, in_, ident):
    return e.matmul(out, lhsT=in_, rhs=ident, start=True, stop=True)


def _bc(ap, shape):
    return ap.to_broadcast(list(shape))


def build(NI=16, SAMPLE=True):
    PASSES = os.environ.get('MK_PASSES', 'sam')
    SAMPLE = SAMPLE and not os.environ.get('MK_NOSAMPLE')
    NG = int(os.environ.get('MK_NG', '4'))
    nc = bass.Bass("TRN2", target_bir_lowering=False)
    dt = nc.dram_tensor
    xT = dt("xT", [D, SEQ], F32, kind="ExternalInput").ap()
    xoT = dt("xoT", [D, 16 * 131], F32, kind="ExternalInput").ap()
    xmT = dt("xmT", [D, 2048], F32, kind="ExternalInput").ap()
    xown = dt("xown", [2048, D], F32, kind="ExternalInput").ap()
    w_in = dt("w_in", [D, DIN], F32, kind="ExternalInput").ap()
    w_a = dt("w_a", [D, D], F32, kind="ExternalInput").ap()
    w_b = dt("w_b", [D, D], F32, kind="ExternalInput").ap()
    w_o = dt("w_o", [D, D], F32, kind="ExternalInput").ap()
    cwT = dt("cwT", [2048, 4], F32, kind="ExternalInput").ap()
    cbT = dt("cbT", [2048, 1], F32, kind="ExternalInput").ap()
    vec16 = dt("vec16", [3, 16], F32, kind="ExternalInput").ap()
    nwv = dt("nwv", [1, D], F32, kind="ExternalInput").ap()
    lng = dt("lng", [1, D], F32, kind="ExternalInput").ap()
    lnb = dt("lnb", [1, D], F32, kind="ExternalInput").ap()
    tri_d = dt("tri", [128, 128], F32, kind="ExternalInput").ap()
    idn_d = dt("idn", [128, 128], F32, kind="ExternalInput").ap()
    oh_d = dt("oh", [128, 4], F32, kind="ExternalInput").ap()
    cs_all = dt("cs_all", [SEQ, 64], F32, kind="ExternalInput").ap()
    cs_own = dt("cs_own", [2048, 64], F32, kind="ExternalInput").ap()
    kbi_d = dt("kbi", [32, SEQ], F32, kind="ExternalInput").ap()
    cm_d = dt("cm", [128, 4 * 512], F32, kind="ExternalInput").ap()
    el_d = dt("el", [3, 16 * 32], F32, kind="ExternalInput").ap()
    xsT = dt("xsT", [D, NS], F32, kind="ExternalInput").ap()
    xs_tok = dt("xs_tok", [NS, D], F32, kind="ExternalInput").ap()
    sc_d = dt("sc_d", [NS, 3 * 2048], F32, kind="ExternalInput").ap()
    ssm_d = dt("ssm_d", [NS, 1024 * 128], F32, kind="ExternalInput").ap()
    pt_d = dt("pt_d", [1, NS * NPG], I32, kind="ExternalInput").ap()
    iota_d = dt("iota_d", [128, 1], F32, kind="ExternalInput").ap()
    ck_d = dt("ck_d", [2560 * 128, 256], F32, kind="ExternalInput").ap()
    cv_d = dt("cv_d", [2560 * 128, 256], F32, kind="ExternalInput").ap()
    cwrow = dt("cwrow", [4, 2048], F32, kind="ExternalInput").ap()
    cbrow = dt("cbrow", [1, 2048], F32, kind="ExternalInput").ap()
    sel_d = dt("sel_d", [NS, NS * 128], F32, kind="ExternalInput").ap()
    selt_d = dt("selt_d", [128, NS * NS], F32, kind="ExternalInput").ap()
    css_d = dt("css_d", [1, 64], F32, kind="ExternalInput").ap()
    ys_o = dt("ys_o", [NS, D], F32, kind="ExternalOutput").ap()
    ks_o = dt("ks_o", [NS, 256], F32, kind="ExternalOutput").ap()
    vs_o = dt("vs_o", [NS, 256], F32, kind="ExternalOutput").ap()
    cvs_o = dt("cvs_o", [NS, 3 * 2048], F32, kind="ExternalOutput").ap()
    ssms_o = dt("ssms_o", [NS, 1024 * 128], F32, kind="ExternalOutput").ap()
    y_own = dt("y_own", [2048, D], F32, kind="ExternalOutput").ap()
    kp_o = dt("kp_o", [SEQ, 256], F32, kind="ExternalOutput").ap()
    vp_o = dt("vp_o", [SEQ, 256], F32, kind="ExternalOutput").ap()
    cvA_o = dt("cvA_o", [4, 128, 3, 3], F32, kind="ExternalOutput").ap()
    cvC_o = dt("cvC_o", [4, 128, 3], F32, kind="ExternalOutput").ap()
    ssm_o = dt("ssm_o", [4, 128, 256], F32, kind="ExternalOutput").ap()

    DBG = bool(os.environ.get("MK_DBG"))
    if DBG:
        dbg_ya = dt("dbg_ya", [128, 8 * 2048], BF16, kind="ExternalOutput").ap()
        dbg_ob = dt("dbg_ob", [128, 8 * 2048], BF16, kind="ExternalOutput").ap()
    with contextlib.ExitStack() as top:
        P = Prog(nc, top)
        tri = P.sb("tri", [128, 128], F32)
        idn = P.sb("idnf", [128, 128], F32)
        idnb = P.sb("idnb", [128, 128], BF16)
        onesf = P.sb("onesf", [128, 128], F32)
        oh = P.sb("oh", [128, 4], F32)
        WS, XB, XOl = [], [], []

        def alloc_io(st, tag, nws=2, nxb=2):
            WS[:] = [P.sb("WS%s%d" % (tag, i), [128, 8, 512], F32, st) for i in range(nws)]
            XB[:] = [P.sb("XB%s%d" % (tag, i), [128, 8, 512], BF16, st) for i in range(nxb)]
            XOl[:] = [P.sb("XO%s" % tag, [128, 8, 131], BF16, st)] if nxb else []
        banks = [P.ps("bank%d" % i) for i in range(8)]

        P.dma("sp", tri[:], tri_d, tri, writes=[tri])
        P.dma("sp", idn[:], idn_d, idn, writes=[idn])
        P.dma("sp", oh[:], oh_d, oh, writes=[oh])
        P.op("pool", lambda e: e.tensor_copy(out=idnb[:], in_=idn[:]), reads=[idn], writes=[idnb])
        P.op("pool", lambda e: e.memset(onesf[:], 1.0), writes=[onesf])
        epsb = P.sb("epsb", [128, 1], F32)
        P.op("pool", lambda e: e.memset(epsb[:], EPS), writes=[epsb])

        stg = [0]

        def load_w(dst, off, src_ap, n):
            s = WS[stg[0] % len(WS)]
            stg[0] += 1
            P.dma("sp", s[:, :, 0:n], src_ap.rearrange("(kc p) c -> p kc c", p=128), s, writes=[s])
            P.op("pool", lambda e: e.tensor_copy(out=dst[:, :, off:off + n], in_=s[:, :, 0:n]),
                 reads=[s], writes=[dst])

        def load_x(i):
            s = WS[stg[0] % len(WS)]
            xb = XB[stg[0] % len(XB)]
            stg[0] += 1
            P.dma("sp", s[:], xT[:, i * 512:(i + 1) * 512].rearrange("(kc p) t -> p kc t", p=128), s, writes=[s])
            P.op("pool", lambda e: e.tensor_copy(out=xb[:], in_=s[:]), reads=[s], writes=[xb])
            return xb

        def load_xo(i):
            s = WS[stg[0] % len(WS)]
            stg[0] += 1
            P.dma("sp", s[:, :, 0:131], xoT[:, i * 131:(i + 1) * 131].rearrange("(kc p) t -> p kc t", p=128),
                  s, writes=[s])
            XO = XOl[0]
            P.op("pool", lambda e: e.tensor_copy(out=XO[:], in_=s[:, :, 0:131]), reads=[s], writes=[XO])

        def mm_chain(ps_ap, lhs_fn, rhs_fn, nk=8):
            def fn(e):
                ins = None
                for kc in range(nk):
                    ins = e.matmul(ps_ap, lhsT=lhs_fn(kc), rhs=rhs_fn(kc), start=(kc == 0), stop=(kc == nk - 1))
                return ins
            return fn


        if SAMPLE:
            with contextlib.ExitStack() as st:
                alloc_io(st, "s", nws=1, nxb=0)
                XST = P.sb("XST", [128, 8, NS], F32, st)
                Us = P.sb("Us", [NS, DIN], F32, st)
                BIGS = P.sb("BIGS", [128, 6144], F32, st)
                SC = View("SC", BIGS, BIGS[0:NS, :].rearrange("p (a c) -> p a c", a=3))
                CWk = P.sb("CWk", [NS, 2048], F32, st)
                cacs = P.sb("cacs", [NS, 2048], F32, st)
                ctmp = P.sb("ctmp", [NS, 2048], F32, st)
                xbc = P.sb("xbc", [NS, 2048], F32, st)
                v16s = P.sb("v16s", [NS, 3, 16], F32, st)
                As = P.sb("As", [NS, 16], F32, st)
                dts = P.sb("dts", [NS, 16], F32, st)
                dte = P.sb("dte", [NS, 16], F32, st)
                dec = P.sb("dec", [NS, 16], F32, st)
                xdt_t = P.sb("xdt_t", [NS, 1024], F32, st)
                dec_t = P.sb("dec_t", [NS, 1024], F32, st)
                mix_t = xdt_t
                pre_s = dec_t
                XDTT = P.sb("XDTT", [128, 8, NS], F32, st)
                DECT = P.sb("DECT", [128, 8, NS], F32, st)
                YTT = P.sb("YTT", [128, 8, NS], F32, st)
                SEL = P.sb("SEL", [NS, NS, 128], F32, st)
                SELT = P.sb("SELT", [128, NS, NS], F32, st)
                ysum = P.sb("ysum", [128, 8], F32, st)
                y_t = P.sb("y_t", [NS, 1024], F32, st)
                szs = P.sb("szs", [NS, 1024], F32, st)
                g_t = P.sb("g_t", [NS, 1024], F32, st)
                gq_t = P.sb("gq_t", [NS, 1024], F32, st)
                ss4 = P.sb("ss4", [NS, 4], F32, st)
                ya_t = P.sb("ya_t", [NS, 1024], F32, st)
                css = P.sb("css", [NS, 64], F32, st)
                qk = P.sb("qk", [NS, 20, 64], F32, st)
                ra_s = P.sb("ra_s", [NS, 20, 32], F32, st)
                rb_s = P.sb("rb_s", [NS, 20, 32], F32, st)
                PTI = P.sb("PTI", [128, NS * NPG], I32, st)
                PTF = P.sb("PTF", [128, NS * NPG], F32, st)
                IDX = P.sb("IDX", [128, NS * NPG], I32, st)
                iop = P.sb("iop", [128, 1], F32, st)
                KV = [P.sb("KVt%d" % i, [128, NPG, 256], F32, st) for i in range(1)]
                nws = View("nws", KV[0], KV[0][0:NS, 0:4, :].rearrange("p a c -> p (a c)"))
                prod = P.sb("prod", [128, NPG, 4, 64], F32, st)
                pfl = prod[:].rearrange("p j a d -> p (j a d)")
                Hs = [View("Hs0", prod, pfl[:, 0:1024].rearrange("p (c s) -> p c s", c=8))]
                ht1 = View("ht1", prod, pfl[:, 1024:2048].rearrange("p (c s) -> p c s", c=8))
                ht2 = View("ht2", prod, pfl[:, 2048:3072].rearrange("p (c s) -> p c s", c=8))
                gbs = View("gbs", prod, pfl[0:NS, 0:1024])
                bbs = View("bbs", prod, pfl[0:NS, 1024:2048])
                xrs = View("xrs", prod, pfl[0:NS, 2048:3072])
                S_all = View("S_all", BIGS, BIGS[:, 0:NS * NPG * 16].rearrange("p (n c) -> p n c", n=NS))
                csum = P.sb("csum", [NS, NPG, 16], F32, st)
                sblk = P.sb("sblk", [NS, 16, 8], F32, st)
                mx8 = P.sb("mx8", [NS, 16, 8], F32, st)
                sl8 = P.sb("sl8", [NS, 16, 8], F32, st)
                MB = P.sb("MB", [NS, NPG, 16], F32, st)
                stmp = P.sb("stmp", [128, NPG * 16], F32, st)
                Pm = P.sb("Pm", [128, NPG, 16], F32, st)
                OT = P.sb("OT", [64, 16, NS], F32, st)
                LT = P.sb("LT", [16, NS], F32, st)
                o_t = View("o_t", y_t, y_t[:].rearrange("p (h d) -> p h d", h=16))
                l_t = P.sb("l_t", [NS, 16], F32, st)
                sown = P.sb("sown", [NS, 16], F32, st)
                sprod = View("sprod", gq_t, gq_t[:].rearrange("p (h d) -> p h d", h=16))
                ob_t = g_t
                TT = P.sb("TT", [128, 8, NS], F32, st)
                sg = cacs
                br = ctmp
                sts = P.sb("sts", [NS, 2, 6], F32, st)
                mvs = P.sb("mvs", [NS, 2], F32, st)
                rss = P.sb("rss", [NS, 1], F32, st)

                PU = [P.view("PU%d" % i, banks[i], banks[i][0:NS, :]) for i in range(2)]
                PTr = P.view("PTr", banks[2], banks[2][:, 0:8 * NS])
                PBC = P.view("PBC", banks[3], banks[3][:, :])
                PCC = P.view("PCC", banks[4], banks[4][:, :])
                PYT = [P.view("PYTs%d" % i, banks[5 + i], banks[5 + i][0:NS, :]) for i in range(2)]
                PQB = [P.view("PQB%d" % i, banks[5 + i], banks[5 + i][:, :]) for i in range(2)]
                PCS = P.view("PCS", banks[7], banks[7][0:NS, 0:256])
                PMBc = P.view("PMBc", banks[3], banks[3][:, 0:256])
                POs = P.view("POs", banks[4], banks[4][0:64, 0:16])
                PLs = P.view("PLs", banks[4], banks[4][0:16, 32:33])
                POt = [P.view("POt%d" % i, banks[i], banks[i][0:NS, :]) for i in range(2)]
                PLt = P.view("PLt", banks[2], banks[2][0:NS, 256:272])

                def bc3(ap2, a, b):
                    return ap2.rearrange("p (a c) -> p a c", c=1).to_broadcast([ap2.shape[0], a, b])

                def transp(src_tok, dstT):
                    def fn(e):
                        ins = None
                        for c in range(8):
                            ins = e.matmul(PTr[:, NS * c:NS * c + NS], lhsT=src_tok[:, 128 * c:128 * c + 128],
                                           rhs=idn[0:NS, 0:NS], start=True, stop=True)
                        return ins
                    P.op("pe", fn, reads=[src_tok, idn], writes=[PTr])
                    P.op("act", lambda e: e.copy(out=dstT[:].rearrange("p a c -> p (a c)"), in_=PTr[:, :]),
                         reads=[PTr], writes=[dstT])

                if os.environ.get("MK_VERBOSE"):
                    print("SBUF remaining in sample pass", nc.sbuf_bytes_remaining)
                if os.environ.get("MK_TIND"):
                    for bnd in (None, 100):
                        for oo in (KV[0][:, 0, :], prod[:, 0, :, :].rearrange("p a d -> p (a d)")):
                            for ii in (IDX[:, 0:1], PTI[:, 0:1]):
                                try:
                                    nc.gpsimd.indirect_dma_start(out=oo, out_offset=None, in_=ck_d,
                                                                 in_offset=bass.IndirectOffsetOnAxis(ap=ii, axis=0),
                                                                 bounds_check=bnd, oob_is_err=False)
                                    print("TIND ok", bnd)
                                except Exception as ex:
                                    print("TIND err", bnd, str(ex)[:60])
                P.dma("sp", XST[:], xsT.rearrange("(kc p) t -> p kc t", p=128), XST, writes=[XST])
                P.dma("sp", SC[:].rearrange("p a c -> p (a c)"), sc_d, BIGS, writes=[SC])
                P.dma("sp", v16s[:], vec16.partition_broadcast(NS), v16s, writes=[v16s])
                P.dma("sp", nws[:], nwv.partition_broadcast(NS).rearrange("p a c -> p (a c)"), KV[0], writes=[nws])
                P.dma("sp", css[:], css_d.partition_broadcast(NS).rearrange("p a c -> p (a c)"), css, writes=[css])
                P.dma("sp", SEL[:].rearrange("p a c -> p (a c)"), sel_d, SEL, writes=[SEL])
                P.dma("sp", SELT[:].rearrange("p a c -> p (a c)"), selt_d, SELT, writes=[SELT])
                P.dma("sp", PTI[:], pt_d.partition_broadcast(128).rearrange("p a c -> p (a c)"), PTI, writes=[PTI])
                P.dma("sp", iop[:], iota_d, iop, writes=[iop])
                P.op("dve", lambda e: e.tensor_copy(out=PTF[:], in_=PTI[:]), reads=[PTI], writes=[PTF])
                P.op("dve", lambda e: e.tensor_scalar(out=PTF[:], in0=PTF[:], scalar1=128.0, scalar2=iop[:, 0:1],
                                                      op0=ALU.mult, op1=ALU.add), reads=[PTF, iop], writes=[PTF])
                P.op("dve", lambda e: e.tensor_copy(out=IDX[:], in_=PTF[:]), reads=[PTF], writes=[IDX])

                def stream_mm(w_ap, ncols, lhs_fn, out_buf, out_off, nk=8):
                    c0 = 0
                    while c0 < ncols:
                        n = min(512, ncols - c0)
                        sw = WS[stg[0] % len(WS)]
                        pu = PU[stg[0] % 2]
                        stg[0] += 1
                        P.dma("sp", sw[:, :, 0:n], w_ap[:, c0:c0 + n].rearrange("(kc p) c -> p kc c", p=128), sw, writes=[sw])
                        P.op("pe", mm_chain(pu[:, 0:n], lhs_fn, lambda kc, sw=sw, n=n: sw[:, kc, 0:n], nk=nk),
                             reads=[sw, XST, TT], writes=[pu])
                        P.op("act", lambda e, pu=pu, n=n, c0=c0: e.copy(out=out_buf[:, out_off + c0:out_off + c0 + n], in_=pu[:, 0:n]),
                             reads=[pu], writes=[out_buf])
                        c0 += n

                stream_mm(w_in, DIN, lambda kc: XST[:, kc, :], Us, 0)

                if os.environ.get("MK_TIND"):
                    try:
                        nc.gpsimd.indirect_dma_start(out=KV[0][:, 0, :], out_offset=None, in_=ck_d,
                                                     in_offset=bass.IndirectOffsetOnAxis(ap=IDX[:, 0:1], axis=0),
                                                     bounds_check=100, oob_is_err=False)
                        print("TIND2 ok A")
                    except Exception as ex:
                        print("TIND2 err A", str(ex)[:60])
                for k in range(4):
                    P.dma("sp", CWk[:], cwrow[k:k + 1, :].partition_broadcast(NS).rearrange("p a c -> p (a c)"), CWk, writes=[CWk])
                    src = SC[:, k, :] if k < 3 else Us[:, C_XA:C_XA + 2048]
                    dst = cacs if k == 0 else ctmp
                    P.op("dve", lambda e, src=src, dst=dst: e.tensor_tensor(out=dst[:], in0=src, in1=CWk[:], op=ALU.mult),
                         reads=[SC, Us, CWk], writes=[dst])
                    if k > 0:
                        P.op("dve", lambda e: e.tensor_tensor(out=cacs[:], in0=cacs[:], in1=ctmp[:], op=ALU.add),
                             reads=[cacs, ctmp], writes=[cacs])
                P.dma("sp", CWk[:], cbrow.partition_broadcast(NS).rearrange("p a c -> p (a c)"), CWk, writes=[CWk])
                P.op("dve", lambda e: e.tensor_tensor(out=cacs[:], in0=cacs[:], in1=CWk[:], op=ALU.add),
                     reads=[cacs, CWk], writes=[cacs])
                P.op("act", lambda e: e.activation(out=xbc[:], in_=cacs[:], func=AF.Silu), reads=[cacs], writes=[xbc])
                cv3 = cvs_o.rearrange("p (a c) -> p a c", a=3)
                P.dma("sp", cv3[:, 0:2, :], SC[:, 1:3, :], BIGS, reads=[SC], is_out=True)
                P.dma("sp", cv3[:, 2, :], Us[:, C_XA:C_XA + 2048], Us, reads=[Us], is_out=True)

                if os.environ.get("MK_TIND"):
                    try:
                        nc.gpsimd.indirect_dma_start(out=KV[0][:, 0, :], out_offset=None, in_=ck_d,
                                                     in_offset=bass.IndirectOffsetOnAxis(ap=IDX[:, 0:1], axis=0),
                                                     bounds_check=100, oob_is_err=False)
                        print("TIND2 ok B")
                    except Exception as ex:
                        print("TIND2 err B", str(ex)[:60])
                P.op("dve", lambda e: e.tensor_tensor(out=dts[:], in0=Us[:, C_DT:C_DT + 16], in1=v16s[:, 0, :], op=ALU.add),
                     reads=[Us, v16s], writes=[dts])
                P.op("act", lambda e: e.activation(out=dte[:], in_=dts[:], func=AF.Exp), reads=[dts], writes=[dte])
                P.op("act", lambda e: e.activation(out=dts[:], in_=dte[:], func=AF.Ln, bias=1.0), reads=[dte], writes=[dts])
                P.op("act", lambda e: e.activation(out=As[:], in_=v16s[:, 1, :], func=AF.Exp), reads=[v16s], writes=[As])
                P.op("dve", lambda e: e.tensor_tensor(out=dec[:], in0=dts[:], in1=As[:], op=ALU.mult), reads=[dts, As], writes=[dec])
                P.op("act", lambda e: e.activation(out=dec[:], in_=dec[:], func=AF.Exp, scale=-1.0), reads=[dec], writes=[dec])
                x3 = xbc[:, 0:1024].rearrange("p (h d) -> p h d", h=16)
                P.op("dve", lambda e: e.tensor_tensor(out=xdt_t[:].rearrange("p (h d) -> p h d", h=16), in0=x3,
                                                      in1=bc3(dts[:], 16, 64), op=ALU.mult), reads=[xbc, dts], writes=[xdt_t])
                P.op("dve", lambda e: e.tensor_copy(out=dec_t[:].rearrange("p (h d) -> p h d", h=16), in_=bc3(dec[:], 16, 64)),
                     reads=[dec], writes=[dec_t])
                transp(xdt_t, XDTT)
                transp(dec_t, DECT)

                if os.environ.get("MK_TIND"):
                    try:
                        nc.gpsimd.indirect_dma_start(out=KV[0][:, 0, :], out_offset=None, in_=ck_d,
                                                     in_offset=bass.IndirectOffsetOnAxis(ap=IDX[:, 0:1], axis=0),
                                                     bounds_check=100, oob_is_err=False)
                        print("TIND2 ok C")
                    except Exception as ex:
                        print("TIND2 err C", str(ex)[:60])
                ssm_in = ssm_d.rearrange("n (c q s) -> n q c s", c=8, q=128)
                ssm_out = ssms_o.rearrange("n (c q s) -> n q c s", c=8, q=128)
                for n in range(NS):
                    H = Hs[0]
                    P.dma("sp", H[:], ssm_in[n], prod, writes=[H])
                    P.op("pe", lambda e, n=n: e.matmul(PBC[:, :], lhsT=SEL[:, n, :], rhs=xbc[:, 1024:1536], start=True, stop=True),
                         reads=[SEL, xbc], writes=[PBC])
                    P.op("pe", lambda e, n=n: e.matmul(PCC[:, :], lhsT=SEL[:, n, :], rhs=xbc[:, 1536:2048], start=True, stop=True),
                         reads=[SEL, xbc], writes=[PCC])
                    b4 = PBC[:, :].rearrange("p (g a s) -> p g a s", g=4, a=1).to_broadcast([128, 4, 2, 128])
                    c4 = PCC[:, :].rearrange("p (g a s) -> p g a s", g=4, a=1).to_broadcast([128, 4, 2, 128])
                    h4 = lambda t: t[:].rearrange("p (g a) s -> p g a s", g=4)
                    xcol = XDTT[:, :, n:n + 1].to_broadcast([128, 8, 128])
                    dcol = DECT[:, :, n:n + 1].to_broadcast([128, 8, 128])
                    P.op("dve", lambda e: e.tensor_tensor(out=h4(ht1), in0=b4, in1=xcol.rearrange("p (g a) s -> p g a s", g=4),
                                                          op=ALU.mult), reads=[PBC, XDTT], writes=[ht1])
                    P.op("dve", lambda e, H=H: e.tensor_tensor(out=ht2[:], in0=H[:], in1=dcol, op=ALU.mult),
                         reads=[H, DECT], writes=[ht2])
                    P.op("dve", lambda e, H=H: e.tensor_tensor(out=H[:], in0=ht1[:], in1=ht2[:], op=ALU.add),
                         reads=[ht1, ht2], writes=[H])
                    P.dma("sp", ssm_out[n], H[:], prod, reads=[H], is_out=True)
                    P.op("dve", lambda e, H=H: e.tensor_tensor(out=h4(ht1), in0=c4, in1=h4(H), op=ALU.mult),
                         reads=[PCC, H], writes=[ht1])
                    P.op("dve", lambda e, n=n: e.tensor_reduce(out=YTT[:, :, n], in_=ht1[:], axis=AX.X, op=ALU.add),
                         reads=[ht1], writes=[YTT])

                if os.environ.get("MK_TIND"):
                    try:
                        nc.gpsimd.indirect_dma_start(out=KV[0][:, 0, :], out_offset=None, in_=ck_d,
                                                     in_offset=bass.IndirectOffsetOnAxis(ap=IDX[:, 0:1], axis=0),
                                                     bounds_check=100, oob_is_err=False)
                        print("TIND2 ok D")
                    except Exception as ex:
                        print("TIND2 err D", str(ex)[:60])
                def fn_yb(e):
                    ins = None
                    for c in range(8):
                        ins = e.matmul(PYT[c // 4][:, 128 * (c % 4):128 * (c % 4) + 128], lhsT=YTT[:, c, :], rhs=idn[:],
                                       start=True, stop=True)
                    return ins
                P.op("pe", fn_yb, reads=[YTT, idn], writes=[PYT[0], PYT[1]])
                for hf in range(2):
                    P.op("act", lambda e, hf=hf: e.copy(out=y_t[:, 512 * hf:512 * hf + 512], in_=PYT[hf][:, :]),
                         reads=[PYT[hf]], writes=[y_t])
                P.op("dve", lambda e: e.tensor_tensor(out=g_t[:].rearrange("p (h d) -> p h d", h=16), in0=x3,
                                                      in1=bc3(v16s[:, 2, :], 16, 64), op=ALU.mult), reads=[xbc, v16s], writes=[g_t])
                P.op("dve", lambda e: e.tensor_tensor(out=y_t[:], in0=y_t[:], in1=g_t[:], op=ALU.add), reads=[y_t, g_t], writes=[y_t])
                P.op("act", lambda e: e.activation(out=szs[:], in_=Us[:, C_ZA:C_ZA + 1024], func=AF.Silu), reads=[Us], writes=[szs])
                P.op("dve", lambda e: e.tensor_tensor(out=g_t[:], in0=y_t[:], in1=szs[:], op=ALU.mult), reads=[y_t, szs], writes=[g_t])
                P.op("dve", lambda e: e.tensor_tensor(out=gq_t[:], in0=g_t[:], in1=g_t[:], op=ALU.mult), reads=[g_t], writes=[gq_t])
                P.op("dve", lambda e: e.tensor_reduce(out=ss4[:], in_=gq_t[:].rearrange("p (g c) -> p g c", g=4), axis=AX.X, op=ALU.add),
                     reads=[gq_t], writes=[ss4])
                P.op("act", lambda e: e.activation(out=ss4[:], in_=ss4[:], func=AF.Sqrt, bias=epsb[0:NS, 0:1], scale=1.0 / 256.0),
                     reads=[ss4, epsb], writes=[ss4])
                P.op("dve", lambda e: e.reciprocal(out=ss4[:], in_=ss4[:]), reads=[ss4], writes=[ss4])
                P.op("dve", lambda e: e.tensor_tensor(out=ya_t[:].rearrange("p (g c) -> p g c", g=4),
                                                      in0=g_t[:].rearrange("p (g c) -> p g c", g=4), in1=bc3(ss4[:], 4, 256),
                                                      op=ALU.mult), reads=[g_t, ss4], writes=[ya_t])
                P.op("dve", lambda e: e.tensor_tensor(out=ya_t[:], in0=ya_t[:], in1=nws[:], op=ALU.mult), reads=[ya_t, nws], writes=[ya_t])

                if os.environ.get("MK_TIND"):
                    try:
                        nc.gpsimd.indirect_dma_start(out=KV[0][:, 0, :], out_offset=None, in_=ck_d,
                                                     in_offset=bass.IndirectOffsetOnAxis(ap=IDX[:, 0:1], axis=0),
                                                     bounds_check=100, oob_is_err=False)
                        print("TIND2 ok E")
                    except Exception as ex:
                        print("TIND2 err E", str(ex)[:60])
                P.op("dve", lambda e: e.tensor_copy(out=qk[:, 0:16, :].rearrange("p h d -> p (h d)"), in_=Us[:, C_Q:C_Q + 1024]),
                     reads=[Us], writes=[qk])
                P.op("dve", lambda e: e.tensor_copy(out=qk[:, 16:20, :].rearrange("p h d -> p (h d)"), in_=Us[:, C_K:C_K + 256]),
                     reads=[Us], writes=[qk])
                cosb = css[:, 0:32].rearrange("p (a c) -> p a c", a=1).to_broadcast([NS, 20, 32])
                sinb = css[:, 32:64].rearrange("p (a c) -> p a c", a=1).to_broadcast([NS, 20, 32])
                q1, q2 = qk[:, :, 0:32], qk[:, :, 32:64]
                P.op("dve", lambda e: e.tensor_tensor(out=ra_s[:], in0=q1, in1=sinb, op=ALU.mult), reads=[qk, css], writes=[ra_s])
                P.op("dve", lambda e: e.tensor_tensor(out=rb_s[:], in0=q2, in1=sinb, op=ALU.mult), reads=[qk, css], writes=[rb_s])
                P.op("dve", lambda e: e.tensor_tensor(out=q1, in0=q1, in1=cosb, op=ALU.mult), reads=[qk, css], writes=[qk])
                P.op("dve", lambda e: e.tensor_tensor(out=q2, in0=q2, in1=cosb, op=ALU.mult), reads=[qk, css], writes=[qk])
                P.op("dve", lambda e: e.tensor_tensor(out=q1, in0=q1, in1=rb_s[:], op=ALU.subtract), reads=[qk, rb_s], writes=[qk])
                P.op("dve", lambda e: e.tensor_tensor(out=q2, in0=q2, in1=ra_s[:], op=ALU.add), reads=[qk, ra_s], writes=[qk])
                P.dma("sp", ks_o, qk[:, 16:20, :].rearrange("p h d -> p (h d)"), qk, reads=[qk], is_out=True)
                P.dma("sp", vs_o, Us[:, C_V:C_V + 256], Us, reads=[Us], is_out=True)

                if os.environ.get("MK_TIND"):
                    try:
                        nc.gpsimd.indirect_dma_start(out=KV[0][:, 0, :], out_offset=None, in_=ck_d,
                                                     in_offset=bass.IndirectOffsetOnAxis(ap=IDX[:, 0:1], axis=0),
                                                     bounds_check=100, oob_is_err=False)
                        print("TIND2 ok F")
                    except Exception as ex:
                        print("TIND2 err F", str(ex)[:60])
                ck_rows = ck_d
                for n in range(NS):
                    Kt = KV[0]
                    for j in range(NPG):
                        P.idma(Kt[:, j, :], ck_rows, IDX[:, NPG * n + j:NPG * n + j + 1], Kt, reads=[IDX], writes=[Kt],
                               bound=None)
                    for hf in range(2):
                        P.op("pe", lambda e, n=n, hf=hf: e.matmul(PQB[hf][:, :], lhsT=SEL[:, n, :],
                                                                  rhs=qk[:, 8 * hf:8 * hf + 8, :].rearrange("p h d -> p (h d)"),
                                                                  start=True, stop=True), reads=[SEL, qk], writes=[PQB[hf]])
                    for kvh in range(4):
                        pq = PQB[kvh // 2]
                        qv = pq[:, 256 * (kvh % 2):256 * (kvh % 2) + 256].rearrange("p (a h d) -> p a h d", a=1, h=4) \
                            .to_broadcast([128, NPG, 4, 64])
                        kvw = Kt[:, :, 64 * kvh:64 * kvh + 64].rearrange("p j (a d) -> p j a d", a=1).to_broadcast([128, NPG, 4, 64])
                        P.op("dve", lambda e, kvw=kvw, qv=qv: e.tensor_tensor(out=prod[:], in0=kvw, in1=qv, op=ALU.mult),
                             reads=[Kt, pq], writes=[prod])
                        P.op("dve", lambda e, n=n, kvh=kvh: e.tensor_reduce(
                            out=S_all[:, n, :].rearrange("p (j h) -> p j h", h=16)[:, :, 4 * kvh:4 * kvh + 4], in_=prod[:],
                            axis=AX.X, op=ALU.add), reads=[prod], writes=[S_all])
                    P.op("pe", lambda e, n=n: e.matmul(PCS[:, :], lhsT=SELT[:, n, :], rhs=S_all[:, n, :],
                                                       start=(n == 0), stop=(n == NS - 1)), reads=[SELT, S_all], writes=[PCS])
                P.op("dve", lambda e: e.tensor_copy(out=csum[:].rearrange("p j h -> p (j h)"), in_=PCS[:, :]), reads=[PCS], writes=[csum])
                cs4 = csum[:].rearrange("p (b a) h -> p b a h", a=2)
                P.op("dve", lambda e: e.tensor_tensor(out=sblk[:].rearrange("p h b -> p b h"), in0=cs4[:, :, 0, :], in1=cs4[:, :, 1, :],
                                                      op=ALU.add), reads=[csum], writes=[sblk])
                for h in range(16):
                    P.op("dve", lambda e, h=h: e.max(out=mx8[:, h, :], in_=sblk[:, h, :]), reads=[sblk], writes=[mx8])
                P.op("dve", lambda e: e.tensor_tensor(out=sl8[:], in0=sblk[:], in1=mx8[:, :, 2:3].to_broadcast([NS, 16, 8]), op=ALU.is_ge),
                     reads=[sblk, mx8], writes=[sl8])
                P.op("dve", lambda e: e.tensor_scalar(out=sl8[:], in0=sl8[:], scalar1=-1.0, scalar2=BIG, op0=ALU.add, op1=ALU.mult),
                     reads=[sl8], writes=[sl8])
                P.op("dve", lambda e: e.tensor_copy(
                    out=MB[:].rearrange("p (b a) h -> p b a h", a=2),
                    in_=sl8[:].rearrange("p h (b a) -> p b a h", a=1).to_broadcast([NS, 8, 2, 16])), reads=[sl8], writes=[MB])

                for n in range(NS):
                    Vt = KV[0]
                    for j in range(NPG):
                        P.idma(Vt[:, j, :], cv_d, IDX[:, NPG * n + j:NPG * n + j + 1], Vt, reads=[IDX], writes=[Vt],
                               bound=None)
                    P.op("pe", lambda e, n=n: e.matmul(PMBc[:, :], lhsT=SEL[:, n, :], rhs=MB[:].rearrange("p j h -> p (j h)"),
                                                       start=True, stop=True), reads=[SEL, MB], writes=[PMBc])
                    P.op("dve", lambda e, n=n: e.scalar_tensor_tensor(out=stmp[:], in0=S_all[:, n, :], scalar=SCALE, in1=PMBc[:, :],
                                                                      op0=ALU.mult, op1=ALU.add), reads=[S_all, PMBc], writes=[stmp])
                    P.op("act", lambda e: e.activation(out=Pm[:].rearrange("p j h -> p (j h)"), in_=stmp[:], func=AF.Exp),
                         reads=[stmp], writes=[Pm])

                    def fn_pv(e, Vt=Vt):
                        ins = None
                        for kvh in range(4):
                            for j in range(NPG):
                                ins = e.matmul(POs[:, 4 * kvh:4 * kvh + 4], lhsT=Vt[:, j, 64 * kvh:64 * kvh + 64],
                                               rhs=Pm[:, j, 4 * kvh:4 * kvh + 4], start=(j == 0), stop=(j == NPG - 1))
                        for j in range(NPG):
                            ins = e.matmul(PLs[:, :], lhsT=Pm[:, j, :], rhs=onesf[:, 0:1], start=(j == 0), stop=(j == NPG - 1))
                        return ins
                    P.op("pe", fn_pv, reads=[Vt, Pm, onesf], writes=[POs])
                    P.op("act", lambda e, n=n: e.copy(out=OT[:, :, n], in_=POs[:, :]), reads=[POs], writes=[OT])
                    P.op("dve", lambda e, n=n: e.tensor_copy(out=LT[:, n:n + 1], in_=PLs[:, :]), reads=[PLs], writes=[LT])

                def fn_ob(e):
                    ins = None
                    for h in range(16):
                        ins = e.matmul(POt[h // 8][:, 64 * (h % 8):64 * (h % 8) + 64], lhsT=OT[:, h, :], rhs=idn[0:64, 0:64],
                                       start=True, stop=True)
                    return ins
                P.op("pe", fn_ob, reads=[OT, idn], writes=[POt[0], POt[1]])
                P.op("pe", lambda e: e.matmul(PLt[:, :], lhsT=LT[:], rhs=idn[0:16, 0:16], start=True, stop=True),
                     reads=[LT, idn], writes=[PLt])
                for hf in range(2):
                    P.op("act", lambda e, hf=hf: e.copy(out=o_t[:, 8 * hf:8 * hf + 8, :].rearrange("p h d -> p (h d)"), in_=POt[hf][:, :]),
                         reads=[POt[hf]], writes=[o_t])
                P.op("dve", lambda e: e.tensor_copy(out=l_t[:], in_=PLt[:, :]), reads=[PLt], writes=[l_t])
                k4 = qk[:, 16:20, :].rearrange("p (k a) d -> p k a d", a=1).to_broadcast([NS, 4, 4, 64])
                v4 = Us[:, C_V:C_V + 256].rearrange("p (k a d) -> p k a d", k=4, a=1).to_broadcast([NS, 4, 4, 64])
                q4 = qk[:, 0:16, :].rearrange("p (k a) d -> p k a d", a=4)
                P.op("dve", lambda e: e.tensor_tensor(out=sprod[:].rearrange("p (k a) d -> p k a d", a=4), in0=q4, in1=k4, op=ALU.mult),
                     reads=[qk], writes=[sprod])
                P.op("dve", lambda e: e.tensor_reduce(out=sown[:], in_=sprod[:], axis=AX.X, op=ALU.add), reads=[sprod], writes=[sown])
                P.op("act", lambda e: e.activation(out=sown[:], in_=sown[:], func=AF.Exp, scale=SCALE), reads=[sown], writes=[sown])
                P.op("dve", lambda e: e.tensor_tensor(out=sprod[:].rearrange("p (k a) d -> p k a d", a=4), in0=v4,
                                                      in1=sown[:].rearrange("p (k a c) -> p k a c", k=4, c=1).to_broadcast([NS, 4, 4, 64]),
                                                      op=ALU.mult), reads=[Us, sown], writes=[sprod])
                P.op("dve", lambda e: e.tensor_tensor(out=o_t[:], in0=o_t[:], in1=sprod[:], op=ALU.add), reads=[o_t, sprod], writes=[o_t])
                P.op("dve", lambda e: e.tensor_tensor(out=l_t[:], in0=l_t[:], in1=sown[:], op=ALU.add), reads=[l_t, sown], writes=[l_t])
                P.op("dve", lambda e: e.reciprocal(out=l_t[:], in_=l_t[:]), reads=[l_t], writes=[l_t])
                P.op("dve", lambda e: e.tensor_tensor(out=o_t[:], in0=o_t[:], in1=bc3(l_t[:], 16, 64), op=ALU.mult),
                     reads=[o_t, l_t], writes=[o_t])
                P.op("act", lambda e: e.activation(out=szs[:], in_=Us[:, C_ZB:C_ZB + 1024], func=AF.Silu), reads=[Us], writes=[szs])
                P.op("dve", lambda e: e.tensor_tensor(out=ob_t[:], in0=o_t[:].rearrange("p h d -> p (h d)"), in1=szs[:], op=ALU.mult),
                     reads=[o_t, szs], writes=[ob_t])

                P.dma("sp", gbs[:], lng.partition_broadcast(NS).rearrange("p a c -> p (a c)"), prod, writes=[gbs])
                P.dma("sp", bbs[:], lnb.partition_broadcast(NS).rearrange("p a c -> p (a c)"), prod, writes=[bbs])
                P.dma("sp", xrs[:], xs_tok, prod, writes=[xrs])
                P.op("act", lambda e: e.activation(out=sg[:], in_=Us[:, C_GA:C_GA + 2048], func=AF.Sigmoid), reads=[Us], writes=[sg])
                transp(ya_t, TT)
                stream_mm(w_a, 1024, lambda kc: TT[:, kc, :], br, 0)
                transp(ob_t, TT)
                stream_mm(w_b, 1024, lambda kc: TT[:, kc, :], br, 1024)
                P.op("dve", lambda e: e.tensor_tensor(out=br[:], in0=br[:], in1=sg[:], op=ALU.mult), reads=[br, sg], writes=[br])
                P.op("dve", lambda e: e.tensor_tensor(out=mix_t[:], in0=br[:, 0:1024], in1=br[:, 1024:2048], op=ALU.add),
                     reads=[br], writes=[mix_t])
                transp(mix_t, TT)
                stream_mm(w_o, 1024, lambda kc: TT[:, kc, :], pre_s, 0)
                P.op("dve", lambda e: e.scalar_tensor_tensor(out=pre_s[:], in0=xrs[:], scalar=ALPHA, in1=pre_s[:], op0=ALU.mult, op1=ALU.add),
                     reads=[xrs, pre_s], writes=[pre_s])
                for hf in range(2):
                    P.op("dve", lambda e, hf=hf: e.bn_stats(out=sts[:, hf, :], in_=pre_s[:, 512 * hf:512 * hf + 512]), reads=[pre_s], writes=[sts])
                P.op("dve", lambda e: e.bn_aggr(out=mvs[:], in_=sts[:].rearrange("p a c -> p (a c)")), reads=[sts], writes=[mvs])
                P.op("act", lambda e: e.activation(out=rss[:], in_=mvs[:, 1:2], func=AF.Sqrt, bias=epsb[0:NS, 0:1]), reads=[mvs, epsb], writes=[rss])
                P.op("dve", lambda e: e.reciprocal(out=rss[:], in_=rss[:]), reads=[rss], writes=[rss])
                P.op("dve", lambda e: e.tensor_scalar(out=pre_s[:], in0=pre_s[:], scalar1=mvs[:, 0:1], scalar2=rss[:, 0:1],
                                                      op0=ALU.subtract, op1=ALU.mult), reads=[pre_s, mvs, rss], writes=[pre_s])
                P.op("dve", lambda e: e.tensor_tensor(out=pre_s[:], in0=pre_s[:], in1=gbs[:], op=ALU.mult), reads=[pre_s, gbs], writes=[pre_s])
                P.op("dve", lambda e: e.tensor_tensor(out=pre_s[:], in0=pre_s[:], in1=bbs[:], op=ALU.add), reads=[pre_s, bbs], writes=[pre_s])
                P.dma("sp", ys_o, pre_s[:], pre_s, reads=[pre_s], is_out=True)
            P.barrier()

        YAT = P.sb("YAT", [128, 8, 2048], BF16)
        OBT = P.sb("OBT", [128, 8, 2048], BF16)
        with contextlib.ExitStack() as st:
            alloc_io(st, "a")
            XO = XOl[0]
            Wf = P.sb("Wf", [128, 8, 512], BF16, st)
            Wt = P.sb("Wt", [128, 8, 260], BF16, st)
            cw = P.sb("cw", [128, 4, 4], F32, st)
            cb = P.sb("cb", [128, 4], F32, st)
            v16 = P.sb("v16", [128, 3, 4], F32, st)
            Aneg = P.sb("Aneg", [128, 4], F32, st)
            nw = P.sb("nw", [128, 256], F32, st)
            U = P.sb("U", [128, 3, 515], F32, st)
            XC = P.sb("XC", [128, 3, 512], F32, st)
            cacc = P.sb("cacc", [128, 512], F32, st)
            UO = P.sb("UO", [128, 4, 131], F32, st)
            XCO = P.sb("XCO", [128, 4, 128], F32, st)
            XCOb = P.sb("XCOb", [128, 4, 128], BF16, st)
            caco = P.sb("caco", [128, 128], F32, st)
            hT = P.sb("hT", [128, 256], F32, st)
            Hsel = P.sb("Hsel", [128, 256], F32, st)
            Hselb = P.sb("Hselb", [128, 256], BF16, st)
            sm = {n: P.sb("sm_" + n, [128, 4], F32, st) for n in
                  ("t0", "e0", "dt", "adt", "acs", "acl", "te", "ecl", "w", "nacs", "eacs")}
            so = {n: P.sb("so_" + n, [128, 4], F32, st) for n in
                  ("t0", "e0", "dt", "adt", "acs", "nacs", "eacs")}
            xs_tok = P.sb("xs_tok", [128, 256], F32, st)
            B_tokb = P.sb("B_tokb", [128, 128], BF16, st)
            xdtw = P.sb("xdtw", [128, 256], BF16, st)
            xso = P.sb("xso", [128, 256], F32, st)
            xdto = P.sb("xdto", [128, 256], BF16, st)
            Radt = P.sb("Radt", [128, 4, 128], F32, st)
            dcl = P.sb("dcl", [128, 4, 128], F32, st)
            dce = P.sb("dce", [128, 4, 128], F32, st)
            cbm = P.sb("cbm", [128, 128], F32, st)
            MT = P.sb("MT", [128, 4, 128], BF16, st)
            Ysb = P.sb("Ysb", [128, 256], F32, st)
            yv = P.sb("yv", [128, 256], F32, st)
            sza = P.sb("sza", [128, 256], F32, st)
            gg = P.sb("gg", [128, 256], F32, st)
            gsq = P.sb("gsq", [128, 256], F32, st)
            ss = P.sb("ss", [128, 1], F32, st)
            rstd = P.sb("rstd", [128, 1], F32, st)
            ya = P.sb("ya", [128, 256], F32, st)
            cvs = P.sb("cvs", [128, 3, 3], F32, st)
            cvc = P.sb("cvc", [128, 3], F32, st)

            PF = [P.view("PF0", banks[0], banks[0][:, :]), P.view("PF1", banks[1], banks[1][:, :])]
            PTc = P.view("PTc", banks[2], banks[2][:, 0:16])
            Pacs = P.view("Pacs", banks[2], banks[2][:, 16:20])
            Pacl = P.view("Pacl", banks[2], banks[2][:, 20:24])
            PX = P.view("PX", banks[2], banks[2][:, 64:448])
            PST = P.view("PST", banks[3], banks[3][:, 0:256])
            PCB = P.view("PCB", banks[3], banks[3][:, 256:384])
            PFo = [P.view("PFo0", banks[4], banks[4][:, 0:131]), P.view("PFo1", banks[4], banks[4][:, 132:263])]
            PTo = P.view("PTo", banks[5], banks[5][:, 0:260])
            Poa = P.view("Poa", banks[5], banks[5][:, 264:268])
            PAB = P.view("PAB", banks[6], banks[6][:, :])
            PXo = P.view("PXo", banks[7], banks[7][:, 0:256])
            PY = P.view("PY", banks[7], banks[7][:, 256:512])
            PYO = PST
            PYT = PX

            for g in range(NG if 's' in PASSES else 0):
                load_w(Wf, 0, w_in[:, C_XA + 256 * g:C_XA + 256 * g + 256], 256)
                load_w(Wf, 256, w_in[:, C_B + 128 * g:C_B + 128 * g + 128], 128)
                load_w(Wf, 384, w_in[:, C_C + 128 * g:C_C + 128 * g + 128], 128)
                load_w(Wt, 0, w_in[:, C_ZA + 256 * g:C_ZA + 256 * g + 256], 256)
                load_w(Wt, 256, w_in[:, C_DT + 4 * g:C_DT + 4 * g + 4], 4)
                chs = [256 * g, 256 * g + 128, 1024 + 128 * g, 1536 + 128 * g]
                for m, c0 in enumerate(chs):
                    P.dma("sp", cw[:, m, :], cwT[c0:c0 + 128, :], cw, writes=[cw])
                    P.dma("sp", cb[:, m:m + 1], cbT[c0:c0 + 128, :], cb, writes=[cb])
                P.dma("sp", v16[:], vec16[:, 4 * g:4 * g + 4].partition_broadcast(128), v16, writes=[v16])
                P.dma("sp", nw[:], nwv[:, 256 * g:256 * g + 256].partition_broadcast(128).rearrange("p a c -> p (a c)"),
                      nw, writes=[nw])
                P.op("act", lambda e: e.activation(out=Aneg[:], in_=v16[:, 1, :], func=AF.Exp), reads=[v16], writes=[Aneg])
                P.op("dve", lambda e: e.tensor_scalar(out=Aneg[:], in0=Aneg[:], scalar1=-1.0, scalar2=None, op0=ALU.mult),
                     reads=[Aneg], writes=[Aneg])
                P.op("pool", lambda e: e.memset(hT[:], 0.0), writes=[hT])
                P.op("pool", lambda e: e.memset(U[:], 0.0), writes=[U])

                def dt_chain(smd, psrc, rd):
                    t0, e0, dtt, adt = smd["t0"], smd["e0"], smd["dt"], smd["adt"]
                    P.op("dve", lambda e: e.tensor_tensor(out=t0[:], in0=psrc, in1=v16[:, 0, :], op=ALU.add),
                         reads=rd + [v16], writes=[t0])
                    P.op("act", lambda e: e.activation(out=e0[:], in_=t0[:], func=AF.Exp), reads=[t0], writes=[e0])
                    P.op("act", lambda e: e.activation(out=dtt[:], in_=e0[:], func=AF.Ln, bias=1.0), reads=[e0], writes=[dtt])
                    P.op("dve", lambda e: e.tensor_tensor(out=adt[:], in0=dtt[:], in1=Aneg[:], op=ALU.mult),
                         reads=[dtt, Aneg], writes=[adt])

                for i in range(NI):
                    xb = load_x(i)
                    for m in range(3):
                        pf = PF[m % 2]
                        P.op("pe", mm_chain(pf[:, :], lambda kc, m=m: Wf[:, kc, m * 128:(m + 1) * 128],
                                            lambda kc, xb=xb: xb[:, kc, :]), reads=[Wf, xb], writes=[pf])
                        P.op("act", lambda e, m=m, pf=pf: e.copy(out=U[:, m, 3:515], in_=pf[:, :]), reads=[pf], writes=[U])
                    for m in range(3):
                        P.op("dve", lambda e, m=m: e.tensor_scalar(out=cacc[:], in0=U[:, m, 0:512], scalar1=cw[:, m, 0:1],
                                                                   scalar2=cb[:, m:m + 1], op0=ALU.mult, op1=ALU.add),
                             reads=[U, cw, cb], writes=[cacc])
                        for k in range(1, 4):
                            P.op("dve", lambda e, m=m, k=k: e.scalar_tensor_tensor(
                                out=cacc[:], in0=U[:, m, k:k + 512], scalar=cw[:, m, k:k + 1], in1=cacc[:],
                                op0=ALU.mult, op1=ALU.add), reads=[U, cw, cacc], writes=[cacc])
                        P.op("act", lambda e, m=m: e.activation(out=XC[:, m, :], in_=cacc[:], func=AF.Silu),
                             reads=[cacc], writes=[XC])
                    if i == NI - 1:
                        P.op("dve", lambda e: e.tensor_copy(out=cvs[:], in_=U[:, :, 512:515]), reads=[U], writes=[cvs])
                        P.dma("sp", cvA_o[g], cvs[:], cvs, reads=[cvs], is_out=True)
                    P.op("dve", lambda e: e.tensor_copy(out=U[:, :, 0:3], in_=U[:, :, 512:515]), reads=[U], writes=[U])
                    def fn_dt(e, xb=xb):
                        ins = None
                        for k in range(4):
                            for kc in range(8):
                                ins = e.matmul(PTc[:, 4 * k:4 * k + 4], lhsT=xb[:, kc, 128 * k:128 * k + 128],
                                               rhs=Wt[:, kc, 256:260], start=(kc == 0), stop=(kc == 7))
                        return ins
                    P.op("pe", fn_dt, reads=[xb, Wt], writes=[PTc])
                    for k in range(4):
                        cs = slice(128 * k, 128 * k + 128)
                        dt_chain(sm, PTc[:, 4 * k:4 * k + 4], [PTc])
                        adt = sm["adt"]
                        P.op("pe", lambda e: e.matmul(Pacs[:, :], lhsT=tri[:], rhs=adt[:], start=True, stop=True),
                             reads=[tri, adt], writes=[Pacs])
                        P.op("pe", lambda e: e.matmul(Pacl[:, :], lhsT=onesf[:], rhs=adt[:], start=True, stop=True),
                             reads=[onesf, adt], writes=[Pacl])
                        te, ecl, w_ = sm["te"], sm["ecl"], sm["w"]
                        P.op("dve", lambda e: e.tensor_copy(out=sm["acs"][:], in_=Pacs[:, :]), reads=[Pacs], writes=[sm["acs"]])
                        P.op("dve", lambda e: e.tensor_tensor(out=te[:], in0=Pacl[:, :], in1=sm["acs"][:], op=ALU.subtract),
                             reads=[Pacl, sm["acs"]], writes=[te])
                        P.op("act", lambda e: e.activation(out=te[:], in_=te[:], func=AF.Exp), reads=[te], writes=[te])
                        P.op("act", lambda e: e.activation(out=ecl[:], in_=Pacl[:, :], func=AF.Exp), reads=[Pacl], writes=[ecl])
                        P.op("dve", lambda e: e.tensor_tensor(out=w_[:], in0=te[:], in1=sm["dt"][:], op=ALU.mult),
                             reads=[te, sm["dt"]], writes=[w_])
                        def fn_tr(e, cs=cs):
                            V = os.environ.get("MK_V", "")
                            if V == "1":
                                return tr(e, PX[:, 0:128], tri[:], idn[:])
                            if V == "4":
                                return e.matmul(PX[:, 0:4], lhsT=tri[:], rhs=sm["adt"][:], start=True, stop=True)
                            if V == "5":
                                return e.matmul(PX[:, 0:128], lhsT=tri[:], rhs=tri[:], start=True, stop=True)
                            if V == "2":
                                return tr(e, PX[:, 0:128], XC[:, 0, cs], tri[:])
                            if V == "3":
                                return tr(e, PX[:, 0:128], XC[:, 0, cs], idn[:])
                            tr(e, PX[:, 0:128], XC[:, 0, cs], idn[:])
                            tr(e, PX[:, 128:256], XC[:, 1, cs], idn[:])
                            return tr(e, PX[:, 256:384], XC[:, 2, cs], idn[:])
                        P.op("pe", fn_tr, reads=[XC, idn], writes=[PX])
                        P.op("act", lambda e: e.copy(out=B_tokb[:], in_=PX[:, 256:384]), reads=[PX], writes=[B_tokb])
                        for h in range(4):
                            P.op("dve", lambda e, h=h: e.tensor_scalar(
                                out=xdtw[:, 64 * h:64 * h + 64], in0=PX[:, 64 * h:64 * h + 64],
                                scalar1=w_[:, h:h + 1], scalar2=None, op0=ALU.mult), reads=[PX, w_], writes=[xdtw])
                        P.op("pe", lambda e: e.matmul(PST[:, :], lhsT=B_tokb[:], rhs=xdtw[:], start=True, stop=True),
                             reads=[B_tokb, xdtw], writes=[PST])
                        if k == 0:
                            P.op("dve", lambda e: e.tensor_scalar(out=Hsel[:], in0=hT[:], scalar1=oh[:, 0:1], scalar2=None,
                                                                  op0=ALU.mult), reads=[hT, oh], writes=[Hsel])
                        else:
                            P.op("dve", lambda e, k=k: e.scalar_tensor_tensor(
                                out=Hsel[:], in0=hT[:], scalar=oh[:, k:k + 1], in1=Hsel[:], op0=ALU.mult, op1=ALU.add),
                                reads=[hT, oh, Hsel], writes=[Hsel])
                        for h in range(4):
                            P.op("dve", lambda e, h=h: e.scalar_tensor_tensor(
                                out=hT[:, 64 * h:64 * h + 64], in0=hT[:, 64 * h:64 * h + 64], scalar=ecl[:, h:h + 1],
                                in1=PST[:, 64 * h:64 * h + 64], op0=ALU.mult, op1=ALU.add),
                                reads=[hT, ecl, PST], writes=[hT])
                    P.op("act", lambda e: e.copy(out=Hselb[:], in_=Hsel[:]), reads=[Hsel], writes=[Hselb])

                    load_xo(i)
                    for m in range(4):
                        pf = PFo[m % 2]
                        P.op("pe", mm_chain(pf[:, :], lambda kc, m=m: Wf[:, kc, m * 128:(m + 1) * 128],
                                            lambda kc: XO[:, kc, :]), reads=[Wf, XO], writes=[pf])
                        P.op("act", lambda e, m=m, pf=pf: e.copy(out=UO[:, m, :], in_=pf[:, :]), reads=[pf], writes=[UO])
                    for m in range(4):
                        P.op("dve", lambda e, m=m: e.tensor_scalar(out=caco[:], in0=UO[:, m, 0:128], scalar1=cw[:, m, 0:1],
                                                                   scalar2=cb[:, m:m + 1], op0=ALU.mult, op1=ALU.add),
                             reads=[UO, cw, cb], writes=[caco])
                        for k in range(1, 4):
                            P.op("dve", lambda e, m=m, k=k: e.scalar_tensor_tensor(
                                out=caco[:], in0=UO[:, m, k:k + 128], scalar=cw[:, m, k:k + 1], in1=caco[:],
                                op0=ALU.mult, op1=ALU.add), reads=[UO, cw, caco], writes=[caco])
                        P.op("act", lambda e, m=m: e.activation(out=XCO[:, m, :], in_=caco[:], func=AF.Silu),
                             reads=[caco], writes=[XCO])
                    P.op("pool", lambda e: e.tensor_copy(out=XCOb[:], in_=XCO[:]), reads=[XCO], writes=[XCOb])
                    if i == NI - 1:
                        P.op("dve", lambda e: e.tensor_copy(out=cvc[:], in_=UO[:, 3, 128:131]), reads=[UO], writes=[cvc])
                        P.dma("sp", cvC_o[g], cvc[:], cvc, reads=[cvc], is_out=True)
                    P.op("pe", mm_chain(PTo[:, :], lambda kc: XO[:, kc, 3:131], lambda kc: Wt[:, kc, :]),
                         reads=[XO, Wt], writes=[PTo])
                    dt_chain(so, PTo[:, 256:260], [PTo])
                    P.op("act", lambda e: e.activation(out=sza[:], in_=PTo[:, 0:256], func=AF.Silu), reads=[PTo], writes=[sza])
                    adto = so["adt"]
                    P.op("pe", lambda e: e.matmul(Poa[:, :], lhsT=tri[:], rhs=adto[:], start=True, stop=True),
                         reads=[tri, adto], writes=[Poa])
                    P.op("dve", lambda e: e.tensor_copy(out=so["acs"][:], in_=Poa[:, :]), reads=[Poa], writes=[so["acs"]])
                    P.op("act", lambda e: e.activation(out=so["eacs"][:], in_=so["acs"][:], func=AF.Exp), reads=[so["acs"]],
                         writes=[so["eacs"]])
                    for h in range(4):
                        P.op("dve", lambda e, h=h: e.tensor_scalar(out=Radt[:, h, :], in0=tri[:], scalar1=adto[:, h:h + 1],
                                                                   scalar2=None, op0=ALU.mult), reads=[tri, adto], writes=[Radt])

                    def fn_ab(e):
                        ins = None
                        for h in range(4):
                            ins = e.matmul(PAB[:, 128 * h:128 * h + 128], lhsT=onesf[:], rhs=Radt[:, h, :], start=True, stop=True)
                        return ins
                    P.op("pe", fn_ab, reads=[onesf, Radt], writes=[PAB])
                    for h in range(4):
                        P.op("dve", lambda e, h=h: e.tensor_scalar(
                            out=dcl[:, h, :], in0=PAB[:, 128 * h:128 * h + 128], scalar1=so["acs"][:, h:h + 1], scalar2=0.0,
                            op0=ALU.subtract, op1=ALU.min), reads=[PAB, so["acs"]], writes=[dcl])
                    P.op("act", lambda e: e.activation(out=dce[:], in_=dcl[:], func=AF.Exp), reads=[dcl], writes=[dce])
                    P.op("pe", lambda e: e.matmul(PCB[:, :], lhsT=XCOb[:, 2, :], rhs=XCOb[:, 3, :], start=True, stop=True),
                         reads=[XCOb], writes=[PCB])
                    P.op("dve", lambda e: e.tensor_tensor(out=cbm[:], in0=PCB[:, :], in1=tri[:], op=ALU.mult),
                         reads=[PCB, tri], writes=[cbm])
                    P.op("dve", lambda e: e.tensor_tensor(
                        out=MT[:], in0=dce[:], in1=cbm[:].rearrange("p (a c) -> p a c", a=1).to_broadcast([128, 4, 128]),
                        op=ALU.mult), reads=[dce, cbm], writes=[MT])

                    def fn_tro(e):
                        tr(e, PXo[:, 0:128], XCO[:, 0, :], idn[:])
                        return tr(e, PXo[:, 128:256], XCO[:, 1, :], idn[:])
                    P.op("pe", fn_tro, reads=[XCO, idn], writes=[PXo])
                    P.op("act", lambda e: e.copy(out=xso[:], in_=PXo[:, :]), reads=[PXo], writes=[xso])
                    for h in range(4):
                        P.op("dve", lambda e, h=h: e.tensor_scalar(
                            out=xdto[:, 64 * h:64 * h + 64], in0=xso[:, 64 * h:64 * h + 64], scalar1=so["dt"][:, h:h + 1],
                            scalar2=None, op0=ALU.mult), reads=[xso, so["dt"]], writes=[xdto])

                    def fn_y(e):
                        ins = None
                        for h in range(4):
                            ins = e.matmul(PY[:, 64 * h:64 * h + 64], lhsT=MT[:, h, :], rhs=xdto[:, 64 * h:64 * h + 64],
                                           start=True, stop=True)
                        return ins
                    P.op("pe", fn_y, reads=[MT, xdto], writes=[PY])
                    P.op("pe", lambda e: e.matmul(PYO[:, :], lhsT=XCOb[:, 3, :], rhs=Hselb[:], start=True, stop=True),
                         reads=[XCOb, Hselb], writes=[PYO])
                    P.op("act", lambda e: e.copy(out=Ysb[:], in_=PY[:, :]), reads=[PY], writes=[Ysb])
                    for h in range(4):
                        hs = slice(64 * h, 64 * h + 64)
                        P.op("dve", lambda e, h=h, hs=hs: e.scalar_tensor_tensor(
                            out=yv[:, hs], in0=PYO[:, hs], scalar=so["eacs"][:, h:h + 1], in1=Ysb[:, hs],
                            op0=ALU.mult, op1=ALU.add), reads=[PYO, so["eacs"], Ysb], writes=[yv])
                        P.op("dve", lambda e, h=h, hs=hs: e.scalar_tensor_tensor(
                            out=yv[:, hs], in0=xso[:, hs], scalar=v16[:, 2, h:h + 1], in1=yv[:, hs],
                            op0=ALU.mult, op1=ALU.add), reads=[xso, v16, yv], writes=[yv])
                    P.op("dve", lambda e: e.tensor_tensor(out=gg[:], in0=yv[:], in1=sza[:], op=ALU.mult),
                         reads=[yv, sza], writes=[gg])
                    P.op("dve", lambda e: e.tensor_tensor(out=gsq[:], in0=gg[:], in1=gg[:], op=ALU.mult),
                         reads=[gg], writes=[gsq])
                    P.op("dve", lambda e: e.tensor_reduce(out=ss[:], in_=gsq[:], axis=AX.X, op=ALU.add),
                         reads=[gsq], writes=[ss])
                    P.op("act", lambda e: e.activation(out=rstd[:], in_=ss[:], func=AF.Sqrt, bias=epsb[:, 0:1], scale=1.0 / 256.0),
                         reads=[ss, epsb], writes=[rstd])
                    P.op("dve", lambda e: e.reciprocal(out=rstd[:], in_=rstd[:]), reads=[rstd], writes=[rstd])
                    P.op("dve", lambda e: e.scalar_tensor_tensor(out=ya[:], in0=gg[:], scalar=rstd[:, 0:1], in1=nw[:],
                                                                 op0=ALU.mult, op1=ALU.mult), reads=[gg, rstd, nw], writes=[ya])

                    def fn_yt(e):
                        tr(e, PYT[:, 0:128], ya[:, 0:128], idn[:])
                        return tr(e, PYT[:, 128:256], ya[:, 128:256], idn[:])
                    P.op("pe", fn_yt, reads=[ya, idn], writes=[PYT])
                    P.op("act", lambda e, g=g, i=i: e.copy(
                        out=YAT[:, 2 * g:2 * g + 2, 128 * i:128 * i + 128],
                        in_=PYT[:, 0:256].rearrange("p (a c) -> p a c", a=2)), reads=[PYT], writes=[YAT])
                P.dma("sp", ssm_o[g], hT[:], hT, reads=[hT], is_out=True)

        P.barrier()
        with contextlib.ExitStack() as st:
            alloc_io(st, "b")
            XO = XOl[0]
            Wkv = P.sb("Wkv", [128, 8, 128], BF16, st)
            Wqz = P.sb("Wqz", [128, 8, 512], BF16, st)
            KTA = P.sb("KTA", [96, SEQ], BF16, st)
            VA = P.sb("VA", [128, 64, 65], BF16, st)
            KTf = P.sb("KTf", [64, 256], F32, st)
            KM = P.sb("KM", [64, 32], F32, st)
            CM = P.sb("CM", [128, 4, 512], BF16, st)
            EL = P.sb("EL", [128, 3, 16, 32], F32, st)
            csa = [P.sb("csa%d" % i, [128, 64], F32, st) for i in range(2)]
            cso = P.sb("cso", [128, 64], F32, st)
            kv = P.sb("kv", [128, 128], F32, st)
            kr = P.sb("kr", [128, 64], F32, st)
            ra = P.sb("ra", [128, 4, 32], F32, st)
            rb = P.sb("rb", [128, 4, 32], F32, st)
            qf = P.sb("qf", [128, 4, 64], F32, st)
            qr = P.sb("qr", [128, 4, 64], F32, st)
            szb = P.sb("szb", [128, 256], F32, st)
            QA = P.sb("QA", [96, 4, 128], BF16, st)
            QTf = P.sb("QTf", [64, 4, 128], F32, st)
            spd = P.sb("spd", [128, 4, 32], F32, st)
            mx = P.sb("mx", [128, 4, 8], F32, st)
            sel = P.sb("sel", [128, 4, 32], F32, st)
            MBP = P.sb("MBP", [128, 4, 96], F32, st)
            PTb = [P.sb("PTb%d" % i, [128, 512], BF16, st) for i in range(3)]
            Osb = P.sb("Osb", [65, 512], F32, st)
            rl = P.sb("rl", [128, 4], F32, st)
            ob = P.sb("ob", [128, 4, 64], F32, st)
            ob2 = P.sb("ob2", [128, 256], F32, st)

            PT2 = P.view("PT2", banks[0], banks[0][:, 0:128])
            PKT = P.view("PKT", banks[0], banks[0][0:64, 128:256])
            PQ = P.view("PQ", banks[1], banks[1][:, :])
            PSB = P.view("PSB", banks[2], banks[2][:, 0:128])
            PMB = P.view("PMB", banks[2], banks[2][0:96, 128:256])
            POT = PQ
            PS = [P.view("PS%d" % i, banks[3 + i], banks[3 + i][:, :]) for i in range(3)]
            PO = [P.view("PO%d" % i, banks[6 + i], banks[6 + i][0:65, :]) for i in range(2)]
            POBT = P.view("POBT", banks[0], banks[0][:, 256:512])

            for j in range(SEQ // 512):
                s = WS[stg[0] % len(WS)]
                stg[0] += 1
                P.dma("sp", s[64:96, 0, :], kbi_d[:, 512 * j:512 * j + 512], s, writes=[s])
                P.op("pool", lambda e, j=j, s=s: e.tensor_copy(out=KTA[64:96, 512 * j:512 * j + 512], in_=s[64:96, 0, :]),
                     reads=[s], writes=[KTA])
            s = WS[stg[0] % len(WS)]
            stg[0] += 1
            P.dma("sp", s[:, 0:4, :], cm_d.rearrange("p (a c) -> p a c", a=4), s, writes=[s])
            P.op("pool", lambda e, s=s: e.tensor_copy(out=CM[:], in_=s[:, 0:4, :]), reads=[s], writes=[CM])
            P.dma("sp", EL[:].rearrange("p a b c -> p a (b c)"), el_d.partition_broadcast(128), EL, writes=[EL])
            P.op("pool", lambda e: e.memset(VA[:, :, 64:65], 1.0), writes=[VA])
            P.op("pool", lambda e: e.memset(MBP[:], 0.0), writes=[MBP])

            def rope(src3, dst3, cs, nh, rd):
                cosb = cs[:, 0:32].rearrange("p (a c) -> p a c", a=1).to_broadcast([128, nh, 32])
                sinb = cs[:, 32:64].rearrange("p (a c) -> p a c", a=1).to_broadcast([128, nh, 32])
                a_, b_ = ra[:, 0:nh, :], rb[:, 0:nh, :]
                x1, x2 = src3[:, :, 0:32], src3[:, :, 32:64]
                P.op("dve", lambda e: e.tensor_tensor(out=a_, in0=x1, in1=cosb, op=ALU.mult), reads=rd, writes=[ra])
                P.op("dve", lambda e: e.tensor_tensor(out=b_, in0=x2, in1=sinb, op=ALU.mult), reads=rd, writes=[rb])
                P.op("dve", lambda e: e.tensor_tensor(out=dst3[:, :, 0:32], in0=a_, in1=b_, op=ALU.subtract),
                     reads=[ra, rb], writes=rd[-1:])
                P.op("dve", lambda e: e.tensor_tensor(out=a_, in0=x2, in1=cosb, op=ALU.mult), reads=rd + [ra], writes=[ra])
                P.op("dve", lambda e: e.tensor_tensor(out=b_, in0=x1, in1=sinb, op=ALU.mult), reads=rd + [rb], writes=[rb])
                P.op("dve", lambda e: e.tensor_tensor(out=dst3[:, :, 32:64], in0=a_, in1=b_, op=ALU.add),
                     reads=[ra, rb], writes=rd[-1:])

            for g in range(NG if 'a' in PASSES else 0):
                load_w(Wkv, 0, w_in[:, C_K + 64 * g:C_K + 64 * g + 64], 64)
                load_w(Wkv, 64, w_in[:, C_V + 64 * g:C_V + 64 * g + 64], 64)
                load_w(Wqz, 0, w_in[:, C_Q + 256 * g:C_Q + 256 * g + 256], 256)
                load_w(Wqz, 256, w_in[:, C_ZB + 256 * g:C_ZB + 256 * g + 256], 256)
                P.op("pool", lambda e: e.memset(KM[:], 0.0), writes=[KM])
                nps = 0
                for i in range(NI):
                    xb = load_x(i)
                    for k in range(4):
                        t = 4 * i + k
                        cs = slice(128 * k, 128 * k + 128)
                        ca = csa[t % 2]
                        P.dma("sp", ca[:], cs_all[128 * t:128 * t + 128, :], ca, writes=[ca])
                        P.op("pe", mm_chain(PT2[:, :], lambda kc, cs=cs, xb=xb: xb[:, kc, cs], lambda kc: Wkv[:, kc, :]),
                             reads=[xb, Wkv], writes=[PT2])
                        P.op("act", lambda e: e.copy(out=kv[:], in_=PT2[:, :]), reads=[PT2], writes=[kv])
                        rope(kv[:, 0:64].rearrange("p (a c) -> p a c", a=1), kr[:].rearrange("p (a c) -> p a c", a=1),
                             ca, 1, [kv, ca, kr])
                        P.dma("sp", kp_o[128 * t:128 * t + 128, 64 * g:64 * g + 64], kr[:], kr, reads=[kr], is_out=True)
                        P.dma("sp", vp_o[128 * t:128 * t + 128, 64 * g:64 * g + 64], kv[:, 64:128], kv, reads=[kv], is_out=True)
                        P.op("pe", lambda e: tr(e, PKT[:, :], kr[:], idn[:]), reads=[kr, idn], writes=[PKT])
                        P.op("act", lambda e, k=k: e.copy(out=KTf[:, 128 * (k % 2):128 * (k % 2) + 128], in_=PKT[:, :]),
                             reads=[PKT], writes=[KTf])
                        P.op("dve", lambda e, t=t, k=k: e.tensor_copy(out=KTA[0:64, 128 * t:128 * t + 128],
                                                                      in_=KTf[:, 128 * (k % 2):128 * (k % 2) + 128]),
                             reads=[KTf], writes=[KTA])
                        P.op("pool", lambda e, t=t: e.tensor_copy(out=VA[:, t, 0:64], in_=kv[:, 64:128]), reads=[kv], writes=[VA])
                        if k % 2 == 1:
                            P.op("dve", lambda e, t=t: e.tensor_reduce(out=KM[:, t // 2:t // 2 + 1], in_=KTf[:], axis=AX.X, op=ALU.add),
                                 reads=[KTf], writes=[KM])
                    load_xo(i)
                    P.dma("sp", cso[:], cs_own[128 * i:128 * i + 128, :], cso, writes=[cso])
                    P.op("pe", mm_chain(PQ[:, :], lambda kc: XO[:, kc, 3:131], lambda kc: Wqz[:, kc, :]),
                         reads=[XO, Wqz], writes=[PQ])
                    P.op("act", lambda e: e.copy(out=qf[:].rearrange("p a c -> p (a c)"), in_=PQ[:, 0:256]), reads=[PQ], writes=[qf])
                    P.op("act", lambda e: e.activation(out=szb[:], in_=PQ[:, 256:512], func=AF.Silu), reads=[PQ], writes=[szb])
                    rope(qf[:], qr[:], cso, 4, [qf, cso, qr])
                    for h in range(4):
                        P.op("pe", lambda e, h=h: tr(e, PKT[:, :], qr[:, h, :], idn[:]), reads=[qr, idn], writes=[PKT])
                        P.op("act", lambda e, h=h: e.copy(out=QTf[:, h, :], in_=PKT[:, :]), reads=[PKT], writes=[QTf])
                        P.op("dve", lambda e, h=h: e.tensor_copy(out=QA[0:64, h, :], in_=QTf[:, h, :]), reads=[QTf], writes=[QA])

                    def fn_sb(e):
                        ins = None
                        for h in range(4):
                            ins = e.matmul(PSB[:, 32 * h:32 * h + 32], lhsT=QTf[:, h, :], rhs=KM[:], start=True, stop=True)
                        return ins
                    P.op("pe", fn_sb, reads=[QTf, KM], writes=[PSB])
                    e01 = EL[:, 0, i:i + 1, :].to_broadcast([128, 4, 32])
                    eng_ = EL[:, 1, i:i + 1, :].to_broadcast([128, 4, 32])
                    own = EL[:, 2, i:i + 1, :].to_broadcast([128, 4, 32])
                    P.op("dve", lambda e, e01=e01: e.tensor_tensor(out=spd[:], in0=PSB[:, :].rearrange("p (a c) -> p a c", a=4),
                                                                   in1=e01, op=ALU.mult), reads=[PSB, EL], writes=[spd])
                    P.op("dve", lambda e, eng_=eng_: e.tensor_tensor(out=spd[:], in0=spd[:], in1=eng_, op=ALU.add),
                         reads=[spd, EL], writes=[spd])
                    for h in range(4):
                        P.op("dve", lambda e, h=h: e.max(out=mx[:, h, :], in_=spd[:, h, :]), reads=[spd], writes=[mx])
                    for h in range(4):
                        P.op("dve", lambda e, h=h: e.tensor_scalar(out=sel[:, h, :], in0=spd[:, h, :], scalar1=mx[:, h, 2:3],
                                                                   scalar2=None, op0=ALU.is_ge), reads=[spd, mx], writes=[sel])
                    P.op("dve", lambda e, e01=e01: e.tensor_tensor(out=sel[:], in0=sel[:], in1=e01, op=ALU.mult),
                         reads=[sel, EL], writes=[sel])
                    P.op("dve", lambda e, own=own: e.tensor_tensor(out=sel[:], in0=sel[:], in1=own, op=ALU.add),
                         reads=[sel, EL], writes=[sel])
                    P.op("dve", lambda e: e.tensor_scalar(out=MBP[:, :, 64:96], in0=sel[:], scalar1=-1.0, scalar2=BIG,
                                                          op0=ALU.add, op1=ALU.mult), reads=[sel], writes=[MBP])
                    for h in range(4):
                        P.op("pe", lambda e, h=h: e.matmul(PMB[:, :], lhsT=MBP[:, h, :], rhs=idn[:], start=True, stop=True),
                             reads=[MBP, idn], writes=[PMB])
                        P.op("act", lambda e, h=h: e.copy(out=QA[64:96, h, :], in_=PMB[64:96, :]), reads=[PMB], writes=[QA])
                    po = PO[i % 2]
                    nkt = 4 * i + 4
                    for kt in range(nkt):
                        pS = PS[nps % 3]
                        pT = PTb[nps % 3]
                        nps += 1

                        def fn_s(e, kt=kt, pS=pS):
                            ins = e.matmul(pS[:, :], lhsT=KTA[0:96, 128 * kt:128 * kt + 128],
                                           rhs=QA[:].rearrange("p a c -> p (a c)"), start=True, stop=(kt < 4 * i))
                            if kt >= 4 * i:
                                ins = e.matmul(pS[:, :], lhsT=idnb[:], rhs=CM[:, kt - 4 * i, :], start=False, stop=True)
                            return ins
                        P.op("pe", fn_s, reads=[KTA, QA, idnb, CM], writes=[pS])
                        P.op("act", lambda e, pS=pS, pT=pT: e.activation(out=pT[:], in_=pS[:, :], func=AF.Exp, scale=SCALE),
                             reads=[pS], writes=[pT])
                        P.op("pe", lambda e, kt=kt, pT=pT, po=po: e.matmul(po[:, :], lhsT=VA[:, kt, :], rhs=pT[:],
                                                                           start=(kt == 0), stop=(kt == nkt - 1)),
                             reads=[VA, pT] + ([po] if kt > 0 else []), writes=[po])
                    P.op("dve", lambda e, po=po: e.tensor_copy(out=Osb[:], in_=po[:, :]), reads=[po], writes=[Osb])

                    def fn_ot(e):
                        ins = None
                        for h in range(4):
                            ins = tr(e, POT[:, 65 * h:65 * h + 65], Osb[:, 128 * h:128 * h + 128], idn[0:65, 0:65])
                        return ins
                    P.op("pe", fn_ot, reads=[Osb, idn], writes=[POT])
                    pot3 = POT[:, 0:260].rearrange("p (a c) -> p a c", a=4)
                    P.op("dve", lambda e: e.reciprocal(out=rl[:], in_=pot3[:, :, 64]), reads=[POT], writes=[rl])
                    P.op("dve", lambda e: e.tensor_tensor(out=ob[:], in0=pot3[:, :, 0:64],
                                                          in1=rl[:].rearrange("p (a c) -> p a c", c=1).to_broadcast([128, 4, 64]),
                                                          op=ALU.mult), reads=[POT, rl], writes=[ob])
                    P.op("dve", lambda e: e.tensor_tensor(out=ob2[:], in0=ob[:].rearrange("p a c -> p (a c)"), in1=szb[:],
                                                          op=ALU.mult), reads=[ob, szb], writes=[ob2])

                    def fn_obt(e):
                        tr(e, POBT[:, 0:128], ob2[:, 0:128], idn[:])
                        return tr(e, POBT[:, 128:256], ob2[:, 128:256], idn[:])
                    P.op("pe", fn_obt, reads=[ob2, idn], writes=[POBT])
                    P.op("act", lambda e, g=g, i=i: e.copy(
                        out=OBT[:, 2 * g:2 * g + 2, 128 * i:128 * i + 128],
                        in_=POBT[:, 0:256].rearrange("p (a c) -> p a c", a=2)), reads=[POBT], writes=[OBT])

        P.barrier()
        if DBG:
            P.dma("sp", dbg_ya, YAT[:].rearrange("p a c -> p (a c)"), YAT, reads=[YAT], is_out=True)
            P.dma("sp", dbg_ob, OBT[:].rearrange("p a c -> p (a c)"), OBT, reads=[OBT], is_out=True)
        with contextlib.ExitStack() as st:
            alloc_io(st, "c", nws=1, nxb=1)
            Wg = P.sb("Wg", [128, 8, 2048], BF16, st)
            Wa = P.sb("Wa", [128, 8, 1024], BF16, st)
            Wb = P.sb("Wb", [128, 8, 1024], BF16, st)
            Wo = P.sb("Wo", [128, 8, 1024], BF16, st)
            gbc = P.sb("gbc", [128, 1024], F32, st)
            bbc = P.sb("bbc", [128, 1024], F32, st)
            XM = XB[0]
            sga = P.sb("sga", [128, 512], F32, st)
            sgb = P.sb("sgb", [128, 512], F32, st)
            t1 = P.sb("t1", [128, 512], F32, st)
            mixT = P.sb("mixT", [128, 8, 512], BF16, st)
            xres = [P.sb("xres%d" % i, [128, 1024], F32, st) for i in range(2)]
            pre = P.sb("pre", [128, 1024], F32, st)
            stats = P.sb("stats", [128, 2, 6], F32, st)
            mv = P.sb("mv", [128, 2], F32, st)
            rs2 = P.sb("rs2", [128, 1], F32, st)
            PGa, PGb, PBa, PBb = [P.view("PG%d" % i, banks[i], banks[i][:, :]) for i in range(4)]
            POa = [P.view("POa%d" % i, banks[4 + i], banks[4 + i][:, :]) for i in range(2)]

            for j in range(4):
                load_w(Wg, 512 * j, w_in[:, C_GA + 512 * j:C_GA + 512 * j + 512], 512)
            for j in range(2):
                load_w(Wa, 512 * j, w_a[:, 512 * j:512 * j + 512], 512)
                load_w(Wb, 512 * j, w_b[:, 512 * j:512 * j + 512], 512)
                load_w(Wo, 512 * j, w_o[:, 512 * j:512 * j + 512], 512)
            P.dma("sp", gbc[:], lng.partition_broadcast(128).rearrange("p a c -> p (a c)"), gbc, writes=[gbc])
            P.dma("sp", bbc[:], lnb.partition_broadcast(128).rearrange("p a c -> p (a c)"), bbc, writes=[bbc])
            NJ = (NI + 3) // 4 if 'm' in PASSES else 0
            for j in range(NJ):
                s = WS[stg[0] % len(WS)]
                stg[0] += 1
                P.dma("sp", s[:], xmT[:, 512 * j:512 * j + 512].rearrange("(kc p) t -> p kc t", p=128), s, writes=[s])
                P.op("pool", lambda e, s=s: e.tensor_copy(out=XM[:], in_=s[:]), reads=[s], writes=[XM])
                ts = slice(512 * j, 512 * j + 512)
                for mc in range(8):
                    ms = slice(128 * mc, 128 * mc + 128)
                    P.op("pe", mm_chain(PGa[:, :], lambda kc, ms=ms: Wg[:, kc, ms], lambda kc: XM[:, kc, :]),
                         reads=[Wg, XM], writes=[PGa])
                    P.op("pe", mm_chain(PGb[:, :], lambda kc, mc=mc: Wg[:, kc, 1024 + 128 * mc:1024 + 128 * mc + 128],
                                        lambda kc: XM[:, kc, :]), reads=[Wg, XM], writes=[PGb])
                    P.op("pe", mm_chain(PBa[:, :], lambda kc, ms=ms: Wa[:, kc, ms], lambda kc, ts=ts: YAT[:, kc, ts]),
                         reads=[Wa, YAT], writes=[PBa])
                    P.op("pe", mm_chain(PBb[:, :], lambda kc, ms=ms: Wb[:, kc, ms], lambda kc, ts=ts: OBT[:, kc, ts]),
                         reads=[Wb, OBT], writes=[PBb])
                    P.op("act", lambda e: e.activation(out=sga[:], in_=PGa[:, :], func=AF.Sigmoid), reads=[PGa], writes=[sga])
                    P.op("act", lambda e: e.activation(out=sgb[:], in_=PGb[:, :], func=AF.Sigmoid), reads=[PGb], writes=[sgb])
                    P.op("dve", lambda e: e.tensor_tensor(out=t1[:], in0=sga[:], in1=PBa[:, :], op=ALU.mult),
                         reads=[sga, PBa], writes=[t1])
                    P.op("dve", lambda e: e.tensor_tensor(out=sgb[:], in0=sgb[:], in1=PBb[:, :], op=ALU.mult),
                         reads=[sgb, PBb], writes=[sgb])
                    P.op("pool", lambda e, mc=mc: e.tensor_tensor(out=mixT[:, mc, :], in0=t1[:], in1=sgb[:], op=ALU.add),
                         reads=[t1, sgb], writes=[mixT])
                for k in range(4):
                    c = 4 * j + k
                    if c >= NI:
                        break
                    xr = xres[c % 2]
                    y_ = pre
                    P.dma("sp", xr[:], xown[128 * c:128 * c + 128, :], xr, writes=[xr])
                    for half in range(2):
                        po = POa[half]
                        P.op("pe", mm_chain(po[:, :], lambda kc, k=k: mixT[:, kc, 128 * k:128 * k + 128],
                                            lambda kc, half=half: Wo[:, kc, 512 * half:512 * half + 512]),
                             reads=[mixT, Wo], writes=[po])
                        P.op("dve", lambda e, half=half, po=po, xr=xr: e.scalar_tensor_tensor(
                            out=pre[:, 512 * half:512 * half + 512], in0=xr[:, 512 * half:512 * half + 512], scalar=ALPHA,
                            in1=po[:, :], op0=ALU.mult, op1=ALU.add), reads=[xr, po], writes=[pre])
                    for half in range(2):
                        P.op("dve", lambda e, half=half: e.bn_stats(out=stats[:, half, :], in_=pre[:, 512 * half:512 * half + 512]),
                             reads=[pre], writes=[stats])
                    P.op("dve", lambda e: e.bn_aggr(out=mv[:], in_=stats[:].rearrange("p a c -> p (a c)")), reads=[stats], writes=[mv])
                    P.op("act", lambda e: e.activation(out=rs2[:], in_=mv[:, 1:2], func=AF.Sqrt, bias=epsb[:, 0:1]),
                         reads=[mv, epsb], writes=[rs2])
                    P.op("dve", lambda e: e.reciprocal(out=rs2[:], in_=rs2[:]), reads=[rs2], writes=[rs2])
                    P.op("dve", lambda e, y_=y_: e.tensor_scalar(out=y_[:], in0=pre[:], scalar1=mv[:, 0:1], scalar2=rs2[:, 0:1],
                                                                 op0=ALU.subtract, op1=ALU.mult), reads=[pre, mv, rs2], writes=[y_])
                    P.op("pool", lambda e, y_=y_: e.tensor_tensor(out=y_[:], in0=y_[:], in1=gbc[:], op=ALU.mult),
                         reads=[y_, gbc], writes=[y_])
                    P.op("pool", lambda e, y_=y_: e.tensor_tensor(out=y_[:], in0=y_[:], in1=bbc[:], op=ALU.add),
                         reads=[y_, bbc], writes=[y_])
                    P.dma("sp", y_own[128 * c:128 * c + 128, :], y_[:], y_, reads=[y_], is_out=True)

        P.emit()
    return nc


def _host_inputs(x_prompt, w_in, conv_w, conv_b, dt_bias, a_log, d_skip, ssm_norm_w, w_a_out, w_b_out, w_out,
                 ln_g, ln_b, x_sample, cache_k, cache_v, state_conv, state_ssm, page_table):
    f32 = np.float32
    pos = np.arange(SEQ, dtype=np.float32)
    inv = np.power(np.float32(10000.0), -np.arange(32, dtype=np.float32) * np.float32(2.0) / np.float32(64.0)).astype(f32)
    ang = (pos[:, None] * inv[None, :]).astype(f32)
    cs_all = np.concatenate([np.cos(ang), np.sin(ang)], axis=1).astype(f32)
    tri = np.triu(np.ones((128, 128), f32))
    idn = np.eye(128, dtype=f32)
    kbi = np.zeros((32, SEQ), f32)
    for b in range(32):
        kbi[b, 256 * b:256 * b + 256] = 1.0
    common = {
        "w_in": np.ascontiguousarray(w_in[0]), "w_a": np.ascontiguousarray(w_a_out[0]),
        "w_b": np.ascontiguousarray(w_b_out[0]), "w_o": np.ascontiguousarray(w_out[0]),
        "cwT": np.ascontiguousarray(conv_w[0].T), "cbT": np.ascontiguousarray(conv_b[0][:, None]),
        "vec16": np.stack([dt_bias[0], a_log[0], d_skip[0]]).astype(f32),
        "nwv": np.ascontiguousarray(ssm_norm_w), "lng": np.ascontiguousarray(ln_g), "lnb": np.ascontiguousarray(ln_b),
        "tri": tri, "idn": idn, "cs_all": cs_all, "kbi": kbi,
        "cwrow": np.ascontiguousarray(conv_w[0]), "cbrow": np.ascontiguousarray(conv_b[0][None, :]),
        "ck_d": cache_k[0].reshape(2560 * 128, 256), "cv_d": cache_v[0].reshape(2560 * 128, 256),
        "iota_d": np.arange(128, dtype=f32)[:, None],
    }
    sel = np.zeros((NS, NS, 128), f32)
    selt = np.zeros((128, NS, NS), f32)
    for n in range(NS):
        sel[n, n, :] = 1.0
        selt[:, n, n] = 1.0
    angs = (np.float32(PAST) * inv).astype(f32)
    common["sel_d"] = sel.reshape(NS, NS * 128)
    common["selt_d"] = selt.reshape(128, NS * NS)
    common["css_d"] = np.concatenate([np.cos(angs), np.sin(angs)])[None, :].astype(f32)
    maps = []
    for c in range(NCORE):
        s, r = c // 4, c % 4
        xs = x_prompt[s]
        xT = np.ascontiguousarray(xs.T)
        own_tok = np.concatenate([np.arange(128 * (4 * i + r), 128 * (4 * i + r) + 128) for i in range(16)])
        xo = np.zeros((16, 131, D), f32)
        for i in range(16):
            t0 = 128 * (4 * i + r)
            lo = max(t0 - 3, 0)
            xo[i, 131 - (t0 + 128 - lo):] = xs[lo:t0 + 128]
        xoT = np.ascontiguousarray(xo.reshape(16 * 131, D).T)
        xown = np.ascontiguousarray(xs[own_tok])
        oh = np.zeros((128, 4), f32)
        oh[:, r] = 1.0
        cm = np.zeros((128, 4, 4, 128), f32)
        diag = np.where(np.arange(128)[:, None] <= np.arange(128)[None, :], 0.0, -BIG).astype(f32)
        for k in range(4):
            if k > r:
                cm[:, k] = -BIG
            elif k == r:
                cm[:, k] = diag[:, None, :]
        el = np.zeros((3, 16, 32), f32)
        for i in range(16):
            qblk = (4 * i + r) // 2
            el[0, i, :qblk] = 1.0
            el[1, i, qblk:] = -1e30
            el[2, i, qblk] = 1.0
        m = dict(common)
        ts = slice(NS * c, NS * c + NS)
        m.update({"xsT": np.ascontiguousarray(x_sample[ts, 0, :].T), "xs_tok": np.ascontiguousarray(x_sample[ts, 0, :]),
                  "sc_d": np.ascontiguousarray(state_conv[0, ts].reshape(NS, 3 * 2048)),
                  "ssm_d": np.ascontiguousarray(state_ssm[0, ts].reshape(NS, 1024 * 128)),
                  "pt_d": np.ascontiguousarray(page_table[ts].reshape(1, NS * NPG)).astype(np.int32)})
        m.update({"xT": xT, "xoT": xoT, "xmT": np.ascontiguousarray(xown.T), "xown": xown, "oh": oh,
                  "cs_own": np.ascontiguousarray(cs_all[own_tok]), "cm": cm.reshape(128, 2048),
                  "el": el.reshape(3, 512)})
        maps.append(m)
    return maps


_NC_CACHE = {}


def kernel(x_prompt, x_sample, cache_k, cache_v, state_conv, state_ssm, page_table,
           w_in, conv_w, conv_b, dt_bias, a_log, d_skip, ssm_norm_w, w_a_out, w_b_out, w_out, ln_g, ln_b):
    NI = int(os.environ.get("MK_NI", "16"))
    args = [np.asarray(a) for a in (x_prompt, w_in, conv_w, conv_b, dt_bias, a_log, d_skip, ssm_norm_w,
                                    w_a_out, w_b_out, w_out, ln_g, ln_b, x_sample, cache_k, cache_v,
                                    state_conv, state_ssm, page_table)]
    maps = _host_inputs(*args)
    key = (NI,)
    if key not in _NC_CACHE:
        _NC_CACHE[key] = build(NI=NI)
    nc = _NC_CACHE[key]
    res = run_bass_kernel_spmd(nc, maps, core_ids=list(range(NCORE))).results
    if os.environ.get("MK_DBG"):
        global _DBG_RES
        _DBG_RES = res
    y_prompt = np.zeros((2, SEQ, D), np.float32)
    for c in range(NCORE):
        s, r = c // 4, c % 4
        yo = res[c]["y_own"].reshape(16, 128, D)
        y_prompt[s].reshape(16, 4, 128, D)[:, r] = yo
    k_prompt = np.stack([res[0]["kp_o"], res[4]["kp_o"]]).reshape(1, 2, SEQ, 4, 64)
    v_prompt = np.stack([res[0]["vp_o"], res[4]["vp_o"]]).reshape(1, 2, SEQ, 4, 64)
    conv_prompt = np.zeros((1, 2, 3, 2048), np.float32)
    ssm_prompt = np.zeros((1, 2, 16, 64, 128), np.float32)
    for s in range(2):
        rr = res[4 * s + 3]
        cva = rr["cvA_o"]
        cvc = rr["cvC_o"]
        for g in range(4):
            conv_prompt[0, s, :, 256 * g:256 * g + 128] = cva[g, :, 0, :].T
            conv_prompt[0, s, :, 256 * g + 128:256 * g + 256] = cva[g, :, 1, :].T
            conv_prompt[0, s, :, 1024 + 128 * g:1024 + 128 * g + 128] = cva[g, :, 2, :].T
            conv_prompt[0, s, :, 1536 + 128 * g:1536 + 128 * g + 128] = cvc[g].T
            hs = rr["ssm_o"][g].reshape(128, 4, 64)
            ssm_prompt[0, s, 4 * g:4 * g + 4] = hs.transpose(1, 2, 0)
    y_sample = np.concatenate([res[c]["ys_o"] for c in range(NCORE)]).reshape(128, 1, D)
    k_sample = np.concatenate([res[c]["ks_o"] for c in range(NCORE)]).reshape(1, 128, 1, 4, 64)
    v_sample = np.concatenate([res[c]["vs_o"] for c in range(NCORE)]).reshape(1, 128, 1, 4, 64)
    conv_sample = np.concatenate([res[c]["cvs_o"] for c in range(NCORE)]).reshape(1, 128, 3, 2048)
    ssm_sample = np.concatenate([res[c]["ssms_o"] for c in range(NCORE)]).reshape(1, 128, 16, 64, 128)
    return (y_prompt, y_sample, k_prompt, v_prompt, k_sample, v_sample, conv_prompt, conv_sample,
            ssm_prompt, ssm_sample)
```

```python
import contextlib
import os

import numpy as np

import concourse.bass as bass
import concourse.mybir as mybir
from concourse.bass_utils import run_bass_kernel_spmd

F32 = mybir.dt.float32
BF16 = mybir.dt.bfloat16
I32 = mybir.dt.int32
AF = mybir.ActivationFunctionType
ALU = mybir.AluOpType
AX = mybir.AxisListType

D = 1024
SEQ = 8192
NCORE = 8
DIN = 7696
C_ZA, C_XA, C_B, C_C, C_DT, C_Q, C_K, C_V, C_ZB, C_GA, C_GB = (
    0, 1024, 2048, 2560, 3072, 3088, 4112, 4368, 4624, 5648, 6672)
EPS = 1e-5
ALPHA = 2.0 ** 0.25
BIG = 30000.0
SCALE = 0.125
NS = 16
PAST = 2048
NPG = 16


class Buf:
    def __init__(self, name, t):
        self.name = name
        self.t = t
        self.w = None
        self.r = []
        self.dsem = None
        self.dcnt = 0

    def __getitem__(self, k):
        return self.t[k]


class View:
    def __init__(self, name, parent, t):
        self.name = name
        self.parent = parent
        self.t = t

    def __getitem__(self, k):
        return self.t[k]

    @property
    def w(self):
        return self.parent.w

    @w.setter
    def w(self, v):
        self.parent.w = v

    @property
    def r(self):
        return self.parent.r

    @r.setter
    def r(self, v):
        self.parent.r = v


class Prog:
    ENG = ("sp", "pe", "act", "dve", "pool")

    def __init__(self, nc, stack):
        self.nc = nc
        self.stack = stack
        self.q = {e: [] for e in self.ENG}
        self.cnt = {e: 0 for e in self.ENG}
        self.seen = {e: {} for e in self.ENG}
        self.esem = {}
        for e in ("pe", "act", "dve", "pool"):
            self.esem[e] = stack.enter_context(nc.semaphore("es_" + e))
        self.nsem = 4
        self.out_tokens = []
        self.dbufs = []
        self.n = 0
        self.stop = int(os.environ.get("MK_STOP", "1000000000"))
        self.verbose = bool(os.environ.get("MK_VERBOSE"))

    def _skip(self, what):
        self.n += 1
        if self.verbose:
            import traceback
            fr = traceback.extract_stack()[-3]
            print("OP %d %s line %d" % (self.n, what, fr.lineno))
        return self.n > self.stop

    def sb(self, name, shape, dt, stack=None):
        t = (stack or self.stack).enter_context(self.nc.sbuf_tensor("s_" + name, list(shape), dt))
        return Buf(name, t)

    def ps(self, name, stack=None):
        t = (stack or self.stack).enter_context(self.nc.psum_tensor("p_" + name, [128, 512], F32))
        return Buf(name, t)

    def view(self, name, bank, ap):
        return View(name, bank, ap)

    def _dsem(self, b):
        if b.dsem is None:
            b.dsem = self.stack.enter_context(self.nc.semaphore("ds%d" % self.nsem))
            self.nsem += 1
            self.dbufs.append(b)
        return b.dsem

    def barrier(self):
        if self.n > self.stop:
            return
        toks = [(self.esem[e], self.cnt[e]) for e in ("pe", "act", "dve", "pool") if self.cnt[e] > 0]
        toks += [(b.dsem, b.dcnt) for b in self.dbufs if b.dcnt > 0]
        for eng in self.ENG:
            seen = self.seen[eng]
            waits = []
            for sem, val in toks:
                if seen.get(sem, 0) < val:
                    seen[sem] = val
                    waits.append((sem, val))
            self._wait(eng, waits)

    def _deps(self, eng, reads, writes):
        toks = []
        for b in reads:
            if b.w is not None:
                toks.append(b.w)
        for b in writes:
            if b.w is not None:
                toks.append(b.w)
            toks.extend(b.r)
        need = {}
        for sem, val in toks:
            if need.get(sem, 0) < val:
                need[sem] = val
        out = []
        seen = self.seen[eng]
        for sem, val in need.items():
            if seen.get(sem, 0) < val:
                seen[sem] = val
                out.append((sem, val))
        return out

    def _commit(self, tok, reads, writes):
        for b in reads:
            b.r.append(tok)
        for b in writes:
            b.w = tok
            b.r = []

    def _eng(self, eng):
        nc = self.nc
        return {"sp": nc.sync, "pe": nc.tensor, "act": nc.scalar, "dve": nc.vector, "pool": nc.gpsimd}[eng]

    def _wait(self, eng, waits):
        e = self._eng(eng)
        for sem, val in waits:
            if eng == "pe" and sem is self.esem["pe"]:
                continue
            e.wait_ge(sem, val)

    def op(self, eng, fn, reads=(), writes=()):
        if self._skip(eng):
            return None
        waits = self._deps(eng, reads, writes)
        self._wait(eng, waits)
        ins = fn(self._eng(eng))
        ins.then_inc(self.esem[eng], 1)
        if self.verbose and self.n in (62, 69):
            print("INS", self.n, str(ins))
        self.cnt[eng] += 1
        tok = (self.esem[eng], self.cnt[eng])
        self._commit(tok, reads, writes)
        return tok

    def dma(self, eng, out, in_, buf, reads=(), writes=(), is_out=False, **kw):
        if self._skip("dma"):
            return None
        waits = self._deps(eng, reads, writes)
        self._wait(eng, waits)
        sem = self._dsem(buf)
        buf.dcnt += 16
        tok = (sem, buf.dcnt)
        self._eng(eng).dma_start(out=out, in_=in_, **kw).then_inc(sem, 16)
        self._commit(tok, reads, writes)
        if is_out:
            self.out_tokens.append(tok)
        return tok

    def idma(self, out, in_, idx_ap, buf, reads=(), writes=(), bound=None):
        waits = self._deps("pool", reads, writes)
        self._wait("pool", waits)
        sem = self._dsem(buf)
        buf.dcnt += 16
        tok = (sem, buf.dcnt)
        self.nc.gpsimd.indirect_dma_start(
            out=out, out_offset=None, in_=in_,
            in_offset=bass.IndirectOffsetOnAxis(ap=idx_ap, axis=0),
            bounds_check=bound, oob_is_err=False).then_inc(sem, 16)
        self._commit(tok, reads, writes)
        return tok

    def emit(self):
        need = {}
        for sem, val in self.out_tokens:
            if need.get(sem, 0) < val:
                need[sem] = val
        for sem, val in need.items():
            self.nc.sync.wait_ge(sem, val)


def tr(e, out, in_, ident):
    return e.matmul(out, lhsT=in_, rhs=ident, start=True, stop=True)


def _bc(ap, shape):
    return ap.to_broadcast(list(shape))


def build(NI=16, SAMPLE=True):
    PASSES = os.environ.get('MK_PASSES', 'sam')
    SAMPLE = SAMPLE and not os.environ.get('MK_NOSAMPLE')
    NG = int(os.environ.get('MK_NG', '4'))
    nc = bass.Bass("TRN2", target_bir_lowering=False)
    dt = nc.dram_tensor
    xT = dt("xT", [D, SEQ], F32, kind="ExternalInput").ap()
    xoT = dt("xoT", [D, 16 * 131], F32, kind="ExternalInput").ap()
    xmT = dt("xmT", [D, 2048], F32, kind="ExternalInput").ap()
    xown = dt("xown", [2048, D], F32, kind="ExternalInput").ap()
    w_in = dt("w_in", [D, DIN], F32, kind="ExternalInput").ap()
    w_a = dt("w_a", [D, D], F32, kind="ExternalInput").ap()
    w_b = dt("w_b", [D, D], F32, kind="ExternalInput").ap()
    w_o = dt("w_o", [D, D], F32, kind="ExternalInput").ap()
    cwT = dt("cwT", [2048, 4], F32, kind="ExternalInput").ap()
    cbT = dt("cbT", [2048, 1], F32, kind="ExternalInput").ap()
    vec16 = dt("vec16", [3, 16], F32, kind="ExternalInput").ap()
    nwv = dt("nwv", [1, D], F32, kind="ExternalInput").ap()
    lng = dt("lng", [1, D], F32, kind="ExternalInput").ap()
    lnb = dt("lnb", [1, D], F32, kind="ExternalInput").ap()
    tri_d = dt("tri", [128, 128], F32, kind="ExternalInput").ap()
    idn_d = dt("idn", [128, 128], F32, kind="ExternalInput").ap()
    oh_d = dt("oh", [128, 4], F32, kind="ExternalInput").ap()
    cs_all = dt("cs_all", [SEQ, 64], F32, kind="ExternalInput").ap()
    cs_own = dt("cs_own", [2048, 64], F32, kind="ExternalInput").ap()
    kbi_d = dt("kbi", [32, SEQ], F32, kind="ExternalInput").ap()
    cm_d = dt("cm", [128, 4 * 512], F32, kind="ExternalInput").ap()
    el_d = dt("el", [3, 16 * 32], F32, kind="ExternalInput").ap()
    xsT = dt("xsT", [D, NS], F32, kind="ExternalInput").ap()
    xs_tok = dt("xs_tok", [NS, D], F32, kind="ExternalInput").ap()
    sc_d = dt("sc_d", [NS, 3 * 2048], F32, kind="ExternalInput").ap()
    ssm_d = dt("ssm_d", [NS, 1024 * 128], F32, kind="ExternalInput").ap()
    pt_d = dt("pt_d", [1, NS * NPG], I32, kind="ExternalInput").ap()
    iota_d = dt("iota_d", [128, 1], F32, kind="ExternalInput").ap()
    if SAMPLE:
        ck_d = dt("ck_d", [2560 * 128, 256], F32, kind="ExternalInput").ap()
        cv_d = dt("cv_d", [2560 * 128, 256], F32, kind="ExternalInput").ap()
    cwrow = dt("cwrow", [4, 2048], F32, kind="ExternalInput").ap()
    cbrow = dt("cbrow", [1, 2048], F32, kind="ExternalInput").ap()
    sel_d = dt("sel_d", [NS, NS * 128], F32, kind="ExternalInput").ap()
    selt_d = dt("selt_d", [128, NS * NS], F32, kind="ExternalInput").ap()
    css_d = dt("css_d", [1, 64], F32, kind="ExternalInput").ap()
    ys_o = dt("ys_o", [NS, D], F32, kind="ExternalOutput").ap()
    ks_o = dt("ks_o", [NS, 256], F32, kind="ExternalOutput").ap()
    vs_o = dt("vs_o", [NS, 256], F32, kind="ExternalOutput").ap()
    cvs_o = dt("cvs_o", [NS, 3 * 2048], F32, kind="ExternalOutput").ap()
    ssms_o = dt("ssms_o", [NS, 1024 * 128], F32, kind="ExternalOutput").ap()
    y_own = dt("y_own", [2048, D], F32, kind="ExternalOutput").ap()
    kp_o = dt("kp_o", [SEQ, 256], F32, kind="ExternalOutput").ap()
    vp_o = dt("vp_o", [SEQ, 256], F32, kind="ExternalOutput").ap()
    cvA_o = dt("cvA_o", [4, 128, 3, 3], F32, kind="ExternalOutput").ap()
    cvC_o = dt("cvC_o", [4, 128, 3], F32, kind="ExternalOutput").ap()
    ssm_o = dt("ssm_o", [4, 128, 256], F32, kind="ExternalOutput").ap()

    DBG = bool(os.environ.get("MK_DBG"))
    if DBG:
        dbg_ya = dt("dbg_ya", [128, 8 * 2048], BF16, kind="ExternalOutput").ap()
        dbg_ob = dt("dbg_ob", [128, 8 * 2048], BF16, kind="ExternalOutput").ap()
    with contextlib.ExitStack() as top:
        P = Prog(nc, top)
        tri = P.sb("tri", [128, 128], F32)
        idn = P.sb("idnf", [128, 128], F32)
        idnb = P.sb("idnb", [128, 128], BF16)
        onesf = P.sb("onesf", [128, 128], F32)
        oh = P.sb("oh", [128, 4], F32)
        WS, XB, XOl = [], [], []

        def alloc_io(st, tag, nws=2, nxb=2):
            WS[:] = [P.sb("WS%s%d" % (tag, i), [128, 8, 512], F32, st) for i in range(nws)]
            XB[:] = [P.sb("XB%s%d" % (tag, i), [128, 8, 512], BF16, st) for i in range(nxb)]
            XOl[:] = [P.sb("XO%s" % tag, [128, 8, 131], BF16, st)] if nxb else []
        banks = [P.ps("bank%d" % i) for i in range(8)]

        P.dma("sp", tri[:], tri_d, tri, writes=[tri])
        P.dma("sp", idn[:], idn_d, idn, writes=[idn])
        P.dma("sp", oh[:], oh_d, oh, writes=[oh])
        P.op("pool", lambda e: e.tensor_copy(out=idnb[:], in_=idn[:]), reads=[idn], writes=[idnb])
        P.op("pool", lambda e: e.memset(onesf[:], 1.0), writes=[onesf])
        epsb = P.sb("epsb", [128, 1], F32)
        P.op("pool", lambda e: e.memset(epsb[:], EPS), writes=[epsb])

        stg = [0]

        def load_w(dst, off, src_ap, n):
            s = WS[stg[0] % len(WS)]
            stg[0] += 1
            P.dma("sp", s[:, :, 0:n], src_ap.rearrange("(kc p) c -> p kc c", p=128), s, writes=[s])
            P.op("pool", lambda e: e.tensor_copy(out=dst[:, :, off:off + n], in_=s[:, :, 0:n]),
                 reads=[s], writes=[dst])

        def load_x(i):
            s = WS[stg[0] % len(WS)]
            xb = XB[stg[0] % len(XB)]
            stg[0] += 1
            P.dma("sp", s[:], xT[:, i * 512:(i + 1) * 512].rearrange("(kc p) t -> p kc t", p=128), s, writes=[s])
            P.op("pool", lambda e: e.tensor_copy(out=xb[:], in_=s[:]), reads=[s], writes=[xb])
            return xb

        def load_xo(i):
            s = WS[stg[0] % len(WS)]
            stg[0] += 1
            P.dma("sp", s[:, :, 0:131], xoT[:, i * 131:(i + 1) * 131].rearrange("(kc p) t -> p kc t", p=128),
                  s, writes=[s])
            XO = XOl[0]
            P.op("pool", lambda e: e.tensor_copy(out=XO[:], in_=s[:, :, 0:131]), reads=[s], writes=[XO])

        def mm_chain(ps_ap, lhs_fn, rhs_fn, nk=8):
            def fn(e):
                ins = None
                for kc in range(nk):
                    ins = e.matmul(ps_ap, lhsT=lhs_fn(kc), rhs=rhs_fn(kc), start=(kc == 0), stop=(kc == nk - 1))
                return ins
            return fn


        if SAMPLE:
            with contextlib.ExitStack() as st:
                alloc_io(st, "s", nws=1, nxb=0)
                XST = P.sb("XST", [128, 8, NS], F32, st)
                Us = P.sb("Us", [NS, DIN], F32, st)
                BIGS = P.sb("BIGS", [128, 6144], F32, st)
                SC = View("SC", BIGS, BIGS[0:NS, :].rearrange("p (a c) -> p a c", a=3))
                CWk = P.sb("CWk", [NS, 2048], F32, st)
                cacs = P.sb("cacs", [NS, 2048], F32, st)
                ctmp = P.sb("ctmp", [NS, 2048], F32, st)
                xbc = P.sb("xbc", [NS, 2048], F32, st)
                v16s = P.sb("v16s", [NS, 3, 16], F32, st)
                As = P.sb("As", [NS, 16], F32, st)
                dts = P.sb("dts", [NS, 16], F32, st)
                dte = P.sb("dte", [NS, 16], F32, st)
                dec = P.sb("dec", [NS, 16], F32, st)
                xdt_t = P.sb("xdt_t", [NS, 1024], F32, st)
                dec_t = P.sb("dec_t", [NS, 1024], F32, st)
                mix_t = xdt_t
                pre_s = dec_t
                XDTT = P.sb("XDTT", [128, 8, NS], F32, st)
                DECT = P.sb("DECT", [128, 8, NS], F32, st)
                YTT = P.sb("YTT", [128, 8, NS], F32, st)
                SEL = P.sb("SEL", [NS, NS, 128], F32, st)
                SELT = P.sb("SELT", [128, NS, NS], F32, st)
                ysum = P.sb("ysum", [128, 8], F32, st)
                y_t = P.sb("y_t", [NS, 1024], F32, st)
                szs = P.sb("szs", [NS, 1024], F32, st)
                g_t = P.sb("g_t", [NS, 1024], F32, st)
                gq_t = P.sb("gq_t", [NS, 1024], F32, st)
                ss4 = P.sb("ss4", [NS, 4], F32, st)
                ya_t = P.sb("ya_t", [NS, 1024], F32, st)
                css = P.sb("css", [NS, 64], F32, st)
                qk = P.sb("qk", [NS, 20, 64], F32, st)
                ra_s = P.sb("ra_s", [NS, 20, 32], F32, st)
                rb_s = P.sb("rb_s", [NS, 20, 32], F32, st)
                PTI = P.sb("PTI", [128, NS * NPG], I32, st)
                PTF = P.sb("PTF", [128, NS * NPG], F32, st)
                IDX = P.sb("IDX", [128, NS * NPG], I32, st)
                iop = P.sb("iop", [128, 1], F32, st)
                KV = [P.sb("KVt%d" % i, [128, NPG, 256], F32, st) for i in range(1)]
                nws = View("nws", KV[0], KV[0][0:NS, 0:4, :].rearrange("p a c -> p (a c)"))
                prod = P.sb("prod", [128, NPG, 4, 64], F32, st)
                pfl = prod[:].rearrange("p j a d -> p (j a d)")
                Hs = [View("Hs0", prod, pfl[:, 0:1024].rearrange("p (c s) -> p c s", c=8))]
                ht1 = View("ht1", prod, pfl[:, 1024:2048].rearrange("p (c s) -> p c s", c=8))
                ht2 = View("ht2", prod, pfl[:, 2048:3072].rearrange("p (c s) -> p c s", c=8))
                gbs = View("gbs", prod, pfl[0:NS, 0:1024])
                bbs = View("bbs", prod, pfl[0:NS, 1024:2048])
                xrs = View("xrs", prod, pfl[0:NS, 2048:3072])
                S_all = View("S_all", BIGS, BIGS[:, 0:NS * NPG * 16].rearrange("p (n c) -> p n c", n=NS))
                csum = P.sb("csum", [NS, NPG, 16], F32, st)
                sblk = P.sb("sblk", [NS, 16, 8], F32, st)
                mx8 = P.sb("mx8", [NS, 16, 8], F32, st)
                sl8 = P.sb("sl8", [NS, 16, 8], F32, st)
                MB = P.sb("MB", [NS, NPG, 16], F32, st)
                stmp = P.sb("stmp", [128, NPG * 16], F32, st)
                Pm = P.sb("Pm", [128, NPG, 16], F32, st)
                OT = P.sb("OT", [64, 16, NS], F32, st)
                LT = P.sb("LT", [16, NS], F32, st)
                o_t = View("o_t", y_t, y_t[:].rearrange("p (h d) -> p h d", h=16))
                l_t = P.sb("l_t", [NS, 16], F32, st)
                sown = P.sb("sown", [NS, 16], F32, st)
                sprod = View("sprod", gq_t, gq_t[:].rearrange("p (h d) -> p h d", h=16))
                ob_t = g_t
                TT = P.sb("TT", [128, 8, NS], F32, st)
                sg = cacs
                br = ctmp
                sts = P.sb("sts", [NS, 2, 6], F32, st)
                mvs = P.sb("mvs", [NS, 2], F32, st)
                rss = P.sb("rss", [NS, 1], F32, st)

                PU = [P.view("PU%d" % i, banks[i], banks[i][0:NS, :]) for i in range(2)]
                PTr = P.view("PTr", banks[2], banks[2][:, 0:8 * NS])
                PBC = P.view("PBC", banks[3], banks[3][:, :])
                PCC = P.view("PCC", banks[4], banks[4][:, :])
                PYT = [P.view("PYTs%d" % i, banks[5 + i], banks[5 + i][0:NS, :]) for i in range(2)]
                PQB = [P.view("PQB%d" % i, banks[5 + i], banks[5 + i][:, :]) for i in range(2)]
                PCS = P.view("PCS", banks[7], banks[7][0:NS, 0:256])
                PMBc = P.view("PMBc", banks[3], banks[3][:, 0:256])
                POs = P.view("POs", banks[4], banks[4][0:64, 0:16])
                PLs = P.view("PLs", banks[4], banks[4][0:16, 32:33])
                POt = [P.view("POt%d" % i, banks[i], banks[i][0:NS, :]) for i in range(2)]
                PLt = P.view("PLt", banks[2], banks[2][0:NS, 256:272])

                def bc3(ap2, a, b):
                    return ap2.rearrange("p (a c) -> p a c", c=1).to_broadcast([ap2.shape[0], a, b])

                def transp(src_tok, dstT):
                    def fn(e):
                        ins = None
                        for c in range(8):
                            ins = e.matmul(PTr[:, NS * c:NS * c + NS], lhsT=src_tok[:, 128 * c:128 * c + 128],
                                           rhs=idn[0:NS, 0:NS], start=True, stop=True)
                        return ins
                    P.op("pe", fn, reads=[src_tok, idn], writes=[PTr])
                    P.op("act", lambda e: e.copy(out=dstT[:].rearrange("p a c -> p (a c)"), in_=PTr[:, :]),
                         reads=[PTr], writes=[dstT])

                if os.environ.get("MK_VERBOSE"):
                    print("SBUF remaining in sample pass", nc.sbuf_bytes_remaining)
                if os.environ.get("MK_TIND"):
                    for bnd in (None, 100):
                        for oo in (KV[0][:, 0, :], prod[:, 0, :, :].rearrange("p a d -> p (a d)")):
                            for ii in (IDX[:, 0:1], PTI[:, 0:1]):
                                try:
                                    nc.gpsimd.indirect_dma_start(out=oo, out_offset=None, in_=ck_d,
                                                                 in_offset=bass.IndirectOffsetOnAxis(ap=ii, axis=0),
                                                                 bounds_check=bnd, oob_is_err=False)
                                    print("TIND ok", bnd)
                                except Exception as ex:
                                    print("TIND err", bnd, str(ex)[:60])
                P.dma("sp", XST[:], xsT.rearrange("(kc p) t -> p kc t", p=128), XST, writes=[XST])
                P.dma("sp", SC[:].rearrange("p a c -> p (a c)"), sc_d, BIGS, writes=[SC])
                P.dma("sp", v16s[:], vec16.partition_broadcast(NS), v16s, writes=[v16s])
                P.dma("sp", nws[:], nwv.partition_broadcast(NS).rearrange("p a c -> p (a c)"), KV[0], writes=[nws])
                P.dma("sp", css[:], css_d.partition_broadcast(NS).rearrange("p a c -> p (a c)"), css, writes=[css])
                P.dma("sp", SEL[:].rearrange("p a c -> p (a c)"), sel_d, SEL, writes=[SEL])
                P.dma("sp", SELT[:].rearrange("p a c -> p (a c)"), selt_d, SELT, writes=[SELT])
                P.dma("sp", PTI[:], pt_d.partition_broadcast(128).rearrange("p a c -> p (a c)"), PTI, writes=[PTI])
                P.dma("sp", iop[:], iota_d, iop, writes=[iop])
                P.op("dve", lambda e: e.tensor_copy(out=PTF[:], in_=PTI[:]), reads=[PTI], writes=[PTF])
                P.op("dve", lambda e: e.tensor_scalar(out=PTF[:], in0=PTF[:], scalar1=128.0, scalar2=iop[:, 0:1],
                                                      op0=ALU.mult, op1=ALU.add), reads=[PTF, iop], writes=[PTF])
                P.op("dve", lambda e: e.tensor_copy(out=IDX[:], in_=PTF[:]), reads=[PTF], writes=[IDX])

                def stream_mm(w_ap, ncols, lhs_fn, out_buf, out_off, nk=8):
                    c0 = 0
                    while c0 < ncols:
                        n = min(512, ncols - c0)
                        sw = WS[stg[0] % len(WS)]
                        pu = PU[stg[0] % 2]
                        stg[0] += 1
                        P.dma("sp", sw[:, :, 0:n], w_ap[:, c0:c0 + n].rearrange("(kc p) c -> p kc c", p=128), sw, writes=[sw])
                        P.op("pe", mm_chain(pu[:, 0:n], lhs_fn, lambda kc, sw=sw, n=n: sw[:, kc, 0:n], nk=nk),
                             reads=[sw, XST, TT], writes=[pu])
                        P.op("act", lambda e, pu=pu, n=n, c0=c0: e.copy(out=out_buf[:, out_off + c0:out_off + c0 + n], in_=pu[:, 0:n]),
                             reads=[pu], writes=[out_buf])
                        c0 += n

                stream_mm(w_in, DIN, lambda kc: XST[:, kc, :], Us, 0)

                if os.environ.get("MK_TIND"):
                    try:
                        nc.gpsimd.indirect_dma_start(out=KV[0][:, 0, :], out_offset=None, in_=ck_d,
                                                     in_offset=bass.IndirectOffsetOnAxis(ap=IDX[:, 0:1], axis=0),
                                                     bounds_check=100, oob_is_err=False)
                        print("TIND2 ok A")
                    except Exception as ex:
                        print("TIND2 err A", str(ex)[:60])
                for k in range(4):
                    P.dma("sp", CWk[:], cwrow[k:k + 1, :].partition_broadcast(NS).rearrange("p a c -> p (a c)"), CWk, writes=[CWk])
                    src = SC[:, k, :] if k < 3 else Us[:, C_XA:C_XA + 2048]
                    dst = cacs if k == 0 else ctmp
                    P.op("dve", lambda e, src=src, dst=dst: e.tensor_tensor(out=dst[:], in0=src, in1=CWk[:], op=ALU.mult),
                         reads=[SC, Us, CWk], writes=[dst])
                    if k > 0:
                        P.op("dve", lambda e: e.tensor_tensor(out=cacs[:], in0=cacs[:], in1=ctmp[:], op=ALU.add),
                             reads=[cacs, ctmp], writes=[cacs])
                P.dma("sp", CWk[:], cbrow.partition_broadcast(NS).rearrange("p a c -> p (a c)"), CWk, writes=[CWk])
                P.op("dve", lambda e: e.tensor_tensor(out=cacs[:], in0=cacs[:], in1=CWk[:], op=ALU.add),
                     reads=[cacs, CWk], writes=[cacs])
                P.op("act", lambda e: e.activation(out=xbc[:], in_=cacs[:], func=AF.Silu), reads=[cacs], writes=[xbc])
                cv3 = cvs_o.rearrange("p (a c) -> p a c", a=3)
                P.dma("sp", cv3[:, 0:2, :], SC[:, 1:3, :], BIGS, reads=[SC], is_out=True)
                P.dma("sp", cv3[:, 2, :], Us[:, C_XA:C_XA + 2048], Us, reads=[Us], is_out=True)

                if os.environ.get("MK_TIND"):
                    try:
                        nc.gpsimd.indirect_dma_start(out=KV[0][:, 0, :], out_offset=None, in_=ck_d,
                                                     in_offset=bass.IndirectOffsetOnAxis(ap=IDX[:, 0:1], axis=0),
                                                     bounds_check=100, oob_is_err=False)
                        print("TIND2 ok B")
                    except Exception as ex:
                        print("TIND2 err B", str(ex)[:60])
                P.op("dve", lambda e: e.tensor_tensor(out=dts[:], in0=Us[:, C_DT:C_DT + 16], in1=v16s[:, 0, :], op=ALU.add),
                     reads=[Us, v16s], writes=[dts])
                P.op("act", lambda e: e.activation(out=dte[:], in_=dts[:], func=AF.Exp), reads=[dts], writes=[dte])
                P.op("act", lambda e: e.activation(out=dts[:], in_=dte[:], func=AF.Ln, bias=1.0), reads=[dte], writes=[dts])
                P.op("act", lambda e: e.activation(out=As[:], in_=v16s[:, 1, :], func=AF.Exp), reads=[v16s], writes=[As])
                P.op("dve", lambda e: e.tensor_tensor(out=dec[:], in0=dts[:], in1=As[:], op=ALU.mult), reads=[dts, As], writes=[dec])
                P.op("act", lambda e: e.activation(out=dec[:], in_=dec[:], func=AF.Exp, scale=-1.0), reads=[dec], writes=[dec])
                x3 = xbc[:, 0:1024].rearrange("p (h d) -> p h d", h=16)
                P.op("dve", lambda e: e.tensor_tensor(out=xdt_t[:].rearrange("p (h d) -> p h d", h=16), in0=x3,
                                                      in1=bc3(dts[:], 16, 64), op=ALU.mult), reads=[xbc, dts], writes=[xdt_t])
                P.op("dve", lambda e: e.tensor_copy(out=dec_t[:].rearrange("p (h d) -> p h d", h=16), in_=bc3(dec[:], 16, 64)),
                     reads=[dec], writes=[dec_t])
                transp(xdt_t, XDTT)
                transp(dec_t, DECT)

                if os.environ.get("MK_TIND"):
                    try:
                        nc.gpsimd.indirect_dma_start(out=KV[0][:, 0, :], out_offset=None, in_=ck_d,
                                                     in_offset=bass.IndirectOffsetOnAxis(ap=IDX[:, 0:1], axis=0),
                                                     bounds_check=100, oob_is_err=False)
                        print("TIND2 ok C")
                    except Exception as ex:
                        print("TIND2 err C", str(ex)[:60])
                ssm_in = ssm_d.rearrange("n (c q s) -> n q c s", c=8, q=128)
                ssm_out = ssms_o.rearrange("n (c q s) -> n q c s", c=8, q=128)
                for n in range(NS):
                    H = Hs[0]
                    P.dma("sp", H[:], ssm_in[n], prod, writes=[H])
                    P.op("pe", lambda e, n=n: e.matmul(PBC[:, :], lhsT=SEL[:, n, :], rhs=xbc[:, 1024:1536], start=True, stop=True),
                         reads=[SEL, xbc], writes=[PBC])
                    P.op("pe", lambda e, n=n: e.matmul(PCC[:, :], lhsT=SEL[:, n, :], rhs=xbc[:, 1536:2048], start=True, stop=True),
                         reads=[SEL, xbc], writes=[PCC])
                    b4 = PBC[:, :].rearrange("p (g a s) -> p g a s", g=4, a=1).to_broadcast([128, 4, 2, 128])
                    c4 = PCC[:, :].rearrange("p (g a s) -> p g a s", g=4, a=1).to_broadcast([128, 4, 2, 128])
                    h4 = lambda t: t[:].rearrange("p (g a) s -> p g a s", g=4)
                    xcol = XDTT[:, :, n:n + 1].to_broadcast([128, 8, 128])
                    dcol = DECT[:, :, n:n + 1].to_broadcast([128, 8, 128])
                    P.op("dve", lambda e: e.tensor_tensor(out=h4(ht1), in0=b4, in1=xcol.rearrange("p (g a) s -> p g a s", g=4),
                                                          op=ALU.mult), reads=[PBC, XDTT], writes=[ht1])
                    P.op("dve", lambda e, H=H: e.tensor_tensor(out=ht2[:], in0=H[:], in1=dcol, op=ALU.mult),
                         reads=[H, DECT], writes=[ht2])
                    P.op("dve", lambda e, H=H: e.tensor_tensor(out=H[:], in0=ht1[:], in1=ht2[:], op=ALU.add),
                         reads=[ht1, ht2], writes=[H])
                    P.dma("sp", ssm_out[n], H[:], prod, reads=[H], is_out=True)
                    P.op("dve", lambda e, H=H: e.tensor_tensor(out=h4(ht1), in0=c4, in1=h4(H), op=ALU.mult),
                         reads=[PCC, H], writes=[ht1])
                    P.op("dve", lambda e, n=n: e.tensor_reduce(out=YTT[:, :, n], in_=ht1[:], axis=AX.X, op=ALU.add),
                         reads=[ht1], writes=[YTT])

                if os.environ.get("MK_TIND"):
                    try:
                        nc.gpsimd.indirect_dma_start(out=KV[0][:, 0, :], out_offset=None, in_=ck_d,
                                                     in_offset=bass.IndirectOffsetOnAxis(ap=IDX[:, 0:1], axis=0),
                                                     bounds_check=100, oob_is_err=False)
                        print("TIND2 ok D")
                    except Exception as ex:
                        print("TIND2 err D", str(ex)[:60])
                def fn_yb(e):
                    ins = None
                    for c in range(8):
                        ins = e.matmul(PYT[c // 4][:, 128 * (c % 4):128 * (c % 4) + 128], lhsT=YTT[:, c, :], rhs=idn[:],
                                       start=True, stop=True)
                    return ins
                P.op("pe", fn_yb, reads=[YTT, idn], writes=[PYT[0], PYT[1]])
                for hf in range(2):
                    P.op("act", lambda e, hf=hf: e.copy(out=y_t[:, 512 * hf:512 * hf + 512], in_=PYT[hf][:, :]),
                         reads=[PYT[hf]], writes=[y_t])
                P.op("dve", lambda e: e.tensor_tensor(out=g_t[:].rearrange("p (h d) -> p h d", h=16), in0=x3,
                                                      in1=bc3(v16s[:, 2, :], 16, 64), op=ALU.mult), reads=[xbc, v16s], writes=[g_t])
                P.op("dve", lambda e: e.tensor_tensor(out=y_t[:], in0=y_t[:], in1=g_t[:], op=ALU.add), reads=[y_t, g_t], writes=[y_t])
                P.op("act", lambda e: e.activation(out=szs[:], in_=Us[:, C_ZA:C_ZA + 1024], func=AF.Silu), reads=[Us], writes=[szs])
                P.op("dve", lambda e: e.tensor_tensor(out=g_t[:], in0=y_t[:], in1=szs[:], op=ALU.mult), reads=[y_t, szs], writes=[g_t])
                P.op("dve", lambda e: e.tensor_tensor(out=gq_t[:], in0=g_t[:], in1=g_t[:], op=ALU.mult), reads=[g_t], writes=[gq_t])
                P.op("dve", lambda e: e.tensor_reduce(out=ss4[:], in_=gq_t[:].rearrange("p (g c) -> p g c", g=4), axis=AX.X, op=ALU.add),
                     reads=[gq_t], writes=[ss4])
                P.op("act", lambda e: e.activation(out=ss4[:], in_=ss4[:], func=AF.Sqrt, bias=epsb[0:NS, 0:1], scale=1.0 / 256.0),
                     reads=[ss4, epsb], writes=[ss4])
                P.op("dve", lambda e: e.reciprocal(out=ss4[:], in_=ss4[:]), reads=[ss4], writes=[ss4])
                P.op("dve", lambda e: e.tensor_tensor(out=ya_t[:].rearrange("p (g c) -> p g c", g=4),
                                                      in0=g_t[:].rearrange("p (g c) -> p g c", g=4), in1=bc3(ss4[:], 4, 256),
                                                      op=ALU.mult), reads=[g_t, ss4], writes=[ya_t])
                P.op("dve", lambda e: e.tensor_tensor(out=ya_t[:], in0=ya_t[:], in1=nws[:], op=ALU.mult), reads=[ya_t, nws], writes=[ya_t])

                if os.environ.get("MK_TIND"):
                    try:
                        nc.gpsimd.indirect_dma_start(out=KV[0][:, 0, :], out_offset=None, in_=ck_d,
                                                     in_offset=bass.IndirectOffsetOnAxis(ap=IDX[:, 0:1], axis=0),
                                                     bounds_check=100, oob_is_err=False)
                        print("TIND2 ok E")
                    except Exception as ex:
                        print("TIND2 err E", str(ex)[:60])
                P.op("dve", lambda e: e.tensor_copy(out=qk[:, 0:16, :].rearrange("p h d -> p (h d)"), in_=Us[:, C_Q:C_Q + 1024]),
                     reads=[Us], writes=[qk])
                P.op("dve", lambda e: e.tensor_copy(out=qk[:, 16:20, :].rearrange("p h d -> p (h d)"), in_=Us[:, C_K:C_K + 256]),
                     reads=[Us], writes=[qk])
                cosb = css[:, 0:32].rearrange("p (a c) -> p a c", a=1).to_broadcast([NS, 20, 32])
                sinb = css[:, 32:64].rearrange("p (a c) -> p a c", a=1).to_broadcast([NS, 20, 32])
                q1, q2 = qk[:, :, 0:32], qk[:, :, 32:64]
                P.op("dve", lambda e: e.tensor_tensor(out=ra_s[:], in0=q1, in1=sinb, op=ALU.mult), reads=[qk, css], writes=[ra_s])
                P.op("dve", lambda e: e.tensor_tensor(out=rb_s[:], in0=q2, in1=sinb, op=ALU.mult), reads=[qk, css], writes=[rb_s])
                P.op("dve", lambda e: e.tensor_tensor(out=q1, in0=q1, in1=cosb, op=ALU.mult), reads=[qk, css], writes=[qk])
                P.op("dve", lambda e: e.tensor_tensor(out=q2, in0=q2, in1=cosb, op=ALU.mult), reads=[qk, css], writes=[qk])
                P.op("dve", lambda e: e.tensor_tensor(out=q1, in0=q1, in1=rb_s[:], op=ALU.subtract), reads=[qk, rb_s], writes=[qk])
                P.op("dve", lambda e: e.tensor_tensor(out=q2, in0=q2, in1=ra_s[:], op=ALU.add), reads=[qk, ra_s], writes=[qk])
                P.dma("sp", ks_o, qk[:, 16:20, :].rearrange("p h d -> p (h d)"), qk, reads=[qk], is_out=True)
                P.dma("sp", vs_o, Us[:, C_V:C_V + 256], Us, reads=[Us], is_out=True)

                if os.environ.get("MK_TIND"):
                    try:
                        nc.gpsimd.indirect_dma_start(out=KV[0][:, 0, :], out_offset=None, in_=ck_d,
                                                     in_offset=bass.IndirectOffsetOnAxis(ap=IDX[:, 0:1], axis=0),
                                                     bounds_check=100, oob_is_err=False)
                        print("TIND2 ok F")
                    except Exception as ex:
                        print("TIND2 err F", str(ex)[:60])
                ck_rows = ck_d
                for n in range(NS):
                    Kt = KV[0]
                    for j in range(NPG):
                        P.idma(Kt[:, j, :], ck_rows, IDX[:, NPG * n + j:NPG * n + j + 1], Kt, reads=[IDX], writes=[Kt],
                               bound=None)
                    for hf in range(2):
                        P.op("pe", lambda e, n=n, hf=hf: e.matmul(PQB[hf][:, :], lhsT=SEL[:, n, :],
                                                                  rhs=qk[:, 8 * hf:8 * hf + 8, :].rearrange("p h d -> p (h d)"),
                                                                  start=True, stop=True), reads=[SEL, qk], writes=[PQB[hf]])
                    for kvh in range(4):
                        pq = PQB[kvh // 2]
                        qv = pq[:, 256 * (kvh % 2):256 * (kvh % 2) + 256].rearrange("p (a h d) -> p a h d", a=1, h=4) \
                            .to_broadcast([128, NPG, 4, 64])
                        kvw = Kt[:, :, 64 * kvh:64 * kvh + 64].rearrange("p j (a d) -> p j a d", a=1).to_broadcast([128, NPG, 4, 64])
                        P.op("dve", lambda e, kvw=kvw, qv=qv: e.tensor_tensor(out=prod[:], in0=kvw, in1=qv, op=ALU.mult),
                             reads=[Kt, pq], writes=[prod])
                        P.op("dve", lambda e, n=n, kvh=kvh: e.tensor_reduce(
                            out=S_all[:, n, :].rearrange("p (j h) -> p j h", h=16)[:, :, 4 * kvh:4 * kvh + 4], in_=prod[:],
                            axis=AX.X, op=ALU.add), reads=[prod], writes=[S_all])
                    P.op("pe", lambda e, n=n: e.matmul(PCS[:, :], lhsT=SELT[:, n, :], rhs=S_all[:, n, :],
                                                       start=(n == 0), stop=(n == NS - 1)), reads=[SELT, S_all], writes=[PCS])
                P.op("dve", lambda e: e.tensor_copy(out=csum[:].rearrange("p j h -> p (j h)"), in_=PCS[:, :]), reads=[PCS], writes=[csum])
                cs4 = csum[:].rearrange("p (b a) h -> p b a h", a=2)
                P.op("dve", lambda e: e.tensor_tensor(out=sblk[:].rearrange("p h b -> p b h"), in0=cs4[:, :, 0, :], in1=cs4[:, :, 1, :],
                                                      op=ALU.add), reads=[csum], writes=[sblk])
                for h in range(16):
                    P.op("dve", lambda e, h=h: e.max(out=mx8[:, h, :], in_=sblk[:, h, :]), reads=[sblk], writes=[mx8])
                P.op("dve", lambda e: e.tensor_tensor(out=sl8[:], in0=sblk[:], in1=mx8[:, :, 2:3].to_broadcast([NS, 16, 8]), op=ALU.is_ge),
                     reads=[sblk, mx8], writes=[sl8])
                P.op("dve", lambda e: e.tensor_scalar(out=sl8[:], in0=sl8[:], scalar1=-1.0, scalar2=BIG, op0=ALU.add, op1=ALU.mult),
                     reads=[sl8], writes=[sl8])
                P.op("dve", lambda e: e.tensor_copy(
                    out=MB[:].rearrange("p (b a) h -> p b a h", a=2),
                    in_=sl8[:].rearrange("p h (b a) -> p b a h", a=1).to_broadcast([NS, 8, 2, 16])), reads=[sl8], writes=[MB])

                for n in range(NS):
                    Vt = KV[0]
                    for j in range(NPG):
                        P.idma(Vt[:, j, :], cv_d, IDX[:, NPG * n + j:NPG * n + j + 1], Vt, reads=[IDX], writes=[Vt],
                               bound=None)
                    P.op("pe", lambda e, n=n: e.matmul(PMBc[:, :], lhsT=SEL[:, n, :], rhs=MB[:].rearrange("p j h -> p (j h)"),
                                                       start=True, stop=True), reads=[SEL, MB], writes=[PMBc])
                    P.op("dve", lambda e, n=n: e.scalar_tensor_tensor(out=stmp[:], in0=S_all[:, n, :], scalar=SCALE, in1=PMBc[:, :],
                                                                      op0=ALU.mult, op1=ALU.add), reads=[S_all, PMBc], writes=[stmp])
                    P.op("act", lambda e: e.activation(out=Pm[:].rearrange("p j h -> p (j h)"), in_=stmp[:], func=AF.Exp),
                         reads=[stmp], writes=[Pm])

                    def fn_pv(e, Vt=Vt):
                        ins = None
                        for kvh in range(4):
                            for j in range(NPG):
                                ins = e.matmul(POs[:, 4 * kvh:4 * kvh + 4], lhsT=Vt[:, j, 64 * kvh:64 * kvh + 64],
                                               rhs=Pm[:, j, 4 * kvh:4 * kvh + 4], start=(j == 0), stop=(j == NPG - 1))
                        for j in range(NPG):
                            ins = e.matmul(PLs[:, :], lhsT=Pm[:, j, :], rhs=onesf[:, 0:1], start=(j == 0), stop=(j == NPG - 1))
                        return ins
                    P.op("pe", fn_pv, reads=[Vt, Pm, onesf], writes=[POs])
                    P.op("act", lambda e, n=n: e.copy(out=OT[:, :, n], in_=POs[:, :]), reads=[POs], writes=[OT])
                    P.op("dve", lambda e, n=n: e.tensor_copy(out=LT[:, n:n + 1], in_=PLs[:, :]), reads=[PLs], writes=[LT])

                def fn_ob(e):
                    ins = None
                    for h in range(16):
                        ins = e.matmul(POt[h // 8][:, 64 * (h % 8):64 * (h % 8) + 64], lhsT=OT[:, h, :], rhs=idn[0:64, 0:64],
                                       start=True, stop=True)
                    return ins
                P.op("pe", fn_ob, reads=[OT, idn], writes=[POt[0], POt[1]])
                P.op("pe", lambda e: e.matmul(PLt[:, :], lhsT=LT[:], rhs=idn[0:16, 0:16], start=True, stop=True),
                     reads=[LT, idn], writes=[PLt])
                for hf in range(2):
                    P.op("act", lambda e, hf=hf: e.copy(out=o_t[:, 8 * hf:8 * hf + 8, :].rearrange("p h d -> p (h d)"), in_=POt[hf][:, :]),
                         reads=[POt[hf]], writes=[o_t])
                P.op("dve", lambda e: e.tensor_copy(out=l_t[:], in_=PLt[:, :]), reads=[PLt], writes=[l_t])
                k4 = qk[:, 16:20, :].rearrange("p (k a) d -> p k a d", a=1).to_broadcast([NS, 4, 4, 64])
                v4 = Us[:, C_V:C_V + 256].rearrange("p (k a d) -> p k a d", k=4, a=1).to_broadcast([NS, 4, 4, 64])
                q4 = qk[:, 0:16, :].rearrange("p (k a) d -> p k a d", a=4)
                P.op("dve", lambda e: e.tensor_tensor(out=sprod[:].rearrange("p (k a) d -> p k a d", a=4), in0=q4, in1=k4, op=ALU.mult),
                     reads=[qk], writes=[sprod])
                P.op("dve", lambda e: e.tensor_reduce(out=sown[:], in_=sprod[:], axis=AX.X, op=ALU.add), reads=[sprod], writes=[sown])
                P.op("act", lambda e: e.activation(out=sown[:], in_=sown[:], func=AF.Exp, scale=SCALE), reads=[sown], writes=[sown])
                P.op("dve", lambda e: e.tensor_tensor(out=sprod[:].rearrange("p (k a) d -> p k a d", a=4), in0=v4,
                                                      in1=sown[:].rearrange("p (k a c) -> p k a c", k=4, c=1).to_broadcast([NS, 4, 4, 64]),
                                                      op=ALU.mult), reads=[Us, sown], writes=[sprod])
                P.op("dve", lambda e: e.tensor_tensor(out=o_t[:], in0=o_t[:], in1=sprod[:], op=ALU.add), reads=[o_t, sprod], writes=[o_t])
                P.op("dve", lambda e: e.tensor_tensor(out=l_t[:], in0=l_t[:], in1=sown[:], op=ALU.add), reads=[l_t, sown], writes=[l_t])
                P.op("dve", lambda e: e.reciprocal(out=l_t[:], in_=l_t[:]), reads=[l_t], writes=[l_t])
                P.op("dve", lambda e: e.tensor_tensor(out=o_t[:], in0=o_t[:], in1=bc3(l_t[:], 16, 64), op=ALU.mult),
                     reads=[o_t, l_t], writes=[o_t])
                P.op("act", lambda e: e.activation(out=szs[:], in_=Us[:, C_ZB:C_ZB + 1024], func=AF.Silu), reads=[Us], writes=[szs])
                P.op("dve", lambda e: e.tensor_tensor(out=ob_t[:], in0=o_t[:].rearrange("p h d -> p (h d)"), in1=szs[:], op=ALU.mult),
                     reads=[o_t, szs], writes=[ob_t])

                P.dma("sp", gbs[:], lng.partition_broadcast(NS).rearrange("p a c -> p (a c)"), prod, writes=[gbs])
                P.dma("sp", bbs[:], lnb.partition_broadcast(NS).rearrange("p a c -> p (a c)"), prod, writes=[bbs])
                P.dma("sp", xrs[:], xs_tok, prod, writes=[xrs])
                P.op("act", lambda e: e.activation(out=sg[:], in_=Us[:, C_GA:C_GA + 2048], func=AF.Sigmoid), reads=[Us], writes=[sg])
                transp(ya_t, TT)
                stream_mm(w_a, 1024, lambda kc: TT[:, kc, :], br, 0)
                transp(ob_t, TT)
                stream_mm(w_b, 1024, lambda kc: TT[:, kc, :], br, 1024)
                P.op("dve", lambda e: e.tensor_tensor(out=br[:], in0=br[:], in1=sg[:], op=ALU.mult), reads=[br, sg], writes=[br])
                P.op("dve", lambda e: e.tensor_tensor(out=mix_t[:], in0=br[:, 0:1024], in1=br[:, 1024:2048], op=ALU.add),
                     reads=[br], writes=[mix_t])
                transp(mix_t, TT)
                stream_mm(w_o, 1024, lambda kc: TT[:, kc, :], pre_s, 0)
                P.op("dve", lambda e: e.scalar_tensor_tensor(out=pre_s[:], in0=xrs[:], scalar=ALPHA, in1=pre_s[:], op0=ALU.mult, op1=ALU.add),
                     reads=[xrs, pre_s], writes=[pre_s])
                for hf in range(2):
                    P.op("dve", lambda e, hf=hf: e.bn_stats(out=sts[:, hf, :], in_=pre_s[:, 512 * hf:512 * hf + 512]), reads=[pre_s], writes=[sts])
                P.op("dve", lambda e: e.bn_aggr(out=mvs[:], in_=sts[:].rearrange("p a c -> p (a c)")), reads=[sts], writes=[mvs])
                P.op("act", lambda e: e.activation(out=rss[:], in_=mvs[:, 1:2], func=AF.Sqrt, bias=epsb[0:NS, 0:1]), reads=[mvs, epsb], writes=[rss])
                P.op("dve", lambda e: e.reciprocal(out=rss[:], in_=rss[:]), reads=[rss], writes=[rss])
                P.op("dve", lambda e: e.tensor_scalar(out=pre_s[:], in0=pre_s[:], scalar1=mvs[:, 0:1], scalar2=rss[:, 0:1],
                                                      op0=ALU.subtract, op1=ALU.mult), reads=[pre_s, mvs, rss], writes=[pre_s])
                P.op("dve", lambda e: e.tensor_tensor(out=pre_s[:], in0=pre_s[:], in1=gbs[:], op=ALU.mult), reads=[pre_s, gbs], writes=[pre_s])
                P.op("dve", lambda e: e.tensor_tensor(out=pre_s[:], in0=pre_s[:], in1=bbs[:], op=ALU.add), reads=[pre_s, bbs], writes=[pre_s])
                P.dma("sp", ys_o, pre_s[:], pre_s, reads=[pre_s], is_out=True)
            P.barrier()

        YAT = P.sb("YAT", [128, 8, 2048], BF16)
        OBT = P.sb("OBT", [128, 8, 2048], BF16)
        with contextlib.ExitStack() as st:
            alloc_io(st, "a")
            XO = XOl[0]
            Wf = P.sb("Wf", [128, 8, 512], BF16, st)
            Wt = P.sb("Wt", [128, 8, 260], BF16, st)
            cw = P.sb("cw", [128, 4, 4], F32, st)
            cb = P.sb("cb", [128, 4], F32, st)
            v16 = P.sb("v16", [128, 3, 4], F32, st)
            Aneg = P.sb("Aneg", [128, 4], F32, st)
            nw = P.sb("nw", [128, 256], F32, st)
            U = P.sb("U", [128, 3, 515], F32, st)
            XC = P.sb("XC", [128, 3, 512], F32, st)
            cacc = P.sb("cacc", [128, 512], F32, st)
            UO = P.sb("UO", [128, 4, 131], F32, st)
            XCO = P.sb("XCO", [128, 4, 128], F32, st)
            XCOb = P.sb("XCOb", [128, 4, 128], BF16, st)
            caco = P.sb("caco", [128, 128], F32, st)
            hT = P.sb("hT", [128, 256], F32, st)
            Hsel = P.sb("Hsel", [128, 256], F32, st)
            Hselb = P.sb("Hselb", [128, 256], BF16, st)
            sm = {n: P.sb("sm_" + n, [128, 4], F32, st) for n in
                  ("t0", "e0", "dt", "adt", "acs", "acl", "te", "ecl", "w", "nacs", "eacs")}
            so = {n: P.sb("so_" + n, [128, 4], F32, st) for n in
                  ("t0", "e0", "dt", "adt", "acs", "nacs", "eacs")}
            xs_tok = P.sb("xs_tok", [128, 256], F32, st)
            B_tokb = P.sb("B_tokb", [128, 128], BF16, st)
            xdtw = P.sb("xdtw", [128, 256], BF16, st)
            xso = P.sb("xso", [128, 256], F32, st)
            xdto = P.sb("xdto", [128, 256], BF16, st)
            Radt = P.sb("Radt", [128, 4, 128], F32, st)
            dcl = P.sb("dcl", [128, 4, 128], F32, st)
            dce = P.sb("dce", [128, 4, 128], F32, st)
            cbm = P.sb("cbm", [128, 128], F32, st)
            MT = P.sb("MT", [128, 4, 128], BF16, st)
            Ysb = P.sb("Ysb", [128, 256], F32, st)
            yv = P.sb("yv", [128, 256], F32, st)
            sza = P.sb("sza", [128, 256], F32, st)
            gg = P.sb("gg", [128, 256], F32, st)
            gsq = P.sb("gsq", [128, 256], F32, st)
            ss = P.sb("ss", [128, 1], F32, st)
            rstd = P.sb("rstd", [128, 1], F32, st)
            ya = P.sb("ya", [128, 256], F32, st)
            cvs = P.sb("cvs", [128, 3, 3], F32, st)
            cvc = P.sb("cvc", [128, 3], F32, st)

            PF = [P.view("PF0", banks[0], banks[0][:, :]), P.view("PF1", banks[1], banks[1][:, :])]
            PTc = P.view("PTc", banks[2], banks[2][:, 0:16])
            Pacs = P.view("Pacs", banks[2], banks[2][:, 16:20])
            Pacl = P.view("Pacl", banks[2], banks[2][:, 20:24])
            PX = P.view("PX", banks[2], banks[2][:, 64:448])
            PST = P.view("PST", banks[3], banks[3][:, 0:256])
            PCB = P.view("PCB", banks[3], banks[3][:, 256:384])
            PFo = [P.view("PFo0", banks[4], banks[4][:, 0:131]), P.view("PFo1", banks[4], banks[4][:, 132:263])]
            PTo = P.view("PTo", banks[5], banks[5][:, 0:260])
            Poa = P.view("Poa", banks[5], banks[5][:, 264:268])
            PAB = P.view("PAB", banks[6], banks[6][:, :])
            PXo = P.view("PXo", banks[7], banks[7][:, 0:256])
            PY = P.view("PY", banks[7], banks[7][:, 256:512])
            PYO = PST
            PYT = PX

            for g in range(NG if 's' in PASSES else 0):
                load_w(Wf, 0, w_in[:, C_XA + 256 * g:C_XA + 256 * g + 256], 256)
                load_w(Wf, 256, w_in[:, C_B + 128 * g:C_B + 128 * g + 128], 128)
                load_w(Wf, 384, w_in[:, C_C + 128 * g:C_C + 128 * g + 128], 128)
                load_w(Wt, 0, w_in[:, C_ZA + 256 * g:C_ZA + 256 * g + 256], 256)
                load_w(Wt, 256, w_in[:, C_DT + 4 * g:C_DT + 4 * g + 4], 4)
                chs = [256 * g, 256 * g + 128, 1024 + 128 * g, 1536 + 128 * g]
                for m, c0 in enumerate(chs):
                    P.dma("sp", cw[:, m, :], cwT[c0:c0 + 128, :], cw, writes=[cw])
                    P.dma("sp", cb[:, m:m + 1], cbT[c0:c0 + 128, :], cb, writes=[cb])
                P.dma("sp", v16[:], vec16[:, 4 * g:4 * g + 4].partition_broadcast(128), v16, writes=[v16])
                P.dma("sp", nw[:], nwv[:, 256 * g:256 * g + 256].partition_broadcast(128).rearrange("p a c -> p (a c)"),
                      nw, writes=[nw])
                P.op("act", lambda e: e.activation(out=Aneg[:], in_=v16[:, 1, :], func=AF.Exp), reads=[v16], writes=[Aneg])
                P.op("dve", lambda e: e.tensor_scalar(out=Aneg[:], in0=Aneg[:], scalar1=-1.0, scalar2=None, op0=ALU.mult),
                     reads=[Aneg], writes=[Aneg])
                P.op("pool", lambda e: e.memset(hT[:], 0.0), writes=[hT])
                P.op("pool", lambda e: e.memset(U[:], 0.0), writes=[U])

                def dt_chain(smd, psrc, rd):
                    t0, e0, dtt, adt = smd["t0"], smd["e0"], smd["dt"], smd["adt"]
                    P.op("dve", lambda e: e.tensor_tensor(out=t0[:], in0=psrc, in1=v16[:, 0, :], op=ALU.add),
                         reads=rd + [v16], writes=[t0])
                    P.op("act", lambda e: e.activation(out=e0[:], in_=t0[:], func=AF.Exp), reads=[t0], writes=[e0])
                    P.op("act", lambda e: e.activation(out=dtt[:], in_=e0[:], func=AF.Ln, bias=1.0), reads=[e0], writes=[dtt])
                    P.op("dve", lambda e: e.tensor_tensor(out=adt[:], in0=dtt[:], in1=Aneg[:], op=ALU.mult),
                         reads=[dtt, Aneg], writes=[adt])

                for i in range(NI):
                    xb = load_x(i)
                    for m in range(3):
                        pf = PF[m % 2]
                        P.op("pe", mm_chain(pf[:, :], lambda kc, m=m: Wf[:, kc, m * 128:(m + 1) * 128],
                                            lambda kc, xb=xb: xb[:, kc, :]), reads=[Wf, xb], writes=[pf])
                        P.op("act", lambda e, m=m, pf=pf: e.copy(out=U[:, m, 3:515], in_=pf[:, :]), reads=[pf], writes=[U])
                    for m in range(3):
                        P.op("dve", lambda e, m=m: e.tensor_scalar(out=cacc[:], in0=U[:, m, 0:512], scalar1=cw[:, m, 0:1],
                                                                   scalar2=cb[:, m:m + 1], op0=ALU.mult, op1=ALU.add),
                             reads=[U, cw, cb], writes=[cacc])
                        for k in range(1, 4):
                            P.op("dve", lambda e, m=m, k=k: e.scalar_tensor_tensor(
                                out=cacc[:], in0=U[:, m, k:k + 512], scalar=cw[:, m, k:k + 1], in1=cacc[:],
                                op0=ALU.mult, op1=ALU.add), reads=[U, cw, cacc], writes=[cacc])
                        P.op("act", lambda e, m=m: e.activation(out=XC[:, m, :], in_=cacc[:], func=AF.Silu),
                             reads=[cacc], writes=[XC])
                    if i == NI - 1:
                        P.op("dve", lambda e: e.tensor_copy(out=cvs[:], in_=U[:, :, 512:515]), reads=[U], writes=[cvs])
                        P.dma("sp", cvA_o[g], cvs[:], cvs, reads=[cvs], is_out=True)
                    P.op("dve", lambda e: e.tensor_copy(out=U[:, :, 0:3], in_=U[:, :, 512:515]), reads=[U], writes=[U])
                    def fn_dt(e, xb=xb):
                        ins = None
                        for k in range(4):
                            for kc in range(8):
                                ins = e.matmul(PTc[:, 4 * k:4 * k + 4], lhsT=xb[:, kc, 128 * k:128 * k + 128],
                                               rhs=Wt[:, kc, 256:260], start=(kc == 0), stop=(kc == 7))
                        return ins
                    P.op("pe", fn_dt, reads=[xb, Wt], writes=[PTc])
                    for k in range(4):
                        cs = slice(128 * k, 128 * k + 128)
                        dt_chain(sm, PTc[:, 4 * k:4 * k + 4], [PTc])
                        adt = sm["adt"]
                        P.op("pe", lambda e: e.matmul(Pacs[:, :], lhsT=tri[:], rhs=adt[:], start=True, stop=True),
                             reads=[tri, adt], writes=[Pacs])
                        P.op("pe", lambda e: e.matmul(Pacl[:, :], lhsT=onesf[:], rhs=adt[:], start=True, stop=True),
                             reads=[onesf, adt], writes=[Pacl])
                        te, ecl, w_ = sm["te"], sm["ecl"], sm["w"]
                        P.op("dve", lambda e: e.tensor_copy(out=sm["acs"][:], in_=Pacs[:, :]), reads=[Pacs], writes=[sm["acs"]])
                        P.op("dve", lambda e: e.tensor_tensor(out=te[:], in0=Pacl[:, :], in1=sm["acs"][:], op=ALU.subtract),
                             reads=[Pacl, sm["acs"]], writes=[te])
                        P.op("act", lambda e: e.activation(out=te[:], in_=te[:], func=AF.Exp), reads=[te], writes=[te])
                        P.op("act", lambda e: e.activation(out=ecl[:], in_=Pacl[:, :], func=AF.Exp), reads=[Pacl], writes=[ecl])
                        P.op("dve", lambda e: e.tensor_tensor(out=w_[:], in0=te[:], in1=sm["dt"][:], op=ALU.mult),
                             reads=[te, sm["dt"]], writes=[w_])
                        def fn_tr(e, cs=cs):
                            V = os.environ.get("MK_V", "")
                            if V == "1":
                                return tr(e, PX[:, 0:128], tri[:], idn[:])
                            if V == "4":
                                return e.matmul(PX[:, 0:4], lhsT=tri[:], rhs=sm["adt"][:], start=True, stop=True)
                            if V == "5":
                                return e.matmul(PX[:, 0:128], lhsT=tri[:], rhs=tri[:], start=True, stop=True)
                            if V == "2":
                                return tr(e, PX[:, 0:128], XC[:, 0, cs], tri[:])
                            if V == "3":
                                return tr(e, PX[:, 0:128], XC[:, 0, cs], idn[:])
                            tr(e, PX[:, 0:128], XC[:, 0, cs], idn[:])
                            tr(e, PX[:, 128:256], XC[:, 1, cs], idn[:])
                            return tr(e, PX[:, 256:384], XC[:, 2, cs], idn[:])
                        P.op("pe", fn_tr, reads=[XC, idn], writes=[PX])
                        P.op("act", lambda e: e.copy(out=B_tokb[:], in_=PX[:, 256:384]), reads=[PX], writes=[B_tokb])
                        for h in range(4):
                            P.op("dve", lambda e, h=h: e.tensor_scalar(
                                out=xdtw[:, 64 * h:64 * h + 64], in0=PX[:, 64 * h:64 * h + 64],
                                scalar1=w_[:, h:h + 1], scalar2=None, op0=ALU.mult), reads=[PX, w_], writes=[xdtw])
                        P.op("pe", lambda e: e.matmul(PST[:, :], lhsT=B_tokb[:], rhs=xdtw[:], start=True, stop=True),
                             reads=[B_tokb, xdtw], writes=[PST])
                        if k == 0:
                            P.op("dve", lambda e: e.tensor_scalar(out=Hsel[:], in0=hT[:], scalar1=oh[:, 0:1], scalar2=None,
                                                                  op0=ALU.mult), reads=[hT, oh], writes=[Hsel])
                        else:
                            P.op("dve", lambda e, k=k: e.scalar_tensor_tensor(
                                out=Hsel[:], in0=hT[:], scalar=oh[:, k:k + 1], in1=Hsel[:], op0=ALU.mult, op1=ALU.add),
                                reads=[hT, oh, Hsel], writes=[Hsel])
                        for h in range(4):
                            P.op("dve", lambda e, h=h: e.scalar_tensor_tensor(
                                out=hT[:, 64 * h:64 * h + 64], in0=hT[:, 64 * h:64 * h + 64], scalar=ecl[:, h:h + 1],
                                in1=PST[:, 64 * h:64 * h + 64], op0=ALU.mult, op1=ALU.add),
                                reads=[hT, ecl, PST], writes=[hT])
                    P.op("act", lambda e: e.copy(out=Hselb[:], in_=Hsel[:]), reads=[Hsel], writes=[Hselb])

                    load_xo(i)
                    for m in range(4):
                        pf = PFo[m % 2]
                        P.op("pe", mm_chain(pf[:, :], lambda kc, m=m: Wf[:, kc, m * 128:(m + 1) * 128],
                                            lambda kc: XO[:, kc, :]), reads=[Wf, XO], writes=[pf])
                        P.op("act", lambda e, m=m, pf=pf: e.copy(out=UO[:, m, :], in_=pf[:, :]), reads=[pf], writes=[UO])
                    for m in range(4):
                        P.op("dve", lambda e, m=m: e.tensor_scalar(out=caco[:], in0=UO[:, m, 0:128], scalar1=cw[:, m, 0:1],
                                                                   scalar2=cb[:, m:m + 1], op0=ALU.mult, op1=ALU.add),
                             reads=[UO, cw, cb], writes=[caco])
                        for k in range(1, 4):
                            P.op("dve", lambda e, m=m, k=k: e.scalar_tensor_tensor(
                                out=caco[:], in0=UO[:, m, k:k + 128], scalar=cw[:, m, k:k + 1], in1=caco[:],
                                op0=ALU.mult, op1=ALU.add), reads=[UO, cw, caco], writes=[caco])
                        P.op("act", lambda e, m=m: e.activation(out=XCO[:, m, :], in_=caco[:], func=AF.Silu),
                             reads=[caco], writes=[XCO])
                    P.op("pool", lambda e: e.tensor_copy(out=XCOb[:], in_=XCO[:]), reads=[XCO], writes=[XCOb])
                    if i == NI - 1:
                        P.op("dve", lambda e: e.tensor_copy(out=cvc[:], in_=UO[:, 3, 128:131]), reads=[UO], writes=[cvc])
                        P.dma("sp", cvC_o[g], cvc[:], cvc, reads=[cvc], is_out=True)
                    P.op("pe", mm_chain(PTo[:, :], lambda kc: XO[:, kc, 3:131], lambda kc: Wt[:, kc, :]),
                         reads=[XO, Wt], writes=[PTo])
                    dt_chain(so, PTo[:, 256:260], [PTo])
                    P.op("act", lambda e: e.activation(out=sza[:], in_=PTo[:, 0:256], func=AF.Silu), reads=[PTo], writes=[sza])
                    adto = so["adt"]
                    P.op("pe", lambda e: e.matmul(Poa[:, :], lhsT=tri[:], rhs=adto[:], start=True, stop=True),
                         reads=[tri, adto], writes=[Poa])
                    P.op("dve", lambda e: e.tensor_copy(out=so["acs"][:], in_=Poa[:, :]), reads=[Poa], writes=[so["acs"]])
                    P.op("act", lambda e: e.activation(out=so["eacs"][:], in_=so["acs"][:], func=AF.Exp), reads=[so["acs"]],
                         writes=[so["eacs"]])
                    for h in range(4):
                        P.op("dve", lambda e, h=h: e.tensor_scalar(out=Radt[:, h, :], in0=tri[:], scalar1=adto[:, h:h + 1],
                                                                   scalar2=None, op0=ALU.mult), reads=[tri, adto], writes=[Radt])

                    def fn_ab(e):
                        ins = None
                        for h in range(4):
                            ins = e.matmul(PAB[:, 128 * h:128 * h + 128], lhsT=onesf[:], rhs=Radt[:, h, :], start=True, stop=True)
                        return ins
                    P.op("pe", fn_ab, reads=[onesf, Radt], writes=[PAB])
                    for h in range(4):
                        P.op("dve", lambda e, h=h: e.tensor_scalar(
                            out=dcl[:, h, :], in0=PAB[:, 128 * h:128 * h + 128], scalar1=so["acs"][:, h:h + 1], scalar2=0.0,
                            op0=ALU.subtract, op1=ALU.min), reads=[PAB, so["acs"]], writes=[dcl])
                    P.op("act", lambda e: e.activation(out=dce[:], in_=dcl[:], func=AF.Exp), reads=[dcl], writes=[dce])
                    P.op("pe", lambda e: e.matmul(PCB[:, :], lhsT=XCOb[:, 2, :], rhs=XCOb[:, 3, :], start=True, stop=True),
                         reads=[XCOb], writes=[PCB])
                    P.op("dve", lambda e: e.tensor_tensor(out=cbm[:], in0=PCB[:, :], in1=tri[:], op=ALU.mult),
                         reads=[PCB, tri], writes=[cbm])
                    P.op("dve", lambda e: e.tensor_tensor(
                        out=MT[:], in0=dce[:], in1=cbm[:].rearrange("p (a c) -> p a c", a=1).to_broadcast([128, 4, 128]),
                        op=ALU.mult), reads=[dce, cbm], writes=[MT])

                    def fn_tro(e):
                        tr(e, PXo[:, 0:128], XCO[:, 0, :], idn[:])
                        return tr(e, PXo[:, 128:256], XCO[:, 1, :], idn[:])
                    P.op("pe", fn_tro, reads=[XCO, idn], writes=[PXo])
                    P.op("act", lambda e: e.copy(out=xso[:], in_=PXo[:, :]), reads=[PXo], writes=[xso])
                    for h in range(4):
                        P.op("dve", lambda e, h=h: e.tensor_scalar(
                            out=xdto[:, 64 * h:64 * h + 64], in0=xso[:, 64 * h:64 * h + 64], scalar1=so["dt"][:, h:h + 1],
                            scalar2=None, op0=ALU.mult), reads=[xso, so["dt"]], writes=[xdto])

                    def fn_y(e):
                        ins = None
                        for h in range(4):
                            ins = e.matmul(PY[:, 64 * h:64 * h + 64], lhsT=MT[:, h, :], rhs=xdto[:, 64 * h:64 * h + 64],
                                           start=True, stop=True)
                        return ins
                    P.op("pe", fn_y, reads=[MT, xdto], writes=[PY])
                    P.op("pe", lambda e: e.matmul(PYO[:, :], lhsT=XCOb[:, 3, :], rhs=Hselb[:], start=True, stop=True),
                         reads=[XCOb, Hselb], writes=[PYO])
                    P.op("act", lambda e: e.copy(out=Ysb[:], in_=PY[:, :]), reads=[PY], writes=[Ysb])
                    for h in range(4):
                        hs = slice(64 * h, 64 * h + 64)
                        P.op("dve", lambda e, h=h, hs=hs: e.scalar_tensor_tensor(
                            out=yv[:, hs], in0=PYO[:, hs], scalar=so["eacs"][:, h:h + 1], in1=Ysb[:, hs],
                            op0=ALU.mult, op1=ALU.add), reads=[PYO, so["eacs"], Ysb], writes=[yv])
                        P.op("dve", lambda e, h=h, hs=hs: e.scalar_tensor_tensor(
                            out=yv[:, hs], in0=xso[:, hs], scalar=v16[:, 2, h:h + 1], in1=yv[:, hs],
                            op0=ALU.mult, op1=ALU.add), reads=[xso, v16, yv], writes=[yv])
                    P.op("dve", lambda e: e.tensor_tensor(out=gg[:], in0=yv[:], in1=sza[:], op=ALU.mult),
                         reads=[yv, sza], writes=[gg])
                    P.op("dve", lambda e: e.tensor_tensor(out=gsq[:], in0=gg[:], in1=gg[:], op=ALU.mult),
                         reads=[gg], writes=[gsq])
                    P.op("dve", lambda e: e.tensor_reduce(out=ss[:], in_=gsq[:], axis=AX.X, op=ALU.add),
                         reads=[gsq], writes=[ss])
                    P.op("act", lambda e: e.activation(out=rstd[:], in_=ss[:], func=AF.Sqrt, bias=epsb[:, 0:1], scale=1.0 / 256.0),
                         reads=[ss, epsb], writes=[rstd])
                    P.op("dve", lambda e: e.reciprocal(out=rstd[:], in_=rstd[:]), reads=[rstd], writes=[rstd])
                    P.op("dve", lambda e: e.scalar_tensor_tensor(out=ya[:], in0=gg[:], scalar=rstd[:, 0:1], in1=nw[:],
                                                                 op0=ALU.mult, op1=ALU.mult), reads=[gg, rstd, nw], writes=[ya])

                    def fn_yt(e):
                        tr(e, PYT[:, 0:128], ya[:, 0:128], idn[:])
                        return tr(e, PYT[:, 128:256], ya[:, 128:256], idn[:])
                    P.op("pe", fn_yt, reads=[ya, idn], writes=[PYT])
                    P.op("act", lambda e, g=g, i=i: e.copy(
                        out=YAT[:, 2 * g:2 * g + 2, 128 * i:128 * i + 128],
                        in_=PYT[:, 0:256].rearrange("p (a c) -> p a c", a=2)), reads=[PYT], writes=[YAT])
                P.dma("sp", ssm_o[g], hT[:], hT, reads=[hT], is_out=True)

        P.barrier()
        with contextlib.ExitStack() as st:
            alloc_io(st, "b")
            XO = XOl[0]
            Wkv = P.sb("Wkv", [128, 8, 128], BF16, st)
            Wqz = P.sb("Wqz", [128, 8, 512], BF16, st)
            KTA = P.sb("KTA", [96, SEQ], BF16, st)
            VA = P.sb("VA", [128, 64, 65], BF16, st)
            KTf = P.sb("KTf", [64, 256], F32, st)
            KM = P.sb("KM", [64, 32], F32, st)
            CM = P.sb("CM", [128, 4, 512], BF16, st)
            EL = P.sb("EL", [128, 3, 16, 32], F32, st)
            csa4 = [P.sb("csa4%d" % i, [128, 4, 64], F32, st) for i in range(2)]
            kv4 = P.sb("kv4", [128, 4, 128], F32, st)
            kr4 = P.sb("kr4", [128, 4, 64], F32, st)
            KTf4 = P.sb("KTf4", [64, 512], F32, st)
            cso = P.sb("cso", [128, 64], F32, st)
            kv = P.sb("kv", [128, 128], F32, st)
            kr = P.sb("kr", [128, 64], F32, st)
            ra = P.sb("ra", [128, 4, 32], F32, st)
            rb = P.sb("rb", [128, 4, 32], F32, st)
            qf = P.sb("qf", [128, 4, 64], F32, st)
            qr = P.sb("qr", [128, 4, 64], F32, st)
            szb = P.sb("szb", [128, 256], F32, st)
            QA = P.sb("QA", [96, 4, 128], BF16, st)
            QTf = P.sb("QTf", [64, 4, 128], F32, st)
            spd = P.sb("spd", [128, 4, 32], F32, st)
            mx = P.sb("mx", [128, 4, 8], F32, st)
            sel = P.sb("sel", [128, 4, 32], F32, st)
            MBP = P.sb("MBP", [128, 4, 96], F32, st)
            PTb = [P.sb("PTb%d" % i, [128, 512], BF16, st) for i in range(3)]
            Osb = P.sb("Osb", [65, 512], F32, st)
            rl = P.sb("rl", [128, 4], F32, st)
            ob = P.sb("ob", [128, 4, 64], F32, st)
            ob2 = P.sb("ob2", [128, 256], F32, st)

            PT2 = P.view("PT2", banks[0], banks[0][:, :])
            PKT = P.view("PKT", banks[2], banks[2][0:64, 256:384])
            PKT4 = P.view("PKT4", banks[1], banks[1][0:64, :])
            PQ = P.view("PQ", banks[1], banks[1][:, :])
            PSB = P.view("PSB", banks[2], banks[2][:, 0:128])
            PMB = P.view("PMB", banks[2], banks[2][0:96, 128:256])
            POT = PQ
            PS = [P.view("PS%d" % i, banks[3 + i], banks[3 + i][:, :]) for i in range(3)]
            PO = [P.view("PO%d" % i, banks[6 + i], banks[6 + i][0:65, :]) for i in range(2)]
            POBT = P.view("POBT", banks[0], banks[0][:, 0:256])

            for j in range(SEQ // 512):
                s = WS[stg[0] % len(WS)]
                stg[0] += 1
                P.dma("sp", s[64:96, 0, :], kbi_d[:, 512 * j:512 * j + 512], s, writes=[s])
                P.op("pool", lambda e, j=j, s=s: e.tensor_copy(out=KTA[64:96, 512 * j:512 * j + 512], in_=s[64:96, 0, :]),
                     reads=[s], writes=[KTA])
            s = WS[stg[0] % len(WS)]
            stg[0] += 1
            P.dma("sp", s[:, 0:4, :], cm_d.rearrange("p (a c) -> p a c", a=4), s, writes=[s])
            P.op("pool", lambda e, s=s: e.tensor_copy(out=CM[:], in_=s[:, 0:4, :]), reads=[s], writes=[CM])
            P.dma("sp", EL[:].rearrange("p a b c -> p a (b c)"), el_d.partition_broadcast(128), EL, writes=[EL])
            P.op("pool", lambda e: e.memset(VA[:, :, 64:65], 1.0), writes=[VA])
            P.op("pool", lambda e: e.memset(MBP[:], 0.0), writes=[MBP])

            def rope(src3, dst3, cs, nh, rd, full=False):
                if full:
                    cosb, sinb = cs[:, :, 0:32], cs[:, :, 32:64]
                else:
                    cosb = cs[:, 0:32].rearrange("p (a c) -> p a c", a=1).to_broadcast([128, nh, 32])
                    sinb = cs[:, 32:64].rearrange("p (a c) -> p a c", a=1).to_broadcast([128, nh, 32])
                a_, b_ = ra[:, 0:nh, :], rb[:, 0:nh, :]
                x1, x2 = src3[:, :, 0:32], src3[:, :, 32:64]
                P.op("dve", lambda e: e.tensor_tensor(out=a_, in0=x1, in1=cosb, op=ALU.mult), reads=rd, writes=[ra])
                P.op("dve", lambda e: e.tensor_tensor(out=b_, in0=x2, in1=sinb, op=ALU.mult), reads=rd, writes=[rb])
                P.op("dve", lambda e: e.tensor_tensor(out=dst3[:, :, 0:32], in0=a_, in1=b_, op=ALU.subtract),
                     reads=[ra, rb], writes=rd[-1:])
                P.op("dve", lambda e: e.tensor_tensor(out=a_, in0=x2, in1=cosb, op=ALU.mult), reads=rd + [ra], writes=[ra])
                P.op("dve", lambda e: e.tensor_tensor(out=b_, in0=x1, in1=sinb, op=ALU.mult), reads=rd + [rb], writes=[rb])
                P.op("dve", lambda e: e.tensor_tensor(out=dst3[:, :, 32:64], in0=a_, in1=b_, op=ALU.add),
                     reads=[ra, rb], writes=rd[-1:])

            for g in range(NG if 'a' in PASSES else 0):
                load_w(Wkv, 0, w_in[:, C_K + 64 * g:C_K + 64 * g + 64], 64)
                load_w(Wkv, 64, w_in[:, C_V + 64 * g:C_V + 64 * g + 64], 64)
                load_w(Wqz, 0, w_in[:, C_Q + 256 * g:C_Q + 256 * g + 256], 256)
                load_w(Wqz, 256, w_in[:, C_ZB + 256 * g:C_ZB + 256 * g + 256], 256)
                P.op("pool", lambda e: e.memset(KM[:], 0.0), writes=[KM])
                nps = 0
                for i in range(NI):
                    xb = load_x(i)
                    ca4 = csa4[i % 2]
                    P.dma("sp", ca4[:], cs_all[512 * i:512 * i + 512, :].rearrange("(c p) d -> p c d", p=128), ca4, writes=[ca4])

                    def fn_kv(e, xb=xb):
                        ins = None
                        for k in range(4):
                            for kc in range(8):
                                ins = e.matmul(PT2[:, 128 * k:128 * k + 128], lhsT=xb[:, kc, 128 * k:128 * k + 128],
                                               rhs=Wkv[:, kc, :], start=(kc == 0), stop=(kc == 7))
                        return ins
                    P.op("pe", fn_kv, reads=[xb, Wkv], writes=[PT2])
                    P.op("act", lambda e: e.copy(out=kv4[:].rearrange("p a c -> p (a c)"), in_=PT2[:, :]), reads=[PT2], writes=[kv4])
                    rope(kv4[:, :, 0:64], kr4[:], ca4, 4, [kv4, ca4, kr4], full=True)
                    P.dma("sp", kp_o[512 * i:512 * i + 512, 64 * g:64 * g + 64].rearrange("(c p) d -> p c d", p=128), kr4[:], kr4,
                          reads=[kr4], is_out=True)
                    P.dma("sp", vp_o[512 * i:512 * i + 512, 64 * g:64 * g + 64].rearrange("(c p) d -> p c d", p=128),
                          kv4[:, :, 64:128], kv4, reads=[kv4], is_out=True)

                    def fn_kt(e):
                        ins = None
                        for k in range(4):
                            ins = tr(e, PKT4[:, 128 * k:128 * k + 128], kr4[:, k, :], idn[:])
                        return ins
                    P.op("pe", fn_kt, reads=[kr4, idn], writes=[PKT4])
                    P.op("act", lambda e: e.copy(out=KTf4[:], in_=PKT4[:, :]), reads=[PKT4], writes=[KTf4])
                    P.op("dve", lambda e, i=i: e.tensor_copy(out=KTA[0:64, 512 * i:512 * i + 512], in_=KTf4[:]), reads=[KTf4], writes=[KTA])
                    P.op("pool", lambda e, i=i: e.tensor_copy(out=VA[:, 4 * i:4 * i + 4, 0:64], in_=kv4[:, :, 64:128]), reads=[kv4], writes=[VA])
                    P.op("dve", lambda e, i=i: e.tensor_reduce(out=KM[:, 2 * i:2 * i + 2], in_=KTf4[:].rearrange("p (b c) -> p b c", b=2),
                                                               axis=AX.X, op=ALU.add), reads=[KTf4], writes=[KM])
                    load_xo(i)
                    P.dma("sp", cso[:], cs_own[128 * i:128 * i + 128, :], cso, writes=[cso])
                    P.op("pe", mm_chain(PQ[:, :], lambda kc: XO[:, kc, 3:131], lambda kc: Wqz[:, kc, :]),
                         reads=[XO, Wqz], writes=[PQ])
                    P.op("act", lambda e: e.copy(out=qf[:].rearrange("p a c -> p (a c)"), in_=PQ[:, 0:256]), reads=[PQ], writes=[qf])
                    P.op("act", lambda e: e.activation(out=szb[:], in_=PQ[:, 256:512], func=AF.Silu), reads=[PQ], writes=[szb])
                    rope(qf[:], qr[:], cso, 4, [qf, cso, qr])
                    for h in range(4):
                        P.op("pe", lambda e, h=h: tr(e, PKT[:, :], qr[:, h, :], idn[:]), reads=[qr, idn], writes=[PKT])
                        P.op("act", lambda e, h=h: e.copy(out=QTf[:, h, :], in_=PKT[:, :]), reads=[PKT], writes=[QTf])
                        P.op("dve", lambda e, h=h: e.tensor_copy(out=QA[0:64, h, :], in_=QTf[:, h, :]), reads=[QTf], writes=[QA])

                    def fn_sb(e):
                        ins = None
                        for h in range(4):
                            ins = e.matmul(PSB[:, 32 * h:32 * h + 32], lhsT=QTf[:, h, :], rhs=KM[:], start=True, stop=True)
                        return ins
                    P.op("pe", fn_sb, reads=[QTf, KM], writes=[PSB])
                    e01 = EL[:, 0, i:i + 1, :].to_broadcast([128, 4, 32])
                    eng_ = EL[:, 1, i:i + 1, :].to_broadcast([128, 4, 32])
                    own = EL[:, 2, i:i + 1, :].to_broadcast([128, 4, 32])
                    P.op("dve", lambda e, e01=e01: e.tensor_tensor(out=spd[:], in0=PSB[:, :].rearrange("p (a c) -> p a c", a=4),
                                                                   in1=e01, op=ALU.mult), reads=[PSB, EL], writes=[spd])
                    P.op("dve", lambda e, eng_=eng_: e.tensor_tensor(out=spd[:], in0=spd[:], in1=eng_, op=ALU.add),
                         reads=[spd, EL], writes=[spd])
                    for h in range(4):
                        P.op("dve", lambda e, h=h: e.max(out=mx[:, h, :], in_=spd[:, h, :]), reads=[spd], writes=[mx])
                    for h in range(4):
                        P.op("dve", lambda e, h=h: e.tensor_scalar(out=sel[:, h, :], in0=spd[:, h, :], scalar1=mx[:, h, 2:3],
                                                                   scalar2=None, op0=ALU.is_ge), reads=[spd, mx], writes=[sel])
                    P.op("dve", lambda e, e01=e01: e.tensor_tensor(out=sel[:], in0=sel[:], in1=e01, op=ALU.mult),
                         reads=[sel, EL], writes=[sel])
                    P.op("dve", lambda e, own=own: e.tensor_tensor(out=sel[:], in0=sel[:], in1=own, op=ALU.add),
                         reads=[sel, EL], writes=[sel])
                    P.op("dve", lambda e: e.tensor_scalar(out=MBP[:, :, 64:96], in0=sel[:], scalar1=-1.0, scalar2=BIG,
                                                          op0=ALU.add, op1=ALU.mult), reads=[sel], writes=[MBP])
                    for h in range(4):
                        P.op("pe", lambda e, h=h: e.matmul(PMB[:, :], lhsT=MBP[:, h, :], rhs=idn[:], start=True, stop=True),
                             reads=[MBP, idn], writes=[PMB])
                        P.op("act", lambda e, h=h: e.copy(out=QA[64:96, h, :], in_=PMB[64:96, :]), reads=[PMB], writes=[QA])
                    po = PO[i % 2]
                    nkt = 4 * i + 4
                    base = nps
                    nps += nkt

                    def emit_s(kt, i=i, base=base):
                        pS = PS[(base + kt) % 3]

                        def fn_s(e):
                            ins = e.matmul(pS[:, :], lhsT=KTA[0:96, 128 * kt:128 * kt + 128],
                                           rhs=QA[:].rearrange("p a c -> p (a c)"), start=True, stop=(kt < 4 * i))
                            if kt >= 4 * i:
                                ins = e.matmul(pS[:, :], lhsT=idnb[:], rhs=CM[:, kt - 4 * i, :], start=False, stop=True)
                            return ins
                        P.op("pe", fn_s, reads=[KTA, QA, idnb, CM], writes=[pS])

                    emit_s(0)
                    for kt in range(nkt):
                        if kt + 1 < nkt:
                            emit_s(kt + 1)
                        pS = PS[(base + kt) % 3]
                        pT = PTb[(base + kt) % 3]
                        P.op("act", lambda e, pS=pS, pT=pT: e.activation(out=pT[:], in_=pS[:, :], func=AF.Exp, scale=SCALE),
                             reads=[pS], writes=[pT])
                        P.op("pe", lambda e, kt=kt, pT=pT, po=po: e.matmul(po[:, :], lhsT=VA[:, kt, :], rhs=pT[:],
                                                                           start=(kt == 0), stop=(kt == nkt - 1)),
                             reads=[VA, pT] + ([po] if kt > 0 else []), writes=[po])
                    P.op("dve", lambda e, po=po: e.tensor_copy(out=Osb[:], in_=po[:, :]), reads=[po], writes=[Osb])

                    def fn_ot(e):
                        ins = None
                        for h in range(4):
                            ins = tr(e, POT[:, 65 * h:65 * h + 65], Osb[:, 128 * h:128 * h + 128], idn[0:65, 0:65])
                        return ins
                    P.op("pe", fn_ot, reads=[Osb, idn], writes=[POT])
                    pot3 = POT[:, 0:260].rearrange("p (a c) -> p a c", a=4)
                    P.op("dve", lambda e: e.reciprocal(out=rl[:], in_=pot3[:, :, 64]), reads=[POT], writes=[rl])
                    P.op("dve", lambda e: e.tensor_tensor(out=ob[:], in0=pot3[:, :, 0:64],
                                                          in1=rl[:].rearrange("p (a c) -> p a c", c=1).to_broadcast([128, 4, 64]),
                                                          op=ALU.mult), reads=[POT, rl], writes=[ob])
                    P.op("dve", lambda e: e.tensor_tensor(out=ob2[:], in0=ob[:].rearrange("p a c -> p (a c)"), in1=szb[:],
                                                          op=ALU.mult), reads=[ob, szb], writes=[ob2])

                    def fn_obt(e):
                        tr(e, POBT[:, 0:128], ob2[:, 0:128], idn[:])
                        return tr(e, POBT[:, 128:256], ob2[:, 128:256], idn[:])
                    P.op("pe", fn_obt, reads=[ob2, idn], writes=[POBT])
                    P.op("act", lambda e, g=g, i=i: e.copy(
                        out=OBT[:, 2 * g:2 * g + 2, 128 * i:128 * i + 128],
                        in_=POBT[:, 0:256].rearrange("p (a c) -> p a c", a=2)), reads=[POBT], writes=[OBT])

        P.barrier()
        if DBG:
            P.dma("sp", dbg_ya, YAT[:].rearrange("p a c -> p (a c)"), YAT, reads=[YAT], is_out=True)
            P.dma("sp", dbg_ob, OBT[:].rearrange("p a c -> p (a c)"), OBT, reads=[OBT], is_out=True)
        with contextlib.ExitStack() as st:
            alloc_io(st, "c", nws=1, nxb=1)
            Wg = P.sb("Wg", [128, 8, 2048], BF16, st)
            Wa = P.sb("Wa", [128, 8, 1024], BF16, st)
            Wb = P.sb("Wb", [128, 8, 1024], BF16, st)
            Wo = P.sb("Wo", [128, 8, 1024], BF16, st)
            gbc = P.sb("gbc", [128, 1024], F32, st)
            bbc = P.sb("bbc", [128, 1024], F32, st)
            XM = XB[0]
            sga = P.sb("sga", [128, 512], F32, st)
            sgb = P.sb("sgb", [128, 512], F32, st)
            t1 = P.sb("t1", [128, 512], F32, st)
            mixT = P.sb("mixT", [128, 8, 512], BF16, st)
            xres = [P.sb("xres%d" % i, [128, 1024], F32, st) for i in range(2)]
            pre = P.sb("pre", [128, 1024], F32, st)
            stats = P.sb("stats", [128, 2, 6], F32, st)
            mv = P.sb("mv", [128, 2], F32, st)
            rs2 = P.sb("rs2", [128, 1], F32, st)
            PGa, PGb, PBa, PBb = [P.view("PG%d" % i, banks[i], banks[i][:, :]) for i in range(4)]
            POa = [P.view("POa%d" % i, banks[4 + i], banks[4 + i][:, :]) for i in range(2)]

            for j in range(4):
                load_w(Wg, 512 * j, w_in[:, C_GA + 512 * j:C_GA + 512 * j + 512], 512)
            for j in range(2):
                load_w(Wa, 512 * j, w_a[:, 512 * j:512 * j + 512], 512)
                load_w(Wb, 512 * j, w_b[:, 512 * j:512 * j + 512], 512)
                load_w(Wo, 512 * j, w_o[:, 512 * j:512 * j + 512], 512)
            P.dma("sp", gbc[:], lng.partition_broadcast(128).rearrange("p a c -> p (a c)"), gbc, writes=[gbc])
            P.dma("sp", bbc[:], lnb.partition_broadcast(128).rearrange("p a c -> p (a c)"), bbc, writes=[bbc])
            NJ = (NI + 3) // 4 if 'm' in PASSES else 0
            for j in range(NJ):
                s = WS[stg[0] % len(WS)]
                stg[0] += 1
                P.dma("sp", s[:], xmT[:, 512 * j:512 * j + 512].rearrange("(kc p) t -> p kc t", p=128), s, writes=[s])
                P.op("pool", lambda e, s=s: e.tensor_copy(out=XM[:], in_=s[:]), reads=[s], writes=[XM])
                ts = slice(512 * j, 512 * j + 512)
                for mc in range(8):
                    ms = slice(128 * mc, 128 * mc + 128)
                    P.op("pe", mm_chain(PGa[:, :], lambda kc, ms=ms: Wg[:, kc, ms], lambda kc: XM[:, kc, :]),
                         reads=[Wg, XM], writes=[PGa])
                    P.op("pe", mm_chain(PGb[:, :], lambda kc, mc=mc: Wg[:, kc, 1024 + 128 * mc:1024 + 128 * mc + 128],
                                        lambda kc: XM[:, kc, :]), reads=[Wg, XM], writes=[PGb])
                    P.op("pe", mm_chain(PBa[:, :], lambda kc, ms=ms: Wa[:, kc, ms], lambda kc, ts=ts: YAT[:, kc, ts]),
                         reads=[Wa, YAT], writes=[PBa])
                    P.op("pe", mm_chain(PBb[:, :], lambda kc, ms=ms: Wb[:, kc, ms], lambda kc, ts=ts: OBT[:, kc, ts]),
                         reads=[Wb, OBT], writes=[PBb])
                    P.op("act", lambda e: e.activation(out=sga[:], in_=PGa[:, :], func=AF.Sigmoid), reads=[PGa], writes=[sga])
                    P.op("act", lambda e: e.activation(out=sgb[:], in_=PGb[:, :], func=AF.Sigmoid), reads=[PGb], writes=[sgb])
                    P.op("dve", lambda e: e.tensor_tensor(out=t1[:], in0=sga[:], in1=PBa[:, :], op=ALU.mult),
                         reads=[sga, PBa], writes=[t1])
                    P.op("dve", lambda e: e.tensor_tensor(out=sgb[:], in0=sgb[:], in1=PBb[:, :], op=ALU.mult),
                         reads=[sgb, PBb], writes=[sgb])
                    P.op("pool", lambda e, mc=mc: e.tensor_tensor(out=mixT[:, mc, :], in0=t1[:], in1=sgb[:], op=ALU.add),
                         reads=[t1, sgb], writes=[mixT])
                for k in range(4):
                    c = 4 * j + k
                    if c >= NI:
                        break
                    xr = xres[c % 2]
                    y_ = pre
                    P.dma("sp", xr[:], xown[128 * c:128 * c + 128, :], xr, writes=[xr])
                    for half in range(2):
                        po = POa[half]
                        P.op("pe", mm_chain(po[:, :], lambda kc, k=k: mixT[:, kc, 128 * k:128 * k + 128],
                                            lambda kc, half=half: Wo[:, kc, 512 * half:512 * half + 512]),
                             reads=[mixT, Wo], writes=[po])
                        P.op("dve", lambda e, half=half, po=po, xr=xr: e.scalar_tensor_tensor(
                            out=pre[:, 512 * half:512 * half + 512], in0=xr[:, 512 * half:512 * half + 512], scalar=ALPHA,
                            in1=po[:, :], op0=ALU.mult, op1=ALU.add), reads=[xr, po], writes=[pre])
                    for half in range(2):
                        P.op("dve", lambda e, half=half: e.bn_stats(out=stats[:, half, :], in_=pre[:, 512 * half:512 * half + 512]),
                             reads=[pre], writes=[stats])
                    P.op("dve", lambda e: e.bn_aggr(out=mv[:], in_=stats[:].rearrange("p a c -> p (a c)")), reads=[stats], writes=[mv])
                    P.op("act", lambda e: e.activation(out=rs2[:], in_=mv[:, 1:2], func=AF.Sqrt, bias=epsb[:, 0:1]),
                         reads=[mv, epsb], writes=[rs2])
                    P.op("dve", lambda e: e.reciprocal(out=rs2[:], in_=rs2[:]), reads=[rs2], writes=[rs2])
                    P.op("dve", lambda e, y_=y_: e.tensor_scalar(out=y_[:], in0=pre[:], scalar1=mv[:, 0:1], scalar2=rs2[:, 0:1],
                                                                 op0=ALU.subtract, op1=ALU.mult), reads=[pre, mv, rs2], writes=[y_])
                    P.op("pool", lambda e, y_=y_: e.tensor_tensor(out=y_[:], in0=y_[:], in1=gbc[:], op=ALU.mult),
                         reads=[y_, gbc], writes=[y_])
                    P.op("pool", lambda e, y_=y_: e.tensor_tensor(out=y_[:], in0=y_[:], in1=bbc[:], op=ALU.add),
                         reads=[y_, bbc], writes=[y_])
                    P.dma("sp", y_own[128 * c:128 * c + 128, :], y_[:], y_, reads=[y_], is_out=True)

        P.emit()
    return nc


def _host_inputs(x_prompt, w_in, conv_w, conv_b, dt_bias, a_log, d_skip, ssm_norm_w, w_a_out, w_b_out, w_out,
                 ln_g, ln_b, x_sample, cache_k, cache_v, state_conv, state_ssm, page_table):
    f32 = np.float32
    pos = np.arange(SEQ, dtype=np.float32)
    inv = np.power(np.float32(10000.0), -np.arange(32, dtype=np.float32) * np.float32(2.0) / np.float32(64.0)).astype(f32)
    ang = (pos[:, None] * inv[None, :]).astype(f32)
    cs_all = np.concatenate([np.cos(ang), np.sin(ang)], axis=1).astype(f32)
    tri = np.triu(np.ones((128, 128), f32))
    idn = np.eye(128, dtype=f32)
    kbi = np.zeros((32, SEQ), f32)
    for b in range(32):
        kbi[b, 256 * b:256 * b + 256] = 1.0
    common = {
        "w_in": np.ascontiguousarray(w_in[0]), "w_a": np.ascontiguousarray(w_a_out[0]),
        "w_b": np.ascontiguousarray(w_b_out[0]), "w_o": np.ascontiguousarray(w_out[0]),
        "cwT": np.ascontiguousarray(conv_w[0].T), "cbT": np.ascontiguousarray(conv_b[0][:, None]),
        "vec16": np.stack([dt_bias[0], a_log[0], d_skip[0]]).astype(f32),
        "nwv": np.ascontiguousarray(ssm_norm_w), "lng": np.ascontiguousarray(ln_g), "lnb": np.ascontiguousarray(ln_b),
        "tri": tri, "idn": idn, "cs_all": cs_all, "kbi": kbi,
        "cwrow": np.ascontiguousarray(conv_w[0]), "cbrow": np.ascontiguousarray(conv_b[0][None, :]),
        "ck_d": cache_k[0].reshape(2560 * 128, 256), "cv_d": cache_v[0].reshape(2560 * 128, 256),
        "iota_d": np.arange(128, dtype=f32)[:, None],
    }
    sel = np.zeros((NS, NS, 128), f32)
    selt = np.zeros((128, NS, NS), f32)
    for n in range(NS):
        sel[n, n, :] = 1.0
        selt[:, n, n] = 1.0
    angs = (np.float32(PAST) * inv).astype(f32)
    common["sel_d"] = sel.reshape(NS, NS * 128)
    common["selt_d"] = selt.reshape(128, NS * NS)
    common["css_d"] = np.concatenate([np.cos(angs), np.sin(angs)])[None, :].astype(f32)
    maps = []
    for c in range(NCORE):
        s, r = c // 4, c % 4
        xs = x_prompt[s]
        xT = np.ascontiguousarray(xs.T)
        own_tok = np.concatenate([np.arange(128 * (4 * i + r), 128 * (4 * i + r) + 128) for i in range(16)])
        xo = np.zeros((16, 131, D), f32)
        for i in range(16):
            t0 = 128 * (4 * i + r)
            lo = max(t0 - 3, 0)
            xo[i, 131 - (t0 + 128 - lo):] = xs[lo:t0 + 128]
        xoT = np.ascontiguousarray(xo.reshape(16 * 131, D).T)
        xown = np.ascontiguousarray(xs[own_tok])
        oh = np.zeros((128, 4), f32)
        oh[:, r] = 1.0
        cm = np.zeros((128, 4, 4, 128), f32)
        diag = np.where(np.arange(128)[:, None] <= np.arange(128)[None, :], 0.0, -BIG).astype(f32)
        for k in range(4):
            if k > r:
                cm[:, k] = -BIG
            elif k == r:
                cm[:, k] = diag[:, None, :]
        el = np.zeros((3, 16, 32), f32)
        for i in range(16):
            qblk = (4 * i + r) // 2
            el[0, i, :qblk] = 1.0
            el[1, i, qblk:] = -1e30
            el[2, i, qblk] = 1.0
        m = dict(common)
        ts = slice(NS * c, NS * c + NS)
        m.update({"xsT": np.ascontiguousarray(x_sample[ts, 0, :].T), "xs_tok": np.ascontiguousarray(x_sample[ts, 0, :]),
                  "sc_d": np.ascontiguousarray(state_conv[0, ts].reshape(NS, 3 * 2048)),
                  "ssm_d": np.ascontiguousarray(state_ssm[0, ts].reshape(NS, 1024 * 128)),
                  "pt_d": np.ascontiguousarray(page_table[ts].reshape(1, NS * NPG)).astype(np.int32)})
        m.update({"xT": xT, "xoT": xoT, "xmT": np.ascontiguousarray(xown.T), "xown": xown, "oh": oh,
                  "cs_own": np.ascontiguousarray(cs_all[own_tok]), "cm": cm.reshape(128, 2048),
                  "el": el.reshape(3, 512)})
        maps.append(m)
    return maps


_NC_CACHE = {}


def kernel(x_prompt, x_sample, cache_k, cache_v, state_conv, state_ssm, page_table,
           w_in, conv_w, conv_b, dt_bias, a_log, d_skip, ssm_norm_w, w_a_out, w_b_out, w_out, ln_g, ln_b):
    NI = int(os.environ.get("MK_NI", "16"))
    args = [np.asarray(a) for a in (x_prompt, w_in, conv_w, conv_b, dt_bias, a_log, d_skip, ssm_norm_w,
                                    w_a_out, w_b_out, w_out, ln_g, ln_b, x_sample, cache_k, cache_v,
                                    state_conv, state_ssm, page_table)]
    maps = _host_inputs(*args)
    key = (NI, os.environ.get("MK_NOSAMPLE"), os.environ.get("MK_PASSES"))
    if key not in _NC_CACHE:
        _NC_CACHE[key] = build(NI=NI)
    nc = _NC_CACHE[key]
    nosample = bool(os.environ.get("MK_NOSAMPLE"))
    if nosample:
        for m in maps:
            m.pop("ck_d")
            m.pop("cv_d")
    res = run_bass_kernel_spmd(nc, maps, core_ids=list(range(NCORE))).results
    if os.environ.get("MK_DBG"):
        global _DBG_RES
        _DBG_RES = res
    y_prompt = np.zeros((2, SEQ, D), np.float32)
    for c in range(NCORE):
        s, r = c // 4, c % 4
        yo = res[c]["y_own"].reshape(16, 128, D)
        y_prompt[s].reshape(16, 4, 128, D)[:, r] = yo
    k_prompt = np.stack([res[0]["kp_o"], res[4]["kp_o"]]).reshape(1, 2, SEQ, 4, 64)
    v_prompt = np.stack([res[0]["vp_o"], res[4]["vp_o"]]).reshape(1, 2, SEQ, 4, 64)
    conv_prompt = np.zeros((1, 2, 3, 2048), np.float32)
    ssm_prompt = np.zeros((1, 2, 16, 64, 128), np.float32)
    for s in range(2):
        rr = res[4 * s + 3]
        cva = rr["cvA_o"]
        cvc = rr["cvC_o"]
        for g in range(4):
            conv_prompt[0, s, :, 256 * g:256 * g + 128] = cva[g, :, 0, :].T
            conv_prompt[0, s, :, 256 * g + 128:256 * g + 256] = cva[g, :, 1, :].T
            conv_prompt[0, s, :, 1024 + 128 * g:1024 + 128 * g + 128] = cva[g, :, 2, :].T
            conv_prompt[0, s, :, 1536 + 128 * g:1536 + 128 * g + 128] = cvc[g].T
            hs = rr["ssm_o"][g].reshape(128, 4, 64)
            ssm_prompt[0, s, 4 * g:4 * g + 4] = hs.transpose(1, 2, 0)
    if nosample:
        return y_prompt, k_prompt, v_prompt, conv_prompt, ssm_prompt
    y_sample = np.concatenate([res[c]["ys_o"] for c in range(NCORE)]).reshape(128, 1, D)
    k_sample = np.concatenate([res[c]["ks_o"] for c in range(NCORE)]).reshape(1, 128, 1, 4, 64)
    v_sample = np.concatenate([res[c]["vs_o"] for c in range(NCORE)]).reshape(1, 128, 1, 4, 64)
    conv_sample = np.concatenate([res[c]["cvs_o"] for c in range(NCORE)]).reshape(1, 128, 3, 2048)
    ssm_sample = np.concatenate([res[c]["ssms_o"] for c in range(NCORE)]).reshape(1, 128, 16, 64, 128)
    return (y_prompt, y_sample, k_prompt, v_prompt, k_sample, v_sample, conv_prompt, conv_sample,
            ssm_prompt, ssm_sample)
```

```python
import contextlib
import os

import numpy as np

import concourse.bass as bass
import concourse.mybir as mybir
from concourse.bass_utils import run_bass_kernel_spmd

F32 = mybir.dt.float32
BF16 = mybir.dt.bfloat16
I32 = mybir.dt.int32
AF = mybir.ActivationFunctionType
ALU = mybir.AluOpType
AX = mybir.AxisListType

D = 1024
SEQ = 8192
NCORE = 8
DIN = 7696
C_ZA, C_XA, C_B, C_C, C_DT, C_Q, C_K, C_V, C_ZB, C_GA, C_GB = (
    0, 1024, 2048, 2560, 3072, 3088, 4112, 4368, 4624, 5648, 6672)
EPS = 1e-5
ALPHA = 2.0 ** 0.25
BIG = 30000.0
SCALE = 0.125
NS = 16
PAST = 2048
NPG = 16


class Buf:
    def __init__(self, name, t):
        self.name = name
        self.t = t
        self.w = None
        self.r = []
        self.dsem = None
        self.dcnt = 0

    def __getitem__(self, k):
        return self.t[k]


class View:
    def __init__(self, name, parent, t):
        self.name = name
        self.parent = parent
        self.t = t

    def __getitem__(self, k):
        return self.t[k]

    @property
    def w(self):
        return self.parent.w

    @w.setter
    def w(self, v):
        self.parent.w = v

    @property
    def r(self):
        return self.parent.r

    @r.setter
    def r(self, v):
        self.parent.r = v


class Prog:
    ENG = ("sp", "pe", "act", "dve", "pool")

    def __init__(self, nc, stack):
        self.nc = nc
        self.stack = stack
        self.q = {e: [] for e in self.ENG}
        self.cnt = {e: 0 for e in self.ENG}
        self.seen = {e: {} for e in self.ENG}
        self.esem = {}
        for e in ("pe", "act", "dve", "pool"):
            self.esem[e] = stack.enter_context(nc.semaphore("es_" + e))
        self.nsem = 4
        self.out_tokens = []
        self.dbufs = []
        self.n = 0
        self.stop = int(os.environ.get("MK_STOP", "1000000000"))
        self.verbose = bool(os.environ.get("MK_VERBOSE"))

    def _skip(self, what):
        self.n += 1
        if self.verbose:
            import traceback
            fr = traceback.extract_stack()[-3]
            print("OP %d %s line %d" % (self.n, what, fr.lineno))
        return self.n > self.stop

    def sb(self, name, shape, dt, stack=None):
        t = (stack or self.stack).enter_context(self.nc.sbuf_tensor("s_" + name, list(shape), dt))
        return Buf(name, t)

    def ps(self, name, stack=None):
        t = (stack or self.stack).enter_context(self.nc.psum_tensor("p_" + name, [128, 512], F32))
        return Buf(name, t)

    def view(self, name, bank, ap):
        return View(name, bank, ap)

    def _dsem(self, b):
        if b.dsem is None:
            b.dsem = self.stack.enter_context(self.nc.semaphore("ds%d" % self.nsem))
            self.nsem += 1
            self.dbufs.append(b)
        return b.dsem

    def barrier(self):
        if self.n > self.stop:
            return
        toks = [(self.esem[e], self.cnt[e]) for e in ("pe", "act", "dve", "pool") if self.cnt[e] > 0]
        toks += [(b.dsem, b.dcnt) for b in self.dbufs if b.dcnt > 0]
        for eng in self.ENG:
            seen = self.seen[eng]
            waits = []
            for sem, val in toks:
                if seen.get(sem, 0) < val:
                    seen[sem] = val
                    waits.append((sem, val))
            self._wait(eng, waits)

    def _deps(self, eng, reads, writes):
        toks = []
        for b in reads:
            if b.w is not None:
                toks.append(b.w)
        for b in writes:
            if b.w is not None:
                toks.append(b.w)
            toks.extend(b.r)
        need = {}
        for sem, val in toks:
            if need.get(sem, 0) < val:
                need[sem] = val
        out = []
        seen = self.seen[eng]
        for sem, val in need.items():
            if seen.get(sem, 0) < val:
                seen[sem] = val
                out.append((sem, val))
        return out

    def _commit(self, tok, reads, writes):
        for b in reads:
            b.r.append(tok)
        for b in writes:
            b.w = tok
            b.r = []

    def _eng(self, eng):
        nc = self.nc
        return {"sp": nc.sync, "pe": nc.tensor, "act": nc.scalar, "dve": nc.vector, "pool": nc.gpsimd}[eng]

    def _wait(self, eng, waits):
        e = self._eng(eng)
        for sem, val in waits:
            if eng == "pe" and sem is self.esem["pe"]:
                continue
            e.wait_ge(sem, val)

    def op(self, eng, fn, reads=(), writes=()):
        if self._skip(eng):
            return None
        waits = self._deps(eng, reads, writes)
        self._wait(eng, waits)
        ins = fn(self._eng(eng))
        ins.then_inc(self.esem[eng], 1)
        if self.verbose and self.n in (62, 69):
            print("INS", self.n, str(ins))
        self.cnt[eng] += 1
        tok = (self.esem[eng], self.cnt[eng])
        self._commit(tok, reads, writes)
        return tok

    def dma(self, eng, out, in_, buf, reads=(), writes=(), is_out=False, **kw):
        if self._skip("dma"):
            return None
        waits = self._deps(eng, reads, writes)
        self._wait(eng, waits)
        sem = self._dsem(buf)
        buf.dcnt += 16
        tok = (sem, buf.dcnt)
        self._eng(eng).dma_start(out=out, in_=in_, **kw).then_inc(sem, 16)
        self._commit(tok, reads, writes)
        if is_out:
            self.out_tokens.append(tok)
        return tok

    def idma(self, out, in_, idx_ap, buf, reads=(), writes=(), bound=None):
        waits = self._deps("pool", reads, writes)
        self._wait("pool", waits)
        sem = self._dsem(buf)
        buf.dcnt += 16
        tok = (sem, buf.dcnt)
        self.nc.gpsimd.indirect_dma_start(
            out=out, out_offset=None, in_=in_,
            in_offset=bass.IndirectOffsetOnAxis(ap=idx_ap, axis=0),
            bounds_check=bound, oob_is_err=False).then_inc(sem, 16)
        self._commit(tok, reads, writes)
        return tok

    def emit(self):
        need = {}
        for sem, val in self.out_tokens:
            if need.get(sem, 0) < val:
                need[sem] = val
        for sem, val in need.items():
            self.nc.sync.wait_ge(sem, val)


def tr(e, out, in_, ident):
    return e.matmul(out, lhsT=in_, rhs=ident, start=True, stop=True)


def _bc(ap, shape):
    return ap.to_broadcast(list(shape))


def build(NI=16, SAMPLE=True):
    PASSES = os.environ.get('MK_PASSES', 'sam')
    SAMPLE = SAMPLE and not os.environ.get('MK_NOSAMPLE')
    NG = int(os.environ.get('MK_NG', '4'))
    nc = bass.Bass("TRN2", target_bir_lowering=False)
    dt = nc.dram_tensor
    xT = dt("xT", [D, SEQ], F32, kind="ExternalInput").ap()
    xoT = dt("xoT", [D, 16 * 131], F32, kind="ExternalInput").ap()
    xmT = dt("xmT", [D, 2048], F32, kind="ExternalInput").ap()
    xown = dt("xown", [2048, D], F32, kind="ExternalInput").ap()
    w_in = dt("w_in", [D, DIN], F32, kind="ExternalInput").ap()
    w_a = dt("w_a", [D, D], F32, kind="ExternalInput").ap()
    w_b = dt("w_b", [D, D], F32, kind="ExternalInput").ap()
    w_o = dt("w_o", [D, D], F32, kind="ExternalInput").ap()
    cwT = dt("cwT", [2048, 4], F32, kind="ExternalInput").ap()
    cbT = dt("cbT", [2048, 1], F32, kind="ExternalInput").ap()
    vec16 = dt("vec16", [3, 16], F32, kind="ExternalInput").ap()
    nwv = dt("nwv", [1, D], F32, kind="ExternalInput").ap()
    lng = dt("lng", [1, D], F32, kind="ExternalInput").ap()
    lnb = dt("lnb", [1, D], F32, kind="ExternalInput").ap()
    tri_d = dt("tri", [128, 128], F32, kind="ExternalInput").ap()
    idn_d = dt("idn", [128, 128], F32, kind="ExternalInput").ap()
    oh_d = dt("oh", [128, 4], F32, kind="ExternalInput").ap()
    cs_all = dt("cs_all", [SEQ, 64], F32, kind="ExternalInput").ap()
    cs_own = dt("cs_own", [2048, 64], F32, kind="ExternalInput").ap()
    kbi_d = dt("kbi", [32, SEQ], F32, kind="ExternalInput").ap()
    cm_d = dt("cm", [128, 4 * 512], F32, kind="ExternalInput").ap()
    el_d = dt("el", [3, 16 * 32], F32, kind="ExternalInput").ap()
    xsT = dt("xsT", [D, NS], F32, kind="ExternalInput").ap()
    xs_tok = dt("xs_tok", [NS, D], F32, kind="ExternalInput").ap()
    sc_d = dt("sc_d", [NS, 3 * 2048], F32, kind="ExternalInput").ap()
    ssm_d = dt("ssm_d", [NS, 1024 * 128], F32, kind="ExternalInput").ap()
    pt_d = dt("pt_d", [1, NS * NPG], I32, kind="ExternalInput").ap()
    iota_d = dt("iota_d", [128, 1], F32, kind="ExternalInput").ap()
    if SAMPLE:
        ck_d = dt("ck_d", [2560 * 128, 256], F32, kind="ExternalInput").ap()
        cv_d = dt("cv_d", [2560 * 128, 256], F32, kind="ExternalInput").ap()
    cwrow = dt("cwrow", [4, 2048], F32, kind="ExternalInput").ap()
    cbrow = dt("cbrow", [1, 2048], F32, kind="ExternalInput").ap()
    sel_d = dt("sel_d", [NS, NS * 128], F32, kind="ExternalInput").ap()
    selt_d = dt("selt_d", [128, NS * NS], F32, kind="ExternalInput").ap()
    css_d = dt("css_d", [1, 64], F32, kind="ExternalInput").ap()
    ys_o = dt("ys_o", [NS, D], F32, kind="ExternalOutput").ap()
    ks_o = dt("ks_o", [NS, 256], F32, kind="ExternalOutput").ap()
    vs_o = dt("vs_o", [NS, 256], F32, kind="ExternalOutput").ap()
    cvs_o = dt("cvs_o", [NS, 3 * 2048], F32, kind="ExternalOutput").ap()
    ssms_o = dt("ssms_o", [NS, 1024 * 128], F32, kind="ExternalOutput").ap()
    y_own = dt("y_own", [2048, D], F32, kind="ExternalOutput").ap()
    kp_o = dt("kp_o", [SEQ, 256], F32, kind="ExternalOutput").ap()
    vp_o = dt("vp_o", [SEQ, 256], F32, kind="ExternalOutput").ap()
    cvA_o = dt("cvA_o", [4, 128, 3, 3], F32, kind="ExternalOutput").ap()
    cvC_o = dt("cvC_o", [4, 128, 3], F32, kind="ExternalOutput").ap()
    ssm_o = dt("ssm_o", [4, 128, 256], F32, kind="ExternalOutput").ap()

    DBG = bool(os.environ.get("MK_DBG"))
    if DBG:
        dbg_ya = dt("dbg_ya", [128, 8 * 2048], BF16, kind="ExternalOutput").ap()
        dbg_ob = dt("dbg_ob", [128, 8 * 2048], BF16, kind="ExternalOutput").ap()
    with contextlib.ExitStack() as top:
        P = Prog(nc, top)
        tri = P.sb("tri", [128, 128], F32)
        idn = P.sb("idnf", [128, 128], F32)
        idnb = P.sb("idnb", [128, 128], BF16)
        onesf = P.sb("onesf", [128, 128], F32)
        oh = P.sb("oh", [128, 4], F32)
        WS, XB, XOl = [], [], []

        def alloc_io(st, tag, nws=2, nxb=2):
            WS[:] = [P.sb("WS%s%d" % (tag, i), [128, 8, 512], F32, st) for i in range(nws)]
            XB[:] = [P.sb("XB%s%d" % (tag, i), [128, 8, 512], BF16, st) for i in range(nxb)]
            XOl[:] = [P.sb("XO%s" % tag, [128, 8, 131], BF16, st)] if nxb else []
        banks = [P.ps("bank%d" % i) for i in range(8)]

        P.dma("sp", tri[:], tri_d, tri, writes=[tri])
        P.dma("sp", idn[:], idn_d, idn, writes=[idn])
        P.dma("sp", oh[:], oh_d, oh, writes=[oh])
        P.op("pool", lambda e: e.tensor_copy(out=idnb[:], in_=idn[:]), reads=[idn], writes=[idnb])
        P.op("pool", lambda e: e.memset(onesf[:], 1.0), writes=[onesf])
        epsb = P.sb("epsb", [128, 1], F32)
        P.op("pool", lambda e: e.memset(epsb[:], EPS), writes=[epsb])

        stg = [0]

        def load_w(dst, off, src_ap, n):
            s = WS[stg[0] % len(WS)]
            stg[0] += 1
            P.dma("sp", s[:, :, 0:n], src_ap.rearrange("(kc p) c -> p kc c", p=128), s, writes=[s])
            P.op("pool", lambda e: e.tensor_copy(out=dst[:, :, off:off + n], in_=s[:, :, 0:n]),
                 reads=[s], writes=[dst])

        def load_x(i):
            s = WS[stg[0] % len(WS)]
            xb = XB[stg[0] % len(XB)]
            stg[0] += 1
            P.dma("sp", s[:], xT[:, i * 512:(i + 1) * 512].rearrange("(kc p) t -> p kc t", p=128), s, writes=[s])
            P.op("pool", lambda e: e.tensor_copy(out=xb[:], in_=s[:]), reads=[s], writes=[xb])
            return xb

        def load_xo(i):
            s = WS[stg[0] % len(WS)]
            stg[0] += 1
            P.dma("sp", s[:, :, 0:131], xoT[:, i * 131:(i + 1) * 131].rearrange("(kc p) t -> p kc t", p=128),
                  s, writes=[s])
            XO = XOl[0]
            P.op("pool", lambda e: e.tensor_copy(out=XO[:], in_=s[:, :, 0:131]), reads=[s], writes=[XO])

        def mm_chain(ps_ap, lhs_fn, rhs_fn, nk=8):
            def fn(e):
                ins = None
                for kc in range(nk):
                    ins = e.matmul(ps_ap, lhsT=lhs_fn(kc), rhs=rhs_fn(kc), start=(kc == 0), stop=(kc == nk - 1))
                return ins
            return fn


        if SAMPLE:
            with contextlib.ExitStack() as st:
                alloc_io(st, "s", nws=1, nxb=0)
                XST = P.sb("XST", [128, 8, NS], F32, st)
                Us = P.sb("Us", [NS, DIN], F32, st)
                BIGS = P.sb("BIGS", [128, 6144], F32, st)
                SC = View("SC", BIGS, BIGS[0:NS, :].rearrange("p (a c) -> p a c", a=3))
                CWk = P.sb("CWk", [NS, 2048], F32, st)
                cacs = P.sb("cacs", [NS, 2048], F32, st)
                ctmp = P.sb("ctmp", [NS, 2048], F32, st)
                xbc = P.sb("xbc", [NS, 2048], F32, st)
                v16s = P.sb("v16s", [NS, 3, 16], F32, st)
                As = P.sb("As", [NS, 16], F32, st)
                dts = P.sb("dts", [NS, 16], F32, st)
                dte = P.sb("dte", [NS, 16], F32, st)
                dec = P.sb("dec", [NS, 16], F32, st)
                xdt_t = P.sb("xdt_t", [NS, 1024], F32, st)
                dec_t = P.sb("dec_t", [NS, 1024], F32, st)
                mix_t = xdt_t
                pre_s = dec_t
                XDTT = P.sb("XDTT", [128, 8, NS], F32, st)
                DECT = P.sb("DECT", [128, 8, NS], F32, st)
                YTT = P.sb("YTT", [128, 8, NS], F32, st)
                SEL = P.sb("SEL", [NS, NS, 128], F32, st)
                SELT = P.sb("SELT", [128, NS, NS], F32, st)
                ysum = P.sb("ysum", [128, 8], F32, st)
                y_t = P.sb("y_t", [NS, 1024], F32, st)
                szs = P.sb("szs", [NS, 1024], F32, st)
                g_t = P.sb("g_t", [NS, 1024], F32, st)
                gq_t = P.sb("gq_t", [NS, 1024], F32, st)
                ss4 = P.sb("ss4", [NS, 4], F32, st)
                ya_t = P.sb("ya_t", [NS, 1024], F32, st)
                css = P.sb("css", [NS, 64], F32, st)
                qk = P.sb("qk", [NS, 20, 64], F32, st)
                ra_s = P.sb("ra_s", [NS, 20, 32], F32, st)
                rb_s = P.sb("rb_s", [NS, 20, 32], F32, st)
                PTI = P.sb("PTI", [128, NS * NPG], I32, st)
                PTF = P.sb("PTF", [128, NS * NPG], F32, st)
                IDX = P.sb("IDX", [128, NS * NPG], I32, st)
                iop = P.sb("iop", [128, 1], F32, st)
                KV = [P.sb("KVt%d" % i, [128, NPG, 256], F32, st) for i in range(1)]
                nws = View("nws", KV[0], KV[0][0:NS, 0:4, :].rearrange("p a c -> p (a c)"))
                prod = P.sb("prod", [128, NPG, 4, 64], F32, st)
                pfl = prod[:].rearrange("p j a d -> p (j a d)")
                Hs = [View("Hs0", prod, pfl[:, 0:1024].rearrange("p (c s) -> p c s", c=8))]
                ht1 = View("ht1", prod, pfl[:, 1024:2048].rearrange("p (c s) -> p c s", c=8))
                ht2 = View("ht2", prod, pfl[:, 2048:3072].rearrange("p (c s) -> p c s", c=8))
                gbs = View("gbs", prod, pfl[0:NS, 0:1024])
                bbs = View("bbs", prod, pfl[0:NS, 1024:2048])
                xrs = View("xrs", prod, pfl[0:NS, 2048:3072])
                S_all = View("S_all", BIGS, BIGS[:, 0:NS * NPG * 16].rearrange("p (n c) -> p n c", n=NS))
                csum = P.sb("csum", [NS, NPG, 16], F32, st)
                sblk = P.sb("sblk", [NS, 16, 8], F32, st)
                mx8 = P.sb("mx8", [NS, 16, 8], F32, st)
                sl8 = P.sb("sl8", [NS, 16, 8], F32, st)
                MB = P.sb("MB", [NS, NPG, 16], F32, st)
                stmp = P.sb("stmp", [128, NPG * 16], F32, st)
                Pm = P.sb("Pm", [128, NPG, 16], F32, st)
                OT = P.sb("OT", [64, 16, NS], F32, st)
                LT = P.sb("LT", [16, NS], F32, st)
                o_t = View("o_t", y_t, y_t[:].rearrange("p (h d) -> p h d", h=16))
                l_t = P.sb("l_t", [NS, 16], F32, st)
                sown = P.sb("sown", [NS, 16], F32, st)
                sprod = View("sprod", gq_t, gq_t[:].rearrange("p (h d) -> p h d", h=16))
                ob_t = g_t
                TT = P.sb("TT", [128, 8, NS], F32, st)
                sg = cacs
                br = ctmp
                sts = P.sb("sts", [NS, 2, 6], F32, st)
                mvs = P.sb("mvs", [NS, 2], F32, st)
                rss = P.sb("rss", [NS, 1], F32, st)

                PU = [P.view("PU%d" % i, banks[i], banks[i][0:NS, :]) for i in range(2)]
                PTr = P.view("PTr", banks[2], banks[2][:, 0:8 * NS])
                PBC = P.view("PBC", banks[3], banks[3][:, :])
                PCC = P.view("PCC", banks[4], banks[4][:, :])
                PYT = [P.view("PYTs%d" % i, banks[5 + i], banks[5 + i][0:NS, :]) for i in range(2)]
                PQB = [P.view("PQB%d" % i, banks[5 + i], banks[5 + i][:, :]) for i in range(2)]
                PCS = P.view("PCS", banks[7], banks[7][0:NS, 0:256])
                PMBc = P.view("PMBc", banks[3], banks[3][:, 0:256])
                POs = P.view("POs", banks[4], banks[4][0:64, 0:16])
                PLs = P.view("PLs", banks[4], banks[4][0:16, 32:33])
                POt = [P.view("POt%d" % i, banks[i], banks[i][0:NS, :]) for i in range(2)]
                PLt = P.view("PLt", banks[2], banks[2][0:NS, 256:272])

                def bc3(ap2, a, b):
                    return ap2.rearrange("p (a c) -> p a c", c=1).to_broadcast([ap2.shape[0], a, b])

                def transp(src_tok, dstT):
                    def fn(e):
                        ins = None
                        for c in range(8):
                            ins = e.matmul(PTr[:, NS * c:NS * c + NS], lhsT=src_tok[:, 128 * c:128 * c + 128],
                                           rhs=idn[0:NS, 0:NS], start=True, stop=True)
                        return ins
                    P.op("pe", fn, reads=[src_tok, idn], writes=[PTr])
                    P.op("act", lambda e: e.copy(out=dstT[:].rearrange("p a c -> p (a c)"), in_=PTr[:, :]),
                         reads=[PTr], writes=[dstT])

                if os.environ.get("MK_VERBOSE"):
                    print("SBUF remaining in sample pass", nc.sbuf_bytes_remaining)
                if os.environ.get("MK_TIND"):
                    for bnd in (None, 100):
                        for oo in (KV[0][:, 0, :], prod[:, 0, :, :].rearrange("p a d -> p (a d)")):
                            for ii in (IDX[:, 0:1], PTI[:, 0:1]):
                                try:
                                    nc.gpsimd.indirect_dma_start(out=oo, out_offset=None, in_=ck_d,
                                                                 in_offset=bass.IndirectOffsetOnAxis(ap=ii, axis=0),
                                                                 bounds_check=bnd, oob_is_err=False)
                                    print("TIND ok", bnd)
                                except Exception as ex:
                                    print("TIND err", bnd, str(ex)[:60])
                P.dma("sp", XST[:], xsT.rearrange("(kc p) t -> p kc t", p=128), XST, writes=[XST])
                P.dma("sp", SC[:].rearrange("p a c -> p (a c)"), sc_d, BIGS, writes=[SC])
                P.dma("sp", v16s[:], vec16.partition_broadcast(NS), v16s, writes=[v16s])
                P.dma("sp", nws[:], nwv.partition_broadcast(NS).rearrange("p a c -> p (a c)"), KV[0], writes=[nws])
                P.dma("sp", css[:], css_d.partition_broadcast(NS).rearrange("p a c -> p (a c)"), css, writes=[css])
                P.dma("sp", SEL[:].rearrange("p a c -> p (a c)"), sel_d, SEL, writes=[SEL])
                P.dma("sp", SELT[:].rearrange("p a c -> p (a c)"), selt_d, SELT, writes=[SELT])
                P.dma("sp", PTI[:], pt_d.partition_broadcast(128).rearrange("p a c -> p (a c)"), PTI, writes=[PTI])
                P.dma("sp", iop[:], iota_d, iop, writes=[iop])
                P.op("dve", lambda e: e.tensor_copy(out=PTF[:], in_=PTI[:]), reads=[PTI], writes=[PTF])
                P.op("dve", lambda e: e.tensor_scalar(out=PTF[:], in0=PTF[:], scalar1=128.0, scalar2=iop[:, 0:1],
                                                      op0=ALU.mult, op1=ALU.add), reads=[PTF, iop], writes=[PTF])
                P.op("dve", lambda e: e.tensor_copy(out=IDX[:], in_=PTF[:]), reads=[PTF], writes=[IDX])

                def stream_mm(w_ap, ncols, lhs_fn, out_buf, out_off, nk=8):
                    c0 = 0
                    while c0 < ncols:
                        n = min(512, ncols - c0)
                        sw = WS[stg[0] % len(WS)]
                        pu = PU[stg[0] % 2]
                        stg[0] += 1
                        P.dma("sp", sw[:, :, 0:n], w_ap[:, c0:c0 + n].rearrange("(kc p) c -> p kc c", p=128), sw, writes=[sw])
                        P.op("pe", mm_chain(pu[:, 0:n], lhs_fn, lambda kc, sw=sw, n=n: sw[:, kc, 0:n], nk=nk),
                             reads=[sw, XST, TT], writes=[pu])
                        P.op("act", lambda e, pu=pu, n=n, c0=c0: e.copy(out=out_buf[:, out_off + c0:out_off + c0 + n], in_=pu[:, 0:n]),
                             reads=[pu], writes=[out_buf])
                        c0 += n

                stream_mm(w_in, DIN, lambda kc: XST[:, kc, :], Us, 0)

                if os.environ.get("MK_TIND"):
                    try:
                        nc.gpsimd.indirect_dma_start(out=KV[0][:, 0, :], out_offset=None, in_=ck_d,
                                                     in_offset=bass.IndirectOffsetOnAxis(ap=IDX[:, 0:1], axis=0),
                                                     bounds_check=100, oob_is_err=False)
                        print("TIND2 ok A")
                    except Exception as ex:
                        print("TIND2 err A", str(ex)[:60])
                for k in range(4):
                    P.dma("sp", CWk[:], cwrow[k:k + 1, :].partition_broadcast(NS).rearrange("p a c -> p (a c)"), CWk, writes=[CWk])
                    src = SC[:, k, :] if k < 3 else Us[:, C_XA:C_XA + 2048]
                    dst = cacs if k == 0 else ctmp
                    P.op("dve", lambda e, src=src, dst=dst: e.tensor_tensor(out=dst[:], in0=src, in1=CWk[:], op=ALU.mult),
                         reads=[SC, Us, CWk], writes=[dst])
                    if k > 0:
                        P.op("dve", lambda e: e.tensor_tensor(out=cacs[:], in0=cacs[:], in1=ctmp[:], op=ALU.add),
                             reads=[cacs, ctmp], writes=[cacs])
                P.dma("sp", CWk[:], cbrow.partition_broadcast(NS).rearrange("p a c -> p (a c)"), CWk, writes=[CWk])
                P.op("dve", lambda e: e.tensor_tensor(out=cacs[:], in0=cacs[:], in1=CWk[:], op=ALU.add),
                     reads=[cacs, CWk], writes=[cacs])
                P.op("act", lambda e: e.activation(out=xbc[:], in_=cacs[:], func=AF.Silu), reads=[cacs], writes=[xbc])
                cv3 = cvs_o.rearrange("p (a c) -> p a c", a=3)
                P.dma("sp", cv3[:, 0:2, :], SC[:, 1:3, :], BIGS, reads=[SC], is_out=True)
                P.dma("sp", cv3[:, 2, :], Us[:, C_XA:C_XA + 2048], Us, reads=[Us], is_out=True)

                if os.environ.get("MK_TIND"):
                    try:
                        nc.gpsimd.indirect_dma_start(out=KV[0][:, 0, :], out_offset=None, in_=ck_d,
                                                     in_offset=bass.IndirectOffsetOnAxis(ap=IDX[:, 0:1], axis=0),
                                                     bounds_check=100, oob_is_err=False)
                        print("TIND2 ok B")
                    except Exception as ex:
                        print("TIND2 err B", str(ex)[:60])
                P.op("dve", lambda e: e.tensor_tensor(out=dts[:], in0=Us[:, C_DT:C_DT + 16], in1=v16s[:, 0, :], op=ALU.add),
                     reads=[Us, v16s], writes=[dts])
                P.op("act", lambda e: e.activation(out=dte[:], in_=dts[:], func=AF.Exp), reads=[dts], writes=[dte])
                P.op("act", lambda e: e.activation(out=dts[:], in_=dte[:], func=AF.Ln, bias=1.0), reads=[dte], writes=[dts])
                P.op("act", lambda e: e.activation(out=As[:], in_=v16s[:, 1, :], func=AF.Exp), reads=[v16s], writes=[As])
                P.op("dve", lambda e: e.tensor_tensor(out=dec[:], in0=dts[:], in1=As[:], op=ALU.mult), reads=[dts, As], writes=[dec])
                P.op("act", lambda e: e.activation(out=dec[:], in_=dec[:], func=AF.Exp, scale=-1.0), reads=[dec], writes=[dec])
                x3 = xbc[:, 0:1024].rearrange("p (h d) -> p h d", h=16)
                P.op("dve", lambda e: e.tensor_tensor(out=xdt_t[:].rearrange("p (h d) -> p h d", h=16), in0=x3,
                                                      in1=bc3(dts[:], 16, 64), op=ALU.mult), reads=[xbc, dts], writes=[xdt_t])
                P.op("dve", lambda e: e.tensor_copy(out=dec_t[:].rearrange("p (h d) -> p h d", h=16), in_=bc3(dec[:], 16, 64)),
                     reads=[dec], writes=[dec_t])
                transp(xdt_t, XDTT)
                transp(dec_t, DECT)

                if os.environ.get("MK_TIND"):
                    try:
                        nc.gpsimd.indirect_dma_start(out=KV[0][:, 0, :], out_offset=None, in_=ck_d,
                                                     in_offset=bass.IndirectOffsetOnAxis(ap=IDX[:, 0:1], axis=0),
                                                     bounds_check=100, oob_is_err=False)
                        print("TIND2 ok C")
                    except Exception as ex:
                        print("TIND2 err C", str(ex)[:60])
                ssm_in = ssm_d.rearrange("n (c q s) -> n q c s", c=8, q=128)
                ssm_out = ssms_o.rearrange("n (c q s) -> n q c s", c=8, q=128)
                for n in range(NS):
                    H = Hs[0]
                    P.dma("sp", H[:], ssm_in[n], prod, writes=[H])
                    P.op("pe", lambda e, n=n: e.matmul(PBC[:, :], lhsT=SEL[:, n, :], rhs=xbc[:, 1024:1536], start=True, stop=True),
                         reads=[SEL, xbc], writes=[PBC])
                    P.op("pe", lambda e, n=n: e.matmul(PCC[:, :], lhsT=SEL[:, n, :], rhs=xbc[:, 1536:2048], start=True, stop=True),
                         reads=[SEL, xbc], writes=[PCC])
                    b4 = PBC[:, :].rearrange("p (g a s) -> p g a s", g=4, a=1).to_broadcast([128, 4, 2, 128])
                    c4 = PCC[:, :].rearrange("p (g a s) -> p g a s", g=4, a=1).to_broadcast([128, 4, 2, 128])
                    h4 = lambda t: t[:].rearrange("p (g a) s -> p g a s", g=4)
                    xcol = XDTT[:, :, n:n + 1].to_broadcast([128, 8, 128])
                    dcol = DECT[:, :, n:n + 1].to_broadcast([128, 8, 128])
                    P.op("dve", lambda e: e.tensor_tensor(out=h4(ht1), in0=b4, in1=xcol.rearrange("p (g a) s -> p g a s", g=4),
                                                          op=ALU.mult), reads=[PBC, XDTT], writes=[ht1])
                    P.op("dve", lambda e, H=H: e.tensor_tensor(out=ht2[:], in0=H[:], in1=dcol, op=ALU.mult),
                         reads=[H, DECT], writes=[ht2])
                    P.op("dve", lambda e, H=H: e.tensor_tensor(out=H[:], in0=ht1[:], in1=ht2[:], op=ALU.add),
                         reads=[ht1, ht2], writes=[H])
                    P.dma("sp", ssm_out[n], H[:], prod, reads=[H], is_out=True)
                    P.op("dve", lambda e, H=H: e.tensor_tensor(out=h4(ht1), in0=c4, in1=h4(H), op=ALU.mult),
                         reads=[PCC, H], writes=[ht1])
                    P.op("dve", lambda e, n=n: e.tensor_reduce(out=YTT[:, :, n], in_=ht1[:], axis=AX.X, op=ALU.add),
                         reads=[ht1], writes=[YTT])

                if os.environ.get("MK_TIND"):
                    try:
                        nc.gpsimd.indirect_dma_start(out=KV[0][:, 0, :], out_offset=None, in_=ck_d,
                                                     in_offset=bass.IndirectOffsetOnAxis(ap=IDX[:, 0:1], axis=0),
                                                     bounds_check=100, oob_is_err=False)
                        print("TIND2 ok D")
                    except Exception as ex:
                        print("TIND2 err D", str(ex)[:60])
                def fn_yb(e):
                    ins = None
                    for c in range(8):
                        ins = e.matmul(PYT[c // 4][:, 128 * (c % 4):128 * (c % 4) + 128], lhsT=YTT[:, c, :], rhs=idn[:],
                                       start=True, stop=True)
                    return ins
                P.op("pe", fn_yb, reads=[YTT, idn], writes=[PYT[0], PYT[1]])
                for hf in range(2):
                    P.op("act", lambda e, hf=hf: e.copy(out=y_t[:, 512 * hf:512 * hf + 512], in_=PYT[hf][:, :]),
                         reads=[PYT[hf]], writes=[y_t])
                P.op("dve", lambda e: e.tensor_tensor(out=g_t[:].rearrange("p (h d) -> p h d", h=16), in0=x3,
                                                      in1=bc3(v16s[:, 2, :], 16, 64), op=ALU.mult), reads=[xbc, v16s], writes=[g_t])
                P.op("dve", lambda e: e.tensor_tensor(out=y_t[:], in0=y_t[:], in1=g_t[:], op=ALU.add), reads=[y_t, g_t], writes=[y_t])
                P.op("act", lambda e: e.activation(out=szs[:], in_=Us[:, C_ZA:C_ZA + 1024], func=AF.Silu), reads=[Us], writes=[szs])
                P.op("dve", lambda e: e.tensor_tensor(out=g_t[:], in0=y_t[:], in1=szs[:], op=ALU.mult), reads=[y_t, szs], writes=[g_t])
                P.op("dve", lambda e: e.tensor_tensor(out=gq_t[:], in0=g_t[:], in1=g_t[:], op=ALU.mult), reads=[g_t], writes=[gq_t])
                P.op("dve", lambda e: e.tensor_reduce(out=ss4[:], in_=gq_t[:].rearrange("p (g c) -> p g c", g=4), axis=AX.X, op=ALU.add),
                     reads=[gq_t], writes=[ss4])
                P.op("act", lambda e: e.activation(out=ss4[:], in_=ss4[:], func=AF.Sqrt, bias=epsb[0:NS, 0:1], scale=1.0 / 256.0),
                     reads=[ss4, epsb], writes=[ss4])
                P.op("dve", lambda e: e.reciprocal(out=ss4[:], in_=ss4[:]), reads=[ss4], writes=[ss4])
                P.op("dve", lambda e: e.tensor_tensor(out=ya_t[:].rearrange("p (g c) -> p g c", g=4),
                                                      in0=g_t[:].rearrange("p (g c) -> p g c", g=4), in1=bc3(ss4[:], 4, 256),
                                                      op=ALU.mult), reads=[g_t, ss4], writes=[ya_t])
                P.op("dve", lambda e: e.tensor_tensor(out=ya_t[:], in0=ya_t[:], in1=nws[:], op=ALU.mult), reads=[ya_t, nws], writes=[ya_t])

                if os.environ.get("MK_TIND"):
                    try:
                        nc.gpsimd.indirect_dma_start(out=KV[0][:, 0, :], out_offset=None, in_=ck_d,
                                                     in_offset=bass.IndirectOffsetOnAxis(ap=IDX[:, 0:1], axis=0),
                                                     bounds_check=100, oob_is_err=False)
                        print("TIND2 ok E")
                    except Exception as ex:
                        print("TIND2 err E", str(ex)[:60])
                P.op("dve", lambda e: e.tensor_copy(out=qk[:, 0:16, :].rearrange("p h d -> p (h d)"), in_=Us[:, C_Q:C_Q + 1024]),
                     reads=[Us], writes=[qk])
                P.op("dve", lambda e: e.tensor_copy(out=qk[:, 16:20, :].rearrange("p h d -> p (h d)"), in_=Us[:, C_K:C_K + 256]),
                     reads=[Us], writes=[qk])
                cosb = css[:, 0:32].rearrange("p (a c) -> p a c", a=1).to_broadcast([NS, 20, 32])
                sinb = css[:, 32:64].rearrange("p (a c) -> p a c", a=1).to_broadcast([NS, 20, 32])
                q1, q2 = qk[:, :, 0:32], qk[:, :, 32:64]
                P.op("dve", lambda e: e.tensor_tensor(out=ra_s[:], in0=q1, in1=sinb, op=ALU.mult), reads=[qk, css], writes=[ra_s])
                P.op("dve", lambda e: e.tensor_tensor(out=rb_s[:], in0=q2, in1=sinb, op=ALU.mult), reads=[qk, css], writes=[rb_s])
                P.op("dve", lambda e: e.tensor_tensor(out=q1, in0=q1, in1=cosb, op=ALU.mult), reads=[qk, css], writes=[qk])
                P.op("dve", lambda e: e.tensor_tensor(out=q2, in0=q2, in1=cosb, op=ALU.mult), reads=[qk, css], writes=[qk])
                P.op("dve", lambda e: e.tensor_tensor(out=q1, in0=q1, in1=rb_s[:], op=ALU.subtract), reads=[qk, rb_s], writes=[qk])
                P.op("dve", lambda e: e.tensor_tensor(out=q2, in0=q2, in1=ra_s[:], op=ALU.add), reads=[qk, ra_s], writes=[qk])
                P.dma("sp", ks_o, qk[:, 16:20, :].rearrange("p h d -> p (h d)"), qk, reads=[qk], is_out=True)
                P.dma("sp", vs_o, Us[:, C_V:C_V + 256], Us, reads=[Us], is_out=True)

                if os.environ.get("MK_TIND"):
                    try:
                        nc.gpsimd.indirect_dma_start(out=KV[0][:, 0, :], out_offset=None, in_=ck_d,
                                                     in_offset=bass.IndirectOffsetOnAxis(ap=IDX[:, 0:1], axis=0),
                                                     bounds_check=100, oob_is_err=False)
                        print("TIND2 ok F")
                    except Exception as ex:
                        print("TIND2 err F", str(ex)[:60])
                ck_rows = ck_d
                for n in range(NS):
                    Kt = KV[0]
                    for j in range(NPG):
                        P.idma(Kt[:, j, :], ck_rows, IDX[:, NPG * n + j:NPG * n + j + 1], Kt, reads=[IDX], writes=[Kt],
                               bound=None)
                    for hf in range(2):
                        P.op("pe", lambda e, n=n, hf=hf: e.matmul(PQB[hf][:, :], lhsT=SEL[:, n, :],
                                                                  rhs=qk[:, 8 * hf:8 * hf + 8, :].rearrange("p h d -> p (h d)"),
                                                                  start=True, stop=True), reads=[SEL, qk], writes=[PQB[hf]])
                    for kvh in range(4):
                        pq = PQB[kvh // 2]
                        qv = pq[:, 256 * (kvh % 2):256 * (kvh % 2) + 256].rearrange("p (a h d) -> p a h d", a=1, h=4) \
                            .to_broadcast([128, NPG, 4, 64])
                        kvw = Kt[:, :, 64 * kvh:64 * kvh + 64].rearrange("p j (a d) -> p j a d", a=1).to_broadcast([128, NPG, 4, 64])
                        P.op("dve", lambda e, kvw=kvw, qv=qv: e.tensor_tensor(out=prod[:], in0=kvw, in1=qv, op=ALU.mult),
                             reads=[Kt, pq], writes=[prod])
                        P.op("dve", lambda e, n=n, kvh=kvh: e.tensor_reduce(
                            out=S_all[:, n, :].rearrange("p (j h) -> p j h", h=16)[:, :, 4 * kvh:4 * kvh + 4], in_=prod[:],
                            axis=AX.X, op=ALU.add), reads=[prod], writes=[S_all])
                    P.op("pe", lambda e, n=n: e.matmul(PCS[:, :], lhsT=SELT[:, n, :], rhs=S_all[:, n, :],
                                                       start=(n == 0), stop=(n == NS - 1)), reads=[SELT, S_all], writes=[PCS])
                P.op("dve", lambda e: e.tensor_copy(out=csum[:].rearrange("p j h -> p (j h)"), in_=PCS[:, :]), reads=[PCS], writes=[csum])
                cs4 = csum[:].rearrange("p (b a) h -> p b a h", a=2)
                P.op("dve", lambda e: e.tensor_tensor(out=sblk[:].rearrange("p h b -> p b h"), in0=cs4[:, :, 0, :], in1=cs4[:, :, 1, :],
                                                      op=ALU.add), reads=[csum], writes=[sblk])
                for h in range(16):
                    P.op("dve", lambda e, h=h: e.max(out=mx8[:, h, :], in_=sblk[:, h, :]), reads=[sblk], writes=[mx8])
                P.op("dve", lambda e: e.tensor_tensor(out=sl8[:], in0=sblk[:], in1=mx8[:, :, 2:3].to_broadcast([NS, 16, 8]), op=ALU.is_ge),
                     reads=[sblk, mx8], writes=[sl8])
                P.op("dve", lambda e: e.tensor_scalar(out=sl8[:], in0=sl8[:], scalar1=-1.0, scalar2=BIG, op0=ALU.add, op1=ALU.mult),
                     reads=[sl8], writes=[sl8])
                P.op("dve", lambda e: e.tensor_copy(
                    out=MB[:].rearrange("p (b a) h -> p b a h", a=2),
                    in_=sl8[:].rearrange("p h (b a) -> p b a h", a=1).to_broadcast([NS, 8, 2, 16])), reads=[sl8], writes=[MB])

                for n in range(NS):
                    Vt = KV[0]
                    for j in range(NPG):
                        P.idma(Vt[:, j, :], cv_d, IDX[:, NPG * n + j:NPG * n + j + 1], Vt, reads=[IDX], writes=[Vt],
                               bound=None)
                    P.op("pe", lambda e, n=n: e.matmul(PMBc[:, :], lhsT=SEL[:, n, :], rhs=MB[:].rearrange("p j h -> p (j h)"),
                                                       start=True, stop=True), reads=[SEL, MB], writes=[PMBc])
                    P.op("dve", lambda e, n=n: e.scalar_tensor_tensor(out=stmp[:], in0=S_all[:, n, :], scalar=SCALE, in1=PMBc[:, :],
                                                                      op0=ALU.mult, op1=ALU.add), reads=[S_all, PMBc], writes=[stmp])
                    P.op("act", lambda e: e.activation(out=Pm[:].rearrange("p j h -> p (j h)"), in_=stmp[:], func=AF.Exp),
                         reads=[stmp], writes=[Pm])

                    def fn_pv(e, Vt=Vt):
                        ins = None
                        for kvh in range(4):
                            for j in range(NPG):
                                ins = e.matmul(POs[:, 4 * kvh:4 * kvh + 4], lhsT=Vt[:, j, 64 * kvh:64 * kvh + 64],
                                               rhs=Pm[:, j, 4 * kvh:4 * kvh + 4], start=(j == 0), stop=(j == NPG - 1))
                        for j in range(NPG):
                            ins = e.matmul(PLs[:, :], lhsT=Pm[:, j, :], rhs=onesf[:, 0:1], start=(j == 0), stop=(j == NPG - 1))
                        return ins
                    P.op("pe", fn_pv, reads=[Vt, Pm, onesf], writes=[POs])
                    P.op("act", lambda e, n=n: e.copy(out=OT[:, :, n], in_=POs[:, :]), reads=[POs], writes=[OT])
                    P.op("dve", lambda e, n=n: e.tensor_copy(out=LT[:, n:n + 1], in_=PLs[:, :]), reads=[PLs], writes=[LT])

                def fn_ob(e):
                    ins = None
                    for h in range(16):
                        ins = e.matmul(POt[h // 8][:, 64 * (h % 8):64 * (h % 8) + 64], lhsT=OT[:, h, :], rhs=idn[0:64, 0:64],
                                       start=True, stop=True)
                    return ins
                P.op("pe", fn_ob, reads=[OT, idn], writes=[POt[0], POt[1]])
                P.op("pe", lambda e: e.matmul(PLt[:, :], lhsT=LT[:], rhs=idn[0:16, 0:16], start=True, stop=True),
                     reads=[LT, idn], writes=[PLt])
                for hf in range(2):
                    P.op("act", lambda e, hf=hf: e.copy(out=o_t[:, 8 * hf:8 * hf + 8, :].rearrange("p h d -> p (h d)"), in_=POt[hf][:, :]),
                         reads=[POt[hf]], writes=[o_t])
                P.op("dve", lambda e: e.tensor_copy(out=l_t[:], in_=PLt[:, :]), reads=[PLt], writes=[l_t])
                k4 = qk[:, 16:20, :].rearrange("p (k a) d -> p k a d", a=1).to_broadcast([NS, 4, 4, 64])
                v4 = Us[:, C_V:C_V + 256].rearrange("p (k a d) -> p k a d", k=4, a=1).to_broadcast([NS, 4, 4, 64])
                q4 = qk[:, 0:16, :].rearrange("p (k a) d -> p k a d", a=4)
                P.op("dve", lambda e: e.tensor_tensor(out=sprod[:].rearrange("p (k a) d -> p k a d", a=4), in0=q4, in1=k4, op=ALU.mult),
                     reads=[qk], writes=[sprod])
                P.op("dve", lambda e: e.tensor_reduce(out=sown[:], in_=sprod[:], axis=AX.X, op=ALU.add), reads=[sprod], writes=[sown])
                P.op("act", lambda e: e.activation(out=sown[:], in_=sown[:], func=AF.Exp, scale=SCALE), reads=[sown], writes=[sown])
                P.op("dve", lambda e: e.tensor_tensor(out=sprod[:].rearrange("p (k a) d -> p k a d", a=4), in0=v4,
                                                      in1=sown[:].rearrange("p (k a c) -> p k a c", k=4, c=1).to_broadcast([NS, 4, 4, 64]),
                                                      op=ALU.mult), reads=[Us, sown], writes=[sprod])
                P.op("dve", lambda e: e.tensor_tensor(out=o_t[:], in0=o_t[:], in1=sprod[:], op=ALU.add), reads=[o_t, sprod], writes=[o_t])
                P.op("dve", lambda e: e.tensor_tensor(out=l_t[:], in0=l_t[:], in1=sown[:], op=ALU.add), reads=[l_t, sown], writes=[l_t])
                P.op("dve", lambda e: e.reciprocal(out=l_t[:], in_=l_t[:]), reads=[l_t], writes=[l_t])
                P.op("dve", lambda e: e.tensor_tensor(out=o_t[:], in0=o_t[:], in1=bc3(l_t[:], 16, 64), op=ALU.mult),
                     reads=[o_t, l_t], writes=[o_t])
                P.op("act", lambda e: e.activation(out=szs[:], in_=Us[:, C_ZB:C_ZB + 1024], func=AF.Silu), reads=[Us], writes=[szs])
                P.op("dve", lambda e: e.tensor_tensor(out=ob_t[:], in0=o_t[:].rearrange("p h d -> p (h d)"), in1=szs[:], op=ALU.mult),
                     reads=[o_t, szs], writes=[ob_t])

                P.dma("sp", gbs[:], lng.partition_broadcast(NS).rearrange("p a c -> p (a c)"), prod, writes=[gbs])
                P.dma("sp", bbs[:], lnb.partition_broadcast(NS).rearrange("p a c -> p (a c)"), prod, writes=[bbs])
                P.dma("sp", xrs[:], xs_tok, prod, writes=[xrs])
                P.op("act", lambda e: e.activation(out=sg[:], in_=Us[:, C_GA:C_GA + 2048], func=AF.Sigmoid), reads=[Us], writes=[sg])
                transp(ya_t, TT)
                stream_mm(w_a, 1024, lambda kc: TT[:, kc, :], br, 0)
                transp(ob_t, TT)
                stream_mm(w_b, 1024, lambda kc: TT[:, kc, :], br, 1024)
                P.op("dve", lambda e: e.tensor_tensor(out=br[:], in0=br[:], in1=sg[:], op=ALU.mult), reads=[br, sg], writes=[br])
                P.op("dve", lambda e: e.tensor_tensor(out=mix_t[:], in0=br[:, 0:1024], in1=br[:, 1024:2048], op=ALU.add),
                     reads=[br], writes=[mix_t])
                transp(mix_t, TT)
                stream_mm(w_o, 1024, lambda kc: TT[:, kc, :], pre_s, 0)
                P.op("dve", lambda e: e.scalar_tensor_tensor(out=pre_s[:], in0=xrs[:], scalar=ALPHA, in1=pre_s[:], op0=ALU.mult, op1=ALU.add),
                     reads=[xrs, pre_s], writes=[pre_s])
                for hf in range(2):
                    P.op("dve", lambda e, hf=hf: e.bn_stats(out=sts[:, hf, :], in_=pre_s[:, 512 * hf:512 * hf + 512]), reads=[pre_s], writes=[sts])
                P.op("dve", lambda e: e.bn_aggr(out=mvs[:], in_=sts[:].rearrange("p a c -> p (a c)")), reads=[sts], writes=[mvs])
                P.op("act", lambda e: e.activation(out=rss[:], in_=mvs[:, 1:2], func=AF.Sqrt, bias=epsb[0:NS, 0:1]), reads=[mvs, epsb], writes=[rss])
                P.op("dve", lambda e: e.reciprocal(out=rss[:], in_=rss[:]), reads=[rss], writes=[rss])
                P.op("dve", lambda e: e.tensor_scalar(out=pre_s[:], in0=pre_s[:], scalar1=mvs[:, 0:1], scalar2=rss[:, 0:1],
                                                      op0=ALU.subtract, op1=ALU.mult), reads=[pre_s, mvs, rss], writes=[pre_s])
                P.op("dve", lambda e: e.tensor_tensor(out=pre_s[:], in0=pre_s[:], in1=gbs[:], op=ALU.mult), reads=[pre_s, gbs], writes=[pre_s])
                P.op("dve", lambda e: e.tensor_tensor(out=pre_s[:], in0=pre_s[:], in1=bbs[:], op=ALU.add), reads=[pre_s, bbs], writes=[pre_s])
                P.dma("sp", ys_o, pre_s[:], pre_s, reads=[pre_s], is_out=True)
            P.barrier()

        YAT = P.sb("YAT", [128, 8, 2048], BF16)
        OBT = P.sb("OBT", [128, 8, 2048], BF16)
        with contextlib.ExitStack() as st:
            alloc_io(st, "a")
            XO = XOl[0]
            Wf = P.sb("Wf", [128, 8, 512], BF16, st)
            Wt = P.sb("Wt", [128, 8, 260], BF16, st)
            cw = P.sb("cw", [128, 4, 4], F32, st)
            cb = P.sb("cb", [128, 4], F32, st)
            v16 = P.sb("v16", [128, 3, 4], F32, st)
            Aneg = P.sb("Aneg", [128, 4], F32, st)
            nw = P.sb("nw", [128, 256], F32, st)
            U = P.sb("U", [128, 3, 515], F32, st)
            XC = P.sb("XC", [128, 3, 512], F32, st)
            cacc = P.sb("cacc", [128, 512], F32, st)
            UO = P.sb("UO", [128, 4, 131], F32, st)
            XCO = P.sb("XCO", [128, 4, 128], F32, st)
            XCOb = P.sb("XCOb", [128, 4, 128], BF16, st)
            caco = P.sb("caco", [128, 128], F32, st)
            hT = P.sb("hT", [128, 256], F32, st)
            Hsel = P.sb("Hsel", [128, 256], F32, st)
            Hselb = P.sb("Hselb", [128, 256], BF16, st)
            sm = {n: P.sb("sm_" + n, [128, 16], F32, st) for n in
                  ("t0", "e0", "dt", "adt", "acs", "te", "ecl", "w")}
            htmp = P.sb("htmp", [128, 256], F32, st)
            so = {n: P.sb("so_" + n, [128, 4], F32, st) for n in
                  ("t0", "e0", "dt", "adt", "acs", "nacs", "eacs")}
            xs_tok = P.sb("xs_tok", [128, 256], F32, st)
            B_tokb = P.sb("B_tokb", [128, 128], BF16, st)
            xdtw = P.sb("xdtw", [128, 256], BF16, st)
            xso = P.sb("xso", [128, 256], F32, st)
            xdto = P.sb("xdto", [128, 256], BF16, st)
            Radt = P.sb("Radt", [128, 4, 128], F32, st)
            dcl = P.sb("dcl", [128, 4, 128], F32, st)
            dce = P.sb("dce", [128, 4, 128], F32, st)
            cbm = P.sb("cbm", [128, 128], F32, st)
            MT = P.sb("MT", [128, 4, 128], BF16, st)
            Ysb = P.sb("Ysb", [128, 256], F32, st)
            yv = P.sb("yv", [128, 256], F32, st)
            sza = P.sb("sza", [128, 256], F32, st)
            gg = P.sb("gg", [128, 256], F32, st)
            gsq = P.sb("gsq", [128, 256], F32, st)
            ss = P.sb("ss", [128, 1], F32, st)
            rstd = P.sb("rstd", [128, 1], F32, st)
            ya = P.sb("ya", [128, 256], F32, st)
            cvs = P.sb("cvs", [128, 3, 3], F32, st)
            cvc = P.sb("cvc", [128, 3], F32, st)

            PF = [P.view("PF0", banks[0], banks[0][:, :]), P.view("PF1", banks[1], banks[1][:, :])]
            PTc = P.view("PTc", banks[2], banks[2][:, 0:16])
            Pacs = P.view("Pacs", banks[2], banks[2][:, 16:32])
            Pacl = P.view("Pacl", banks[2], banks[2][:, 32:48])
            PX = P.view("PX", banks[2], banks[2][:, 64:448])
            PST = P.view("PST", banks[3], banks[3][:, 0:256])
            PCB = P.view("PCB", banks[3], banks[3][:, 256:384])
            PFo = [P.view("PFo0", banks[4], banks[4][:, 0:131]), P.view("PFo1", banks[4], banks[4][:, 132:263])]
            PTo = P.view("PTo", banks[5], banks[5][:, 0:260])
            Poa = P.view("Poa", banks[5], banks[5][:, 264:268])
            PAB = P.view("PAB", banks[6], banks[6][:, :])
            PXo = P.view("PXo", banks[7], banks[7][:, 0:256])
            PY = P.view("PY", banks[7], banks[7][:, 256:512])
            PYO = PST
            PYT = PX

            for g in range(NG if 's' in PASSES else 0):
                load_w(Wf, 0, w_in[:, C_XA + 256 * g:C_XA + 256 * g + 256], 256)
                load_w(Wf, 256, w_in[:, C_B + 128 * g:C_B + 128 * g + 128], 128)
                load_w(Wf, 384, w_in[:, C_C + 128 * g:C_C + 128 * g + 128], 128)
                load_w(Wt, 0, w_in[:, C_ZA + 256 * g:C_ZA + 256 * g + 256], 256)
                load_w(Wt, 256, w_in[:, C_DT + 4 * g:C_DT + 4 * g + 4], 4)
                chs = [256 * g, 256 * g + 128, 1024 + 128 * g, 1536 + 128 * g]
                for m, c0 in enumerate(chs):
                    P.dma("sp", cw[:, m, :], cwT[c0:c0 + 128, :], cw, writes=[cw])
                    P.dma("sp", cb[:, m:m + 1], cbT[c0:c0 + 128, :], cb, writes=[cb])
                P.dma("sp", v16[:], vec16[:, 4 * g:4 * g + 4].partition_broadcast(128), v16, writes=[v16])
                P.dma("sp", nw[:], nwv[:, 256 * g:256 * g + 256].partition_broadcast(128).rearrange("p a c -> p (a c)"),
                      nw, writes=[nw])
                P.op("act", lambda e: e.activation(out=Aneg[:], in_=v16[:, 1, :], func=AF.Exp), reads=[v16], writes=[Aneg])
                P.op("dve", lambda e: e.tensor_scalar(out=Aneg[:], in0=Aneg[:], scalar1=-1.0, scalar2=None, op0=ALU.mult),
                     reads=[Aneg], writes=[Aneg])
                P.op("pool", lambda e: e.memset(hT[:], 0.0), writes=[hT])
                P.op("pool", lambda e: e.memset(U[:], 0.0), writes=[U])

                def dt_chain(smd, psrc, rd, nk=1):
                    n4 = 4 * nk
                    t0, e0, dtt, adt = smd["t0"], smd["e0"], smd["dt"], smd["adt"]
                    v3 = lambda ap: ap.rearrange("p (k h) -> p k h", h=4)
                    b3 = lambda ap: ap.rearrange("p (a h) -> p a h", a=1).to_broadcast([128, nk, 4])
                    P.op("dve", lambda e: e.tensor_tensor(out=v3(t0[:, 0:n4]), in0=v3(psrc), in1=b3(v16[:, 0, :]), op=ALU.add),
                         reads=rd + [v16], writes=[t0])
                    P.op("act", lambda e: e.activation(out=e0[:, 0:n4], in_=t0[:, 0:n4], func=AF.Exp), reads=[t0], writes=[e0])
                    P.op("act", lambda e: e.activation(out=dtt[:, 0:n4], in_=e0[:, 0:n4], func=AF.Ln, bias=1.0), reads=[e0], writes=[dtt])
                    P.op("dve", lambda e: e.tensor_tensor(out=v3(adt[:, 0:n4]), in0=v3(dtt[:, 0:n4]), in1=b3(Aneg[:]), op=ALU.mult),
                         reads=[dtt, Aneg], writes=[adt])

                for i in range(NI):
                    xb = load_x(i)
                    for m in range(3):
                        pf = PF[m % 2]
                        P.op("pe", mm_chain(pf[:, :], lambda kc, m=m: Wf[:, kc, m * 128:(m + 1) * 128],
                                            lambda kc, xb=xb: xb[:, kc, :]), reads=[Wf, xb], writes=[pf])
                        P.op("act", lambda e, m=m, pf=pf: e.copy(out=U[:, m, 3:515], in_=pf[:, :]), reads=[pf], writes=[U])
                    for m in range(3):
                        P.op("dve", lambda e, m=m: e.tensor_scalar(out=cacc[:], in0=U[:, m, 0:512], scalar1=cw[:, m, 0:1],
                                                                   scalar2=cb[:, m:m + 1], op0=ALU.mult, op1=ALU.add),
                             reads=[U, cw, cb], writes=[cacc])
                        for k in range(1, 4):
                            P.op("dve", lambda e, m=m, k=k: e.scalar_tensor_tensor(
                                out=cacc[:], in0=U[:, m, k:k + 512], scalar=cw[:, m, k:k + 1], in1=cacc[:],
                                op0=ALU.mult, op1=ALU.add), reads=[U, cw, cacc], writes=[cacc])
                        P.op("act", lambda e, m=m: e.activation(out=XC[:, m, :], in_=cacc[:], func=AF.Silu),
                             reads=[cacc], writes=[XC])
                    if i == NI - 1:
                        P.op("dve", lambda e: e.tensor_copy(out=cvs[:], in_=U[:, :, 512:515]), reads=[U], writes=[cvs])
                        P.dma("sp", cvA_o[g], cvs[:], cvs, reads=[cvs], is_out=True)
                    P.op("dve", lambda e: e.tensor_copy(out=U[:, :, 0:3], in_=U[:, :, 512:515]), reads=[U], writes=[U])
                    def fn_dt(e, xb=xb):
                        ins = None
                        for k in range(4):
                            for kc in range(8):
                                ins = e.matmul(PTc[:, 4 * k:4 * k + 4], lhsT=xb[:, kc, 128 * k:128 * k + 128],
                                               rhs=Wt[:, kc, 256:260], start=(kc == 0), stop=(kc == 7))
                        return ins
                    P.op("pe", fn_dt, reads=[xb, Wt], writes=[PTc])
                    dt_chain(sm, PTc[:, 0:16], [PTc], nk=4)
                    adt = sm["adt"]
                    P.op("pe", lambda e: e.matmul(Pacs[:, :], lhsT=tri[:], rhs=adt[:], start=True, stop=True),
                         reads=[tri, adt], writes=[Pacs])
                    P.op("pe", lambda e: e.matmul(Pacl[:, :], lhsT=onesf[:], rhs=adt[:], start=True, stop=True),
                         reads=[onesf, adt], writes=[Pacl])
                    te, ecl, w_ = sm["te"], sm["ecl"], sm["w"]
                    P.op("dve", lambda e: e.tensor_copy(out=sm["acs"][:], in_=Pacs[:, :]), reads=[Pacs], writes=[sm["acs"]])
                    P.op("dve", lambda e: e.tensor_tensor(out=te[:], in0=Pacl[:, :], in1=sm["acs"][:], op=ALU.subtract),
                         reads=[Pacl, sm["acs"]], writes=[te])
                    P.op("act", lambda e: e.activation(out=te[:], in_=te[:], func=AF.Exp), reads=[te], writes=[te])
                    P.op("act", lambda e: e.activation(out=ecl[:], in_=Pacl[:, :], func=AF.Exp), reads=[Pacl], writes=[ecl])
                    P.op("dve", lambda e: e.tensor_tensor(out=w_[:], in0=te[:], in1=sm["dt"][:], op=ALU.mult),
                         reads=[te, sm["dt"]], writes=[w_])
                    for k in range(4):
                        cs = slice(128 * k, 128 * k + 128)

                        def fn_tr(e, cs=cs):
                            tr(e, PX[:, 0:128], XC[:, 0, cs], idn[:])
                            tr(e, PX[:, 128:256], XC[:, 1, cs], idn[:])
                            return tr(e, PX[:, 256:384], XC[:, 2, cs], idn[:])
                        P.op("pe", fn_tr, reads=[XC, idn], writes=[PX])
                        P.op("act", lambda e: e.copy(out=B_tokb[:], in_=PX[:, 256:384]), reads=[PX], writes=[B_tokb])
                        for h in range(4):
                            P.op("dve", lambda e, h=h: e.tensor_scalar(
                                out=xdtw[:, 64 * h:64 * h + 64], in0=PX[:, 64 * h:64 * h + 64],
                                scalar1=w_[:, 4 * k + h:4 * k + h + 1], scalar2=None, op0=ALU.mult), reads=[PX, w_], writes=[xdtw])
                        P.op("pe", lambda e: e.matmul(PST[:, :], lhsT=B_tokb[:], rhs=xdtw[:], start=True, stop=True),
                             reads=[B_tokb, xdtw], writes=[PST])
                        if k == 0:
                            P.op("dve", lambda e: e.tensor_scalar(out=Hsel[:], in0=hT[:], scalar1=oh[:, 0:1], scalar2=None,
                                                                  op0=ALU.mult), reads=[hT, oh], writes=[Hsel])
                        else:
                            P.op("dve", lambda e, k=k: e.scalar_tensor_tensor(
                                out=Hsel[:], in0=hT[:], scalar=oh[:, k:k + 1], in1=Hsel[:], op0=ALU.mult, op1=ALU.add),
                                reads=[hT, oh, Hsel], writes=[Hsel])
                        for h in range(4):
                            P.op("dve", lambda e, h=h: e.scalar_tensor_tensor(
                                out=hT[:, 64 * h:64 * h + 64], in0=hT[:, 64 * h:64 * h + 64], scalar=ecl[:, 4 * k + h:4 * k + h + 1],
                                in1=PST[:, 64 * h:64 * h + 64], op0=ALU.mult, op1=ALU.add),
                                reads=[hT, ecl, PST], writes=[hT])
                    P.op("act", lambda e: e.copy(out=Hselb[:], in_=Hsel[:]), reads=[Hsel], writes=[Hselb])

                    load_xo(i)
                    for m in range(4):
                        pf = PFo[m % 2]
                        P.op("pe", mm_chain(pf[:, :], lambda kc, m=m: Wf[:, kc, m * 128:(m + 1) * 128],
                                            lambda kc: XO[:, kc, :]), reads=[Wf, XO], writes=[pf])
                        P.op("act", lambda e, m=m, pf=pf: e.copy(out=UO[:, m, :], in_=pf[:, :]), reads=[pf], writes=[UO])
                    for m in range(4):
                        P.op("dve", lambda e, m=m: e.tensor_scalar(out=caco[:], in0=UO[:, m, 0:128], scalar1=cw[:, m, 0:1],
                                                                   scalar2=cb[:, m:m + 1], op0=ALU.mult, op1=ALU.add),
                             reads=[UO, cw, cb], writes=[caco])
                        for k in range(1, 4):
                            P.op("dve", lambda e, m=m, k=k: e.scalar_tensor_tensor(
                                out=caco[:], in0=UO[:, m, k:k + 128], scalar=cw[:, m, k:k + 1], in1=caco[:],
                                op0=ALU.mult, op1=ALU.add), reads=[UO, cw, caco], writes=[caco])
                        P.op("act", lambda e, m=m: e.activation(out=XCO[:, m, :], in_=caco[:], func=AF.Silu),
                             reads=[caco], writes=[XCO])
                    P.op("pool", lambda e: e.tensor_copy(out=XCOb[:], in_=XCO[:]), reads=[XCO], writes=[XCOb])
                    if i == NI - 1:
                        P.op("dve", lambda e: e.tensor_copy(out=cvc[:], in_=UO[:, 3, 128:131]), reads=[UO], writes=[cvc])
                        P.dma("sp", cvC_o[g], cvc[:], cvc, reads=[cvc], is_out=True)
                    P.op("pe", mm_chain(PTo[:, :], lambda kc: XO[:, kc, 3:131], lambda kc: Wt[:, kc, :]),
                         reads=[XO, Wt], writes=[PTo])
                    dt_chain(so, PTo[:, 256:260], [PTo])
                    P.op("act", lambda e: e.activation(out=sza[:], in_=PTo[:, 0:256], func=AF.Silu), reads=[PTo], writes=[sza])
                    adto = so["adt"]
                    P.op("pe", lambda e: e.matmul(Poa[:, :], lhsT=tri[:], rhs=adto[:], start=True, stop=True),
                         reads=[tri, adto], writes=[Poa])
                    P.op("dve", lambda e: e.tensor_copy(out=so["acs"][:], in_=Poa[:, :]), reads=[Poa], writes=[so["acs"]])
                    P.op("act", lambda e: e.activation(out=so["eacs"][:], in_=so["acs"][:], func=AF.Exp), reads=[so["acs"]],
                         writes=[so["eacs"]])
                    for h in range(4):
                        P.op("dve", lambda e, h=h: e.tensor_scalar(out=Radt[:, h, :], in0=tri[:], scalar1=adto[:, h:h + 1],
                                                                   scalar2=None, op0=ALU.mult), reads=[tri, adto], writes=[Radt])

                    def fn_ab(e):
                        ins = None
                        for h in range(4):
                            ins = e.matmul(PAB[:, 128 * h:128 * h + 128], lhsT=onesf[:], rhs=Radt[:, h, :], start=True, stop=True)
                        return ins
                    P.op("pe", fn_ab, reads=[onesf, Radt], writes=[PAB])
                    for h in range(4):
                        P.op("dve", lambda e, h=h: e.tensor_scalar(
                            out=dcl[:, h, :], in0=PAB[:, 128 * h:128 * h + 128], scalar1=so["acs"][:, h:h + 1], scalar2=0.0,
                            op0=ALU.subtract, op1=ALU.min), reads=[PAB, so["acs"]], writes=[dcl])
                    P.op("act", lambda e: e.activation(out=dce[:], in_=dcl[:], func=AF.Exp), reads=[dcl], writes=[dce])
                    P.op("pe", lambda e: e.matmul(PCB[:, :], lhsT=XCOb[:, 2, :], rhs=XCOb[:, 3, :], start=True, stop=True),
                         reads=[XCOb], writes=[PCB])
                    P.op("dve", lambda e: e.tensor_tensor(out=cbm[:], in0=PCB[:, :], in1=tri[:], op=ALU.mult),
                         reads=[PCB, tri], writes=[cbm])
                    P.op("dve", lambda e: e.tensor_tensor(
                        out=MT[:], in0=dce[:], in1=cbm[:].rearrange("p (a c) -> p a c", a=1).to_broadcast([128, 4, 128]),
                        op=ALU.mult), reads=[dce, cbm], writes=[MT])

                    def fn_tro(e):
                        tr(e, PXo[:, 0:128], XCO[:, 0, :], idn[:])
                        return tr(e, PXo[:, 128:256], XCO[:, 1, :], idn[:])
                    P.op("pe", fn_tro, reads=[XCO, idn], writes=[PXo])
                    P.op("act", lambda e: e.copy(out=xso[:], in_=PXo[:, :]), reads=[PXo], writes=[xso])
                    for h in range(4):
                        P.op("dve", lambda e, h=h: e.tensor_scalar(
                            out=xdto[:, 64 * h:64 * h + 64], in0=xso[:, 64 * h:64 * h + 64], scalar1=so["dt"][:, h:h + 1],
                            scalar2=None, op0=ALU.mult), reads=[xso, so["dt"]], writes=[xdto])

                    def fn_y(e):
                        ins = None
                        for h in range(4):
                            ins = e.matmul(PY[:, 64 * h:64 * h + 64], lhsT=MT[:, h, :], rhs=xdto[:, 64 * h:64 * h + 64],
                                           start=True, stop=True)
                        return ins
                    P.op("pe", fn_y, reads=[MT, xdto], writes=[PY])
                    P.op("pe", lambda e: e.matmul(PYO[:, :], lhsT=XCOb[:, 3, :], rhs=Hselb[:], start=True, stop=True),
                         reads=[XCOb, Hselb], writes=[PYO])
                    P.op("act", lambda e: e.copy(out=Ysb[:], in_=PY[:, :]), reads=[PY], writes=[Ysb])
                    for h in range(4):
                        hs = slice(64 * h, 64 * h + 64)
                        P.op("dve", lambda e, h=h, hs=hs: e.scalar_tensor_tensor(
                            out=yv[:, hs], in0=PYO[:, hs], scalar=so["eacs"][:, h:h + 1], in1=Ysb[:, hs],
                            op0=ALU.mult, op1=ALU.add), reads=[PYO, so["eacs"], Ysb], writes=[yv])
                        P.op("dve", lambda e, h=h, hs=hs: e.scalar_tensor_tensor(
                            out=yv[:, hs], in0=xso[:, hs], scalar=v16[:, 2, h:h + 1], in1=yv[:, hs],
                            op0=ALU.mult, op1=ALU.add), reads=[xso, v16, yv], writes=[yv])
                    P.op("dve", lambda e: e.tensor_tensor(out=gg[:], in0=yv[:], in1=sza[:], op=ALU.mult),
                         reads=[yv, sza], writes=[gg])
                    P.op("dve", lambda e: e.tensor_tensor(out=gsq[:], in0=gg[:], in1=gg[:], op=ALU.mult),
                         reads=[gg], writes=[gsq])
                    P.op("dve", lambda e: e.tensor_reduce(out=ss[:], in_=gsq[:], axis=AX.X, op=ALU.add),
                         reads=[gsq], writes=[ss])
                    P.op("act", lambda e: e.activation(out=rstd[:], in_=ss[:], func=AF.Sqrt, bias=epsb[:, 0:1], scale=1.0 / 256.0),
                         reads=[ss, epsb], writes=[rstd])
                    P.op("dve", lambda e: e.reciprocal(out=rstd[:], in_=rstd[:]), reads=[rstd], writes=[rstd])
                    P.op("dve", lambda e: e.scalar_tensor_tensor(out=ya[:], in0=gg[:], scalar=rstd[:, 0:1], in1=nw[:],
                                                                 op0=ALU.mult, op1=ALU.mult), reads=[gg, rstd, nw], writes=[ya])

                    def fn_yt(e):
                        tr(e, PYT[:, 0:128], ya[:, 0:128], idn[:])
                        return tr(e, PYT[:, 128:256], ya[:, 128:256], idn[:])
                    P.op("pe", fn_yt, reads=[ya, idn], writes=[PYT])
                    P.op("act", lambda e, g=g, i=i: e.copy(
                        out=YAT[:, 2 * g:2 * g + 2, 128 * i:128 * i + 128],
                        in_=PYT[:, 0:256].rearrange("p (a c) -> p a c", a=2)), reads=[PYT], writes=[YAT])
                P.dma("sp", ssm_o[g], hT[:], hT, reads=[hT], is_out=True)

        P.barrier()
        with contextlib.ExitStack() as st:
            alloc_io(st, "b")
            XO = XOl[0]
            Wkv = P.sb("Wkv", [128, 8, 128], BF16, st)
            Wqz = P.sb("Wqz", [128, 8, 512], BF16, st)
            KTA = P.sb("KTA", [96, SEQ], BF16, st)
            VA = P.sb("VA", [128, 64, 65], BF16, st)
            KTf = P.sb("KTf", [64, 256], F32, st)
            KM = P.sb("KM", [64, 32], F32, st)
            CM = P.sb("CM", [128, 4, 512], BF16, st)
            EL = P.sb("EL", [128, 3, 16, 32], F32, st)
            csa4 = [P.sb("csa4%d" % i, [128, 4, 64], F32, st) for i in range(2)]
            kv4 = P.sb("kv4", [128, 4, 128], F32, st)
            kr4 = P.sb("kr4", [128, 4, 64], F32, st)
            KTf4 = P.sb("KTf4", [64, 512], F32, st)
            cso = P.sb("cso", [128, 64], F32, st)
            kv = P.sb("kv", [128, 128], F32, st)
            kr = P.sb("kr", [128, 64], F32, st)
            ra = P.sb("ra", [128, 4, 32], F32, st)
            rb = P.sb("rb", [128, 4, 32], F32, st)
            qf = P.sb("qf", [128, 4, 64], F32, st)
            qr = P.sb("qr", [128, 4, 64], F32, st)
            szb = P.sb("szb", [128, 256], F32, st)
            QA = P.sb("QA", [96, 4, 128], BF16, st)
            QTf = P.sb("QTf", [64, 4, 128], F32, st)
            spd = P.sb("spd", [128, 4, 32], F32, st)
            mx = P.sb("mx", [128, 4, 8], F32, st)
            sel = P.sb("sel", [128, 4, 32], F32, st)
            MBP = P.sb("MBP", [128, 4, 96], F32, st)
            PTb = [P.sb("PTb%d" % i, [128, 512], BF16, st) for i in range(3)]
            Osb = P.sb("Osb", [65, 512], F32, st)
            rl = P.sb("rl", [128, 4], F32, st)
            ob = P.sb("ob", [128, 4, 64], F32, st)
            ob2 = P.sb("ob2", [128, 256], F32, st)

            PT2 = P.view("PT2", banks[0], banks[0][:, :])
            PKT = P.view("PKT", banks[2], banks[2][0:64, 256:384])
            PKT4 = P.view("PKT4", banks[1], banks[1][0:64, :])
            PQ = P.view("PQ", banks[1], banks[1][:, :])
            PSB = P.view("PSB", banks[2], banks[2][:, 0:128])
            PMB = P.view("PMB", banks[2], banks[2][0:96, 128:256])
            POT = PQ
            PS = [P.view("PS%d" % i, banks[3 + i], banks[3 + i][:, :]) for i in range(3)]
            PO = [P.view("PO%d" % i, banks[6 + i], banks[6 + i][0:65, :]) for i in range(2)]
            POBT = P.view("POBT", banks[0], banks[0][:, 0:256])

            for j in range(SEQ // 512):
                s = WS[stg[0] % len(WS)]
                stg[0] += 1
                P.dma("sp", s[64:96, 0, :], kbi_d[:, 512 * j:512 * j + 512], s, writes=[s])
                P.op("pool", lambda e, j=j, s=s: e.tensor_copy(out=KTA[64:96, 512 * j:512 * j + 512], in_=s[64:96, 0, :]),
                     reads=[s], writes=[KTA])
            s = WS[stg[0] % len(WS)]
            stg[0] += 1
            P.dma("sp", s[:, 0:4, :], cm_d.rearrange("p (a c) -> p a c", a=4), s, writes=[s])
            P.op("pool", lambda e, s=s: e.tensor_copy(out=CM[:], in_=s[:, 0:4, :]), reads=[s], writes=[CM])
            P.dma("sp", EL[:].rearrange("p a b c -> p a (b c)"), el_d.partition_broadcast(128), EL, writes=[EL])
            P.op("pool", lambda e: e.memset(VA[:, :, 64:65], 1.0), writes=[VA])
            P.op("pool", lambda e: e.memset(MBP[:], 0.0), writes=[MBP])

            def rope(src3, dst3, cs, nh, rd, full=False, sink=None):
                if full:
                    cosb, sinb = cs[:, :, 0:32], cs[:, :, 32:64]
                else:
                    cosb = cs[:, 0:32].rearrange("p (a c) -> p a c", a=1).to_broadcast([128, nh, 32])
                    sinb = cs[:, 32:64].rearrange("p (a c) -> p a c", a=1).to_broadcast([128, nh, 32])
                a_, b_ = ra[:, 0:nh, :], rb[:, 0:nh, :]
                x1, x2 = src3[:, :, 0:32], src3[:, :, 32:64]
                steps = [
                    lambda: P.op("dve", lambda e: e.tensor_tensor(out=a_, in0=x1, in1=cosb, op=ALU.mult), reads=rd, writes=[ra]),
                    lambda: P.op("dve", lambda e: e.tensor_tensor(out=b_, in0=x2, in1=sinb, op=ALU.mult), reads=rd, writes=[rb]),
                    lambda: P.op("dve", lambda e: e.tensor_tensor(out=dst3[:, :, 0:32], in0=a_, in1=b_, op=ALU.subtract),
                                 reads=[ra, rb], writes=rd[-1:]),
                    lambda: P.op("dve", lambda e: e.tensor_tensor(out=a_, in0=x2, in1=cosb, op=ALU.mult), reads=rd + [ra], writes=[ra]),
                    lambda: P.op("dve", lambda e: e.tensor_tensor(out=b_, in0=x1, in1=sinb, op=ALU.mult), reads=rd + [rb], writes=[rb]),
                    lambda: P.op("dve", lambda e: e.tensor_tensor(out=dst3[:, :, 32:64], in0=a_, in1=b_, op=ALU.add),
                                 reads=[ra, rb], writes=rd[-1:]),
                ]
                if sink is None:
                    for f in steps:
                        f()
                else:
                    sink.extend(steps)

            KTAb = [Buf("KTAb%d" % i, KTA.t) for i in range(16)]
            VAb = [Buf("VAb%d" % i, VA.t) for i in range(16)]
            QA2 = [QA, P.sb("QAb", [96, 4, 128], BF16, st)]
            szb2 = [szb, P.sb("szbb", [128, 256], F32, st)]
            npsc = [0]

            def common_ops(g, i):
                ops = []
                stt = {}

                def o(f):
                    ops.append(f)
                ca4 = csa4[i % 2]
                o(lambda: stt.__setitem__("xb", load_x(i)))
                o(lambda: P.dma("sp", ca4[:], cs_all[512 * i:512 * i + 512, :].rearrange("(c p) d -> p c d", p=128), ca4, writes=[ca4]))

                def fn_kv(e):
                    xb = stt["xb"]
                    ins = None
                    for k in range(4):
                        for kc in range(8):
                            ins = e.matmul(PT2[:, 128 * k:128 * k + 128], lhsT=xb[:, kc, 128 * k:128 * k + 128],
                                           rhs=Wkv[:, kc, :], start=(kc == 0), stop=(kc == 7))
                    return ins
                o(lambda: P.op("pe", fn_kv, reads=[stt["xb"], Wkv], writes=[PT2]))
                o(lambda: P.op("act", lambda e: e.copy(out=kv4[:].rearrange("p a c -> p (a c)"), in_=PT2[:, :]), reads=[PT2], writes=[kv4]))
                rope(kv4[:, :, 0:64], kr4[:], ca4, 4, [kv4, ca4, kr4], full=True, sink=ops)
                o(lambda: P.dma("sp", kp_o[512 * i:512 * i + 512, 64 * g:64 * g + 64].rearrange("(c p) d -> p c d", p=128), kr4[:], kr4,
                                reads=[kr4], is_out=True))
                o(lambda: P.dma("sp", vp_o[512 * i:512 * i + 512, 64 * g:64 * g + 64].rearrange("(c p) d -> p c d", p=128),
                                kv4[:, :, 64:128], kv4, reads=[kv4], is_out=True))

                def fn_kt(e):
                    ins = None
                    for k in range(4):
                        ins = tr(e, PKT4[:, 128 * k:128 * k + 128], kr4[:, k, :], idn[:])
                    return ins
                o(lambda: P.op("pe", fn_kt, reads=[kr4, idn], writes=[PKT4]))
                o(lambda: P.op("act", lambda e: e.copy(out=KTf4[:], in_=PKT4[:, :]), reads=[PKT4], writes=[KTf4]))
                o(lambda: P.op("dve", lambda e: e.tensor_copy(out=KTA[0:64, 512 * i:512 * i + 512], in_=KTf4[:]), reads=[KTf4], writes=[KTAb[i]]))
                o(lambda: P.op("pool", lambda e: e.tensor_copy(out=VA[:, 4 * i:4 * i + 4, 0:64], in_=kv4[:, :, 64:128]), reads=[kv4], writes=[VAb[i]]))
                o(lambda: P.op("dve", lambda e: e.tensor_reduce(out=KM[:, 2 * i:2 * i + 2], in_=KTf4[:].rearrange("p (b c) -> p b c", b=2),
                                                                axis=AX.X, op=ALU.add), reads=[KTf4], writes=[KM]))
                return ops

            def head_ops(g, i):
                ops = []

                def o(f):
                    ops.append(f)
                QAi = QA2[i % 2]
                szbi = szb2[i % 2]
                o(lambda: load_xo(i))
                o(lambda: P.dma("sp", cso[:], cs_own[128 * i:128 * i + 128, :], cso, writes=[cso]))
                o(lambda: P.op("pe", mm_chain(PQ[:, :], lambda kc: XOl[0][:, kc, 3:131], lambda kc: Wqz[:, kc, :]),
                               reads=[XOl[0], Wqz], writes=[PQ]))
                o(lambda: P.op("act", lambda e: e.copy(out=qf[:].rearrange("p a c -> p (a c)"), in_=PQ[:, 0:256]), reads=[PQ], writes=[qf]))
                o(lambda: P.op("act", lambda e: e.activation(out=szbi[:], in_=PQ[:, 256:512], func=AF.Silu), reads=[PQ], writes=[szbi]))
                rope(qf[:], qr[:], cso, 4, [qf, cso, qr], sink=ops)
                for h in range(4):
                    o(lambda h=h: P.op("pe", lambda e: tr(e, PKT[:, :], qr[:, h, :], idn[:]), reads=[qr, idn], writes=[PKT]))
                    o(lambda h=h: P.op("act", lambda e: e.copy(out=QTf[:, h, :], in_=PKT[:, :]), reads=[PKT], writes=[QTf]))
                    o(lambda h=h: P.op("dve", lambda e: e.tensor_copy(out=QAi[0:64, h, :], in_=QTf[:, h, :]), reads=[QTf], writes=[QAi]))

                def fn_sb(e):
                    ins = None
                    for h in range(4):
                        ins = e.matmul(PSB[:, 32 * h:32 * h + 32], lhsT=QTf[:, h, :], rhs=KM[:], start=True, stop=True)
                    return ins
                o(lambda: P.op("pe", fn_sb, reads=[QTf, KM], writes=[PSB]))
                e01 = EL[:, 0, i:i + 1, :].to_broadcast([128, 4, 32])
                eng_ = EL[:, 1, i:i + 1, :].to_broadcast([128, 4, 32])
                own = EL[:, 2, i:i + 1, :].to_broadcast([128, 4, 32])
                o(lambda: P.op("dve", lambda e: e.tensor_tensor(out=spd[:], in0=PSB[:, :].rearrange("p (a c) -> p a c", a=4),
                                                                in1=e01, op=ALU.mult), reads=[PSB, EL], writes=[spd]))
                o(lambda: P.op("dve", lambda e: e.tensor_tensor(out=spd[:], in0=spd[:], in1=eng_, op=ALU.add),
                               reads=[spd, EL], writes=[spd]))
                for h in range(4):
                    o(lambda h=h: P.op("dve", lambda e: e.max(out=mx[:, h, :], in_=spd[:, h, :]), reads=[spd], writes=[mx]))
                o(lambda: P.op("dve", lambda e: e.tensor_tensor(out=sel[:], in0=spd[:], in1=mx[:, :, 2:3].to_broadcast([128, 4, 32]),
                                                                op=ALU.is_ge), reads=[spd, mx], writes=[sel]))
                o(lambda: P.op("dve", lambda e: e.tensor_tensor(out=sel[:], in0=sel[:], in1=e01, op=ALU.mult),
                               reads=[sel, EL], writes=[sel]))
                o(lambda: P.op("dve", lambda e: e.tensor_tensor(out=sel[:], in0=sel[:], in1=own, op=ALU.add),
                               reads=[sel, EL], writes=[sel]))
                o(lambda: P.op("dve", lambda e: e.tensor_scalar(out=MBP[:, :, 64:96], in0=sel[:], scalar1=-1.0, scalar2=BIG,
                                                                op0=ALU.add, op1=ALU.mult), reads=[sel], writes=[MBP]))
                for h in range(4):
                    o(lambda h=h: P.op("pe", lambda e: e.matmul(PMB[:, :], lhsT=MBP[:, h, :], rhs=idn[:], start=True, stop=True),
                                       reads=[MBP, idn], writes=[PMB]))
                    o(lambda h=h: P.op("act", lambda e: e.copy(out=QAi[64:96, h, :], in_=PMB[64:96, :]), reads=[PMB], writes=[QAi]))
                return ops

            def tile_loop(g, i, fillers):
                QAi = QA2[i % 2]
                po = PO[i % 2]
                nkt = 4 * i + 4
                base = npsc[0]
                npsc[0] += nkt

                def emit_s(kt):
                    pS = PS[(base + kt) % 3]

                    def fn_s(e):
                        ins = e.matmul(pS[:, :], lhsT=KTA[0:96, 128 * kt:128 * kt + 128],
                                       rhs=QAi[:].rearrange("p a c -> p (a c)"), start=True, stop=(kt < 4 * i))
                        if kt >= 4 * i:
                            ins = e.matmul(pS[:, :], lhsT=idnb[:], rhs=CM[:, kt - 4 * i, :], start=False, stop=True)
                        return ins
                    P.op("pe", fn_s, reads=[KTA, KTAb[kt // 4], QAi, idnb, CM], writes=[pS])

                per = max(1, -(-len(fillers) // max(1, nkt - 2)))
                emit_s(0)
                for kt in range(nkt):
                    if kt + 1 < nkt:
                        emit_s(kt + 1)
                    pS = PS[(base + kt) % 3]
                    pT = PTb[(base + kt) % 3]
                    P.op("act", lambda e: e.activation(out=pT[:], in_=pS[:, :], func=AF.Exp, scale=SCALE), reads=[pS], writes=[pT])
                    P.op("pe", lambda e: e.matmul(po[:, :], lhsT=VA[:, kt, :], rhs=pT[:], start=(kt == 0), stop=(kt == nkt - 1)),
                         reads=[VA, VAb[kt // 4], pT] + ([po] if kt > 0 else []), writes=[po])
                    for _ in range(per):
                        if fillers:
                            fillers.pop(0)()
                while fillers:
                    fillers.pop(0)()
                szbi = szb2[i % 2]
                P.op("dve", lambda e: e.tensor_copy(out=Osb[:], in_=po[:, :]), reads=[po], writes=[Osb])

                def fn_ot(e):
                    ins = None
                    for h in range(4):
                        ins = tr(e, POT[:, 65 * h:65 * h + 65], Osb[:, 128 * h:128 * h + 128], idn[0:65, 0:65])
                    return ins
                P.op("pe", fn_ot, reads=[Osb, idn], writes=[POT])
                pot3 = POT[:, 0:260].rearrange("p (a c) -> p a c", a=4)
                P.op("dve", lambda e: e.reciprocal(out=rl[:], in_=pot3[:, :, 64]), reads=[POT], writes=[rl])
                P.op("dve", lambda e: e.tensor_tensor(out=ob[:], in0=pot3[:, :, 0:64],
                                                      in1=rl[:].rearrange("p (a c) -> p a c", c=1).to_broadcast([128, 4, 64]),
                                                      op=ALU.mult), reads=[POT, rl], writes=[ob])
                P.op("dve", lambda e: e.tensor_tensor(out=ob2[:], in0=ob[:].rearrange("p a c -> p (a c)"), in1=szbi[:],
                                                      op=ALU.mult), reads=[ob, szbi], writes=[ob2])

                def fn_obt(e):
                    tr(e, POBT[:, 0:128], ob2[:, 0:128], idn[:])
                    return tr(e, POBT[:, 128:256], ob2[:, 128:256], idn[:])
                P.op("pe", fn_obt, reads=[ob2, idn], writes=[POBT])
                P.op("act", lambda e: e.copy(out=OBT[:, 2 * g:2 * g + 2, 128 * i:128 * i + 128],
                                             in_=POBT[:, 0:256].rearrange("p (a c) -> p a c", a=2)), reads=[POBT], writes=[OBT])

            for g in range(NG if 'a' in PASSES else 0):
                load_w(Wkv, 0, w_in[:, C_K + 64 * g:C_K + 64 * g + 64], 64)
                load_w(Wkv, 64, w_in[:, C_V + 64 * g:C_V + 64 * g + 64], 64)
                load_w(Wqz, 0, w_in[:, C_Q + 256 * g:C_Q + 256 * g + 256], 256)
                load_w(Wqz, 256, w_in[:, C_ZB + 256 * g:C_ZB + 256 * g + 256], 256)
                P.op("pool", lambda e: e.memset(KM[:], 0.0), writes=[KM])
                for f in common_ops(g, 0) + head_ops(g, 0):
                    f()
                for i in range(NI):
                    nxt = (common_ops(g, i + 1) + head_ops(g, i + 1)) if i + 1 < NI else []
                    tile_loop(g, i, nxt)

        P.barrier()
        if DBG:
            P.dma("sp", dbg_ya, YAT[:].rearrange("p a c -> p (a c)"), YAT, reads=[YAT], is_out=True)
            P.dma("sp", dbg_ob, OBT[:].rearrange("p a c -> p (a c)"), OBT, reads=[OBT], is_out=True)
        with contextlib.ExitStack() as st:
            alloc_io(st, "c", nws=1, nxb=1)
            Wg = P.sb("Wg", [128, 8, 2048], BF16, st)
            Wa = P.sb("Wa", [128, 8, 1024], BF16, st)
            Wb = P.sb("Wb", [128, 8, 1024], BF16, st)
            Wo = P.sb("Wo", [128, 8, 1024], BF16, st)
            gbc = P.sb("gbc", [128, 1024], F32, st)
            bbc = P.sb("bbc", [128, 1024], F32, st)
            XM = XB[0]
            sga = P.sb("sga", [128, 512], F32, st)
            sgb = P.sb("sgb", [128, 512], F32, st)
            t1 = P.sb("t1", [128, 512], F32, st)
            mixT = P.sb("mixT", [128, 8, 512], BF16, st)
            xres = [P.sb("xres%d" % i, [128, 1024], F32, st) for i in range(2)]
            pre = P.sb("pre", [128, 1024], F32, st)
            stats = P.sb("stats", [128, 2, 6], F32, st)
            mv = P.sb("mv", [128, 2], F32, st)
            rs2 = P.sb("rs2", [128, 1], F32, st)
            PGa, PGb, PBa, PBb = [P.view("PG%d" % i, banks[i], banks[i][:, :]) for i in range(4)]
            POa = [P.view("POa%d" % i, banks[4 + i], banks[4 + i][:, :]) for i in range(2)]

            for j in range(4):
                load_w(Wg, 512 * j, w_in[:, C_GA + 512 * j:C_GA + 512 * j + 512], 512)
            for j in range(2):
                load_w(Wa, 512 * j, w_a[:, 512 * j:512 * j + 512], 512)
                load_w(Wb, 512 * j, w_b[:, 512 * j:512 * j + 512], 512)
                load_w(Wo, 512 * j, w_o[:, 512 * j:512 * j + 512], 512)
            P.dma("sp", gbc[:], lng.partition_broadcast(128).rearrange("p a c -> p (a c)"), gbc, writes=[gbc])
            P.dma("sp", bbc[:], lnb.partition_broadcast(128).rearrange("p a c -> p (a c)"), bbc, writes=[bbc])
            NJ = (NI + 3) // 4 if 'm' in PASSES else 0
            for j in range(NJ):
                s = WS[stg[0] % len(WS)]
                stg[0] += 1
                P.dma("sp", s[:], xmT[:, 512 * j:512 * j + 512].rearrange("(kc p) t -> p kc t", p=128), s, writes=[s])
                P.op("pool", lambda e, s=s: e.tensor_copy(out=XM[:], in_=s[:]), reads=[s], writes=[XM])
                ts = slice(512 * j, 512 * j + 512)
                for mc in range(8):
                    ms = slice(128 * mc, 128 * mc + 128)
                    P.op("pe", mm_chain(PGa[:, :], lambda kc, ms=ms: Wg[:, kc, ms], lambda kc: XM[:, kc, :]),
                         reads=[Wg, XM], writes=[PGa])
                    P.op("pe", mm_chain(PGb[:, :], lambda kc, mc=mc: Wg[:, kc, 1024 + 128 * mc:1024 + 128 * mc + 128],
                                        lambda kc: XM[:, kc, :]), reads=[Wg, XM], writes=[PGb])
                    P.op("pe", mm_chain(PBa[:, :], lambda kc, ms=ms: Wa[:, kc, ms], lambda kc, ts=ts: YAT[:, kc, ts]),
                         reads=[Wa, YAT], writes=[PBa])
                    P.op("pe", mm_chain(PBb[:, :], lambda kc, ms=ms: Wb[:, kc, ms], lambda kc, ts=ts: OBT[:, kc, ts]),
                         reads=[Wb, OBT], writes=[PBb])
                    P.op("act", lambda e: e.activation(out=sga[:], in_=PGa[:, :], func=AF.Sigmoid), reads=[PGa], writes=[sga])
                    P.op("act", lambda e: e.activation(out=sgb[:], in_=PGb[:, :], func=AF.Sigmoid), reads=[PGb], writes=[sgb])
                    P.op("dve", lambda e: e.tensor_tensor(out=t1[:], in0=sga[:], in1=PBa[:, :], op=ALU.mult),
                         reads=[sga, PBa], writes=[t1])
                    P.op("dve", lambda e: e.tensor_tensor(out=sgb[:], in0=sgb[:], in1=PBb[:, :], op=ALU.mult),
                         reads=[sgb, PBb], writes=[sgb])
                    P.op("pool", lambda e, mc=mc: e.tensor_tensor(out=mixT[:, mc, :], in0=t1[:], in1=sgb[:], op=ALU.add),
                         reads=[t1, sgb], writes=[mixT])
                for k in range(4):
                    c = 4 * j + k
                    if c >= NI:
                        break
                    xr = xres[c % 2]
                    y_ = pre
                    P.dma("sp", xr[:], xown[128 * c:128 * c + 128, :], xr, writes=[xr])
                    for half in range(2):
                        po = POa[half]
                        P.op("pe", mm_chain(po[:, :], lambda kc, k=k: mixT[:, kc, 128 * k:128 * k + 128],
                                            lambda kc, half=half: Wo[:, kc, 512 * half:512 * half + 512]),
                             reads=[mixT, Wo], writes=[po])
                        P.op("dve", lambda e, half=half, po=po, xr=xr: e.scalar_tensor_tensor(
                            out=pre[:, 512 * half:512 * half + 512], in0=xr[:, 512 * half:512 * half + 512], scalar=ALPHA,
                            in1=po[:, :], op0=ALU.mult, op1=ALU.add), reads=[xr, po], writes=[pre])
                    for half in range(2):
                        P.op("dve", lambda e, half=half: e.bn_stats(out=stats[:, half, :], in_=pre[:, 512 * half:512 * half + 512]),
                             reads=[pre], writes=[stats])
                    P.op("dve", lambda e: e.bn_aggr(out=mv[:], in_=stats[:].rearrange("p a c -> p (a c)")), reads=[stats], writes=[mv])
                    P.op("act", lambda e: e.activation(out=rs2[:], in_=mv[:, 1:2], func=AF.Sqrt, bias=epsb[:, 0:1]),
                         reads=[mv, epsb], writes=[rs2])
                    P.op("dve", lambda e: e.reciprocal(out=rs2[:], in_=rs2[:]), reads=[rs2], writes=[rs2])
                    P.op("dve", lambda e, y_=y_: e.tensor_scalar(out=y_[:], in0=pre[:], scalar1=mv[:, 0:1], scalar2=rs2[:, 0:1],
                                                                 op0=ALU.subtract, op1=ALU.mult), reads=[pre, mv, rs2], writes=[y_])
                    P.op("pool", lambda e, y_=y_: e.tensor_tensor(out=y_[:], in0=y_[:], in1=gbc[:], op=ALU.mult),
                         reads=[y_, gbc], writes=[y_])
                    P.op("pool", lambda e, y_=y_: e.tensor_tensor(out=y_[:], in0=y_[:], in1=bbc[:], op=ALU.add),
                         reads=[y_, bbc], writes=[y_])
                    P.dma("sp", y_own[128 * c:128 * c + 128, :], y_[:], y_, reads=[y_], is_out=True)

        P.emit()
    return nc


def _host_inputs(x_prompt, w_in, conv_w, conv_b, dt_bias, a_log, d_skip, ssm_norm_w, w_a_out, w_b_out, w_out,
                 ln_g, ln_b, x_sample, cache_k, cache_v, state_conv, state_ssm, page_table):
    f32 = np.float32
    pos = np.arange(SEQ, dtype=np.float32)
    inv = np.power(np.float32(10000.0), -np.arange(32, dtype=np.float32) * np.float32(2.0) / np.float32(64.0)).astype(f32)
    ang = (pos[:, None] * inv[None, :]).astype(f32)
    cs_all = np.concatenate([np.cos(ang), np.sin(ang)], axis=1).astype(f32)
    tri = np.triu(np.ones((128, 128), f32))
    idn = np.eye(128, dtype=f32)
    kbi = np.zeros((32, SEQ), f32)
    for b in range(32):
        kbi[b, 256 * b:256 * b + 256] = 1.0
    common = {
        "w_in": np.ascontiguousarray(w_in[0]), "w_a": np.ascontiguousarray(w_a_out[0]),
        "w_b": np.ascontiguousarray(w_b_out[0]), "w_o": np.ascontiguousarray(w_out[0]),
        "cwT": np.ascontiguousarray(conv_w[0].T), "cbT": np.ascontiguousarray(conv_b[0][:, None]),
        "vec16": np.stack([dt_bias[0], a_log[0], d_skip[0]]).astype(f32),
        "nwv": np.ascontiguousarray(ssm_norm_w), "lng": np.ascontiguousarray(ln_g), "lnb": np.ascontiguousarray(ln_b),
        "tri": tri, "idn": idn, "cs_all": cs_all, "kbi": kbi,
        "cwrow": np.ascontiguousarray(conv_w[0]), "cbrow": np.ascontiguousarray(conv_b[0][None, :]),
        "ck_d": cache_k[0].reshape(2560 * 128, 256), "cv_d": cache_v[0].reshape(2560 * 128, 256),
        "iota_d": np.arange(128, dtype=f32)[:, None],
    }
    sel = np.zeros((NS, NS, 128), f32)
    selt = np.zeros((128, NS, NS), f32)
    for n in range(NS):
        sel[n, n, :] = 1.0
        selt[:, n, n] = 1.0
    angs = (np.float32(PAST) * inv).astype(f32)
    common["sel_d"] = sel.reshape(NS, NS * 128)
    common["selt_d"] = selt.reshape(128, NS * NS)
    common["css_d"] = np.concatenate([np.cos(angs), np.sin(angs)])[None, :].astype(f32)
    maps = []
    for c in range(NCORE):
        s, r = c // 4, c % 4
        xs = x_prompt[s]
        xT = np.ascontiguousarray(xs.T)
        own_tok = np.concatenate([np.arange(128 * (4 * i + r), 128 * (4 * i + r) + 128) for i in range(16)])
        xo = np.zeros((16, 131, D), f32)
        for i in range(16):
            t0 = 128 * (4 * i + r)
            lo = max(t0 - 3, 0)
            xo[i, 131 - (t0 + 128 - lo):] = xs[lo:t0 + 128]
        xoT = np.ascontiguousarray(xo.reshape(16 * 131, D).T)
        xown = np.ascontiguousarray(xs[own_tok])
        oh = np.zeros((128, 4), f32)
        oh[:, r] = 1.0
        cm = np.zeros((128, 4, 4, 128), f32)
        diag = np.where(np.arange(128)[:, None] <= np.arange(128)[None, :], 0.0, -BIG).astype(f32)
        for k in range(4):
            if k > r:
                cm[:, k] = -BIG
            elif k == r:
                cm[:, k] = diag[:, None, :]
        el = np.zeros((3, 16, 32), f32)
        for i in range(16):
            qblk = (4 * i + r) // 2
            el[0, i, :qblk] = 1.0
            el[1, i, qblk:] = -1e30
            el[2, i, qblk] = 1.0
        m = dict(common)
        ts = slice(NS * c, NS * c + NS)
        m.update({"xsT": np.ascontiguousarray(x_sample[ts, 0, :].T), "xs_tok": np.ascontiguousarray(x_sample[ts, 0, :]),
                  "sc_d": np.ascontiguousarray(state_conv[0, ts].reshape(NS, 3 * 2048)),
                  "ssm_d": np.ascontiguousarray(state_ssm[0, ts].reshape(NS, 1024 * 128)),
                  "pt_d": np.ascontiguousarray(page_table[ts].reshape(1, NS * NPG)).astype(np.int32)})
        m.update({"xT": xT, "xoT": xoT, "xmT": np.ascontiguousarray(xown.T), "xown": xown, "oh": oh,
                  "cs_own": np.ascontiguousarray(cs_all[own_tok]), "cm": cm.reshape(128, 2048),
                  "el": el.reshape(3, 512)})
        maps.append(m)
    return maps


_NC_CACHE = {}


def kernel(x_prompt, x_sample, cache_k, cache_v, state_conv, state_ssm, page_table,
           w_in, conv_w, conv_b, dt_bias, a_log, d_skip, ssm_norm_w, w_a_out, w_b_out, w_out, ln_g, ln_b):
    NI = int(os.environ.get("MK_NI", "16"))
    args = [np.asarray(a) for a in (x_prompt, w_in, conv_w, conv_b, dt_bias, a_log, d_skip, ssm_norm_w,
                                    w_a_out, w_b_out, w_out, ln_g, ln_b, x_sample, cache_k, cache_v,
                                    state_conv, state_ssm, page_table)]
    maps = _host_inputs(*args)
    key = (NI, os.environ.get("MK_NOSAMPLE"), os.environ.get("MK_PASSES"))
    if key not in _NC_CACHE:
        _NC_CACHE[key] = build(NI=NI)
    nc = _NC_CACHE[key]
    nosample = bool(os.environ.get("MK_NOSAMPLE"))
    if nosample:
        for m in maps:
            m.pop("ck_d")
            m.pop("cv_d")
    res = run_bass_kernel_spmd(nc, maps, core_ids=list(range(NCORE))).results
    if os.environ.get("MK_DBG"):
        global _DBG_RES
        _DBG_RES = res
    y_prompt = np.zeros((2, SEQ, D), np.float32)
    for c in range(NCORE):
        s, r = c // 4, c % 4
        yo = res[c]["y_own"].reshape(16, 128, D)
        y_prompt[s].reshape(16, 4, 128, D)[:, r] = yo
    k_prompt = np.stack([res[0]["kp_o"], res[4]["kp_o"]]).reshape(1, 2, SEQ, 4, 64)
    v_prompt = np.stack([res[0]["vp_o"], res[4]["vp_o"]]).reshape(1, 2, SEQ, 4, 64)
    conv_prompt = np.zeros((1, 2, 3, 2048), np.float32)
    ssm_prompt = np.zeros((1, 2, 16, 64, 128), np.float32)
    for s in range(2):
        rr = res[4 * s + 3]
        cva = rr["cvA_o"]
        cvc = rr["cvC_o"]
        for g in range(4):
            conv_prompt[0, s, :, 256 * g:256 * g + 128] = cva[g, :, 0, :].T
            conv_prompt[0, s, :, 256 * g + 128:256 * g + 256] = cva[g, :, 1, :].T
            conv_prompt[0, s, :, 1024 + 128 * g:1024 + 128 * g + 128] = cva[g, :, 2, :].T
            conv_prompt[0, s, :, 1536 + 128 * g:1536 + 128 * g + 128] = cvc[g].T
            hs = rr["ssm_o"][g].reshape(128, 4, 64)
            ssm_prompt[0, s, 4 * g:4 * g + 4] = hs.transpose(1, 2, 0)
    if nosample:
        return y_prompt, k_prompt, v_prompt, conv_prompt, ssm_prompt
    y_sample = np.concatenate([res[c]["ys_o"] for c in range(NCORE)]).reshape(128, 1, D)
    k_sample = np.concatenate([res[c]["ks_o"] for c in range(NCORE)]).reshape(1, 128, 1, 4, 64)
    v_sample = np.concatenate([res[c]["vs_o"] for c in range(NCORE)]).reshape(1, 128, 1, 4, 64)
    conv_sample = np.concatenate([res[c]["cvs_o"] for c in range(NCORE)]).reshape(1, 128, 3, 2048)
    ssm_sample = np.concatenate([res[c]["ssms_o"] for c in range(NCORE)]).reshape(1, 128, 16, 64, 128)
    return (y_prompt, y_sample, k_prompt, v_prompt, k_sample, v_sample, conv_prompt, conv_sample,
            ssm_prompt, ssm_sample)
```

```python
import contextlib
import os

import numpy as np

import concourse.bass as bass
import concourse.mybir as mybir
from concourse.bass_utils import run_bass_kernel_spmd

F32 = mybir.dt.float32
BF16 = mybir.dt.bfloat16
I32 = mybir.dt.int32
AF = mybir.ActivationFunctionType
ALU = mybir.AluOpType
AX = mybir.AxisListType

D = 1024
SEQ = 8192
NCORE = 8
DIN = 7696
C_ZA, C_XA, C_B, C_C, C_DT, C_Q, C_K, C_V, C_ZB, C_GA, C_GB = (
    0, 1024, 2048, 2560, 3072, 3088, 4112, 4368, 4624, 5648, 6672)
EPS = 1e-5
ALPHA = 2.0 ** 0.25
BIG = 30000.0
SCALE = 0.125
NS = 16
PAST = 2048
NPG = 16


class Buf:
    def __init__(self, name, t):
        self.name = name
        self.t = t
        self.w = None
        self.r = []
        self.dsem = None
        self.dcnt = 0

    def __getitem__(self, k):
        return self.t[k]


class View:
    def __init__(self, name, parent, t):
        self.name = name
        self.parent = parent
        self.t = t

    def __getitem__(self, k):
        return self.t[k]

    @property
    def w(self):
        return self.parent.w

    @w.setter
    def w(self, v):
        self.parent.w = v

    @property
    def r(self):
        return self.parent.r

    @r.setter
    def r(self, v):
        self.parent.r = v


class Prog:
    ENG = ("sp", "pe", "act", "dve", "pool")

    def __init__(self, nc, stack):
        self.nc = nc
        self.stack = stack
        self.q = {e: [] for e in self.ENG}
        self.cnt = {e: 0 for e in self.ENG}
        self.seen = {e: {} for e in self.ENG}
        self.esem = {}
        for e in ("pe", "act", "dve", "pool"):
            self.esem[e] = stack.enter_context(nc.semaphore("es_" + e))
        self.nsem = 4
        self.out_tokens = []
        self.dbufs = []
        self.n = 0
        self.stop = int(os.environ.get("MK_STOP", "1000000000"))
        self.verbose = bool(os.environ.get("MK_VERBOSE"))

    def _skip(self, what):
        self.n += 1
        if self.verbose:
            import traceback
            fr = traceback.extract_stack()[-3]
            print("OP %d %s line %d" % (self.n, what, fr.lineno))
        return self.n > self.stop

    def sb(self, name, shape, dt, stack=None):
        t = (stack or self.stack).enter_context(self.nc.sbuf_tensor("s_" + name, list(shape), dt))
        return Buf(name, t)

    def ps(self, name, stack=None):
        t = (stack or self.stack).enter_context(self.nc.psum_tensor("p_" + name, [128, 512], F32))
        return Buf(name, t)

    def view(self, name, bank, ap):
        return View(name, bank, ap)

    def _dsem(self, b):
        if b.dsem is None:
            b.dsem = self.stack.enter_context(self.nc.semaphore("ds%d" % self.nsem))
            self.nsem += 1
            self.dbufs.append(b)
        return b.dsem

    def barrier(self):
        if self.n > self.stop:
            return
        toks = [(self.esem[e], self.cnt[e]) for e in ("pe", "act", "dve", "pool") if self.cnt[e] > 0]
        toks += [(b.dsem, b.dcnt) for b in self.dbufs if b.dcnt > 0]
        for eng in self.ENG:
            seen = self.seen[eng]
            waits = []
            for sem, val in toks:
                if seen.get(sem, 0) < val:
                    seen[sem] = val
                    waits.append((sem, val))
            self._wait(eng, waits)

    def _deps(self, eng, reads, writes):
        toks = []
        for b in reads:
            if b.w is not None:
                toks.append(b.w)
        for b in writes:
            if b.w is not None:
                toks.append(b.w)
            toks.extend(b.r)
        need = {}
        for sem, val in toks:
            if need.get(sem, 0) < val:
                need[sem] = val
        out = []
        seen = self.seen[eng]
        for sem, val in need.items():
            if seen.get(sem, 0) < val:
                seen[sem] = val
                out.append((sem, val))
        return out

    def _commit(self, tok, reads, writes):
        for b in reads:
            b.r.append(tok)
        for b in writes:
            b.w = tok
            b.r = []

    def _eng(self, eng):
        nc = self.nc
        return {"sp": nc.sync, "pe": nc.tensor, "act": nc.scalar, "dve": nc.vector, "pool": nc.gpsimd}[eng]

    def _wait(self, eng, waits):
        e = self._eng(eng)
        for sem, val in waits:
            if eng == "pe" and sem is self.esem["pe"]:
                continue
            e.wait_ge(sem, val)

    def op(self, eng, fn, reads=(), writes=()):
        if self._skip(eng):
            return None
        waits = self._deps(eng, reads, writes)
        self._wait(eng, waits)
        ins = fn(self._eng(eng))
        ins.then_inc(self.esem[eng], 1)
        if self.verbose and self.n in (62, 69):
            print("INS", self.n, str(ins))
        self.cnt[eng] += 1
        tok = (self.esem[eng], self.cnt[eng])
        self._commit(tok, reads, writes)
        return tok

    def dma(self, eng, out, in_, buf, reads=(), writes=(), is_out=False, **kw):
        if self._skip("dma"):
            return None
        waits = self._deps(eng, reads, writes)
        self._wait(eng, waits)
        sem = self._dsem(buf)
        buf.dcnt += 16
        tok = (sem, buf.dcnt)
        self._eng(eng).dma_start(out=out, in_=in_, **kw).then_inc(sem, 16)
        self._commit(tok, reads, writes)
        if is_out:
            self.out_tokens.append(tok)
        return tok

    def idma(self, out, in_, idx_ap, buf, reads=(), writes=(), bound=None):
        waits = self._deps("pool", reads, writes)
        self._wait("pool", waits)
        sem = self._dsem(buf)
        buf.dcnt += 16
        tok = (sem, buf.dcnt)
        self.nc.gpsimd.indirect_dma_start(
            out=out, out_offset=None, in_=in_,
            in_offset=bass.IndirectOffsetOnAxis(ap=idx_ap, axis=0),
            bounds_check=bound, oob_is_err=False).then_inc(sem, 16)
        self._commit(tok, reads, writes)
        return tok

    def emit(self):
        need = {}
        for sem, val in self.out_tokens:
            if need.get(sem, 0) < val:
                need[sem] = val
        for sem, val in need.items():
            self.nc.sync.wait_ge(sem, val)


def tr(e, out, in_, ident):
    return e.matmul(out, lhsT=in_, rhs=ident, start=True, stop=True)


def _bc(ap, shape):
    return ap.to_broadcast(list(shape))


def build(NI=16, SAMPLE=True):
    PASSES = os.environ.get('MK_PASSES', 'sam')
    SAMPLE = SAMPLE and not os.environ.get('MK_NOSAMPLE')
    NG = int(os.environ.get('MK_NG', '4'))
    nc = bass.Bass("TRN2", target_bir_lowering=False)
    dt = nc.dram_tensor
    xT = dt("xT", [D, SEQ], F32, kind="ExternalInput").ap()
    xoT = dt("xoT", [D, 16 * 131], F32, kind="ExternalInput").ap()
    xmT = dt("xmT", [D, 2048], F32, kind="ExternalInput").ap()
    xown = dt("xown", [2048, D], F32, kind="ExternalInput").ap()
    w_in = dt("w_in", [D, DIN], F32, kind="ExternalInput").ap()
    w_a = dt("w_a", [D, D], F32, kind="ExternalInput").ap()
    w_b = dt("w_b", [D, D], F32, kind="ExternalInput").ap()
    w_o = dt("w_o", [D, D], F32, kind="ExternalInput").ap()
    cwT = dt("cwT", [2048, 4], F32, kind="ExternalInput").ap()
    cbT = dt("cbT", [2048, 1], F32, kind="ExternalInput").ap()
    vec16 = dt("vec16", [3, 16], F32, kind="ExternalInput").ap()
    nwv = dt("nwv", [1, D], F32, kind="ExternalInput").ap()
    lng = dt("lng", [1, D], F32, kind="ExternalInput").ap()
    lnb = dt("lnb", [1, D], F32, kind="ExternalInput").ap()
    tri_d = dt("tri", [128, 128], F32, kind="ExternalInput").ap()
    idn_d = dt("idn", [128, 128], F32, kind="ExternalInput").ap()
    oh_d = dt("oh", [128, 4], F32, kind="ExternalInput").ap()
    cs_all = dt("cs_all", [SEQ, 64], F32, kind="ExternalInput").ap()
    cs_own = dt("cs_own", [2048, 64], F32, kind="ExternalInput").ap()
    kbi_d = dt("kbi", [32, SEQ], F32, kind="ExternalInput").ap()
    cm_d = dt("cm", [128, 4 * 512], F32, kind="ExternalInput").ap()
    el_d = dt("el", [3, 16 * 32], F32, kind="ExternalInput").ap()
    xsT = dt("xsT", [D, NS], F32, kind="ExternalInput").ap()
    xs_tok = dt("xs_tok", [NS, D], F32, kind="ExternalInput").ap()
    sc_d = dt("sc_d", [NS, 3 * 2048], F32, kind="ExternalInput").ap()
    ssm_d = dt("ssm_d", [NS, 1024 * 128], F32, kind="ExternalInput").ap()
    pt_d = dt("pt_d", [1, NS * NPG], I32, kind="ExternalInput").ap()
    iota_d = dt("iota_d", [128, 1], F32, kind="ExternalInput").ap()
    if SAMPLE:
        ck_d = dt("ck_d", [2560 * 128, 256], F32, kind="ExternalInput").ap()
        cv_d = dt("cv_d", [2560 * 128, 256], F32, kind="ExternalInput").ap()
    cwrow = dt("cwrow", [4, 2048], F32, kind="ExternalInput").ap()
    cbrow = dt("cbrow", [1, 2048], F32, kind="ExternalInput").ap()
    sel_d = dt("sel_d", [NS, NS * 128], F32, kind="ExternalInput").ap()
    selt_d = dt("selt_d", [128, NS * NS], F32, kind="ExternalInput").ap()
    css_d = dt("css_d", [1, 64], F32, kind="ExternalInput").ap()
    ys_o = dt("ys_o", [NS, D], F32, kind="ExternalOutput").ap()
    ks_o = dt("ks_o", [NS, 256], F32, kind="ExternalOutput").ap()
    vs_o = dt("vs_o", [NS, 256], F32, kind="ExternalOutput").ap()
    cvs_o = dt("cvs_o", [NS, 3 * 2048], F32, kind="ExternalOutput").ap()
    ssms_o = dt("ssms_o", [NS, 1024 * 128], F32, kind="ExternalOutput").ap()
    y_own = dt("y_own", [2048, D], F32, kind="ExternalOutput").ap()
    kp_o = dt("kp_o", [SEQ, 256], F32, kind="ExternalOutput").ap()
    vp_o = dt("vp_o", [SEQ, 256], F32, kind="ExternalOutput").ap()
    cvA_o = dt("cvA_o", [4, 128, 3, 3], F32, kind="ExternalOutput").ap()
    cvC_o = dt("cvC_o", [4, 128, 3], F32, kind="ExternalOutput").ap()
    ssm_o = dt("ssm_o", [4, 128, 256], F32, kind="ExternalOutput").ap()

    DBG = bool(os.environ.get("MK_DBG"))
    if DBG:
        dbg_ya = dt("dbg_ya", [128, 8 * 2048], BF16, kind="ExternalOutput").ap()
        dbg_ob = dt("dbg_ob", [128, 8 * 2048], BF16, kind="ExternalOutput").ap()
    with contextlib.ExitStack() as top:
        P = Prog(nc, top)
        tri = P.sb("tri", [128, 128], F32)
        idn = P.sb("idnf", [128, 128], F32)
        idnb = P.sb("idnb", [128, 128], BF16)
        onesf = P.sb("onesf", [128, 128], F32)
        oh = P.sb("oh", [128, 4], F32)
        WS, XB, XOl = [], [], []

        def alloc_io(st, tag, nws=2, nxb=2):
            WS[:] = [P.sb("WS%s%d" % (tag, i), [128, 8, 512], F32, st) for i in range(nws)]
            XB[:] = [P.sb("XB%s%d" % (tag, i), [128, 8, 512], BF16, st) for i in range(nxb)]
            XOl[:] = [P.sb("XO%s" % tag, [128, 8, 131], BF16, st)] if nxb else []
        banks = [P.ps("bank%d" % i) for i in range(8)]

        P.dma("sp", tri[:], tri_d, tri, writes=[tri])
        P.dma("sp", idn[:], idn_d, idn, writes=[idn])
        P.dma("sp", oh[:], oh_d, oh, writes=[oh])
        P.op("pool", lambda e: e.tensor_copy(out=idnb[:], in_=idn[:]), reads=[idn], writes=[idnb])
        P.op("pool", lambda e: e.memset(onesf[:], 1.0), writes=[onesf])
        epsb = P.sb("epsb", [128, 1], F32)
        P.op("pool", lambda e: e.memset(epsb[:], EPS), writes=[epsb])

        stg = [0]

        def load_w(dst, off, src_ap, n):
            s = WS[stg[0] % len(WS)]
            stg[0] += 1
            P.dma("sp", s[:, :, 0:n], src_ap.rearrange("(kc p) c -> p kc c", p=128), s, writes=[s])
            P.op("pool", lambda e: e.tensor_copy(out=dst[:, :, off:off + n], in_=s[:, :, 0:n]),
                 reads=[s], writes=[dst])

        def load_x(i):
            s = WS[stg[0] % len(WS)]
            xb = XB[stg[0] % len(XB)]
            stg[0] += 1
            P.dma("sp", s[:], xT[:, i * 512:(i + 1) * 512].rearrange("(kc p) t -> p kc t", p=128), s, writes=[s])
            P.op("pool", lambda e: e.tensor_copy(out=xb[:], in_=s[:]), reads=[s], writes=[xb])
            return xb

        def load_xo(i):
            s = WS[stg[0] % len(WS)]
            stg[0] += 1
            P.dma("sp", s[:, :, 0:131], xoT[:, i * 131:(i + 1) * 131].rearrange("(kc p) t -> p kc t", p=128),
                  s, writes=[s])
            XO = XOl[0]
            P.op("pool", lambda e: e.tensor_copy(out=XO[:], in_=s[:, :, 0:131]), reads=[s], writes=[XO])

        def mm_chain(ps_ap, lhs_fn, rhs_fn, nk=8):
            def fn(e):
                ins = None
                for kc in range(nk):
                    ins = e.matmul(ps_ap, lhsT=lhs_fn(kc), rhs=rhs_fn(kc), start=(kc == 0), stop=(kc == nk - 1))
                return ins
            return fn


        if SAMPLE:
            with contextlib.ExitStack() as st:
                alloc_io(st, "s", nws=1, nxb=0)
                XST = P.sb("XST", [128, 8, NS], F32, st)
                Us = P.sb("Us", [NS, DIN], F32, st)
                BIGS = P.sb("BIGS", [128, 6144], F32, st)
                SC = View("SC", BIGS, BIGS[0:NS, :].rearrange("p (a c) -> p a c", a=3))
                CWk = P.sb("CWk", [NS, 2048], F32, st)
                cacs = P.sb("cacs", [NS, 2048], F32, st)
                ctmp = P.sb("ctmp", [NS, 2048], F32, st)
                xbc = P.sb("xbc", [NS, 2048], F32, st)
                v16s = P.sb("v16s", [NS, 3, 16], F32, st)
                As = P.sb("As", [NS, 16], F32, st)
                dts = P.sb("dts", [NS, 16], F32, st)
                dte = P.sb("dte", [NS, 16], F32, st)
                dec = P.sb("dec", [NS, 16], F32, st)
                xdt_t = P.sb("xdt_t", [NS, 1024], F32, st)
                dec_t = P.sb("dec_t", [NS, 1024], F32, st)
                mix_t = xdt_t
                pre_s = dec_t
                XDTT = P.sb("XDTT", [128, 8, NS], F32, st)
                DECT = P.sb("DECT", [128, 8, NS], F32, st)
                YTT = P.sb("YTT", [128, 8, NS], F32, st)
                SEL = P.sb("SEL", [NS, NS, 128], F32, st)
                SELT = P.sb("SELT", [128, NS, NS], F32, st)
                ysum = P.sb("ysum", [128, 8], F32, st)
                y_t = P.sb("y_t", [NS, 1024], F32, st)
                szs = P.sb("szs", [NS, 1024], F32, st)
                g_t = P.sb("g_t", [NS, 1024], F32, st)
                gq_t = P.sb("gq_t", [NS, 1024], F32, st)
                ss4 = P.sb("ss4", [NS, 4], F32, st)
                ya_t = P.sb("ya_t", [NS, 1024], F32, st)
                css = P.sb("css", [NS, 64], F32, st)
                qk = P.sb("qk", [NS, 20, 64], F32, st)
                ra_s = P.sb("ra_s", [NS, 20, 32], F32, st)
                rb_s = P.sb("rb_s", [NS, 20, 32], F32, st)
                PTI = P.sb("PTI", [128, NS * NPG], I32, st)
                PTF = P.sb("PTF", [128, NS * NPG], F32, st)
                IDX = P.sb("IDX", [128, NS * NPG], I32, st)
                iop = P.sb("iop", [128, 1], F32, st)
                KV = [P.sb("KVt%d" % i, [128, NPG, 256], F32, st) for i in range(1)]
                nws = View("nws", KV[0], KV[0][0:NS, 0:4, :].rearrange("p a c -> p (a c)"))
                prod = P.sb("prod", [128, NPG, 4, 64], F32, st)
                pfl = prod[:].rearrange("p j a d -> p (j a d)")
                Hs = [View("Hs0", prod, pfl[:, 0:1024].rearrange("p (c s) -> p c s", c=8))]
                ht1 = View("ht1", prod, pfl[:, 1024:2048].rearrange("p (c s) -> p c s", c=8))
                ht2 = View("ht2", prod, pfl[:, 2048:3072].rearrange("p (c s) -> p c s", c=8))
                gbs = View("gbs", prod, pfl[0:NS, 0:1024])
                bbs = View("bbs", prod, pfl[0:NS, 1024:2048])
                xrs = View("xrs", prod, pfl[0:NS, 2048:3072])
                S_all = View("S_all", BIGS, BIGS[:, 0:NS * NPG * 16].rearrange("p (n c) -> p n c", n=NS))
                csum = P.sb("csum", [NS, NPG, 16], F32, st)
                sblk = P.sb("sblk", [NS, 16, 8], F32, st)
                mx8 = P.sb("mx8", [NS, 16, 8], F32, st)
                sl8 = P.sb("sl8", [NS, 16, 8], F32, st)
                MB = P.sb("MB", [NS, NPG, 16], F32, st)
                stmp = P.sb("stmp", [128, NPG * 16], F32, st)
                Pm = P.sb("Pm", [128, NPG, 16], F32, st)
                OT = P.sb("OT", [64, 16, NS], F32, st)
                LT = P.sb("LT", [16, NS], F32, st)
                o_t = View("o_t", y_t, y_t[:].rearrange("p (h d) -> p h d", h=16))
                l_t = P.sb("l_t", [NS, 16], F32, st)
                sown = P.sb("sown", [NS, 16], F32, st)
                sprod = View("sprod", gq_t, gq_t[:].rearrange("p (h d) -> p h d", h=16))
                ob_t = g_t
                TT = P.sb("TT", [128, 8, NS], F32, st)
                sg = cacs
                br = ctmp
                sts = P.sb("sts", [NS, 2, 6], F32, st)
                mvs = P.sb("mvs", [NS, 2], F32, st)
                rss = P.sb("rss", [NS, 1], F32, st)

                PU = [P.view("PU%d" % i, banks[i], banks[i][0:NS, :]) for i in range(2)]
                PTr = P.view("PTr", banks[2], banks[2][:, 0:8 * NS])
                PBC = P.view("PBC", banks[3], banks[3][:, :])
                PCC = P.view("PCC", banks[4], banks[4][:, :])
                PYT = [P.view("PYTs%d" % i, banks[5 + i], banks[5 + i][0:NS, :]) for i in range(2)]
                PQB = [P.view("PQB%d" % i, banks[5 + i], banks[5 + i][:, :]) for i in range(2)]
                PCS = P.view("PCS", banks[7], banks[7][0:NS, 0:256])
                PMBc = P.view("PMBc", banks[3], banks[3][:, 0:256])
                POs = P.view("POs", banks[4], banks[4][0:64, 0:16])
                PLs = P.view("PLs", banks[4], banks[4][0:16, 32:33])
                POt = [P.view("POt%d" % i, banks[i], banks[i][0:NS, :]) for i in range(2)]
                PLt = P.view("PLt", banks[2], banks[2][0:NS, 256:272])

                def bc3(ap2, a, b):
                    return ap2.rearrange("p (a c) -> p a c", c=1).to_broadcast([ap2.shape[0], a, b])

                def transp(src_tok, dstT):
                    def fn(e):
                        ins = None
                        for c in range(8):
                            ins = e.matmul(PTr[:, NS * c:NS * c + NS], lhsT=src_tok[:, 128 * c:128 * c + 128],
                                           rhs=idn[0:NS, 0:NS], start=True, stop=True)
                        return ins
                    P.op("pe", fn, reads=[src_tok, idn], writes=[PTr])
                    P.op("act", lambda e: e.copy(out=dstT[:].rearrange("p a c -> p (a c)"), in_=PTr[:, :]),
                         reads=[PTr], writes=[dstT])

                if os.environ.get("MK_VERBOSE"):
                    print("SBUF remaining in sample pass", nc.sbuf_bytes_remaining)
                if os.environ.get("MK_TIND"):
                    for bnd in (None, 100):
                        for oo in (KV[0][:, 0, :], prod[:, 0, :, :].rearrange("p a d -> p (a d)")):
                            for ii in (IDX[:, 0:1], PTI[:, 0:1]):
                                try:
                                    nc.gpsimd.indirect_dma_start(out=oo, out_offset=None, in_=ck_d,
                                                                 in_offset=bass.IndirectOffsetOnAxis(ap=ii, axis=0),
                                                                 bounds_check=bnd, oob_is_err=False)
                                    print("TIND ok", bnd)
                                except Exception as ex:
                                    print("TIND err", bnd, str(ex)[:60])
                P.dma("sp", XST[:], xsT.rearrange("(kc p) t -> p kc t", p=128), XST, writes=[XST])
                P.dma("sp", SC[:].rearrange("p a c -> p (a c)"), sc_d, BIGS, writes=[SC])
                P.dma("sp", v16s[:], vec16.partition_broadcast(NS), v16s, writes=[v16s])
                P.dma("sp", nws[:], nwv.partition_broadcast(NS).rearrange("p a c -> p (a c)"), KV[0], writes=[nws])
                P.dma("sp", css[:], css_d.partition_broadcast(NS).rearrange("p a c -> p (a c)"), css, writes=[css])
                P.dma("sp", SEL[:].rearrange("p a c -> p (a c)"), sel_d, SEL, writes=[SEL])
                P.dma("sp", SELT[:].rearrange("p a c -> p (a c)"), selt_d, SELT, writes=[SELT])
                P.dma("sp", PTI[:], pt_d.partition_broadcast(128).rearrange("p a c -> p (a c)"), PTI, writes=[PTI])
                P.dma("sp", iop[:], iota_d, iop, writes=[iop])
                P.op("dve", lambda e: e.tensor_copy(out=PTF[:], in_=PTI[:]), reads=[PTI], writes=[PTF])
                P.op("dve", lambda e: e.tensor_scalar(out=PTF[:], in0=PTF[:], scalar1=128.0, scalar2=iop[:, 0:1],
                                                      op0=ALU.mult, op1=ALU.add), reads=[PTF, iop], writes=[PTF])
                P.op("dve", lambda e: e.tensor_copy(out=IDX[:], in_=PTF[:]), reads=[PTF], writes=[IDX])

                def stream_mm(w_ap, ncols, lhs_fn, out_buf, out_off, nk=8):
                    c0 = 0
                    while c0 < ncols:
                        n = min(512, ncols - c0)
                        sw = WS[stg[0] % len(WS)]
                        pu = PU[stg[0] % 2]
                        stg[0] += 1
                        P.dma("sp", sw[:, :, 0:n], w_ap[:, c0:c0 + n].rearrange("(kc p) c -> p kc c", p=128), sw, writes=[sw])
                        P.op("pe", mm_chain(pu[:, 0:n], lhs_fn, lambda kc, sw=sw, n=n: sw[:, kc, 0:n], nk=nk),
                             reads=[sw, XST, TT], writes=[pu])
                        P.op("act", lambda e, pu=pu, n=n, c0=c0: e.copy(out=out_buf[:, out_off + c0:out_off + c0 + n], in_=pu[:, 0:n]),
                             reads=[pu], writes=[out_buf])
                        c0 += n

                stream_mm(w_in, DIN, lambda kc: XST[:, kc, :], Us, 0)

                if os.environ.get("MK_TIND"):
                    try:
                        nc.gpsimd.indirect_dma_start(out=KV[0][:, 0, :], out_offset=None, in_=ck_d,
                                                     in_offset=bass.IndirectOffsetOnAxis(ap=IDX[:, 0:1], axis=0),
                                                     bounds_check=100, oob_is_err=False)
                        print("TIND2 ok A")
                    except Exception as ex:
                        print("TIND2 err A", str(ex)[:60])
                for k in range(4):
                    P.dma("sp", CWk[:], cwrow[k:k + 1, :].partition_broadcast(NS).rearrange("p a c -> p (a c)"), CWk, writes=[CWk])
                    src = SC[:, k, :] if k < 3 else Us[:, C_XA:C_XA + 2048]
                    dst = cacs if k == 0 else ctmp
                    P.op("dve", lambda e, src=src, dst=dst: e.tensor_tensor(out=dst[:], in0=src, in1=CWk[:], op=ALU.mult),
                         reads=[SC, Us, CWk], writes=[dst])
                    if k > 0:
                        P.op("dve", lambda e: e.tensor_tensor(out=cacs[:], in0=cacs[:], in1=ctmp[:], op=ALU.add),
                             reads=[cacs, ctmp], writes=[cacs])
                P.dma("sp", CWk[:], cbrow.partition_broadcast(NS).rearrange("p a c -> p (a c)"), CWk, writes=[CWk])
                P.op("dve", lambda e: e.tensor_tensor(out=cacs[:], in0=cacs[:], in1=CWk[:], op=ALU.add),
                     reads=[cacs, CWk], writes=[cacs])
                P.op("act", lambda e: e.activation(out=xbc[:], in_=cacs[:], func=AF.Silu), reads=[cacs], writes=[xbc])
                cv3 = cvs_o.rearrange("p (a c) -> p a c", a=3)
                P.dma("sp", cv3[:, 0:2, :], SC[:, 1:3, :], BIGS, reads=[SC], is_out=True)
                P.dma("sp", cv3[:, 2, :], Us[:, C_XA:C_XA + 2048], Us, reads=[Us], is_out=True)

                if os.environ.get("MK_TIND"):
                    try:
                        nc.gpsimd.indirect_dma_start(out=KV[0][:, 0, :], out_offset=None, in_=ck_d,
                                                     in_offset=bass.IndirectOffsetOnAxis(ap=IDX[:, 0:1], axis=0),
                                                     bounds_check=100, oob_is_err=False)
                        print("TIND2 ok B")
                    except Exception as ex:
                        print("TIND2 err B", str(ex)[:60])
                P.op("dve", lambda e: e.tensor_tensor(out=dts[:], in0=Us[:, C_DT:C_DT + 16], in1=v16s[:, 0, :], op=ALU.add),
                     reads=[Us, v16s], writes=[dts])
                P.op("act", lambda e: e.activation(out=dte[:], in_=dts[:], func=AF.Exp), reads=[dts], writes=[dte])
                P.op("act", lambda e: e.activation(out=dts[:], in_=dte[:], func=AF.Ln, bias=1.0), reads=[dte], writes=[dts])
                P.op("act", lambda e: e.activation(out=As[:], in_=v16s[:, 1, :], func=AF.Exp), reads=[v16s], writes=[As])
                P.op("dve", lambda e: e.tensor_tensor(out=dec[:], in0=dts[:], in1=As[:], op=ALU.mult), reads=[dts, As], writes=[dec])
                P.op("act", lambda e: e.activation(out=dec[:], in_=dec[:], func=AF.Exp, scale=-1.0), reads=[dec], writes=[dec])
                x3 = xbc[:, 0:1024].rearrange("p (h d) -> p h d", h=16)
                P.op("dve", lambda e: e.tensor_tensor(out=xdt_t[:].rearrange("p (h d) -> p h d", h=16), in0=x3,
                                                      in1=bc3(dts[:], 16, 64), op=ALU.mult), reads=[xbc, dts], writes=[xdt_t])
                P.op("dve", lambda e: e.tensor_copy(out=dec_t[:].rearrange("p (h d) -> p h d", h=16), in_=bc3(dec[:], 16, 64)),
                     reads=[dec], writes=[dec_t])
                transp(xdt_t, XDTT)
                transp(dec_t, DECT)

                if os.environ.get("MK_TIND"):
                    try:
                        nc.gpsimd.indirect_dma_start(out=KV[0][:, 0, :], out_offset=None, in_=ck_d,
                                                     in_offset=bass.IndirectOffsetOnAxis(ap=IDX[:, 0:1], axis=0),
                                                     bounds_check=100, oob_is_err=False)
                        print("TIND2 ok C")
                    except Exception as ex:
                        print("TIND2 err C", str(ex)[:60])
                ssm_in = ssm_d.rearrange("n (c q s) -> n q c s", c=8, q=128)
                ssm_out = ssms_o.rearrange("n (c q s) -> n q c s", c=8, q=128)
                for n in range(NS):
                    H = Hs[0]
                    P.dma("sp", H[:], ssm_in[n], prod, writes=[H])
                    P.op("pe", lambda e, n=n: e.matmul(PBC[:, :], lhsT=SEL[:, n, :], rhs=xbc[:, 1024:1536], start=True, stop=True),
                         reads=[SEL, xbc], writes=[PBC])
                    P.op("pe", lambda e, n=n: e.matmul(PCC[:, :], lhsT=SEL[:, n, :], rhs=xbc[:, 1536:2048], start=True, stop=True),
                         reads=[SEL, xbc], writes=[PCC])
                    b4 = PBC[:, :].rearrange("p (g a s) -> p g a s", g=4, a=1).to_broadcast([128, 4, 2, 128])
                    c4 = PCC[:, :].rearrange("p (g a s) -> p g a s", g=4, a=1).to_broadcast([128, 4, 2, 128])
                    h4 = lambda t: t[:].rearrange("p (g a) s -> p g a s", g=4)
                    xcol = XDTT[:, :, n:n + 1].to_broadcast([128, 8, 128])
                    dcol = DECT[:, :, n:n + 1].to_broadcast([128, 8, 128])
                    P.op("dve", lambda e: e.tensor_tensor(out=h4(ht1), in0=b4, in1=xcol.rearrange("p (g a) s -> p g a s", g=4),
                                                          op=ALU.mult), reads=[PBC, XDTT], writes=[ht1])
                    P.op("dve", lambda e, H=H: e.tensor_tensor(out=ht2[:], in0=H[:], in1=dcol, op=ALU.mult),
                         reads=[H, DECT], writes=[ht2])
                    P.op("dve", lambda e, H=H: e.tensor_tensor(out=H[:], in0=ht1[:], in1=ht2[:], op=ALU.add),
                         reads=[ht1, ht2], writes=[H])
                    P.dma("sp", ssm_out[n], H[:], prod, reads=[H], is_out=True)
                    P.op("dve", lambda e, H=H: e.tensor_tensor(out=h4(ht1), in0=c4, in1=h4(H), op=ALU.mult),
                         reads=[PCC, H], writes=[ht1])
                    P.op("dve", lambda e, n=n: e.tensor_reduce(out=YTT[:, :, n], in_=ht1[:], axis=AX.X, op=ALU.add),
                         reads=[ht1], writes=[YTT])

                if os.environ.get("MK_TIND"):
                    try:
                        nc.gpsimd.indirect_dma_start(out=KV[0][:, 0, :], out_offset=None, in_=ck_d,
                                                     in_offset=bass.IndirectOffsetOnAxis(ap=IDX[:, 0:1], axis=0),
                                                     bounds_check=100, oob_is_err=False)
                        print("TIND2 ok D")
                    except Exception as ex:
                        print("TIND2 err D", str(ex)[:60])
                def fn_yb(e):
                    ins = None
                    for c in range(8):
                        ins = e.matmul(PYT[c // 4][:, 128 * (c % 4):128 * (c % 4) + 128], lhsT=YTT[:, c, :], rhs=idn[:],
                                       start=True, stop=True)
                    return ins
                P.op("pe", fn_yb, reads=[YTT, idn], writes=[PYT[0], PYT[1]])
                for hf in range(2):
                    P.op("act", lambda e, hf=hf: e.copy(out=y_t[:, 512 * hf:512 * hf + 512], in_=PYT[hf][:, :]),
                         reads=[PYT[hf]], writes=[y_t])
                P.op("dve", lambda e: e.tensor_tensor(out=g_t[:].rearrange("p (h d) -> p h d", h=16), in0=x3,
                                                      in1=bc3(v16s[:, 2, :], 16, 64), op=ALU.mult), reads=[xbc, v16s], writes=[g_t])
                P.op("dve", lambda e: e.tensor_tensor(out=y_t[:], in0=y_t[:], in1=g_t[:], op=ALU.add), reads=[y_t, g_t], writes=[y_t])
                P.op("act", lambda e: e.activation(out=szs[:], in_=Us[:, C_ZA:C_ZA + 1024], func=AF.Silu), reads=[Us], writes=[szs])
                P.op("dve", lambda e: e.tensor_tensor(out=g_t[:], in0=y_t[:], in1=szs[:], op=ALU.mult), reads=[y_t, szs], writes=[g_t])
                P.op("dve", lambda e: e.tensor_tensor(out=gq_t[:], in0=g_t[:], in1=g_t[:], op=ALU.mult), reads=[g_t], writes=[gq_t])
                P.op("dve", lambda e: e.tensor_reduce(out=ss4[:], in_=gq_t[:].rearrange("p (g c) -> p g c", g=4), axis=AX.X, op=ALU.add),
                     reads=[gq_t], writes=[ss4])
                P.op("act", lambda e: e.activation(out=ss4[:], in_=ss4[:], func=AF.Sqrt, bias=epsb[0:NS, 0:1], scale=1.0 / 256.0),
                     reads=[ss4, epsb], writes=[ss4])
                P.op("dve", lambda e: e.reciprocal(out=ss4[:], in_=ss4[:]), reads=[ss4], writes=[ss4])
                P.op("dve", lambda e: e.tensor_tensor(out=ya_t[:].rearrange("p (g c) -> p g c", g=4),
                                                      in0=g_t[:].rearrange("p (g c) -> p g c", g=4), in1=bc3(ss4[:], 4, 256),
                                                      op=ALU.mult), reads=[g_t, ss4], writes=[ya_t])
                P.op("dve", lambda e: e.tensor_tensor(out=ya_t[:], in0=ya_t[:], in1=nws[:], op=ALU.mult), reads=[ya_t, nws], writes=[ya_t])

                if os.environ.get("MK_TIND"):
                    try:
                        nc.gpsimd.indirect_dma_start(out=KV[0][:, 0, :], out_offset=None, in_=ck_d,
                                                     in_offset=bass.IndirectOffsetOnAxis(ap=IDX[:, 0:1], axis=0),
                                                     bounds_check=100, oob_is_err=False)
                        print("TIND2 ok E")
                    except Exception as ex:
                        print("TIND2 err E", str(ex)[:60])
                P.op("dve", lambda e: e.tensor_copy(out=qk[:, 0:16, :].rearrange("p h d -> p (h d)"), in_=Us[:, C_Q:C_Q + 1024]),
                     reads=[Us], writes=[qk])
                P.op("dve", lambda e: e.tensor_copy(out=qk[:, 16:20, :].rearrange("p h d -> p (h d)"), in_=Us[:, C_K:C_K + 256]),
                     reads=[Us], writes=[qk])
                cosb = css[:, 0:32].rearrange("p (a c) -> p a c", a=1).to_broadcast([NS, 20, 32])
                sinb = css[:, 32:64].rearrange("p (a c) -> p a c", a=1).to_broadcast([NS, 20, 32])
                q1, q2 = qk[:, :, 0:32], qk[:, :, 32:64]
                P.op("dve", lambda e: e.tensor_tensor(out=ra_s[:], in0=q1, in1=sinb, op=ALU.mult), reads=[qk, css], writes=[ra_s])
                P.op("dve", lambda e: e.tensor_tensor(out=rb_s[:], in0=q2, in1=sinb, op=ALU.mult), reads=[qk, css], writes=[rb_s])
                P.op("dve", lambda e: e.tensor_tensor(out=q1, in0=q1, in1=cosb, op=ALU.mult), reads=[qk, css], writes=[qk])
                P.op("dve", lambda e: e.tensor_tensor(out=q2, in0=q2, in1=cosb, op=ALU.mult), reads=[qk, css], writes=[qk])
                P.op("dve", lambda e: e.tensor_tensor(out=q1, in0=q1, in1=rb_s[:], op=ALU.subtract), reads=[qk, rb_s], writes=[qk])
                P.op("dve", lambda e: e.tensor_tensor(out=q2, in0=q2, in1=ra_s[:], op=ALU.add), reads=[qk, ra_s], writes=[qk])
                P.dma("sp", ks_o, qk[:, 16:20, :].rearrange("p h d -> p (h d)"), qk, reads=[qk], is_out=True)
                P.dma("sp", vs_o, Us[:, C_V:C_V + 256], Us, reads=[Us], is_out=True)

                if os.environ.get("MK_TIND"):
                    try:
                        nc.gpsimd.indirect_dma_start(out=KV[0][:, 0, :], out_offset=None, in_=ck_d,
                                                     in_offset=bass.IndirectOffsetOnAxis(ap=IDX[:, 0:1], axis=0),
                                                     bounds_check=100, oob_is_err=False)
                        print("TIND2 ok F")
                    except Exception as ex:
                        print("TIND2 err F", str(ex)[:60])
                ck_rows = ck_d
                for n in range(NS):
                    Kt = KV[0]
                    for j in range(NPG):
                        P.idma(Kt[:, j, :], ck_rows, IDX[:, NPG * n + j:NPG * n + j + 1], Kt, reads=[IDX], writes=[Kt],
                               bound=None)
                    for hf in range(2):
                        P.op("pe", lambda e, n=n, hf=hf: e.matmul(PQB[hf][:, :], lhsT=SEL[:, n, :],
                                                                  rhs=qk[:, 8 * hf:8 * hf + 8, :].rearrange("p h d -> p (h d)"),
                                                                  start=True, stop=True), reads=[SEL, qk], writes=[PQB[hf]])
                    for kvh in range(4):
                        pq = PQB[kvh // 2]
                        qv = pq[:, 256 * (kvh % 2):256 * (kvh % 2) + 256].rearrange("p (a h d) -> p a h d", a=1, h=4) \
                            .to_broadcast([128, NPG, 4, 64])
                        kvw = Kt[:, :, 64 * kvh:64 * kvh + 64].rearrange("p j (a d) -> p j a d", a=1).to_broadcast([128, NPG, 4, 64])
                        P.op("dve", lambda e, kvw=kvw, qv=qv: e.tensor_tensor(out=prod[:], in0=kvw, in1=qv, op=ALU.mult),
                             reads=[Kt, pq], writes=[prod])
                        P.op("dve", lambda e, n=n, kvh=kvh: e.tensor_reduce(
                            out=S_all[:, n, :].rearrange("p (j h) -> p j h", h=16)[:, :, 4 * kvh:4 * kvh + 4], in_=prod[:],
                            axis=AX.X, op=ALU.add), reads=[prod], writes=[S_all])
                    P.op("pe", lambda e, n=n: e.matmul(PCS[:, :], lhsT=SELT[:, n, :], rhs=S_all[:, n, :],
                                                       start=(n == 0), stop=(n == NS - 1)), reads=[SELT, S_all], writes=[PCS])
                P.op("dve", lambda e: e.tensor_copy(out=csum[:].rearrange("p j h -> p (j h)"), in_=PCS[:, :]), reads=[PCS], writes=[csum])
                cs4 = csum[:].rearrange("p (b a) h -> p b a h", a=2)
                P.op("dve", lambda e: e.tensor_tensor(out=sblk[:].rearrange("p h b -> p b h"), in0=cs4[:, :, 0, :], in1=cs4[:, :, 1, :],
                                                      op=ALU.add), reads=[csum], writes=[sblk])
                for h in range(16):
                    P.op("dve", lambda e, h=h: e.max(out=mx8[:, h, :], in_=sblk[:, h, :]), reads=[sblk], writes=[mx8])
                P.op("dve", lambda e: e.tensor_tensor(out=sl8[:], in0=sblk[:], in1=mx8[:, :, 2:3].to_broadcast([NS, 16, 8]), op=ALU.is_ge),
                     reads=[sblk, mx8], writes=[sl8])
                P.op("dve", lambda e: e.tensor_scalar(out=sl8[:], in0=sl8[:], scalar1=-1.0, scalar2=BIG, op0=ALU.add, op1=ALU.mult),
                     reads=[sl8], writes=[sl8])
                P.op("dve", lambda e: e.tensor_copy(
                    out=MB[:].rearrange("p (b a) h -> p b a h", a=2),
                    in_=sl8[:].rearrange("p h (b a) -> p b a h", a=1).to_broadcast([NS, 8, 2, 16])), reads=[sl8], writes=[MB])

                for n in range(NS):
                    Vt = KV[0]
                    for j in range(NPG):
                        P.idma(Vt[:, j, :], cv_d, IDX[:, NPG * n + j:NPG * n + j + 1], Vt, reads=[IDX], writes=[Vt],
                               bound=None)
                    P.op("pe", lambda e, n=n: e.matmul(PMBc[:, :], lhsT=SEL[:, n, :], rhs=MB[:].rearrange("p j h -> p (j h)"),
                                                       start=True, stop=True), reads=[SEL, MB], writes=[PMBc])
                    P.op("dve", lambda e, n=n: e.scalar_tensor_tensor(out=stmp[:], in0=S_all[:, n, :], scalar=SCALE, in1=PMBc[:, :],
                                                                      op0=ALU.mult, op1=ALU.add), reads=[S_all, PMBc], writes=[stmp])
                    P.op("act", lambda e: e.activation(out=Pm[:].rearrange("p j h -> p (j h)"), in_=stmp[:], func=AF.Exp),
                         reads=[stmp], writes=[Pm])

                    def fn_pv(e, Vt=Vt):
                        ins = None
                        for kvh in range(4):
                            for j in range(NPG):
                                ins = e.matmul(POs[:, 4 * kvh:4 * kvh + 4], lhsT=Vt[:, j, 64 * kvh:64 * kvh + 64],
                                               rhs=Pm[:, j, 4 * kvh:4 * kvh + 4], start=(j == 0), stop=(j == NPG - 1))
                        for j in range(NPG):
                            ins = e.matmul(PLs[:, :], lhsT=Pm[:, j, :], rhs=onesf[:, 0:1], start=(j == 0), stop=(j == NPG - 1))
                        return ins
                    P.op("pe", fn_pv, reads=[Vt, Pm, onesf], writes=[POs])
                    P.op("act", lambda e, n=n: e.copy(out=OT[:, :, n], in_=POs[:, :]), reads=[POs], writes=[OT])
                    P.op("dve", lambda e, n=n: e.tensor_copy(out=LT[:, n:n + 1], in_=PLs[:, :]), reads=[PLs], writes=[LT])

                def fn_ob(e):
                    ins = None
                    for h in range(16):
                        ins = e.matmul(POt[h // 8][:, 64 * (h % 8):64 * (h % 8) + 64], lhsT=OT[:, h, :], rhs=idn[0:64, 0:64],
                                       start=True, stop=True)
                    return ins
                P.op("pe", fn_ob, reads=[OT, idn], writes=[POt[0], POt[1]])
                P.op("pe", lambda e: e.matmul(PLt[:, :], lhsT=LT[:], rhs=idn[0:16, 0:16], start=True, stop=True),
                     reads=[LT, idn], writes=[PLt])
                for hf in range(2):
                    P.op("act", lambda e, hf=hf: e.copy(out=o_t[:, 8 * hf:8 * hf + 8, :].rearrange("p h d -> p (h d)"), in_=POt[hf][:, :]),
                         reads=[POt[hf]], writes=[o_t])
                P.op("dve", lambda e: e.tensor_copy(out=l_t[:], in_=PLt[:, :]), reads=[PLt], writes=[l_t])
                k4 = qk[:, 16:20, :].rearrange("p (k a) d -> p k a d", a=1).to_broadcast([NS, 4, 4, 64])
                v4 = Us[:, C_V:C_V + 256].rearrange("p (k a d) -> p k a d", k=4, a=1).to_broadcast([NS, 4, 4, 64])
                q4 = qk[:, 0:16, :].rearrange("p (k a) d -> p k a d", a=4)
                P.op("dve", lambda e: e.tensor_tensor(out=sprod[:].rearrange("p (k a) d -> p k a d", a=4), in0=q4, in1=k4, op=ALU.mult),
                     reads=[qk], writes=[sprod])
                P.op("dve", lambda e: e.tensor_reduce(out=sown[:], in_=sprod[:], axis=AX.X, op=ALU.add), reads=[sprod], writes=[sown])
                P.op("act", lambda e: e.activation(out=sown[:], in_=sown[:], func=AF.Exp, scale=SCALE), reads=[sown], writes=[sown])
                P.op("dve", lambda e: e.tensor_tensor(out=sprod[:].rearrange("p (k a) d -> p k a d", a=4), in0=v4,
                                                      in1=sown[:].rearrange("p (k a c) -> p k a c", k=4, c=1).to_broadcast([NS, 4, 4, 64]),
                                                      op=ALU.mult), reads=[Us, sown], writes=[sprod])
                P.op("dve", lambda e: e.tensor_tensor(out=o_t[:], in0=o_t[:], in1=sprod[:], op=ALU.add), reads=[o_t, sprod], writes=[o_t])
                P.op("dve", lambda e: e.tensor_tensor(out=l_t[:], in0=l_t[:], in1=sown[:], op=ALU.add), reads=[l_t, sown], writes=[l_t])
                P.op("dve", lambda e: e.reciprocal(out=l_t[:], in_=l_t[:]), reads=[l_t], writes=[l_t])
                P.op("dve", lambda e: e.tensor_tensor(out=o_t[:], in0=o_t[:], in1=bc3(l_t[:], 16, 64), op=ALU.mult),
                     reads=[o_t, l_t], writes=[o_t])
                P.op("act", lambda e: e.activation(out=szs[:], in_=Us[:, C_ZB:C_ZB + 1024], func=AF.Silu), reads=[Us], writes=[szs])
                P.op("dve", lambda e: e.tensor_tensor(out=ob_t[:], in0=o_t[:].rearrange("p h d -> p (h d)"), in1=szs[:], op=ALU.mult),
                     reads=[o_t, szs], writes=[ob_t])

                P.dma("sp", gbs[:], lng.partition_broadcast(NS).rearrange("p a c -> p (a c)"), prod, writes=[gbs])
                P.dma("sp", bbs[:], lnb.partition_broadcast(NS).rearrange("p a c -> p (a c)"), prod, writes=[bbs])
                P.dma("sp", xrs[:], xs_tok, prod, writes=[xrs])
                P.op("act", lambda e: e.activation(out=sg[:], in_=Us[:, C_GA:C_GA + 2048], func=AF.Sigmoid), reads=[Us], writes=[sg])
                transp(ya_t, TT)
                stream_mm(w_a, 1024, lambda kc: TT[:, kc, :], br, 0)
                transp(ob_t, TT)
                stream_mm(w_b, 1024, lambda kc: TT[:, kc, :], br, 1024)
                P.op("dve", lambda e: e.tensor_tensor(out=br[:], in0=br[:], in1=sg[:], op=ALU.mult), reads=[br, sg], writes=[br])
                P.op("dve", lambda e: e.tensor_tensor(out=mix_t[:], in0=br[:, 0:1024], in1=br[:, 1024:2048], op=ALU.add),
                     reads=[br], writes=[mix_t])
                transp(mix_t, TT)
                stream_mm(w_o, 1024, lambda kc: TT[:, kc, :], pre_s, 0)
                P.op("dve", lambda e: e.scalar_tensor_tensor(out=pre_s[:], in0=xrs[:], scalar=ALPHA, in1=pre_s[:], op0=ALU.mult, op1=ALU.add),
                     reads=[xrs, pre_s], writes=[pre_s])
                for hf in range(2):
                    P.op("dve", lambda e, hf=hf: e.bn_stats(out=sts[:, hf, :], in_=pre_s[:, 512 * hf:512 * hf + 512]), reads=[pre_s], writes=[sts])
                P.op("dve", lambda e: e.bn_aggr(out=mvs[:], in_=sts[:].rearrange("p a c -> p (a c)")), reads=[sts], writes=[mvs])
                P.op("act", lambda e: e.activation(out=rss[:], in_=mvs[:, 1:2], func=AF.Sqrt, bias=epsb[0:NS, 0:1]), reads=[mvs, epsb], writes=[rss])
                P.op("dve", lambda e: e.reciprocal(out=rss[:], in_=rss[:]), reads=[rss], writes=[rss])
                P.op("dve", lambda e: e.tensor_scalar(out=pre_s[:], in0=pre_s[:], scalar1=mvs[:, 0:1], scalar2=rss[:, 0:1],
                                                      op0=ALU.subtract, op1=ALU.mult), reads=[pre_s, mvs, rss], writes=[pre_s])
                P.op("dve", lambda e: e.tensor_tensor(out=pre_s[:], in0=pre_s[:], in1=gbs[:], op=ALU.mult), reads=[pre_s, gbs], writes=[pre_s])
                P.op("dve", lambda e: e.tensor_tensor(out=pre_s[:], in0=pre_s[:], in1=bbs[:], op=ALU.add), reads=[pre_s, bbs], writes=[pre_s])
                P.dma("sp", ys_o, pre_s[:], pre_s, reads=[pre_s], is_out=True)
            P.barrier()

        YAT = P.sb("YAT", [128, 8, 2048], BF16)
        OBT = P.sb("OBT", [128, 8, 2048], BF16)
        with contextlib.ExitStack() as st:
            alloc_io(st, "a")
            XO = XOl[0]
            Wf = P.sb("Wf", [128, 8, 512], BF16, st)
            Wt = P.sb("Wt", [128, 8, 260], BF16, st)
            cw = P.sb("cw", [128, 4, 4], F32, st)
            cb = P.sb("cb", [128, 4], F32, st)
            v16 = P.sb("v16", [128, 3, 4], F32, st)
            Aneg = P.sb("Aneg", [128, 4], F32, st)
            nw = P.sb("nw", [128, 256], F32, st)
            U = P.sb("U", [128, 3, 515], F32, st)
            XC = P.sb("XC", [128, 3, 512], F32, st)
            caccs = [P.sb("cacc%d" % i, [128, 512], F32, st) for i in range(3)]
            UO = P.sb("UO", [128, 4, 131], F32, st)
            XCO = P.sb("XCO", [128, 4, 128], F32, st)
            XCOb = P.sb("XCOb", [128, 4, 128], BF16, st)
            cacos = [P.sb("caco%d" % i, [128, 128], F32, st) for i in range(4)]
            hT = P.sb("hT", [128, 256], F32, st)
            Hsel = P.sb("Hsel", [128, 256], F32, st)
            Hselb = P.sb("Hselb", [128, 256], BF16, st)
            sm = {n: P.sb("sm_" + n, [128, 16], F32, st) for n in
                  ("t0", "e0", "dt", "adt", "acs", "te", "ecl", "w")}
            htmp = P.sb("htmp", [128, 256], F32, st)
            so = {n: P.sb("so_" + n, [128, 4], F32, st) for n in
                  ("t0", "e0", "dt", "adt", "acs", "nacs", "eacs")}
            xs_tok = P.sb("xs_tok", [128, 256], F32, st)
            B_tokb = P.sb("B_tokb", [128, 128], BF16, st)
            xdtw = P.sb("xdtw", [128, 256], BF16, st)
            xso = P.sb("xso", [128, 256], F32, st)
            xdto = P.sb("xdto", [128, 256], BF16, st)
            Radt = P.sb("Radt", [128, 4, 128], F32, st)
            dcl = P.sb("dcl", [128, 4, 128], F32, st)
            dce = P.sb("dce", [128, 4, 128], F32, st)
            cbm = P.sb("cbm", [128, 128], F32, st)
            MT = P.sb("MT", [128, 4, 128], BF16, st)
            Ysb = P.sb("Ysb", [128, 256], F32, st)
            yv = P.sb("yv", [128, 256], F32, st)
            sza = P.sb("sza", [128, 256], F32, st)
            gg = P.sb("gg", [128, 256], F32, st)
            gsq = P.sb("gsq", [128, 256], F32, st)
            ss = P.sb("ss", [128, 1], F32, st)
            rstd = P.sb("rstd", [128, 1], F32, st)
            ya = P.sb("ya", [128, 256], F32, st)
            cvs = P.sb("cvs", [128, 3, 3], F32, st)
            cvc = P.sb("cvc", [128, 3], F32, st)

            PF = [P.view("PF0", banks[0], banks[0][:, :]), P.view("PF1", banks[1], banks[1][:, :])]
            PTc = P.view("PTc", banks[2], banks[2][:, 0:16])
            Pacs = P.view("Pacs", banks[2], banks[2][:, 16:32])
            Pacl = P.view("Pacl", banks[2], banks[2][:, 32:48])
            PX = P.view("PX", banks[2], banks[2][:, 64:448])
            PST = P.view("PST", banks[3], banks[3][:, 0:256])
            PCB = P.view("PCB", banks[3], banks[3][:, 256:384])
            PFo = [P.view("PFo0", banks[4], banks[4][:, 0:131]), P.view("PFo1", banks[4], banks[4][:, 132:263])]
            PTo = P.view("PTo", banks[5], banks[5][:, 0:260])
            Poa = P.view("Poa", banks[5], banks[5][:, 264:268])
            PAB = P.view("PAB", banks[6], banks[6][:, :])
            PXo = P.view("PXo", banks[7], banks[7][:, 0:256])
            PY = P.view("PY", banks[7], banks[7][:, 256:512])
            PYO = PST
            PYT = PX

            for g in range(NG if 's' in PASSES else 0):
                load_w(Wf, 0, w_in[:, C_XA + 256 * g:C_XA + 256 * g + 256], 256)
                load_w(Wf, 256, w_in[:, C_B + 128 * g:C_B + 128 * g + 128], 128)
                load_w(Wf, 384, w_in[:, C_C + 128 * g:C_C + 128 * g + 128], 128)
                load_w(Wt, 0, w_in[:, C_ZA + 256 * g:C_ZA + 256 * g + 256], 256)
                load_w(Wt, 256, w_in[:, C_DT + 4 * g:C_DT + 4 * g + 4], 4)
                chs = [256 * g, 256 * g + 128, 1024 + 128 * g, 1536 + 128 * g]
                for m, c0 in enumerate(chs):
                    P.dma("sp", cw[:, m, :], cwT[c0:c0 + 128, :], cw, writes=[cw])
                    P.dma("sp", cb[:, m:m + 1], cbT[c0:c0 + 128, :], cb, writes=[cb])
                P.dma("sp", v16[:], vec16[:, 4 * g:4 * g + 4].partition_broadcast(128), v16, writes=[v16])
                P.dma("sp", nw[:], nwv[:, 256 * g:256 * g + 256].partition_broadcast(128).rearrange("p a c -> p (a c)"),
                      nw, writes=[nw])
                P.op("act", lambda e: e.activation(out=Aneg[:], in_=v16[:, 1, :], func=AF.Exp), reads=[v16], writes=[Aneg])
                P.op("dve", lambda e: e.tensor_scalar(out=Aneg[:], in0=Aneg[:], scalar1=-1.0, scalar2=None, op0=ALU.mult),
                     reads=[Aneg], writes=[Aneg])
                P.op("pool", lambda e: e.memset(hT[:], 0.0), writes=[hT])
                P.op("pool", lambda e: e.memset(U[:], 0.0), writes=[U])

                def dt_chain(smd, psrc, rd, nk=1):
                    n4 = 4 * nk
                    t0, e0, dtt, adt = smd["t0"], smd["e0"], smd["dt"], smd["adt"]
                    v3 = lambda ap: ap.rearrange("p (k h) -> p k h", h=4)
                    b3 = lambda ap: ap.rearrange("p (a h) -> p a h", a=1).to_broadcast([128, nk, 4])
                    P.op("dve", lambda e: e.tensor_tensor(out=v3(t0[:, 0:n4]), in0=v3(psrc), in1=b3(v16[:, 0, :]), op=ALU.add),
                         reads=rd + [v16], writes=[t0])
                    P.op("act", lambda e: e.activation(out=e0[:, 0:n4], in_=t0[:, 0:n4], func=AF.Exp), reads=[t0], writes=[e0])
                    P.op("act", lambda e: e.activation(out=dtt[:, 0:n4], in_=e0[:, 0:n4], func=AF.Ln, bias=1.0), reads=[e0], writes=[dtt])
                    P.op("dve", lambda e: e.tensor_tensor(out=v3(adt[:, 0:n4]), in0=v3(dtt[:, 0:n4]), in1=b3(Aneg[:]), op=ALU.mult),
                         reads=[dtt, Aneg], writes=[adt])

                for i in range(NI):
                    xb = load_x(i)
                    for m in range(3):
                        pf = PF[m % 2]
                        P.op("pe", mm_chain(pf[:, :], lambda kc, m=m: Wf[:, kc, m * 128:(m + 1) * 128],
                                            lambda kc, xb=xb: xb[:, kc, :]), reads=[Wf, xb], writes=[pf])
                        P.op("act", lambda e, m=m, pf=pf: e.copy(out=U[:, m, 3:515], in_=pf[:, :]), reads=[pf], writes=[U])
                    for m in range(3):
                        cacc = caccs[m]
                        P.op("dve", lambda e, m=m: e.tensor_scalar(out=cacc[:], in0=U[:, m, 0:512], scalar1=cw[:, m, 0:1],
                                                                   scalar2=cb[:, m:m + 1], op0=ALU.mult, op1=ALU.add),
                             reads=[U, cw, cb], writes=[cacc])
                        for k in range(1, 4):
                            P.op("dve", lambda e, m=m, k=k: e.scalar_tensor_tensor(
                                out=cacc[:], in0=U[:, m, k:k + 512], scalar=cw[:, m, k:k + 1], in1=cacc[:],
                                op0=ALU.mult, op1=ALU.add), reads=[U, cw, cacc], writes=[cacc])
                        P.op("act", lambda e, m=m: e.activation(out=XC[:, m, :], in_=cacc[:], func=AF.Silu),
                             reads=[cacc], writes=[XC])
                    if i == NI - 1:
                        P.op("dve", lambda e: e.tensor_copy(out=cvs[:], in_=U[:, :, 512:515]), reads=[U], writes=[cvs])
                        P.dma("sp", cvA_o[g], cvs[:], cvs, reads=[cvs], is_out=True)
                    P.op("dve", lambda e: e.tensor_copy(out=U[:, :, 0:3], in_=U[:, :, 512:515]), reads=[U], writes=[U])
                    def fn_dt(e, xb=xb):
                        ins = None
                        for k in range(4):
                            for kc in range(8):
                                ins = e.matmul(PTc[:, 4 * k:4 * k + 4], lhsT=xb[:, kc, 128 * k:128 * k + 128],
                                               rhs=Wt[:, kc, 256:260], start=(kc == 0), stop=(kc == 7))
                        return ins
                    P.op("pe", fn_dt, reads=[xb, Wt], writes=[PTc])
                    dt_chain(sm, PTc[:, 0:16], [PTc], nk=4)
                    adt = sm["adt"]
                    P.op("pe", lambda e: e.matmul(Pacs[:, :], lhsT=tri[:], rhs=adt[:], start=True, stop=True),
                         reads=[tri, adt], writes=[Pacs])
                    P.op("pe", lambda e: e.matmul(Pacl[:, :], lhsT=onesf[:], rhs=adt[:], start=True, stop=True),
                         reads=[onesf, adt], writes=[Pacl])
                    te, ecl, w_ = sm["te"], sm["ecl"], sm["w"]
                    P.op("dve", lambda e: e.tensor_copy(out=sm["acs"][:], in_=Pacs[:, :]), reads=[Pacs], writes=[sm["acs"]])
                    P.op("dve", lambda e: e.tensor_tensor(out=te[:], in0=Pacl[:, :], in1=sm["acs"][:], op=ALU.subtract),
                         reads=[Pacl, sm["acs"]], writes=[te])
                    P.op("act", lambda e: e.activation(out=te[:], in_=te[:], func=AF.Exp), reads=[te], writes=[te])
                    P.op("act", lambda e: e.activation(out=ecl[:], in_=Pacl[:, :], func=AF.Exp), reads=[Pacl], writes=[ecl])
                    P.op("dve", lambda e: e.tensor_tensor(out=w_[:], in0=te[:], in1=sm["dt"][:], op=ALU.mult),
                         reads=[te, sm["dt"]], writes=[w_])
                    for k in range(4):
                        cs = slice(128 * k, 128 * k + 128)

                        def fn_tr(e, cs=cs):
                            tr(e, PX[:, 0:128], XC[:, 0, cs], idn[:])
                            tr(e, PX[:, 128:256], XC[:, 1, cs], idn[:])
                            return tr(e, PX[:, 256:384], XC[:, 2, cs], idn[:])
                        P.op("pe", fn_tr, reads=[XC, idn], writes=[PX])
                        P.op("act", lambda e: e.copy(out=B_tokb[:], in_=PX[:, 256:384]), reads=[PX], writes=[B_tokb])
                        for h in range(4):
                            P.op("dve", lambda e, h=h: e.tensor_scalar(
                                out=xdtw[:, 64 * h:64 * h + 64], in0=PX[:, 64 * h:64 * h + 64],
                                scalar1=w_[:, 4 * k + h:4 * k + h + 1], scalar2=None, op0=ALU.mult), reads=[PX, w_], writes=[xdtw])
                        P.op("pe", lambda e: e.matmul(PST[:, :], lhsT=B_tokb[:], rhs=xdtw[:], start=True, stop=True),
                             reads=[B_tokb, xdtw], writes=[PST])
                        if k == 0:
                            P.op("dve", lambda e: e.tensor_scalar(out=Hsel[:], in0=hT[:], scalar1=oh[:, 0:1], scalar2=None,
                                                                  op0=ALU.mult), reads=[hT, oh], writes=[Hsel])
                        else:
                            P.op("dve", lambda e, k=k: e.scalar_tensor_tensor(
                                out=Hsel[:], in0=hT[:], scalar=oh[:, k:k + 1], in1=Hsel[:], op0=ALU.mult, op1=ALU.add),
                                reads=[hT, oh, Hsel], writes=[Hsel])
                        for h in range(4):
                            P.op("dve", lambda e, h=h: e.scalar_tensor_tensor(
                                out=hT[:, 64 * h:64 * h + 64], in0=hT[:, 64 * h:64 * h + 64], scalar=ecl[:, 4 * k + h:4 * k + h + 1],
                                in1=PST[:, 64 * h:64 * h + 64], op0=ALU.mult, op1=ALU.add),
                                reads=[hT, ecl, PST], writes=[hT])
                    P.op("act", lambda e: e.copy(out=Hselb[:], in_=Hsel[:]), reads=[Hsel], writes=[Hselb])

                    load_xo(i)
                    for m in range(4):
                        pf = PFo[m % 2]
                        P.op("pe", mm_chain(pf[:, :], lambda kc, m=m: Wf[:, kc, m * 128:(m + 1) * 128],
                                            lambda kc: XO[:, kc, :]), reads=[Wf, XO], writes=[pf])
                        P.op("act", lambda e, m=m, pf=pf: e.copy(out=UO[:, m, :], in_=pf[:, :]), reads=[pf], writes=[UO])
                    for m in range(4):
                        caco = cacos[m]
                        P.op("dve", lambda e, m=m: e.tensor_scalar(out=caco[:], in0=UO[:, m, 0:128], scalar1=cw[:, m, 0:1],
                                                                   scalar2=cb[:, m:m + 1], op0=ALU.mult, op1=ALU.add),
                             reads=[UO, cw, cb], writes=[caco])
                        for k in range(1, 4):
                            P.op("dve", lambda e, m=m, k=k: e.scalar_tensor_tensor(
                                out=caco[:], in0=UO[:, m, k:k + 128], scalar=cw[:, m, k:k + 1], in1=caco[:],
                                op0=ALU.mult, op1=ALU.add), reads=[UO, cw, caco], writes=[caco])
                        P.op("act", lambda e, m=m: e.activation(out=XCO[:, m, :], in_=caco[:], func=AF.Silu),
                             reads=[caco], writes=[XCO])
                    P.op("pool", lambda e: e.tensor_copy(out=XCOb[:], in_=XCO[:]), reads=[XCO], writes=[XCOb])
                    if i == NI - 1:
                        P.op("dve", lambda e: e.tensor_copy(out=cvc[:], in_=UO[:, 3, 128:131]), reads=[UO], writes=[cvc])
                        P.dma("sp", cvC_o[g], cvc[:], cvc, reads=[cvc], is_out=True)
                    P.op("pe", mm_chain(PTo[:, :], lambda kc: XO[:, kc, 3:131], lambda kc: Wt[:, kc, :]),
                         reads=[XO, Wt], writes=[PTo])
                    dt_chain(so, PTo[:, 256:260], [PTo])
                    P.op("act", lambda e: e.activation(out=sza[:], in_=PTo[:, 0:256], func=AF.Silu), reads=[PTo], writes=[sza])
                    adto = so["adt"]
                    P.op("pe", lambda e: e.matmul(Poa[:, :], lhsT=tri[:], rhs=adto[:], start=True, stop=True),
                         reads=[tri, adto], writes=[Poa])
                    P.op("dve", lambda e: e.tensor_copy(out=so["acs"][:], in_=Poa[:, :]), reads=[Poa], writes=[so["acs"]])
                    P.op("act", lambda e: e.activation(out=so["eacs"][:], in_=so["acs"][:], func=AF.Exp), reads=[so["acs"]],
                         writes=[so["eacs"]])
                    for h in range(4):
                        P.op("dve", lambda e, h=h: e.tensor_scalar(out=Radt[:, h, :], in0=tri[:], scalar1=adto[:, h:h + 1],
                                                                   scalar2=None, op0=ALU.mult), reads=[tri, adto], writes=[Radt])

                    def fn_ab(e):
                        ins = None
                        for h in range(4):
                            ins = e.matmul(PAB[:, 128 * h:128 * h + 128], lhsT=onesf[:], rhs=Radt[:, h, :], start=True, stop=True)
                        return ins
                    P.op("pe", fn_ab, reads=[onesf, Radt], writes=[PAB])
                    for h in range(4):
                        P.op("dve", lambda e, h=h: e.tensor_scalar(
                            out=dcl[:, h, :], in0=PAB[:, 128 * h:128 * h + 128], scalar1=so["acs"][:, h:h + 1], scalar2=0.0,
                            op0=ALU.subtract, op1=ALU.min), reads=[PAB, so["acs"]], writes=[dcl])
                    P.op("act", lambda e: e.activation(out=dce[:], in_=dcl[:], func=AF.Exp), reads=[dcl], writes=[dce])
                    P.op("pe", lambda e: e.matmul(PCB[:, :], lhsT=XCOb[:, 2, :], rhs=XCOb[:, 3, :], start=True, stop=True),
                         reads=[XCOb], writes=[PCB])
                    P.op("dve", lambda e: e.tensor_tensor(out=cbm[:], in0=PCB[:, :], in1=tri[:], op=ALU.mult),
                         reads=[PCB, tri], writes=[cbm])
                    P.op("dve", lambda e: e.tensor_tensor(
                        out=MT[:], in0=dce[:], in1=cbm[:].rearrange("p (a c) -> p a c", a=1).to_broadcast([128, 4, 128]),
                        op=ALU.mult), reads=[dce, cbm], writes=[MT])

                    def fn_tro(e):
                        tr(e, PXo[:, 0:128], XCO[:, 0, :], idn[:])
                        return tr(e, PXo[:, 128:256], XCO[:, 1, :], idn[:])
                    P.op("pe", fn_tro, reads=[XCO, idn], writes=[PXo])
                    P.op("act", lambda e: e.copy(out=xso[:], in_=PXo[:, :]), reads=[PXo], writes=[xso])
                    for h in range(4):
                        P.op("dve", lambda e, h=h: e.tensor_scalar(
                            out=xdto[:, 64 * h:64 * h + 64], in0=xso[:, 64 * h:64 * h + 64], scalar1=so["dt"][:, h:h + 1],
                            scalar2=None, op0=ALU.mult), reads=[xso, so["dt"]], writes=[xdto])

                    def fn_y(e):
                        ins = None
                        for h in range(4):
                            ins = e.matmul(PY[:, 64 * h:64 * h + 64], lhsT=MT[:, h, :], rhs=xdto[:, 64 * h:64 * h + 64],
                                           start=True, stop=True)
                        return ins
                    P.op("pe", fn_y, reads=[MT, xdto], writes=[PY])
                    P.op("pe", lambda e: e.matmul(PYO[:, :], lhsT=XCOb[:, 3, :], rhs=Hselb[:], start=True, stop=True),
                         reads=[XCOb, Hselb], writes=[PYO])
                    P.op("act", lambda e: e.copy(out=Ysb[:], in_=PY[:, :]), reads=[PY], writes=[Ysb])
                    for h in range(4):
                        hs = slice(64 * h, 64 * h + 64)
                        P.op("dve", lambda e, h=h, hs=hs: e.scalar_tensor_tensor(
                            out=yv[:, hs], in0=PYO[:, hs], scalar=so["eacs"][:, h:h + 1], in1=Ysb[:, hs],
                            op0=ALU.mult, op1=ALU.add), reads=[PYO, so["eacs"], Ysb], writes=[yv])
                        P.op("dve", lambda e, h=h, hs=hs: e.scalar_tensor_tensor(
                            out=yv[:, hs], in0=xso[:, hs], scalar=v16[:, 2, h:h + 1], in1=yv[:, hs],
                            op0=ALU.mult, op1=ALU.add), reads=[xso, v16, yv], writes=[yv])
                    P.op("dve", lambda e: e.tensor_tensor(out=gg[:], in0=yv[:], in1=sza[:], op=ALU.mult),
                         reads=[yv, sza], writes=[gg])
                    P.op("dve", lambda e: e.tensor_tensor(out=gsq[:], in0=gg[:], in1=gg[:], op=ALU.mult),
                         reads=[gg], writes=[gsq])
                    P.op("dve", lambda e: e.tensor_reduce(out=ss[:], in_=gsq[:], axis=AX.X, op=ALU.add),
                         reads=[gsq], writes=[ss])
                    P.op("act", lambda e: e.activation(out=rstd[:], in_=ss[:], func=AF.Sqrt, bias=epsb[:, 0:1], scale=1.0 / 256.0),
                         reads=[ss, epsb], writes=[rstd])
                    P.op("dve", lambda e: e.reciprocal(out=rstd[:], in_=rstd[:]), reads=[rstd], writes=[rstd])
                    P.op("dve", lambda e: e.scalar_tensor_tensor(out=ya[:], in0=gg[:], scalar=rstd[:, 0:1], in1=nw[:],
                                                                 op0=ALU.mult, op1=ALU.mult), reads=[gg, rstd, nw], writes=[ya])

                    def fn_yt(e):
                        tr(e, PYT[:, 0:128], ya[:, 0:128], idn[:])
                        return tr(e, PYT[:, 128:256], ya[:, 128:256], idn[:])
                    P.op("pe", fn_yt, reads=[ya, idn], writes=[PYT])
                    P.op("act", lambda e, g=g, i=i: e.copy(
                        out=YAT[:, 2 * g:2 * g + 2, 128 * i:128 * i + 128],
                        in_=PYT[:, 0:256].rearrange("p (a c) -> p a c", a=2)), reads=[PYT], writes=[YAT])
                P.dma("sp", ssm_o[g], hT[:], hT, reads=[hT], is_out=True)

        P.barrier()
        with contextlib.ExitStack() as st:
            alloc_io(st, "b")
            XO = XOl[0]
            Wkv = P.sb("Wkv", [128, 8, 128], BF16, st)
            Wqz = P.sb("Wqz", [128, 8, 512], BF16, st)
            KTA = P.sb("KTA", [96, SEQ], BF16, st)
            VA = P.sb("VA", [128, 64, 65], BF16, st)
            KTf = P.sb("KTf", [64, 256], F32, st)
            KM = P.sb("KM", [64, 32], F32, st)
            CM = P.sb("CM", [128, 4, 512], BF16, st)
            EL = P.sb("EL", [128, 3, 16, 32], F32, st)
            csa4 = [P.sb("csa4%d" % i, [128, 4, 64], F32, st) for i in range(2)]
            kv4 = P.sb("kv4", [128, 4, 128], F32, st)
            kr4 = P.sb("kr4", [128, 4, 64], F32, st)
            KTf4 = P.sb("KTf4", [64, 512], F32, st)
            cso = P.sb("cso", [128, 64], F32, st)
            kv = P.sb("kv", [128, 128], F32, st)
            kr = P.sb("kr", [128, 64], F32, st)
            ra = P.sb("ra", [128, 4, 32], F32, st)
            rb = P.sb("rb", [128, 4, 32], F32, st)
            qf = P.sb("qf", [128, 4, 64], F32, st)
            qr = P.sb("qr", [128, 4, 64], F32, st)
            szb = P.sb("szb", [128, 256], F32, st)
            QA = P.sb("QA", [96, 4, 128], BF16, st)
            QTf = P.sb("QTf", [64, 4, 128], F32, st)
            spd = P.sb("spd", [128, 4, 32], F32, st)
            mx = P.sb("mx", [128, 4, 8], F32, st)
            sel = P.sb("sel", [128, 4, 32], F32, st)
            MBP = P.sb("MBP", [128, 4, 96], F32, st)
            PTb = [P.sb("PTb%d" % i, [128, 512], BF16, st) for i in range(3)]
            Osb = P.sb("Osb", [65, 512], F32, st)
            rl = P.sb("rl", [128, 4], F32, st)
            ob = P.sb("ob", [128, 4, 64], F32, st)
            ob2 = P.sb("ob2", [128, 256], F32, st)

            PT2 = P.view("PT2", banks[0], banks[0][:, :])
            PKT = P.view("PKT", banks[2], banks[2][0:64, 256:384])
            PKT4 = P.view("PKT4", banks[1], banks[1][0:64, :])
            PQ = P.view("PQ", banks[1], banks[1][:, :])
            PSB = P.view("PSB", banks[2], banks[2][:, 0:128])
            PMB = P.view("PMB", banks[2], banks[2][0:96, 128:256])
            POT = PQ
            PS = [P.view("PS%d" % i, banks[3 + i], banks[3 + i][:, :]) for i in range(3)]
            PO = [P.view("PO%d" % i, banks[6 + i], banks[6 + i][0:65, :]) for i in range(2)]
            POBT = P.view("POBT", banks[0], banks[0][:, 0:256])

            for j in range(SEQ // 512):
                s = WS[stg[0] % len(WS)]
                stg[0] += 1
                P.dma("sp", s[64:96, 0, :], kbi_d[:, 512 * j:512 * j + 512], s, writes=[s])
                P.op("pool", lambda e, j=j, s=s: e.tensor_copy(out=KTA[64:96, 512 * j:512 * j + 512], in_=s[64:96, 0, :]),
                     reads=[s], writes=[KTA])
            s = WS[stg[0] % len(WS)]
            stg[0] += 1
            P.dma("sp", s[:, 0:4, :], cm_d.rearrange("p (a c) -> p a c", a=4), s, writes=[s])
            P.op("pool", lambda e, s=s: e.tensor_copy(out=CM[:], in_=s[:, 0:4, :]), reads=[s], writes=[CM])
            P.dma("sp", EL[:].rearrange("p a b c -> p a (b c)"), el_d.partition_broadcast(128), EL, writes=[EL])
            P.op("pool", lambda e: e.memset(VA[:, :, 64:65], 1.0), writes=[VA])
            P.op("pool", lambda e: e.memset(MBP[:], 0.0), writes=[MBP])

            def rope(src3, dst3, cs, nh, rd, full=False, sink=None):
                if full:
                    cosb, sinb = cs[:, :, 0:32], cs[:, :, 32:64]
                else:
                    cosb = cs[:, 0:32].rearrange("p (a c) -> p a c", a=1).to_broadcast([128, nh, 32])
                    sinb = cs[:, 32:64].rearrange("p (a c) -> p a c", a=1).to_broadcast([128, nh, 32])
                a_, b_ = ra[:, 0:nh, :], rb[:, 0:nh, :]
                x1, x2 = src3[:, :, 0:32], src3[:, :, 32:64]
                steps = [
                    lambda: P.op("dve", lambda e: e.tensor_tensor(out=a_, in0=x1, in1=cosb, op=ALU.mult), reads=rd, writes=[ra]),
                    lambda: P.op("dve", lambda e: e.tensor_tensor(out=b_, in0=x2, in1=sinb, op=ALU.mult), reads=rd, writes=[rb]),
                    lambda: P.op("dve", lambda e: e.tensor_tensor(out=dst3[:, :, 0:32], in0=a_, in1=b_, op=ALU.subtract),
                                 reads=[ra, rb], writes=rd[-1:]),
                    lambda: P.op("dve", lambda e: e.tensor_tensor(out=a_, in0=x2, in1=cosb, op=ALU.mult), reads=rd + [ra], writes=[ra]),
                    lambda: P.op("dve", lambda e: e.tensor_tensor(out=b_, in0=x1, in1=sinb, op=ALU.mult), reads=rd + [rb], writes=[rb]),
                    lambda: P.op("dve", lambda e: e.tensor_tensor(out=dst3[:, :, 32:64], in0=a_, in1=b_, op=ALU.add),
                                 reads=[ra, rb], writes=rd[-1:]),
                ]
                if sink is None:
                    for f in steps:
                        f()
                else:
                    sink.extend(steps)

            KTAb = [Buf("KTAb%d" % i, KTA.t) for i in range(16)]
            VAb = [Buf("VAb%d" % i, VA.t) for i in range(16)]
            QA2 = [QA, P.sb("QAb", [96, 4, 128], BF16, st)]
            szb2 = [szb, P.sb("szbb", [128, 256], F32, st)]
            npsc = [0]

            def common_ops(g, i):
                ops = []
                stt = {}

                def o(f):
                    ops.append(f)
                ca4 = csa4[i % 2]
                o(lambda: stt.__setitem__("xb", load_x(i)))
                o(lambda: P.dma("sp", ca4[:], cs_all[512 * i:512 * i + 512, :].rearrange("(c p) d -> p c d", p=128), ca4, writes=[ca4]))

                def fn_kv(e):
                    xb = stt["xb"]
                    ins = None
                    for k in range(4):
                        for kc in range(8):
                            ins = e.matmul(PT2[:, 128 * k:128 * k + 128], lhsT=xb[:, kc, 128 * k:128 * k + 128],
                                           rhs=Wkv[:, kc, :], start=(kc == 0), stop=(kc == 7))
                    return ins
                o(lambda: P.op("pe", fn_kv, reads=[stt["xb"], Wkv], writes=[PT2]))
                o(lambda: P.op("act", lambda e: e.copy(out=kv4[:].rearrange("p a c -> p (a c)"), in_=PT2[:, :]), reads=[PT2], writes=[kv4]))
                rope(kv4[:, :, 0:64], kr4[:], ca4, 4, [kv4, ca4, kr4], full=True, sink=ops)
                o(lambda: P.dma("sp", kp_o[512 * i:512 * i + 512, 64 * g:64 * g + 64].rearrange("(c p) d -> p c d", p=128), kr4[:], kr4,
                                reads=[kr4], is_out=True))
                o(lambda: P.dma("sp", vp_o[512 * i:512 * i + 512, 64 * g:64 * g + 64].rearrange("(c p) d -> p c d", p=128),
                                kv4[:, :, 64:128], kv4, reads=[kv4], is_out=True))

                def fn_kt(e):
                    ins = None
                    for k in range(4):
                        ins = tr(e, PKT4[:, 128 * k:128 * k + 128], kr4[:, k, :], idn[:])
                    return ins
                o(lambda: P.op("pe", fn_kt, reads=[kr4, idn], writes=[PKT4]))
                o(lambda: P.op("act", lambda e: e.copy(out=KTf4[:], in_=PKT4[:, :]), reads=[PKT4], writes=[KTf4]))
                o(lambda: P.op("dve", lambda e: e.tensor_copy(out=KTA[0:64, 512 * i:512 * i + 512], in_=KTf4[:]), reads=[KTf4], writes=[KTAb[i]]))
                o(lambda: P.op("pool", lambda e: e.tensor_copy(out=VA[:, 4 * i:4 * i + 4, 0:64], in_=kv4[:, :, 64:128]), reads=[kv4], writes=[VAb[i]]))
                o(lambda: P.op("dve", lambda e: e.tensor_reduce(out=KM[:, 2 * i:2 * i + 2], in_=KTf4[:].rearrange("p (b c) -> p b c", b=2),
                                                                axis=AX.X, op=ALU.add), reads=[KTf4], writes=[KM]))
                return ops

            def head_ops(g, i):
                ops = []

                def o(f):
                    ops.append(f)
                QAi = QA2[i % 2]
                szbi = szb2[i % 2]
                o(lambda: load_xo(i))
                o(lambda: P.dma("sp", cso[:], cs_own[128 * i:128 * i + 128, :], cso, writes=[cso]))
                o(lambda: P.op("pe", mm_chain(PQ[:, :], lambda kc: XOl[0][:, kc, 3:131], lambda kc: Wqz[:, kc, :]),
                               reads=[XOl[0], Wqz], writes=[PQ]))
                o(lambda: P.op("act", lambda e: e.copy(out=qf[:].rearrange("p a c -> p (a c)"), in_=PQ[:, 0:256]), reads=[PQ], writes=[qf]))
                o(lambda: P.op("act", lambda e: e.activation(out=szbi[:], in_=PQ[:, 256:512], func=AF.Silu), reads=[PQ], writes=[szbi]))
                rope(qf[:], qr[:], cso, 4, [qf, cso, qr], sink=ops)
                for h in range(4):
                    o(lambda h=h: P.op("pe", lambda e: tr(e, PKT[:, :], qr[:, h, :], idn[:]), reads=[qr, idn], writes=[PKT]))
                    o(lambda h=h: P.op("act", lambda e: e.copy(out=QTf[:, h, :], in_=PKT[:, :]), reads=[PKT], writes=[QTf]))
                    o(lambda h=h: P.op("dve", lambda e: e.tensor_copy(out=QAi[0:64, h, :], in_=QTf[:, h, :]), reads=[QTf], writes=[QAi]))

                def fn_sb(e):
                    ins = None
                    for h in range(4):
                        ins = e.matmul(PSB[:, 32 * h:32 * h + 32], lhsT=QTf[:, h, :], rhs=KM[:], start=True, stop=True)
                    return ins
                o(lambda: P.op("pe", fn_sb, reads=[QTf, KM], writes=[PSB]))
                e01 = EL[:, 0, i:i + 1, :].to_broadcast([128, 4, 32])
                eng_ = EL[:, 1, i:i + 1, :].to_broadcast([128, 4, 32])
                own = EL[:, 2, i:i + 1, :].to_broadcast([128, 4, 32])
                o(lambda: P.op("dve", lambda e: e.tensor_tensor(out=spd[:], in0=PSB[:, :].rearrange("p (a c) -> p a c", a=4),
                                                                in1=e01, op=ALU.mult), reads=[PSB, EL], writes=[spd]))
                o(lambda: P.op("dve", lambda e: e.tensor_tensor(out=spd[:], in0=spd[:], in1=eng_, op=ALU.add),
                               reads=[spd, EL], writes=[spd]))
                for h in range(4):
                    o(lambda h=h: P.op("dve", lambda e: e.max(out=mx[:, h, :], in_=spd[:, h, :]), reads=[spd], writes=[mx]))
                o(lambda: P.op("dve", lambda e: e.tensor_tensor(out=sel[:], in0=spd[:], in1=mx[:, :, 2:3].to_broadcast([128, 4, 32]),
                                                                op=ALU.is_ge), reads=[spd, mx], writes=[sel]))
                o(lambda: P.op("dve", lambda e: e.tensor_tensor(out=sel[:], in0=sel[:], in1=e01, op=ALU.mult),
                               reads=[sel, EL], writes=[sel]))
                o(lambda: P.op("dve", lambda e: e.tensor_tensor(out=sel[:], in0=sel[:], in1=own, op=ALU.add),
                               reads=[sel, EL], writes=[sel]))
                o(lambda: P.op("dve", lambda e: e.tensor_scalar(out=MBP[:, :, 64:96], in0=sel[:], scalar1=-1.0, scalar2=BIG,
                                                                op0=ALU.add, op1=ALU.mult), reads=[sel], writes=[MBP]))
                for h in range(4):
                    o(lambda h=h: P.op("pe", lambda e: e.matmul(PMB[:, :], lhsT=MBP[:, h, :], rhs=idn[:], start=True, stop=True),
                                       reads=[MBP, idn], writes=[PMB]))
                    o(lambda h=h: P.op("act", lambda e: e.copy(out=QAi[64:96, h, :], in_=PMB[64:96, :]), reads=[PMB], writes=[QAi]))
                return ops

            def tile_loop(g, i, fillers):
                QAi = QA2[i % 2]
                po = PO[i % 2]
                nkt = 4 * i + 4
                base = npsc[0]
                npsc[0] += nkt

                def emit_s(kt):
                    pS = PS[(base + kt) % 3]

                    def fn_s(e):
                        ins = e.matmul(pS[:, :], lhsT=KTA[0:96, 128 * kt:128 * kt + 128],
                                       rhs=QAi[:].rearrange("p a c -> p (a c)"), start=True, stop=(kt < 4 * i))
                        if kt >= 4 * i:
                            ins = e.matmul(pS[:, :], lhsT=idnb[:], rhs=CM[:, kt - 4 * i, :], start=False, stop=True)
                        return ins
                    P.op("pe", fn_s, reads=[KTA, KTAb[kt // 4], QAi, idnb, CM], writes=[pS])

                per = max(1, -(-len(fillers) // max(1, nkt - 2)))
                emit_s(0)
                for kt in range(nkt):
                    if kt + 1 < nkt:
                        emit_s(kt + 1)
                    pS = PS[(base + kt) % 3]
                    pT = PTb[(base + kt) % 3]
                    P.op("act", lambda e: e.activation(out=pT[:], in_=pS[:, :], func=AF.Exp, scale=SCALE), reads=[pS], writes=[pT])
                    P.op("pe", lambda e: e.matmul(po[:, :], lhsT=VA[:, kt, :], rhs=pT[:], start=(kt == 0), stop=(kt == nkt - 1)),
                         reads=[VA, VAb[kt // 4], pT] + ([po] if kt > 0 else []), writes=[po])
                    for _ in range(per):
                        if fillers:
                            fillers.pop(0)()
                while fillers:
                    fillers.pop(0)()
                szbi = szb2[i % 2]
                P.op("dve", lambda e: e.tensor_copy(out=Osb[:], in_=po[:, :]), reads=[po], writes=[Osb])

                def fn_ot(e):
                    ins = None
                    for h in range(4):
                        ins = tr(e, POT[:, 65 * h:65 * h + 65], Osb[:, 128 * h:128 * h + 128], idn[0:65, 0:65])
                    return ins
                P.op("pe", fn_ot, reads=[Osb, idn], writes=[POT])
                pot3 = POT[:, 0:260].rearrange("p (a c) -> p a c", a=4)
                P.op("dve", lambda e: e.reciprocal(out=rl[:], in_=pot3[:, :, 64]), reads=[POT], writes=[rl])
                P.op("dve", lambda e: e.tensor_tensor(out=ob[:], in0=pot3[:, :, 0:64],
                                                      in1=rl[:].rearrange("p (a c) -> p a c", c=1).to_broadcast([128, 4, 64]),
                                                      op=ALU.mult), reads=[POT, rl], writes=[ob])
                P.op("dve", lambda e: e.tensor_tensor(out=ob2[:], in0=ob[:].rearrange("p a c -> p (a c)"), in1=szbi[:],
                                                      op=ALU.mult), reads=[ob, szbi], writes=[ob2])

                def fn_obt(e):
                    tr(e, POBT[:, 0:128], ob2[:, 0:128], idn[:])
                    return tr(e, POBT[:, 128:256], ob2[:, 128:256], idn[:])
                P.op("pe", fn_obt, reads=[ob2, idn], writes=[POBT])
                P.op("act", lambda e: e.copy(out=OBT[:, 2 * g:2 * g + 2, 128 * i:128 * i + 128],
                                             in_=POBT[:, 0:256].rearrange("p (a c) -> p a c", a=2)), reads=[POBT], writes=[OBT])

            for g in range(NG if 'a' in PASSES else 0):
                load_w(Wkv, 0, w_in[:, C_K + 64 * g:C_K + 64 * g + 64], 64)
                load_w(Wkv, 64, w_in[:, C_V + 64 * g:C_V + 64 * g + 64], 64)
                load_w(Wqz, 0, w_in[:, C_Q + 256 * g:C_Q + 256 * g + 256], 256)
                load_w(Wqz, 256, w_in[:, C_ZB + 256 * g:C_ZB + 256 * g + 256], 256)
                P.op("pool", lambda e: e.memset(KM[:], 0.0), writes=[KM])
                for f in common_ops(g, 0) + head_ops(g, 0):
                    f()
                for i in range(NI):
                    nxt = (common_ops(g, i + 1) + head_ops(g, i + 1)) if i + 1 < NI else []
                    tile_loop(g, i, nxt)

        P.barrier()
        if DBG:
            P.dma("sp", dbg_ya, YAT[:].rearrange("p a c -> p (a c)"), YAT, reads=[YAT], is_out=True)
            P.dma("sp", dbg_ob, OBT[:].rearrange("p a c -> p (a c)"), OBT, reads=[OBT], is_out=True)
        with contextlib.ExitStack() as st:
            alloc_io(st, "c", nws=1, nxb=1)
            Wg = P.sb("Wg", [128, 8, 2048], BF16, st)
            Wa = P.sb("Wa", [128, 8, 1024], BF16, st)
            Wb = P.sb("Wb", [128, 8, 1024], BF16, st)
            Wo = P.sb("Wo", [128, 8, 1024], BF16, st)
            gbc = P.sb("gbc", [128, 1024], F32, st)
            bbc = P.sb("bbc", [128, 1024], F32, st)
            XM = XB[0]
            sga = P.sb("sga", [128, 512], F32, st)
            sgb = P.sb("sgb", [128, 512], F32, st)
            t1 = P.sb("t1", [128, 512], F32, st)
            mixT = P.sb("mixT", [128, 8, 512], BF16, st)
            xres = [P.sb("xres%d" % i, [128, 1024], F32, st) for i in range(2)]
            pre = P.sb("pre", [128, 1024], F32, st)
            stats = P.sb("stats", [128, 2, 6], F32, st)
            mv = P.sb("mv", [128, 2], F32, st)
            rs2 = P.sb("rs2", [128, 1], F32, st)
            PGa, PGb, PBa, PBb = [P.view("PG%d" % i, banks[i], banks[i][:, :]) for i in range(4)]
            POa = [P.view("POa%d" % i, banks[4 + i], banks[4 + i][:, :]) for i in range(2)]

            for j in range(4):
                load_w(Wg, 512 * j, w_in[:, C_GA + 512 * j:C_GA + 512 * j + 512], 512)
            for j in range(2):
                load_w(Wa, 512 * j, w_a[:, 512 * j:512 * j + 512], 512)
                load_w(Wb, 512 * j, w_b[:, 512 * j:512 * j + 512], 512)
                load_w(Wo, 512 * j, w_o[:, 512 * j:512 * j + 512], 512)
            P.dma("sp", gbc[:], lng.partition_broadcast(128).rearrange("p a c -> p (a c)"), gbc, writes=[gbc])
            P.dma("sp", bbc[:], lnb.partition_broadcast(128).rearrange("p a c -> p (a c)"), bbc, writes=[bbc])
            NJ = (NI + 3) // 4 if 'm' in PASSES else 0
            for j in range(NJ):
                s = WS[stg[0] % len(WS)]
                stg[0] += 1
                P.dma("sp", s[:], xmT[:, 512 * j:512 * j + 512].rearrange("(kc p) t -> p kc t", p=128), s, writes=[s])
                P.op("pool", lambda e, s=s: e.tensor_copy(out=XM[:], in_=s[:]), reads=[s], writes=[XM])
                ts = slice(512 * j, 512 * j + 512)
                for mc in range(8):
                    ms = slice(128 * mc, 128 * mc + 128)
                    P.op("pe", mm_chain(PGa[:, :], lambda kc, ms=ms: Wg[:, kc, ms], lambda kc: XM[:, kc, :]),
                         reads=[Wg, XM], writes=[PGa])
                    P.op("pe", mm_chain(PGb[:, :], lambda kc, mc=mc: Wg[:, kc, 1024 + 128 * mc:1024 + 128 * mc + 128],
                                        lambda kc: XM[:, kc, :]), reads=[Wg, XM], writes=[PGb])
                    P.op("pe", mm_chain(PBa[:, :], lambda kc, ms=ms: Wa[:, kc, ms], lambda kc, ts=ts: YAT[:, kc, ts]),
                         reads=[Wa, YAT], writes=[PBa])
                    P.op("pe", mm_chain(PBb[:, :], lambda kc, ms=ms: Wb[:, kc, ms], lambda kc, ts=ts: OBT[:, kc, ts]),
                         reads=[Wb, OBT], writes=[PBb])
                    P.op("act", lambda e: e.activation(out=sga[:], in_=PGa[:, :], func=AF.Sigmoid), reads=[PGa], writes=[sga])
                    P.op("act", lambda e: e.activation(out=sgb[:], in_=PGb[:, :], func=AF.Sigmoid), reads=[PGb], writes=[sgb])
                    P.op("dve", lambda e: e.tensor_tensor(out=t1[:], in0=sga[:], in1=PBa[:, :], op=ALU.mult),
                         reads=[sga, PBa], writes=[t1])
                    P.op("dve", lambda e: e.tensor_tensor(out=sgb[:], in0=sgb[:], in1=PBb[:, :], op=ALU.mult),
                         reads=[sgb, PBb], writes=[sgb])
                    P.op("pool", lambda e, mc=mc: e.tensor_tensor(out=mixT[:, mc, :], in0=t1[:], in1=sgb[:], op=ALU.add),
                         reads=[t1, sgb], writes=[mixT])
                for k in range(4):
                    c = 4 * j + k
                    if c >= NI:
                        break
                    xr = xres[c % 2]
                    y_ = pre
                    P.dma("sp", xr[:], xown[128 * c:128 * c + 128, :], xr, writes=[xr])
                    for half in range(2):
                        po = POa[half]
                        P.op("pe", mm_chain(po[:, :], lambda kc, k=k: mixT[:, kc, 128 * k:128 * k + 128],
                                            lambda kc, half=half: Wo[:, kc, 512 * half:512 * half + 512]),
                             reads=[mixT, Wo], writes=[po])
                        P.op("dve", lambda e, half=half, po=po, xr=xr: e.scalar_tensor_tensor(
                            out=pre[:, 512 * half:512 * half + 512], in0=xr[:, 512 * half:512 * half + 512], scalar=ALPHA,
                            in1=po[:, :], op0=ALU.mult, op1=ALU.add), reads=[xr, po], writes=[pre])
                    for half in range(2):
                        P.op("dve", lambda e, half=half: e.bn_stats(out=stats[:, half, :], in_=pre[:, 512 * half:512 * half + 512]),
                             reads=[pre], writes=[stats])
                    P.op("dve", lambda e: e.bn_aggr(out=mv[:], in_=stats[:].rearrange("p a c -> p (a c)")), reads=[stats], writes=[mv])
                    P.op("act", lambda e: e.activation(out=rs2[:], in_=mv[:, 1:2], func=AF.Sqrt, bias=epsb[:, 0:1]),
                         reads=[mv, epsb], writes=[rs2])
                    P.op("dve", lambda e: e.reciprocal(out=rs2[:], in_=rs2[:]), reads=[rs2], writes=[rs2])
                    P.op("dve", lambda e, y_=y_: e.tensor_scalar(out=y_[:], in0=pre[:], scalar1=mv[:, 0:1], scalar2=rs2[:, 0:1],
                                                                 op0=ALU.subtract, op1=ALU.mult), reads=[pre, mv, rs2], writes=[y_])
                    P.op("pool", lambda e, y_=y_: e.tensor_tensor(out=y_[:], in0=y_[:], in1=gbc[:], op=ALU.mult),
                         reads=[y_, gbc], writes=[y_])
                    P.op("pool", lambda e, y_=y_: e.tensor_tensor(out=y_[:], in0=y_[:], in1=bbc[:], op=ALU.add),
                         reads=[y_, bbc], writes=[y_])
                    P.dma("sp", y_own[128 * c:128 * c + 128, :], y_[:], y_, reads=[y_], is_out=True)

        P.emit()
    return nc


def _host_inputs(x_prompt, w_in, conv_w, conv_b, dt_bias, a_log, d_skip, ssm_norm_w, w_a_out, w_b_out, w_out,
                 ln_g, ln_b, x_sample, cache_k, cache_v, state_conv, state_ssm, page_table):
    f32 = np.float32
    pos = np.arange(SEQ, dtype=np.float32)
    inv = np.power(np.float32(10000.0), -np.arange(32, dtype=np.float32) * np.float32(2.0) / np.float32(64.0)).astype(f32)
    ang = (pos[:, None] * inv[None, :]).astype(f32)
    cs_all = np.concatenate([np.cos(ang), np.sin(ang)], axis=1).astype(f32)
    tri = np.triu(np.ones((128, 128), f32))
    idn = np.eye(128, dtype=f32)
    kbi = np.zeros((32, SEQ), f32)
    for b in range(32):
        kbi[b, 256 * b:256 * b + 256] = 1.0
    common = {
        "w_in": np.ascontiguousarray(w_in[0]), "w_a": np.ascontiguousarray(w_a_out[0]),
        "w_b": np.ascontiguousarray(w_b_out[0]), "w_o": np.ascontiguousarray(w_out[0]),
        "cwT": np.ascontiguousarray(conv_w[0].T), "cbT": np.ascontiguousarray(conv_b[0][:, None]),
        "vec16": np.stack([dt_bias[0], a_log[0], d_skip[0]]).astype(f32),
        "nwv": np.ascontiguousarray(ssm_norm_w), "lng": np.ascontiguousarray(ln_g), "lnb": np.ascontiguousarray(ln_b),
        "tri": tri, "idn": idn, "cs_all": cs_all, "kbi": kbi,
        "cwrow": np.ascontiguousarray(conv_w[0]), "cbrow": np.ascontiguousarray(conv_b[0][None, :]),
        "ck_d": cache_k[0].reshape(2560 * 128, 256), "cv_d": cache_v[0].reshape(2560 * 128, 256),
        "iota_d": np.arange(128, dtype=f32)[:, None],
    }
    sel = np.zeros((NS, NS, 128), f32)
    selt = np.zeros((128, NS, NS), f32)
    for n in range(NS):
        sel[n, n, :] = 1.0
        selt[:, n, n] = 1.0
    angs = (np.float32(PAST) * inv).astype(f32)
    common["sel_d"] = sel.reshape(NS, NS * 128)
    common["selt_d"] = selt.reshape(128, NS * NS)
    common["css_d"] = np.concatenate([np.cos(angs), np.sin(angs)])[None, :].astype(f32)
    maps = []
    for c in range(NCORE):
        s, r = c // 4, c % 4
        xs = x_prompt[s]
        xT = np.ascontiguousarray(xs.T)
        own_tok = np.concatenate([np.arange(128 * (4 * i + r), 128 * (4 * i + r) + 128) for i in range(16)])
        xo = np.zeros((16, 131, D), f32)
        for i in range(16):
            t0 = 128 * (4 * i + r)
            lo = max(t0 - 3, 0)
            xo[i, 131 - (t0 + 128 - lo):] = xs[lo:t0 + 128]
        xoT = np.ascontiguousarray(xo.reshape(16 * 131, D).T)
        xown = np.ascontiguousarray(xs[own_tok])
        oh = np.zeros((128, 4), f32)
        oh[:, r] = 1.0
        cm = np.zeros((128, 4, 4, 128), f32)
        diag = np.where(np.arange(128)[:, None] <= np.arange(128)[None, :], 0.0, -BIG).astype(f32)
        for k in range(4):
            if k > r:
                cm[:, k] = -BIG
            elif k == r:
                cm[:, k] = diag[:, None, :]
        el = np.zeros((3, 16, 32), f32)
        for i in range(16):
            qblk = (4 * i + r) // 2
            el[0, i, :qblk] = 1.0
            el[1, i, qblk:] = -1e30
            el[2, i, qblk] = 1.0
        m = dict(common)
        ts = slice(NS * c, NS * c + NS)
        m.update({"xsT": np.ascontiguousarray(x_sample[ts, 0, :].T), "xs_tok": np.ascontiguousarray(x_sample[ts, 0, :]),
                  "sc_d": np.ascontiguousarray(state_conv[0, ts].reshape(NS, 3 * 2048)),
                  "ssm_d": np.ascontiguousarray(state_ssm[0, ts].reshape(NS, 1024 * 128)),
                  "pt_d": np.ascontiguousarray(page_table[ts].reshape(1, NS * NPG)).astype(np.int32)})
        m.update({"xT": xT, "xoT": xoT, "xmT": np.ascontiguousarray(xown.T), "xown": xown, "oh": oh,
                  "cs_own": np.ascontiguousarray(cs_all[own_tok]), "cm": cm.reshape(128, 2048),
                  "el": el.reshape(3, 512)})
        maps.append(m)
    return maps


_NC_CACHE = {}


def kernel(x_prompt, x_sample, cache_k, cache_v, state_conv, state_ssm, page_table,
           w_in, conv_w, conv_b, dt_bias, a_log, d_skip, ssm_norm_w, w_a_out, w_b_out, w_out, ln_g, ln_b):
    NI = int(os.environ.get("MK_NI", "16"))
    args = [np.asarray(a) for a in (x_prompt, w_in, conv_w, conv_b, dt_bias, a_log, d_skip, ssm_norm_w,
                                    w_a_out, w_b_out, w_out, ln_g, ln_b, x_sample, cache_k, cache_v,
                                    state_conv, state_ssm, page_table)]
    maps = _host_inputs(*args)
    key = (NI, os.environ.get("MK_NOSAMPLE"), os.environ.get("MK_PASSES"))
    if key not in _NC_CACHE:
        _NC_CACHE[key] = build(NI=NI)
    nc = _NC_CACHE[key]
    nosample = bool(os.environ.get("MK_NOSAMPLE"))
    if nosample:
        for m in maps:
            m.pop("ck_d")
            m.pop("cv_d")
    res = run_bass_kernel_spmd(nc, maps, core_ids=list(range(NCORE))).results
    if os.environ.get("MK_DBG"):
        global _DBG_RES
        _DBG_RES = res
    y_prompt = np.zeros((2, SEQ, D), np.float32)
    for c in range(NCORE):
        s, r = c // 4, c % 4
        yo = res[c]["y_own"].reshape(16, 128, D)
        y_prompt[s].reshape(16, 4, 128, D)[:, r] = yo
    k_prompt = np.stack([res[0]["kp_o"], res[4]["kp_o"]]).reshape(1, 2, SEQ, 4, 64)
    v_prompt = np.stack([res[0]["vp_o"], res[4]["vp_o"]]).reshape(1, 2, SEQ, 4, 64)
    conv_prompt = np.zeros((1, 2, 3, 2048), np.float32)
    ssm_prompt = np.zeros((1, 2, 16, 64, 128), np.float32)
    for s in range(2):
        rr = res[4 * s + 3]
        cva = rr["cvA_o"]
        cvc = rr["cvC_o"]
        for g in range(4):
            conv_prompt[0, s, :, 256 * g:256 * g + 128] = cva[g, :, 0, :].T
            conv_prompt[0, s, :, 256 * g + 128:256 * g + 256] = cva[g, :, 1, :].T
            conv_prompt[0, s, :, 1024 + 128 * g:1024 + 128 * g + 128] = cva[g, :, 2, :].T
            conv_prompt[0, s, :, 1536 + 128 * g:1536 + 128 * g + 128] = cvc[g].T
            hs = rr["ssm_o"][g].reshape(128, 4, 64)
            ssm_prompt[0, s, 4 * g:4 * g + 4] = hs.transpose(1, 2, 0)
    if nosample:
        return y_prompt, k_prompt, v_prompt, conv_prompt, ssm_prompt
    y_sample = np.concatenate([res[c]["ys_o"] for c in range(NCORE)]).reshape(128, 1, D)
    k_sample = np.concatenate([res[c]["ks_o"] for c in range(NCORE)]).reshape(1, 128, 1, 4, 64)
    v_sample = np.concatenate([res[c]["vs_o"] for c in range(NCORE)]).reshape(1, 128, 1, 4, 64)
    conv_sample = np.concatenate([res[c]["cvs_o"] for c in range(NCORE)]).reshape(1, 128, 3, 2048)
    ssm_sample = np.concatenate([res[c]["ssms_o"] for c in range(NCORE)]).reshape(1, 128, 16, 64, 128)
    return (y_prompt, y_sample, k_prompt, v_prompt, k_sample, v_sample, conv_prompt, conv_sample,
            ssm_prompt, ssm_sample)
```
